# Optimizing a Trainium2 kernel written in Bass

```python
import jax, jax.numpy as jnp
from jax import lax
import numpy as np

D_MODEL = 1024
BATCH = 8
SEQ = 4096
DEPTH = 2

GRID_W = 64
D_MIX = D_MODEL
D_MLSTM = D_MIX // 2
D_NA = D_MIX - D_MLSTM
MLSTM_HEADS = 4
MLSTM_HD = D_MLSTM // MLSTM_HEADS
NA_HEADS = 8
NA_HD = D_NA // NA_HEADS
CHUNK = 64
CONV_K = 3
NA_KH_MAX = 8
NA_KW = 16
D_FF = 4 * D_MODEL
N_GATES = 4 * MLSTM_HEADS
D_IN = 4 * D_MLSTM + N_GATES + 3 * D_NA
EPS = 1e-6

kernel_name = "hybrid_mlstm_natten_encoder"


def rms_norm(x, w):
    xf = x.astype(jnp.float32)
    y = xf * lax.rsqrt(jnp.mean(xf * xf, axis=-1, keepdims=True) + EPS)
    return (y * w.astype(jnp.float32)).astype(x.dtype)


def centered_dwconv(x, w, b):
    c = x.shape[-1]
    y = lax.conv_general_dilated(
        x, w[:, None, :].astype(x.dtype), window_strides=(1,),
        padding=[(CONV_K // 2, CONV_K // 2)],
        dimension_numbers=("NWC", "WIO", "NWC"), feature_group_count=c)
    return y + b.astype(x.dtype)


def mlstm_chunkwise(q, k, v, i_pre, f_pre):
    bsz, nh, s, dh = q.shape
    nc = s // CHUNK
    q = q.reshape(bsz, nh, nc, CHUNK, dh)
    k = k.reshape(bsz, nh, nc, CHUNK, dh) * (dh ** -0.5)
    v = v.reshape(bsz, nh, nc, CHUNK, dh)
    logf = jax.nn.log_sigmoid(f_pre).reshape(bsz, nh, nc, CHUNK)
    ig = i_pre.reshape(bsz, nh, nc, CHUNK)
    b = jnp.cumsum(logf, axis=-1)
    total = b[..., -1]

    a = total[..., None] - b + ig
    m_loc = jnp.max(a, axis=-1)
    wa = jnp.exp(a - m_loc[..., None])
    c_loc = jnp.einsum("bhnlv,bhnlk->bhnvk", wa[..., None] * v, k)
    n_loc = jnp.einsum("bhnl,bhnlk->bhnk", wa, k)

    def step(carry, inp):
        c_st, n_st, m_st = carry
        tot, ml, cl, nl = inp
        m_new = jnp.maximum(tot + m_st, ml)
        s_old = jnp.exp(tot + m_st - m_new)
        s_loc = jnp.exp(ml - m_new)
        c_new = s_old[..., None, None] * c_st + s_loc[..., None, None] * cl
        n_new = s_old[..., None] * n_st + s_loc[..., None] * nl
        return (c_new, n_new, m_new), (c_st, n_st, m_st)

    init = (jnp.zeros((bsz, nh, dh, dh), jnp.float32),
            jnp.zeros((bsz, nh, dh), jnp.float32),
            jnp.zeros((bsz, nh), jnp.float32))
    xs = (jnp.moveaxis(total, 2, 0), jnp.moveaxis(m_loc, 2, 0),
          jnp.moveaxis(c_loc, 2, 0), jnp.moveaxis(n_loc, 2, 0))
    _, (c_prev, n_prev, m_prev) = lax.scan(step, init, xs)
    c_prev = jnp.moveaxis(c_prev, 0, 2)
    n_prev = jnp.moveaxis(n_prev, 0, 2)
    m_prev = jnp.moveaxis(m_prev, 0, 2)

    tri = jnp.tril(jnp.ones((CHUNK, CHUNK), dtype=bool))
    dmat = b[..., :, None] - b[..., None, :] + ig[..., None, :]
    dmat = jnp.where(tri, dmat, -jnp.inf)
    inter = b + m_prev[..., None]
    m = jnp.maximum(inter, jnp.max(dmat, axis=-1))
    w_inter = jnp.exp(inter - m)
    p = jnp.exp(dmat - m[..., None]) * jnp.einsum("bhnjd,bhnsd->bhnjs", q, k)
    num = (w_inter[..., None] * jnp.einsum("bhnvk,bhnjk->bhnjv", c_prev, q)
           + jnp.einsum("bhnjs,bhnsv->bhnjv", p, v))
    den = w_inter * jnp.einsum("bhnk,bhnjk->bhnj", n_prev, q) + jnp.sum(p, axis=-1)
    h = num / jnp.maximum(jnp.abs(den), jnp.exp(-m))[..., None]
    return h.reshape(bsz, nh, s, dh)


def mlstm_mixer(q, k, v, o_pre, gates, gate_b, norm_w):
    bsz, s, _ = v.shape
    f32 = jnp.float32

    def heads(t):
        return t.astype(f32).reshape(bsz, s, MLSTM_HEADS, MLSTM_HD).transpose(0, 2, 1, 3)

    qh, kh, vh = heads(q), heads(k), heads(v)
    g = (gates.astype(f32) + gate_b.astype(f32)).reshape(bsz, s, 4, MLSTM_HEADS)
    g = g.transpose(2, 0, 3, 1)
    h_fwd = mlstm_chunkwise(qh, kh, vh, g[0], g[1])

    def flip(t):
        return jnp.flip(t, axis=2)

    h_bwd = flip(mlstm_chunkwise(flip(qh), flip(kh), flip(vh), flip(g[2]), flip(g[3])))
    h = h_fwd + h_bwd
    h = h * lax.rsqrt(jnp.mean(h * h, axis=-1, keepdims=True) + EPS)
    h = h.transpose(0, 2, 1, 3).reshape(bsz, s, D_MLSTM) * norm_w.astype(f32)
    return (jax.nn.sigmoid(o_pre.astype(f32)) * h).astype(v.dtype)


def neighbourhood_attention(q, k, v, rpb):
    bsz, s, _ = q.shape
    rows = s // GRID_W
    kh = min(NA_KH_MAX, rows)

    def grid(t):
        return t.reshape(bsz, rows, GRID_W, NA_HEADS, NA_HD)

    qg = grid(q) * (NA_HD ** -0.5)
    kg, vg = grid(k), grid(v)
    cols = np.arange(GRID_W)
    c_start = np.clip(cols - NA_KW // 2, 0, GRID_W - NA_KW)
    col_idx = c_start[:, None] + np.arange(NA_KW)[None, :]
    rel_c = col_idx - cols[:, None] + NA_KW - 1
    rpb_c = rpb[:, :, rel_c]

    def row_block(args):
        r, q_row = args
        rs = jnp.clip(r - kh // 2, 0, rows - kh)
        k_band = lax.dynamic_slice_in_dim(kg, rs, kh, axis=1)
        v_band = lax.dynamic_slice_in_dim(vg, rs, kh, axis=1)
        k_win = k_band[:, :, col_idx]
        v_win = v_band[:, :, col_idx]
        rel_r = rs + jnp.arange(kh) - r + NA_KH_MAX - 1
        bias = rpb_c[:, rel_r].transpose(0, 2, 1, 3)
        sc = jnp.einsum("bchd,brcwhd->bhcrw", q_row, k_win).astype(jnp.float32)
        sc = sc + bias[None].astype(jnp.float32)
        p = jax.nn.softmax(sc.reshape(bsz, NA_HEADS, GRID_W, kh * NA_KW), axis=-1)
        p = p.reshape(bsz, NA_HEADS, GRID_W, kh, NA_KW).astype(v.dtype)
        return jnp.einsum("bhcrw,brcwhd->bchd", p, v_win)

    out = lax.map(row_block, (jnp.arange(rows), jnp.moveaxis(qg, 1, 0)))
    return jnp.moveaxis(out, 0, 1).reshape(bsz, s, D_NA)


def setup_inputs(seed: int = 0) -> dict:
    key = jax.random.key(seed)
    ks = jax.random.split(key, 16)

    def nrm(k, shape, scale):
        return jax.random.normal(k, shape, jnp.float32) * scale

    x = nrm(ks[0], (BATCH, SEQ, D_MODEL), 1.0)
    norm1_w = 1.0 + nrm(ks[1], (DEPTH, D_MODEL), 0.02)
    w_in = nrm(ks[2], (DEPTH, D_MODEL, D_IN), D_MODEL ** -0.5)
    conv_w = nrm(ks[3], (DEPTH, CONV_K, 2 * D_MLSTM), CONV_K ** -0.5)
    conv_b = nrm(ks[4], (DEPTH, 2 * D_MLSTM), 0.02)
    f_bias = jnp.linspace(3.0, 6.0, MLSTM_HEADS, dtype=jnp.float32)
    i_bias = jnp.zeros((MLSTM_HEADS,), jnp.float32)
    base = jnp.concatenate([i_bias, f_bias, i_bias, f_bias])
    gate_b = base[None, :] + nrm(ks[5], (DEPTH, N_GATES), 0.1)
    mlstm_norm_w = 1.0 + nrm(ks[6], (DEPTH, D_MLSTM), 0.02)
    rpb = nrm(ks[7], (DEPTH, NA_HEADS, 2 * NA_KH_MAX - 1, 2 * NA_KW - 1), 0.1)
    w_out = nrm(ks[8], (DEPTH, D_MIX, D_MODEL), D_MIX ** -0.5)
    norm2_w = 1.0 + nrm(ks[9], (DEPTH, D_MODEL), 0.02)
    w_ff1 = nrm(ks[10], (DEPTH, D_MODEL, D_FF), D_MODEL ** -0.5)
    w_ff2 = nrm(ks[11], (DEPTH, D_FF, D_MODEL), D_FF ** -0.5)
    final_norm_w = 1.0 + nrm(ks[12], (D_MODEL,), 0.02)
    return {"x": x, "norm1_w": norm1_w, "w_in": w_in, "conv_w": conv_w, "conv_b": conv_b,
            "gate_b": gate_b, "mlstm_norm_w": mlstm_norm_w, "rpb": rpb, "w_out": w_out,
            "norm2_w": norm2_w, "w_ff1": w_ff1, "w_ff2": w_ff2, "final_norm_w": final_norm_w}


def reference(x, norm1_w, w_in, conv_w, conv_b, gate_b, mlstm_norm_w, rpb, w_out,
              norm2_w, w_ff1, w_ff2, final_norm_w):
    splits = np.cumsum([D_MLSTM, D_MLSTM, D_MLSTM, D_MLSTM, N_GATES, D_NA, D_NA]).tolist()
    for l in range(DEPTH):
        h = rms_norm(x, norm1_w[l])
        proj = h @ w_in[l]
        q_m, k_m, v_m, o_m, gates, q_n, k_n, v_n = jnp.split(proj, splits, axis=-1)
        qk = jax.nn.silu(centered_dwconv(jnp.concatenate([q_m, k_m], axis=-1), conv_w[l], conv_b[l]))
        q_m, k_m = qk[..., :D_MLSTM], qk[..., D_MLSTM:]
        y_m = mlstm_mixer(q_m, k_m, v_m, o_m, gates, gate_b[l], mlstm_norm_w[l])
        y_n = neighbourhood_attention(q_n, k_n, v_n, rpb[l])
        y = jnp.concatenate([y_m.astype(x.dtype), y_n.astype(x.dtype)], axis=-1)
        x = x + y @ w_out[l]
        h = rms_norm(x, norm2_w[l])
        x = x + jnp.square(jax.nn.relu(h @ w_ff1[l])) @ w_ff2[l]
    return rms_norm(x, final_norm_w)
```

```python
import numpy as np
from contextlib import ExitStack
import concourse.bass as bass
import concourse.mybir as mybir
from concourse.bass_utils import run_bass_kernel_spmd

F32 = mybir.dt.float32
BF16 = mybir.dt.bfloat16
AF = mybir.ActivationFunctionType
ALU = mybir.AluOpType
AX = mybir.AxisListType

T = 4096
D = 1024
DEPTH = 2
DIN = 3600
DFF = 4096
NB = 8
TB = 512
NCH = 32
EPS = 1e-6
ALPHA = 128.0 ** -0.5

ENGS = ("pe", "act", "dve", "pool", "sp")
EPOCH = 4000
NDMASEM = 20
SAME_SYNC = True


class Op:
    __slots__ = ("eng", "fn", "deps", "dma", "idx", "sig", "semk", "val")

    def __init__(self, eng, fn, dma):
        self.eng = eng
        self.fn = fn
        self.dma = dma
        self.deps = set()
        self.sig = False
        self.semk = None
        self.val = 0


class Ctx:
    def __init__(self, nc):
        self.nc = nc
        self.sems = {}
        self.ccount = {e: 0 for e in ENGS}
        self.dval = {}
        self.dk = {e: 0 for e in ENGS}
        self.waited = {e: {} for e in ENGS}
        self.nops = 0
        self.nwaits = 0

    def sem(self, k):
        if k not in self.sems:
            self.sems[k] = self.nc.alloc_semaphore("s_" + "_".join(str(x) for x in k))
        return self.sems[k]


class Prog:
    def __init__(self, ctx):
        self.ctx = ctx
        self.nc = ctx.nc
        self.ops = []
        self.last_w = {}
        self.readers = {}

    def op(self, eng, fn, reads=(), writes=(), dma=False):
        o = Op(eng, fn, dma)
        o.idx = len(self.ops)
        for r in reads:
            w = self.last_w.get(r)
            if w is not None:
                o.deps.add(w)
        for w_ in writes:
            w = self.last_w.get(w_)
            if w is not None:
                o.deps.add(w)
            for rd in self.readers.get(w_, ()):
                o.deps.add(rd)
        o.deps.discard(o.idx)
        for r in reads:
            self.readers.setdefault(r, []).append(o.idx)
        for w_ in writes:
            self.last_w[w_] = o.idx
            self.readers[w_] = []
        self.ops.append(o)
        return o

    def pe(self, fn, reads=(), writes=()):
        return self.op("pe", fn, reads, writes)

    def act(self, fn, reads=(), writes=()):
        return self.op("act", fn, reads, writes)

    def dve(self, fn, reads=(), writes=()):
        return self.op("dve", fn, reads, writes)

    def pool(self, fn, reads=(), writes=()):
        return self.op("pool", fn, reads, writes)

    def dma(self, fn, reads=(), writes=(), q="sp"):
        return self.op(q, fn, reads, writes, dma=True)

    def _unsynced(self, od, o):
        return (od.eng == o.eng and not od.dma and not o.dma
                and (od.eng == "pe" or not SAME_SYNC))

    def emit(self):
        ctx = self.ctx
        nc = self.nc
        ops = self.ops
        per = {e: [o for o in ops if o.eng == e] for e in ENGS}
        for o in ops:
            for d in o.deps:
                od = ops[d]
                if od.dma or self._unsynced(od, o):
                    continue
                od.sig = True
        for e in ENGS:
            for o in reversed(per[e]):
                if not o.dma:
                    o.sig = True
                    break
        final = {}
        for e in ENGS:
            for o in per[e]:
                if o.dma:
                    i = ctx.dk[e] % NDMASEM
                    ctx.dk[e] += 1
                    k = ("d", e, i)
                    ctx.dval[k] = ctx.dval.get(k, 0) + 16
                    o.semk = k
                    o.val = ctx.dval[k]
                    final[k] = o.val
                elif o.sig:
                    n = ctx.ccount[e]
                    ctx.ccount[e] += 1
                    k = ("c", e, n // EPOCH)
                    o.semk = k
                    o.val = n % EPOCH + 1
                    final[k] = o.val
        for k in final:
            ctx.sem(k)
        ctx.nops += len(ops)

        def run_engine(e, eng):
            waited = ctx.waited[e]

            def wait(k, v):
                if waited.get(k, 0) >= v:
                    return
                waited[k] = v
                eng.wait_ge(ctx.sems[k], v)
                ctx.nwaits += 1

            for o in per[e]:
                need = {}
                for d in o.deps:
                    od = ops[d]
                    if od.semk is None or self._unsynced(od, o):
                        continue
                    if need.get(od.semk, 0) < od.val:
                        need[od.semk] = od.val
                if o.dma and o.val > 16:
                    if need.get(o.semk, 0) < o.val - 16:
                        need[o.semk] = o.val - 16
                for k, v in need.items():
                    wait(k, v)
                ins = o.fn(eng)
                if o.dma:
                    ins.then_inc(ctx.sems[o.semk], 16)
                elif o.sig:
                    ins.then_inc(ctx.sems[o.semk], 1)
            for k, v in final.items():
                if k[0] == "c" and k[1] == e:
                    if e == "pe" or not SAME_SYNC:
                        pass
                wait(k, v)

        with nc.Block() as block:
            block.tensor(lambda eng: run_engine("pe", eng))
            block.scalar(lambda eng: run_engine("act", eng))
            block.vector(lambda eng: run_engine("dve", eng))
            block.gpsimd(lambda eng: run_engine("pool", eng))
            block.sync(lambda eng: run_engine("sp", eng))


def mk(ap, dims, off=0):
    a = ap.ap
    return bass.AP(ap.tensor, ap.offset + off, [list(a[0])] + [list(d) for d in dims])


class Builder:
    def __init__(self, debug=False, stop_after=None):
        self.debug = debug
        self.stop_after = stop_after
        self.nc = bass.Bass("TRN2", target_bir_lowering=False)
        self.ctx = Ctx(self.nc)
        self.dbg_names = []

    def din(self, name, shape, dt=F32):
        return self.nc.dram_tensor(name, list(shape), dt, kind="ExternalInput").ap()

    def dscr(self, name, shape, dt):
        kind = "ExternalOutput" if self.debug else "Internal"
        if self.debug:
            self.dbg_names.append(name)
        return self.nc.dram_tensor(name, list(shape), dt, kind=kind).ap()

    def build(self):
        nc = self.nc
        I = {}
        I["xT"] = self.din("xT", [D, T])
        I["w_in"] = self.din("w_in", [DEPTH, D, DIN])
        I["w_out"] = self.din("w_out", [DEPTH, D, D])
        I["w_ff1"] = self.din("w_ff1", [DEPTH, D, DFF])
        I["w_ff2"] = self.din("w_ff2", [DEPTH, DFF, D])
        I["n1"] = self.din("n1", [128, DEPTH, 8])
        I["n2"] = self.din("n2", [128, DEPTH, 8])
        I["nf"] = self.din("nf", [128, 8])
        I["cw"] = self.din("cw", [128, DEPTH, 3, 8])
        I["cb"] = self.din("cb", [128, DEPTH, 8])
        I["gb"] = self.din("gb", [36, DEPTH, 2])
        I["mnw"] = self.din("mnw", [128, DEPTH, 512])
        I["nbi"] = self.din("nbi", [DEPTH, 128, 8, 5, 128])
        I["nbe"] = self.din("nbe", [DEPTH, 4, 128, 8, 4, 128])
        I["ident"] = self.din("ident", [128, 128])
        I["trif"] = self.din("trif", [128, 128])
        I["trib"] = self.din("trib", [128, 128])
        self.I = I
        self.out = nc.dram_tensor("outT", [D, T], F32, kind="ExternalOutput").ap()
        S = {}
        S["QT"] = self.dscr("QT", [512, T], BF16)
        S["KT"] = self.dscr("KT", [512, T], BF16)
        S["Ktok"] = self.dscr("Ktok", [T, 512], BF16)
        S["Vtok"] = self.dscr("Vtok", [T, 512], BF16)
        S["Osig"] = self.dscr("Osig", [T, 512], F32)
        S["G"] = self.dscr("G", [16, T], F32)
        S["QnT"] = self.dscr("QnT", [512, T], BF16)
        S["KnT"] = self.dscr("KnT", [512, T], BF16)
        S["Vn"] = self.dscr("Vn", [T, 512], BF16)
        S["YT"] = self.dscr("YT", [D, T], BF16)
        S["Mser"] = self.dscr("Mser", [8, T], F32)
        S["DCs"] = self.dscr("DCs", [8, NCH], F32)
        S["xres"] = self.dscr("xres", [D, T], F32)
        if self.debug:
            S["dbgser"] = self.dscr("dbgser", [128, 4, NCH, 8], F32)
        self.S = S
        phases = []
        for l in range(DEPTH):
            phases += [("A", l), ("B", l), ("C", l), ("D", l), ("E", l)]
        for i, (ph, l) in enumerate(phases):
            getattr(self, "phase_" + ph)(l)
            if self.stop_after is not None and i + 1 >= self.stop_after:
                break
        return nc

    def phase_A(self, l):
        nc, I, S = self.nc, self.I, self.S
        P = Prog(self.ctx)
        xsrc = I["xT"] if l == 0 else S["xres"]
        with ExitStack() as es:
            def sb(name, shape, dt):
                return es.enter_context(nc.sbuf_tensor(f"L{l}" + name, list(shape), dt))

            def psum(name, shape, dt=F32):
                return es.enter_context(nc.psum_tensor(f"L{l}" + name, list(shape), dt))

            wbf = sb("A_wbf", [128, 8, DIN], BF16)
            wst = [sb(f"A_wst{i}", [128, DIN], F32) for i in range(2)]
            n1 = sb("A_n1", [128, DEPTH, 8], F32)
            cw = sb("A_cw", [128, DEPTH, 3, 8], F32)
            cb = sb("A_cb", [128, DEPTH, 8], F32)
            cws = sb("A_cws", [128, 3, 8], F32)
            cbs = sb("A_cbs", [128, 8], F32)
            idf = sb("A_idf", [128, 128], F32)
            idb = sb("A_idb", [128, 128], BF16)
            ones = sb("A_ones", [128, 128], BF16)
            epsb = sb("A_epsb", [128, 1], F32)
            xin = [sb(f"A_xin{i}", [128, 8, 514], F32) for i in range(2)]
            xsq = sb("A_xsq", [128, 8, 514], BF16)
            rstd = sb("A_rstd", [128, 514], F32)
            hn = sb("A_hn", [128, 8, 514], BF16)
            pre = [sb(f"A_pre{i}", [128, 514], F32) for i in range(2)]
            acc = [sb(f"A_acc{i}", [128, 512], F32) for i in range(2)]
            sg = [sb(f"A_sg{i}", [128, 512], F32) for i in range(2)]
            qko = [sb(f"A_qko{i}", [128, 512], BF16) for i in range(3)]
            ktok = sb("A_ktok", [128, 4, 512], BF16)
            vtok = sb("A_vtok", [128, 4, 512], BF16)
            vntok = sb("A_vntok", [128, 4, 512], BF16)
            osig = sb("A_osig", [128, 4, 512], F32)
            gsb = sb("A_gsb", [16, 512], F32)
            ps_ssq = psum("A_ps_ssq", [128, 512])
            ps_h = psum("A_ps_h", [128, 512])
            ps_pre = [psum(f"A_ps_pre{i}", [128, 512]) for i in range(2)]
            ps_tok = [psum(f"A_ps_tok{i}", [128, 512]) for i in range(2)]
            ps_g = psum("A_ps_g", [128, 512])
            ps_t = psum("A_ps_t", [128, 1024], BF16)

            P.dma(lambda e: e.dma_start(out=n1[:], in_=I["n1"]), writes=["n1"])
            P.dma(lambda e: e.dma_start(out=cw[:], in_=I["cw"]), writes=["cw"])
            P.dma(lambda e: e.dma_start(out=cb[:], in_=I["cb"]), writes=["cb"])
            P.dma(lambda e: e.dma_start(out=idf[:], in_=I["ident"]), writes=["idf"])
            P.dve(lambda e: e.tensor_copy(out=idb[:], in_=idf[:]), reads=["idf"], writes=["idb"])
            P.dve(lambda e: e.memset(ones[:], 1.0), writes=["ones"])
            P.dve(lambda e: e.memset(epsb[:], float(D * EPS)), writes=["epsb"])
            P.dve(lambda e: e.tensor_scalar(out=cws[:, :, 0:4], in0=cw[:, l, :, 0:4], scalar1=ALPHA, scalar2=None, op0=ALU.mult),
                  reads=["cw"], writes=["cws"])
            P.dve(lambda e: e.tensor_copy(out=cws[:, :, 4:8], in_=cw[:, l, :, 4:8]), reads=["cw"], writes=["cws"])
            P.dve(lambda e: e.tensor_scalar(out=cbs[:, 0:4], in0=cb[:, l, 0:4], scalar1=ALPHA, scalar2=None, op0=ALU.mult),
                  reads=["cb"], writes=["cbs"])
            P.dve(lambda e: e.tensor_copy(out=cbs[:, 4:8], in_=cb[:, l, 4:8]), reads=["cb"], writes=["cbs"])
            for kc in range(8):
                st = wst[kc % 2]
                P.dma(lambda e, st=st, kc=kc: e.dma_start(out=st[:], in_=I["w_in"][l, kc * 128:(kc + 1) * 128, :]),
                      writes=[f"wst{kc % 2}"])
                half = DIN // 2
                P.pool(lambda e, st=st, kc=kc: e.tensor_scalar(out=wbf[:, kc, 0:half], in0=st[:, 0:half], scalar1=n1[:, l, kc:kc + 1], scalar2=None, op0=ALU.mult),
                       reads=[f"wst{kc % 2}", "n1"], writes=[f"wbf{kc}a"])
                P.dve(lambda e, st=st, kc=kc: e.tensor_scalar(out=wbf[:, kc, half:DIN], in0=st[:, half:DIN], scalar1=n1[:, l, kc:kc + 1], scalar2=None, op0=ALU.mult),
                      reads=[f"wst{kc % 2}", "n1"], writes=[f"wbf{kc}b"])
            WB = [f"wbf{kc}{h}" for kc in range(8) for h in "ab"]

            def load_x(b):
                xi = xin[b % 2]
                t0 = b * TB
                lo = t0 - 1
                hi = t0 + TB + 1
                c0 = 0
                tok = f"xin{b % 2}"
                if b == 0:
                    lo = 0
                    c0 = 1
                    P.dve(lambda e, xi=xi: e.memset(xi[:, :, 0:1], 0.0), writes=[tok])
                if b == NB - 1:
                    hi = T
                    P.dve(lambda e, xi=xi: e.memset(xi[:, :, 513:514], 0.0), writes=[tok])
                n = hi - lo
                src = xsrc[:, lo:hi].rearrange("(kc p) t -> p kc t", p=128)
                P.dma(lambda e, xi=xi, src=src, c0=c0, n=n: e.dma_start(out=xi[:, :, c0:c0 + n], in_=src), writes=[tok])

            load_x(0)
            qi = 0
            for b in range(NB):
                if b + 1 < NB:
                    load_x(b + 1)
                xi = xin[b % 2]
                xt = f"xin{b % 2}"
                t0 = b * TB
                P.act(lambda e, xi=xi: e.activation(out=xsq[:], in_=xi[:], func=AF.Square), reads=[xt], writes=["xsq"])
                for kc in range(8):
                    P.pe(lambda e, kc=kc: e.matmul(ps_ssq[:], lhsT=ones[:], rhs=xsq[:, kc, 1:513], start=(kc == 0), stop=(kc == 7)),
                         reads=["ones", "xsq"], writes=["ps_ssq"])
                for kc in range(8):
                    P.pe(lambda e, kc=kc: e.matmul(ps_h[:, 0:2], lhsT=ones[:], rhs=mk(xsq[:, kc, 0:1], [[513, 2]]), start=(kc == 0), stop=(kc == 7)),
                         reads=["ones", "xsq"], writes=["ps_hs"])
                P.act(lambda e: e.activation(out=rstd[:, 1:513], in_=ps_ssq[:], func=AF.Ln, bias=epsb[:, 0:1]), reads=["ps_ssq", "epsb"], writes=["rstd"])
                P.act(lambda e: e.activation(out=mk(rstd[:, 0:1], [[513, 2]]), in_=ps_h[:, 0:2], func=AF.Ln, bias=epsb[:, 0:1]), reads=["ps_hs", "epsb"], writes=["rstd"])
                P.act(lambda e: e.activation(out=rstd[:], in_=rstd[:], func=AF.Exp, scale=-0.5), reads=["rstd"], writes=["rstd"])
                for kc in range(8):
                    eng = P.dve
                    eng(lambda e, kc=kc, xi=xi: e.scalar_tensor_tensor(out=hn[:, kc, :], in0=xi[:, kc, :], scalar=32.0, in1=rstd[:], op0=ALU.mult, op1=ALU.mult),
                        reads=[xt, "rstd"], writes=[f"hn{kc}"])
                HN = [f"hn{kc}" for kc in range(8)]

                for fc in range(8):
                    pp = ps_pre[qi % 2]
                    ppt = f"ps_pre{qi % 2}"
                    pr = pre[qi % 2]
                    prt = f"pre{qi % 2}"
                    ac = acc[qi % 2]
                    act_ = f"acc{qi % 2}"
                    sgg = sg[qi % 2]
                    sgt = f"sg{qi % 2}"
                    qo = qko[qi % 3]
                    qot = f"qko{qi % 3}"
                    qi += 1
                    c0 = fc * 128
                    for kc in range(8):
                        P.pe(lambda e, kc=kc, pp=pp, c0=c0: e.matmul(pp[:], lhsT=wbf[:, kc, c0:c0 + 128], rhs=hn[:, kc, 1:513], start=(kc == 0), stop=(kc == 7)),
                             reads=WB + HN, writes=[ppt])
                    hc = 8 + 2 * (fc % 2)
                    for kc in range(8):
                        P.pe(lambda e, kc=kc, c0=c0, hc=hc: e.matmul(ps_h[:, hc:hc + 2], lhsT=wbf[:, kc, c0:c0 + 128], rhs=mk(hn[:, kc, 0:1], [[513, 2]]), start=(kc == 0), stop=(kc == 7)),
                             reads=WB + HN, writes=[f"ps_hp{fc % 2}"])
                    P.act(lambda e, pr=pr, pp=pp: e.activation(out=pr[:, 1:513], in_=pp[:], func=AF.Copy), reads=[ppt], writes=[prt])
                    P.act(lambda e, pr=pr, hc=hc: e.activation(out=mk(pr[:, 0:1], [[513, 2]]), in_=ps_h[:, hc:hc + 2], func=AF.Copy),
                          reads=[f"ps_hp{fc % 2}"], writes=[prt])
                    P.pool(lambda e, ac=ac, pr=pr, fc=fc: e.tensor_scalar(out=ac[:], in0=pr[:, 0:512], scalar1=cws[:, 0, fc:fc + 1], scalar2=None, op0=ALU.mult),
                           reads=[prt, "cws"], writes=[act_])
                    P.dve(lambda e, ac=ac, pr=pr, fc=fc: e.scalar_tensor_tensor(out=ac[:], in0=pr[:, 1:513], scalar=cws[:, 1, fc:fc + 1], in1=ac[:], op0=ALU.mult, op1=ALU.add),
                           reads=[prt, "cws", act_], writes=[act_])
                    P.dve(lambda e, ac=ac, pr=pr, fc=fc: e.scalar_tensor_tensor(out=ac[:], in0=pr[:, 2:514], scalar=cws[:, 2, fc:fc + 1], in1=ac[:], op0=ALU.mult, op1=ALU.add),
                           reads=[prt, "cws", act_], writes=[act_])
                    sc = (1.0 / ALPHA) if fc < 4 else 1.0
                    P.act(lambda e, ac=ac, sgg=sgg, fc=fc, sc=sc: e.activation(out=sgg[:], in_=ac[:], func=AF.Sigmoid, bias=cb[:, l, fc:fc + 1], scale=sc),
                          reads=[act_, "cb"], writes=[sgt])
                    P.dve(lambda e, ac=ac, sgg=sgg, fc=fc, qo=qo: e.scalar_tensor_tensor(out=qo[:], in0=ac[:], scalar=cbs[:, fc:fc + 1], in1=sgg[:], op0=ALU.add, op1=ALU.mult),
                          reads=[act_, sgt, "cbs"], writes=[qot])
                    dst = (S["QT"] if fc < 4 else S["KT"])[(fc % 4) * 128:(fc % 4 + 1) * 128, t0:t0 + TB]
                    P.dma(lambda e, qo=qo, dst=dst: e.dma_start(out=dst, in_=qo[:]), reads=[qot], writes=["dram_qk"])
                    if fc >= 4:
                        h = fc - 4
                        half = (h % 2) * 512
                        for tt in range(4):
                            P.pe(lambda e, qo=qo, tt=tt, half=half: e.transpose(ps_t[:, half + tt * 128: half + (tt + 1) * 128], qo[:, tt * 128:(tt + 1) * 128], idb[:]),
                                 reads=[qot, "idb"], writes=[f"ps_t{h % 2}"])
                        P.act(lambda e, h=h, half=half: e.activation(out=ktok[:, :, h * 128:(h + 1) * 128], in_=mk(ps_t[:, half:half + 1], [[128, 4], [1, 128]]), func=AF.Copy),
                              reads=[f"ps_t{h % 2}"], writes=["ktok"])
                P.dma(lambda e, t0=t0: e.dma_start(out=S["Ktok"][t0:t0 + TB, :].rearrange("(tt p) c -> p tt c", p=128), in_=ktok[:]),
                      reads=["ktok"], writes=["dram_ktok"])

                ti = 0
                for (c0, kind) in ((1024, "v"), (1536, "o"), (3088, "vn")):
                    for tt in range(4):
                        pt = ps_tok[ti % 2]
                        ptt = f"ps_tok{ti % 2}"
                        ti += 1
                        for kc in range(8):
                            P.pe(lambda e, kc=kc, pt=pt, tt=tt, c0=c0: e.matmul(pt[:], lhsT=hn[:, kc, 1 + tt * 128:1 + (tt + 1) * 128], rhs=wbf[:, kc, c0:c0 + 512], start=(kc == 0), stop=(kc == 7)),
                                 reads=WB + HN, writes=[ptt])
                        if kind == "v":
                            P.act(lambda e, pt=pt, tt=tt: e.activation(out=vtok[:, tt, :], in_=pt[:], func=AF.Copy), reads=[ptt], writes=["vtok"])
                        elif kind == "vn":
                            P.dve(lambda e, pt=pt, tt=tt: e.tensor_copy(out=vntok[:, tt, :], in_=pt[:]), reads=[ptt], writes=["vntok"])
                        else:
                            P.act(lambda e, pt=pt, tt=tt: e.activation(out=osig[:, tt, :], in_=pt[:], func=AF.Sigmoid), reads=[ptt], writes=["osig"])
                P.dma(lambda e, t0=t0: e.dma_start(out=S["Vtok"][t0:t0 + TB, :].rearrange("(tt p) c -> p tt c", p=128), in_=vtok[:]),
                      reads=["vtok"], writes=["dram_vtok"])
                P.dma(lambda e, t0=t0: e.dma_start(out=S["Vn"][t0:t0 + TB, :].rearrange("(tt p) c -> p tt c", p=128), in_=vntok[:]),
                      reads=["vntok"], writes=["dram_vn"])
                P.dma(lambda e, t0=t0: e.dma_start(out=S["Osig"][t0:t0 + TB, :].rearrange("(tt p) c -> p tt c", p=128), in_=osig[:]),
                      reads=["osig"], writes=["dram_osig"])

                for kc in range(8):
                    P.pe(lambda e, kc=kc: e.matmul(ps_g[0:16, :], lhsT=wbf[:, kc, 2048:2064], rhs=hn[:, kc, 1:513], start=(kc == 0), stop=(kc == 7)),
                         reads=WB + HN, writes=["ps_g"])
                P.dve(lambda e: e.tensor_copy(out=gsb[:], in_=ps_g[0:16, :]), reads=["ps_g"], writes=["gsb"])
                P.dma(lambda e, t0=t0: e.dma_start(out=S["G"][:, t0:t0 + TB], in_=gsb[:]), reads=["gsb"], writes=["dram_g"])

                for j in range(8):
                    pp = ps_pre[qi % 2]
                    ppt = f"ps_pre{qi % 2}"
                    qo = qko[qi % 3]
                    qot = f"qko{qi % 3}"
                    qi += 1
                    c0 = 2064 + j * 128
                    for kc in range(8):
                        P.pe(lambda e, kc=kc, pp=pp, c0=c0: e.matmul(pp[:], lhsT=wbf[:, kc, c0:c0 + 128], rhs=hn[:, kc, 1:513], start=(kc == 0), stop=(kc == 7)),
                             reads=WB + HN, writes=[ppt])
                    sc = 0.125 if j < 4 else 1.0
                    P.act(lambda e, pp=pp, qo=qo, sc=sc: e.activation(out=qo[:], in_=pp[:], func=AF.Copy, scale=sc), reads=[ppt], writes=[qot])
                    dst = (S["QnT"] if j < 4 else S["KnT"])[(j % 4) * 128:(j % 4 + 1) * 128, t0:t0 + TB]
                    P.dma(lambda e, qo=qo, dst=dst: e.dma_start(out=dst, in_=qo[:]), reads=[qot], writes=["dram_qkn"])
            P.emit()

    def phase_B(self, l):
        nc, I, S = self.nc, self.I, self.S
        P = Prog(self.ctx)
        with ExitStack() as es:
            def sb(name, shape, dt):
                return es.enter_context(nc.sbuf_tensor(f"L{l}" + name, list(shape), dt))

            def psum(name, shape, dt=F32):
                return es.enter_context(nc.psum_tensor(f"L{l}" + name, list(shape), dt))

            NP = 36
            idf = sb("B_idf", [128, 128], F32)
            idb = sb("B_idb", [128, 128], BF16)
            trif = sb("B_trif", [128, 128], F32)
            trib = sb("B_trib", [128, 128], F32)
            tokser = sb("B_tokser", [128, 4, NCH, 8], F32)
            dcb = sb("B_dcb", [128, 8, NCH], F32)
            mnw = sb("B_mnw", [128, 512], F32)
            epsb = sb("B_epsb", [128, 1], F32)
            es1 = ExitStack()

            def sb1(name, shape, dt):
                return es1.enter_context(nc.sbuf_tensor(f"L{l}" + name, list(shape), dt))

            ser = {n: sb1("B_" + n, [NP, T], F32) for n in ("Ipre", "F", "Bp", "U", "M", "tmp", "W", "EM", "ES")}
            onesr = sb1("B_onesr", [NP, T], F32)
            gb = sb1("B_gb", [NP, DEPTH, 2], F32)
            ngb = sb1("B_ngb", [NP, 1], F32)
            mend = sb1("B_mend", [NP, NCH], F32)
            mprev = sb1("B_mprev", [NP, NCH], F32)
            dcs = sb1("B_dcs", [NP, NCH], F32)
            ps_ser = es1.enter_context(nc.psum_tensor(f"L{l}B_ps_ser", [128, 512], F32))

            for n in ("Ipre", "F"):
                P.dve(lambda e, n=n: e.memset(ser[n][:], 0.0), writes=[n])
            P.pool(lambda e: e.memset(onesr[:], 1.0), writes=["onesr"])
            P.dve(lambda e: e.memset(epsb[:], float(128 * EPS)), writes=["epsb"])
            P.dma(lambda e: e.dma_start(out=gb[:], in_=I["gb"]), writes=["gb"])
            P.dma(lambda e: e.dma_start(out=idf[:], in_=I["ident"]), writes=["idf"])
            P.dma(lambda e: e.dma_start(out=trif[:], in_=I["trif"]), writes=["trif"])
            P.dma(lambda e: e.dma_start(out=trib[:], in_=I["trib"]), writes=["trib"])
            P.dma(lambda e: e.dma_start(out=mnw[:], in_=I["mnw"][:, l, :]), writes=["mnw"])
            P.dve(lambda e: e.tensor_copy(out=idb[:], in_=idf[:]), reads=["idf"], writes=["idb"])
            P.dve(lambda e: e.tensor_scalar(out=mnw[:], in0=mnw[:], scalar1=float(128.0 ** 0.5), scalar2=None, op0=ALU.mult), reads=["mnw"], writes=["mnw"])
            G = S["G"]
            P.dma(lambda e: e.dma_start(out=ser["Ipre"][0:4, :], in_=G[0:4, :]), writes=["Ipre"])
            P.dma(lambda e: e.dma_start(out=ser["F"][0:4, :], in_=G[4:8, :]), writes=["F"])
            P.dma(lambda e: e.dma_start(out=ser["Ipre"][32:36, :], in_=G[8:12, :]), writes=["Ipre"])
            P.dma(lambda e: e.dma_start(out=ser["F"][32:36, :], in_=G[12:16, :]), writes=["F"])
            P.dve(lambda e: e.tensor_scalar(out=ngb[:], in0=gb[:, l, 1:2], scalar1=-1.0, scalar2=None, op0=ALU.mult), reads=["gb"], writes=["ngb"])
            P.dve(lambda e: e.tensor_scalar(out=ser["Ipre"][:], in0=ser["Ipre"][:], scalar1=gb[:, l, 0:1], scalar2=None, op0=ALU.add),
                  reads=["Ipre", "gb"], writes=["Ipre"])
            P.act(lambda e: e.activation(out=ser["tmp"][:], in_=ser["F"][:], func=AF.Exp, bias=ngb[:, 0:1], scale=-1.0), reads=["F", "ngb"], writes=["tmp"])
            P.act(lambda e: e.activation(out=ser["F"][:], in_=ser["tmp"][:], func=AF.Ln, bias=1.0), reads=["tmp"], writes=["F"])

            def rev(ap):
                a = ap.ap
                return bass.AP(ap.tensor, ap.offset + (a[-1][1] - 1) * a[-1][0], [list(p) for p in a[:-1]] + [[-a[-1][0], a[-1][1]]])

            def scan(out, data, op1, wtok, rtok):
                P.dve(lambda e: e.tensor_tensor_scan(out=out[0:4, :], data0=onesr[0:4, :], data1=data[0:4, :], initial=0.0, op0=ALU.mult, op1=op1),
                      reads=[rtok, "onesr"], writes=[wtok])
                P.dve(lambda e: e.tensor_tensor_scan(out=rev(out[32:36, :]), data0=onesr[32:36, :], data1=rev(data[32:36, :]), initial=0.0, op0=ALU.mult, op1=op1),
                      reads=[rtok, "onesr"], writes=[wtok])

            P.dve(lambda e: e.memset(ser["Bp"][:], 0.0), writes=["Bp"])
            P.pool(lambda e: e.memset(ser["M"][:], 0.0), writes=["M"])
            scan(ser["Bp"], ser["F"], ALU.add, "Bp", "F")
            P.dve(lambda e: e.tensor_tensor(out=ser["U"][:], in0=ser["Ipre"][:], in1=ser["Bp"][:], op=ALU.add), reads=["Ipre", "Bp"], writes=["U"])
            scan(ser["M"], ser["U"], ALU.max, "M", "U")
            P.dve(lambda e: e.tensor_tensor(out=ser["tmp"][:], in0=ser["Bp"][:], in1=ser["M"][:], op=ALU.subtract), reads=["Bp", "M"], writes=["tmp"])
            P.act(lambda e: e.activation(out=ser["EM"][:], in_=ser["tmp"][:], func=AF.Exp), reads=["tmp"], writes=["EM"])
            Mv = ser["M"]
            P.dve(lambda e: e.memset(mprev[:], 0.0), writes=["mprev"])
            P.dve(lambda e: e.memset(mend[:], 0.0), writes=["mend"])
            P.dve(lambda e: e.tensor_copy(out=mend[0:4, :], in_=mk(Mv[0:4, 127:128], [[128, NCH]])), reads=["M"], writes=["mend"])
            P.dve(lambda e: e.tensor_copy(out=mend[32:36, :], in_=mk(Mv[32:36, 0:1], [[128, NCH]])), reads=["M"], writes=["mend"])
            P.dve(lambda e: e.tensor_copy(out=mprev[0:4, 1:NCH], in_=mk(Mv[0:4, 127:128], [[128, NCH - 1]])), reads=["M"], writes=["mprev"])
            P.dve(lambda e: e.tensor_copy(out=mprev[32:36, 0:NCH - 1], in_=mk(Mv[32:36, 128:129], [[128, NCH - 1]])), reads=["M"], writes=["mprev"])
            P.dve(lambda e: e.tensor_tensor(out=ser["tmp"][:].rearrange("p (c t) -> p c t", t=128), in0=mk(mprev[:], [[1, NCH], [0, 128]]),
                                            in1=ser["M"][:].rearrange("p (c t) -> p c t", t=128), op=ALU.subtract),
                  reads=["mprev", "M", "EM"], writes=["tmp"])
            P.act(lambda e: e.activation(out=ser["W"][:], in_=ser["tmp"][:], func=AF.Exp), reads=["tmp"], writes=["W"])
            P.dve(lambda e: e.tensor_tensor(out=ser["tmp"][:].rearrange("p (c t) -> p c t", t=128), in0=ser["U"][:].rearrange("p (c t) -> p c t", t=128),
                                            in1=mk(mend[:], [[1, NCH], [0, 128]]), op=ALU.subtract),
                  reads=["mend", "U", "W"], writes=["tmp"])
            P.act(lambda e: e.activation(out=ser["ES"][:], in_=ser["tmp"][:], func=AF.Exp), reads=["tmp"], writes=["ES"])
            P.dve(lambda e: e.tensor_tensor(out=dcs[:], in0=mprev[:], in1=mend[:], op=ALU.subtract), reads=["mprev", "mend"], writes=["dcs"])
            P.act(lambda e: e.activation(out=dcs[:], in_=dcs[:], func=AF.Exp), reads=["dcs"], writes=["dcs"])
            P.dma(lambda e: e.dma_start(out=S["Mser"][0:4, :], in_=ser["M"][0:4, :]), reads=["M"], writes=["dram_M"])
            P.dma(lambda e: e.dma_start(out=S["Mser"][4:8, :], in_=ser["M"][32:36, :]), reads=["M"], writes=["dram_M"])
            P.dma(lambda e: e.dma_start(out=S["DCs"][0:4, :], in_=dcs[0:4, :]), reads=["dcs"], writes=["dram_DC"])
            P.dma(lambda e: e.dma_start(out=S["DCs"][4:8, :], in_=dcs[32:36, :]), reads=["dcs"], writes=["dram_DC"])
            P.dma(lambda e: e.dma_start(out=dcb[:], in_=bass.AP(S["DCs"].tensor, S["DCs"].offset, [[0, 128], [NCH, 8], [1, NCH]])),
                  reads=["dram_DC"], writes=["dcb"])
            for si, n in enumerate(("U", "W", "EM", "ES")):
                for c0 in range(0, NCH, 8):
                    for c in range(c0, c0 + 8):
                        P.pe(lambda e, n=n, c=c, c0=c0: e.matmul(ps_ser[:, (c - c0) * 64:(c - c0) * 64 + NP], lhsT=ser[n][:, c * 128:(c + 1) * 128], rhs=idf[0:NP, 0:NP], start=True, stop=True),
                             reads=[n, "idf"], writes=["ps_ser"])
                    P.dve(lambda e, si=si, c0=c0: e.tensor_copy(out=tokser[:, si, c0:c0 + 8, 0:4], in_=mk(ps_ser[:, 0:1], [[64, 8], [1, 4]])),
                          reads=["ps_ser"], writes=["tokser"])
                    P.dve(lambda e, si=si, c0=c0: e.tensor_copy(out=tokser[:, si, c0:c0 + 8, 4:8], in_=mk(ps_ser[:, 32:33], [[64, 8], [1, 4]])),
                          reads=["ps_ser"], writes=["tokser"])
            if self.debug:
                P.dma(lambda e: e.dma_start(out=S["dbgser"], in_=tokser[:]), reads=["tokser"], writes=["dram_dbgser"])

            P.emit()
            es1.close()
            P = Prog(self.ctx)

            cst = sb("B_cst", [128, 2, NCH, 4 * 129], BF16)
            c32 = sb("B_c32", [128, 2, 4 * 129], F32)
            kt_ = [sb(f"B_ktok{i}", [128, 512], BF16) for i in range(3)]
            v1_ = [sb(f"B_v1{i}", [128, 4, 129], BF16) for i in range(3)]
            vs_ = [sb(f"B_vs{i}", [128, 4, 129], BF16) for i in range(2)]
            ps_c = [psum(f"B_ps_c{i}", [128, 512]) for i in range(2)]
            P.dve(lambda e: e.memset(c32[:], 0.0), writes=["c32_0", "c32_1"])
            for i in range(3):
                P.pool(lambda e, i=i: e.memset(v1_[i][:, :, 128:129], 1.0), writes=[f"v1{i}"])
            li = 0
            vi = 0
            for d in range(2):
                order = list(range(NCH)) if d == 0 else list(range(NCH - 1, -1, -1))
                for step, c in enumerate(order):
                    P.act(lambda e, d=d, c=c: e.activation(out=cst[:, d, c, :], in_=c32[:, d, :], func=AF.Copy), reads=[f"c32_{d}"], writes=[f"cst{d}_{c}"])
                    if step == NCH - 1:
                        break
                    kt = kt_[li % 3]
                    v1 = v1_[li % 3]
                    ktt = f"ktokb{li % 3}"
                    v1t = f"v1{li % 3}"
                    li += 1
                    P.dma(lambda e, kt=kt, c=c: e.dma_start(out=kt[:], in_=S["Ktok"][c * 128:(c + 1) * 128, :]), writes=[ktt])
                    P.dma(lambda e, v1=v1, c=c: e.dma_start(out=v1[:, :, 0:128], in_=S["Vtok"][c * 128:(c + 1) * 128, :].rearrange("p (h x) -> p h x", h=4)), writes=[v1t])
                    vs = vs_[vi % 2]
                    vst = f"vs{vi % 2}"
                    vi += 1
                    P.dve(lambda e, vs=vs, v1=v1, c=c, d=d: e.tensor_tensor(out=vs[:], in0=v1[:], in1=mk(tokser[:, 3, c, d * 4:d * 4 + 1], [[1, 4], [0, 129]]), op=ALU.mult),
                          reads=[v1t, "tokser"], writes=[vst])
                    for hp in range(2):
                        pc = ps_c[hp]
                        for hh in range(2):
                            h = hp * 2 + hh
                            P.pe(lambda e, pc=pc, hh=hh, h=h, kt=kt, vs=vs: e.matmul(pc[:, hh * 129:(hh + 1) * 129], lhsT=kt[:, h * 128:(h + 1) * 128], rhs=vs[:, h, :], start=True, stop=True),
                                 reads=[ktt, vst], writes=[f"ps_c{hp}"])
                        for hh in range(2):
                            h = hp * 2 + hh
                            P.dve(lambda e, pc=pc, hh=hh, h=h, d=d, c=c: e.scalar_tensor_tensor(out=c32[:, d, h * 129:(h + 1) * 129], in0=c32[:, d, h * 129:(h + 1) * 129],
                                                                                     scalar=dcb[:, d * 4 + h, c:c + 1], in1=pc[:, hh * 129:(hh + 1) * 129], op0=ALU.mult, op1=ALU.add),
                                  reads=[f"ps_c{hp}", "dcb", f"c32_{d}"], writes=[f"c32_{d}"])

            qT_ = [sb(f"B_qT{i}", [128, 4, 128], BF16) for i in range(2)]
            kT_ = [sb(f"B_kT{i}", [128, 4, 128], BF16) for i in range(2)]
            mb_ = [sb(f"B_mb{i}", [128, 8, 128], F32) for i in range(2)]
            og_ = [sb(f"B_og{i}", [128, 512], F32) for i in range(2)]
            gg = sb("B_gg", [128, 512], F32)
            sm_ = [sb(f"B_sm{i}", [128, 2, 128], F32) for i in range(2)]
            ee_ = [sb(f"B_ee{i}", [128, 2, 128], F32) for i in range(2)]
            pp_ = [sb(f"B_pp{i}", [128, 2, 128], BF16) for i in range(2)]
            tin_ = [sb(f"B_tin{i}", [128, 2, 129], F32) for i in range(2)]
            num_ = [sb(f"B_num{i}", [128, 2, 129], F32) for i in range(2)]
            dd = sb("B_dd", [128, 2], F32)
            hall = sb("B_hall", [128, 512], F32)
            hsq = sb("B_hsq", [128, 512], F32)
            ss = sb("B_ss", [128, 4], F32)
            ybf = sb("B_ybf", [128, 512], BF16)
            yT_ = [sb(f"B_yT{i}", [128, 4, 128], BF16) for i in range(2)]
            ps_s = [psum(f"B_ps_s{i}", [128, 512]) for i in range(2)]
            ps_o = [psum(f"B_ps_o{i}", [128, 4, 128]) for i in range(2)]
            ps_y = psum("B_ps_y", [128, 1024], BF16)
            ui = 0
            for c in range(NCH):
                s = c % 2
                qT, kT, mb, og = qT_[s], kT_[s], mb_[s], og_[s]
                kt = kt_[c % 3]
                v1 = v1_[c % 3]
                ktt = f"ktokb{c % 3}"
                v1t = f"v1{c % 3}"
                tk = c * 128
                P.dma(lambda e, qT=qT, tk=tk: e.dma_start(out=qT[:], in_=S["QT"][:, tk:tk + 128].rearrange("(h p) t -> p h t", p=128)), writes=[f"qT{s}"])
                P.dma(lambda e, kT=kT, tk=tk: e.dma_start(out=kT[:], in_=S["KT"][:, tk:tk + 128].rearrange("(h p) t -> p h t", p=128)), writes=[f"kT{s}"])
                P.dma(lambda e, mb=mb, tk=tk: e.dma_start(out=mb[:], in_=bass.AP(S["Mser"].tensor, S["Mser"].offset + tk, [[0, 128], [T, 8], [1, 128]])), writes=[f"mb{s}"])
                P.dma(lambda e, og=og, tk=tk: e.dma_start(out=og[:], in_=S["Osig"][tk:tk + 128, :]), writes=[f"og{s}"])
                P.dma(lambda e, v1=v1, tk=tk: e.dma_start(out=v1[:, :, 0:128], in_=S["Vtok"][tk:tk + 128, :].rearrange("p (h x) -> p h x", h=4)), writes=[v1t])
                P.pool(lambda e, og=og: e.tensor_tensor(out=gg[:], in0=og[:], in1=mnw[:], op=ALU.mult),
                       reads=[f"og{s}", "mnw"], writes=["gg"])
                for h in range(4):
                    u = ui % 2
                    ui += 1
                    ps, po = ps_s[u], ps_o[u]
                    sm, ee, pp, tin, num = sm_[u], ee_[u], pp_[u], tin_[u], num_[u]
                    P.pe(lambda e, ps=ps, kT=kT, qT=qT, h=h: e.matmul(ps[:, 0:128], lhsT=kT[:, h, :], rhs=qT[:, h, :], start=True, stop=True),
                         reads=[f"kT{s}", f"qT{s}"], writes=[f"ps_s{u}"])
                    P.dve(lambda e, ps=ps, sm=sm: e.tensor_tensor(out=sm[:, 0, :], in0=ps[:, 0:128], in1=trif[:], op=ALU.mult), reads=[f"ps_s{u}", "trif"], writes=[f"sm{u}"])
                    P.dve(lambda e, ps=ps, sm=sm: e.tensor_tensor(out=sm[:, 1, :], in0=ps[:, 0:128], in1=trib[:], op=ALU.mult), reads=[f"ps_s{u}", "trib"], writes=[f"sm{u}"])
                    for d in range(2):
                        P.act(lambda e, ee=ee, mb=mb, d=d, h=h, c=c: e.activation(out=ee[:, d, :], in_=mb[:, d * 4 + h, :], func=AF.Exp, bias=tokser[:, 0, c, d * 4 + h:d * 4 + h + 1], scale=-1.0),
                              reads=[f"mb{s}", "tokser"], writes=[f"ee{u}"])
                    P.pool(lambda e, pp=pp, sm=sm, ee=ee: e.tensor_tensor(out=pp[:], in0=sm[:], in1=ee[:], op=ALU.mult), reads=[f"sm{u}", f"ee{u}"], writes=[f"pp{u}"])
                    for d in range(2):
                        P.pe(lambda e, po=po, qT=qT, h=h, d=d, c=c: e.matmul(po[:, 2 * d, :], lhsT=qT[:, h, :], rhs=cst[:, d, c, h * 129:h * 129 + 128], start=True, stop=True),
                             reads=[f"qT{s}", f"cst{d}_{c}"], writes=[f"ps_o{u}"])
                        P.pe(lambda e, po=po, pp=pp, v1=v1, h=h, d=d: e.matmul(po[:, 2 * d + 1, :], lhsT=pp[:, d, :], rhs=v1[:, h, 0:128], start=True, stop=True),
                             reads=[f"pp{u}", v1t], writes=[f"ps_o{u}"])
                        P.pe(lambda e, ps=ps, qT=qT, h=h, d=d, c=c: e.matmul(ps[:, 256 + 2 * d:257 + 2 * d], lhsT=qT[:, h, :], rhs=cst[:, d, c, h * 129 + 128:h * 129 + 129], start=True, stop=True),
                             reads=[f"qT{s}", f"cst{d}_{c}"], writes=[f"ps_d{u}"])
                        P.pe(lambda e, ps=ps, pp=pp, v1=v1, h=h, d=d: e.matmul(ps[:, 257 + 2 * d:258 + 2 * d], lhsT=pp[:, d, :], rhs=v1[:, h, 128:129], start=True, stop=True),
                             reads=[f"pp{u}", v1t], writes=[f"ps_d{u}"])
                    for d in range(2):
                        wcol = tokser[:, 1, c, d * 4 + h:d * 4 + h + 1]
                        P.act(lambda e, tin=tin, po=po, d=d, wcol=wcol: e.activation(out=tin[:, d, 0:128], in_=po[:, 2 * d, :], func=AF.Copy, scale=wcol),
                              reads=[f"ps_o{u}", "tokser"], writes=[f"tin{u}"])
                        P.act(lambda e, tin=tin, ps=ps, d=d, wcol=wcol: e.activation(out=tin[:, d, 128:129], in_=ps[:, 256 + 2 * d:257 + 2 * d], func=AF.Copy, scale=wcol),
                              reads=[f"ps_d{u}", "tokser"], writes=[f"tin{u}"])
                        P.dve(lambda e, num=num, tin=tin, po=po, d=d: e.tensor_tensor(out=num[:, d, 0:128], in0=tin[:, d, 0:128], in1=po[:, 2 * d + 1, :], op=ALU.add),
                              reads=[f"tin{u}", f"ps_o{u}"], writes=[f"num{u}"])
                        P.dve(lambda e, num=num, tin=tin, ps=ps, d=d: e.tensor_tensor(out=num[:, d, 128:129], in0=tin[:, d, 128:129], in1=ps[:, 257 + 2 * d:258 + 2 * d], op=ALU.add),
                              reads=[f"tin{u}", f"ps_d{u}"], writes=[f"num{u}"])
                        P.dve(lambda e, num=num, d=d: e.scalar_tensor_tensor(out=dd[:, d:d + 1], in0=num[:, d, 128:129], scalar=-1.0, in1=num[:, d, 128:129], op0=ALU.mult, op1=ALU.max),
                              reads=[f"num{u}"], writes=["dd"])
                        P.dve(lambda e, d=d, h=h, c=c: e.tensor_tensor(out=dd[:, d:d + 1], in0=dd[:, d:d + 1], in1=tokser[:, 2, c, d * 4 + h:d * 4 + h + 1], op=ALU.max),
                              reads=["dd", "tokser"], writes=["dd"])
                    P.dve(lambda e: e.reciprocal(out=dd[:], in_=dd[:]), reads=["dd"], writes=["dd"])
                    P.dve(lambda e, num=num, h=h: e.tensor_scalar(out=hall[:, h * 128:(h + 1) * 128], in0=num[:, 0, 0:128], scalar1=dd[:, 0:1], scalar2=None, op0=ALU.mult),
                          reads=[f"num{u}", "dd"], writes=["hall"])
                    P.dve(lambda e, num=num, h=h: e.scalar_tensor_tensor(out=hall[:, h * 128:(h + 1) * 128], in0=num[:, 1, 0:128], scalar=dd[:, 1:2], in1=hall[:, h * 128:(h + 1) * 128], op0=ALU.mult, op1=ALU.add),
                          reads=[f"num{u}", "dd", "hall"], writes=["hall"])
                P.pool(lambda e: e.tensor_tensor(out=hsq[:], in0=hall[:], in1=hall[:], op=ALU.mult), reads=["hall"], writes=["hsq"])
                P.dve(lambda e: e.tensor_reduce(out=ss[:], in_=hsq[:].rearrange("p (h x) -> p h x", h=4), axis=AX.X, op=ALU.add), reads=["hsq"], writes=["ss"])
                P.act(lambda e: e.activation(out=ss[:], in_=ss[:], func=AF.Ln, bias=epsb[:, 0:1]), reads=["ss", "epsb"], writes=["ss"])
                P.act(lambda e: e.activation(out=ss[:], in_=ss[:], func=AF.Exp, scale=-0.5), reads=["ss"], writes=["ss"])
                for h in range(4):
                    P.dve(lambda e, h=h: e.scalar_tensor_tensor(out=ybf[:, h * 128:(h + 1) * 128], in0=hall[:, h * 128:(h + 1) * 128], scalar=ss[:, h:h + 1], in1=gg[:, h * 128:(h + 1) * 128], op0=ALU.mult, op1=ALU.mult),
                          reads=["hall", "ss", "gg"], writes=["ybf"])
                yT = yT_[s]
                for h in range(4):
                    P.pe(lambda e, h=h, s=s: e.transpose(ps_y[:, s * 512 + h * 128: s * 512 + (h + 1) * 128], ybf[:, h * 128:(h + 1) * 128], idb[:]),
                         reads=["ybf", "idb"], writes=[f"ps_y{s}"])
                P.act(lambda e, yT=yT, s=s: e.activation(out=yT[:], in_=ps_y[:, s * 512:(s + 1) * 512].rearrange("p (h t) -> p h t", h=4), func=AF.Copy),
                      reads=[f"ps_y{s}"], writes=[f"yT{s}"])
                P.dma(lambda e, yT=yT, tk=tk: e.dma_start(out=S["YT"][0:512, tk:tk + 128].rearrange("(h p) t -> p h t", p=128), in_=yT[:]),
                      reads=[f"yT{s}"], writes=["dram_yt"])
            P.emit()

    def phase_C(self, l):
        nc, I, S = self.nc, self.I, self.S
        P = Prog(self.ctx)
        with ExitStack() as es:
            def sb(name, shape, dt):
                return es.enter_context(nc.sbuf_tensor(f"L{l}" + name, list(shape), dt))

            def psum(name, shape, dt=F32):
                return es.enter_context(nc.psum_tensor(f"L{l}" + name, list(shape), dt))

            qn = sb("C_qn", [128, 4, T], BF16)
            kn = sb("C_kn", [128, 4, T], BF16)
            vn = sb("C_vn", [128, 32, 8, 65], BF16)
            nbi = sb("C_nbi", [128, 8, 5, 128], F32)
            nbe = sb("C_nbe", [128, 8, 4, 128], F32)
            idf = sb("C_idf", [128, 128], F32)
            idb = sb("C_idb", [128, 128], BF16)
            sbb_ = [sb(f"C_sb{i}", [128, 5, 128], F32) for i in range(2)]
            pt_ = [sb(f"C_pt{i}", [128, 5, 128], BF16) for i in range(2)]
            rd = sb("C_rd", [128, 8], F32)
            yn = sb("C_yn", [128, 8, 64], BF16)
            yT_ = [sb(f"C_yT{i}", [128, 4, 128], BF16) for i in range(2)]
            ps_s = [psum(f"C_ps_s{i}", [128, 1024]) for i in range(2)]
            ps_o = psum("C_ps_o", [128, 512])
            ps_y = psum("C_ps_y", [128, 1024], BF16)
            P.dma(lambda e: e.dma_start(out=idf[:], in_=I["ident"]), writes=["idf"])
            P.dve(lambda e: e.tensor_copy(out=idb[:], in_=idf[:]), reads=["idf"], writes=["idb"])
            P.dma(lambda e: e.dma_start(out=nbi[:], in_=I["nbi"][l]), writes=["nbi"])
            for g in range(4):
                P.dma(lambda e, g=g: e.dma_start(out=qn[:, g, :], in_=S["QnT"][g * 128:(g + 1) * 128, :]), writes=["qn"])
                P.dma(lambda e, g=g: e.dma_start(out=kn[:, g, :], in_=S["KnT"][g * 128:(g + 1) * 128, :]), writes=["kn"])
            P.pool(lambda e: e.memset(vn[:, :, :, 64:65], 1.0), writes=["vn"])
            for m in range(32):
                P.dma(lambda e, m=m: e.dma_start(out=vn[:, m, :, 0:64], in_=S["Vn"][m * 128:(m + 1) * 128, :].rearrange("p (h x) -> p h x", h=8)), writes=["vn"])
            ui = 0
            for i in range(32):
                if i < 2:
                    tiles = [0, 1, 2, 3]
                    eidx = i
                elif i >= 30:
                    tiles = [28, 29, 30, 31]
                    eidx = i - 28
                else:
                    tiles = [i - 2, i - 1, i, i + 1, i + 2]
                    eidx = None
                if eidx is not None:
                    P.dma(lambda e, eidx=eidx: e.dma_start(out=nbe[:], in_=I["nbe"][l, eidx]), writes=["nbe"])
                nt = len(tiles)
                for h in range(8):
                    g, hp = h // 2, (h % 2) * 64
                    u = ui % 2
                    ui += 1
                    ps = ps_s[u]
                    sbb, pt = sbb_[u], pt_[u]
                    for j, m in enumerate(tiles):
                        P.pe(lambda e, ps=ps, j=j, m=m, g=g, hp=hp, i=i: e.matmul(ps[:, j * 128:(j + 1) * 128], lhsT=kn[hp:hp + 64, g, m * 128:(m + 1) * 128], rhs=qn[hp:hp + 64, g, i * 128:(i + 1) * 128], start=True, stop=True),
                             reads=["kn", "qn"], writes=[f"ps_s{u}"])
                    bias = nbi[:, h, :, :] if eidx is None else nbe[:, h, :, :]
                    btok = "nbi" if eidx is None else "nbe"
                    P.dve(lambda e, ps=ps, sbb=sbb, bias=bias, nt=nt: e.tensor_tensor(out=sbb[:, 0:nt, :], in0=ps[:, 0:nt * 128].rearrange("p (j x) -> p j x", x=128), in1=bias, op=ALU.add),
                          reads=[f"ps_s{u}", btok], writes=[f"sb{u}"])
                    P.act(lambda e, sbb=sbb, pt=pt, nt=nt: e.activation(out=pt[:, 0:nt, :], in_=sbb[:, 0:nt, :], func=AF.Exp), reads=[f"sb{u}"], writes=[f"pt{u}"])
                    so = (ui % 4) * 128
                    sot = f"ps_o{ui % 4}"
                    for j, m in enumerate(tiles):
                        P.pe(lambda e, pt=pt, j=j, m=m, h=h, nt=nt, so=so: e.matmul(ps_o[:, so:so + 65], lhsT=pt[:, j, :], rhs=vn[:, m, h, :], start=(j == 0), stop=(j == nt - 1)),
                             reads=[f"pt{u}", "vn"], writes=[sot])
                    P.dve(lambda e, h=h, so=so: e.reciprocal(out=rd[:, h:h + 1], in_=ps_o[:, so + 64:so + 65]), reads=[sot], writes=[f"rd{h}"])
                    P.act(lambda e, h=h, so=so: e.activation(out=yn[:, h, :], in_=ps_o[:, so:so + 64], func=AF.Copy, scale=rd[:, h:h + 1]), reads=[sot, f"rd{h}"], writes=["yn"])
                s = i % 2
                yT = yT_[s]
                for g in range(4):
                    P.pe(lambda e, g=g, s=s: e.transpose(ps_y[:, s * 512 + g * 128:s * 512 + (g + 1) * 128], yn[:, 2 * g:2 * g + 2, :].rearrange("p a b -> p (a b)"), idb[:]),
                         reads=["yn", "idb"], writes=[f"ps_y{s}"])
                P.dve(lambda e, yT=yT, s=s: e.tensor_copy(out=yT[:], in_=ps_y[:, s * 512:(s + 1) * 512].rearrange("p (h t) -> p h t", h=4)),
                      reads=[f"ps_y{s}"], writes=[f"yT{s}"])
                P.dma(lambda e, yT=yT, i=i: e.dma_start(out=S["YT"][512:1024, i * 128:(i + 1) * 128].rearrange("(h p) t -> p h t", p=128), in_=yT[:]),
                      reads=[f"yT{s}"], writes=["dram_yt"])
            P.emit()

    def phase_D(self, l):
        nc, I, S = self.nc, self.I, self.S
        P = Prog(self.ctx)
        xsrc = I["xT"] if l == 0 else S["xres"]
        with ExitStack() as es:
            def sb(name, shape, dt):
                return es.enter_context(nc.sbuf_tensor(f"L{l}" + name, list(shape), dt))

            def psum(name, shape, dt=F32):
                return es.enter_context(nc.psum_tensor(f"L{l}" + name, list(shape), dt))

            wbf = sb("D_wbf", [128, 8, D], BF16)
            wst = [sb(f"D_wst{i}", [128, D], F32) for i in range(2)]
            xin = [sb(f"D_xin{i}", [128, 8, TB], F32) for i in range(2)]
            yin = [sb(f"D_yin{i}", [128, 8, TB], BF16) for i in range(2)]
            pso = [psum(f"D_ps{i}", [128, 512]) for i in range(4)]
            for kc in range(8):
                st = wst[kc % 2]
                P.dma(lambda e, st=st, kc=kc: e.dma_start(out=st[:], in_=I["w_out"][l, kc * 128:(kc + 1) * 128, :]), writes=[f"wst{kc % 2}"])
                eng = P.pool if kc % 2 == 0 else P.act
                if kc % 2 == 0:
                    P.pool(lambda e, st=st, kc=kc: e.tensor_copy(out=wbf[:, kc, :], in_=st[:]), reads=[f"wst{kc % 2}"], writes=[f"wbf{kc}"])
                else:
                    P.act(lambda e, st=st, kc=kc: e.activation(out=wbf[:, kc, :], in_=st[:], func=AF.Copy), reads=[f"wst{kc % 2}"], writes=[f"wbf{kc}"])
            WB = [f"wbf{kc}" for kc in range(8)]

            def load(b):
                t0 = b * TB
                P.dma(lambda e, b=b, t0=t0: e.dma_start(out=xin[b % 2][:], in_=xsrc[:, t0:t0 + TB].rearrange("(kc p) t -> p kc t", p=128)), writes=[f"xin{b % 2}"])
                P.dma(lambda e, b=b, t0=t0: e.dma_start(out=yin[b % 2][:], in_=S["YT"][:, t0:t0 + TB].rearrange("(kc p) t -> p kc t", p=128)), writes=[f"yin{b % 2}"])

            load(0)
            pi = 0
            for b in range(NB):
                if b + 1 < NB:
                    load(b + 1)
                t0 = b * TB
                xi, yi = xin[b % 2], yin[b % 2]
                for dc in range(8):
                    ps = pso[pi % 4]
                    pst = f"ps{pi % 4}"
                    pi += 1
                    for kc in range(8):
                        P.pe(lambda e, ps=ps, kc=kc, dc=dc, yi=yi: e.matmul(ps[:], lhsT=wbf[:, kc, dc * 128:(dc + 1) * 128], rhs=yi[:, kc, :], start=(kc == 0), stop=(kc == 7)),
                             reads=WB + [f"yin{b % 2}"], writes=[pst])
                    P.dve(lambda e, ps=ps, dc=dc, xi=xi: e.tensor_tensor(out=xi[:, dc, :], in0=xi[:, dc, :], in1=ps[:], op=ALU.add),
                          reads=[pst, f"xin{b % 2}"], writes=[f"xin{b % 2}"])
                P.dma(lambda e, xi=xi, t0=t0: e.dma_start(out=S["xres"][:, t0:t0 + TB].rearrange("(kc p) t -> p kc t", p=128), in_=xi[:]),
                      reads=[f"xin{b % 2}"], writes=["dram_x"])
            P.emit()

    def phase_E(self, l):
        nc, I, S = self.nc, self.I, self.S
        P = Prog(self.ctx)
        last = (l == DEPTH - 1)
        with ExitStack() as es:
            def sb(name, shape, dt):
                return es.enter_context(nc.sbuf_tensor(f"L{l}" + name, list(shape), dt))

            def psum(name, shape, dt=F32):
                return es.enter_context(nc.psum_tensor(f"L{l}" + name, list(shape), dt))

            w1 = sb("E_w1", [128, 8, DFF], BF16)
            w2 = sb("E_w2", [128, 32, D], BF16)
            wst = [sb(f"E_wst{i}", [128, 1024], F32) for i in range(2)]
            n2 = sb("E_n2", [128, DEPTH, 8], F32)
            nf = sb("E_nf", [128, 8], F32)
            ones = sb("E_ones", [128, 128], BF16)
            epsb = sb("E_epsb", [128, 1], F32)
            xin = [sb(f"E_xin{i}", [128, 8, TB], F32) for i in range(2)]
            xsq = sb("E_xsq", [128, 8, TB], BF16)
            rstd = sb("E_rstd", [128, TB], F32)
            hn = sb("E_hn", [128, 8, TB], BF16)
            uu = [sb(f"E_u{i}", [128, 8, TB], BF16) for i in range(2)]
            rtmp = [sb(f"E_rtmp{i}", [128, TB], F32) for i in range(2)]
            ps_ssq = psum("E_ps_ssq", [128, 512])
            ps1 = [psum(f"E_ps1_{i}", [128, 512]) for i in range(3)]
            ps2 = [psum(f"E_ps2_{i}", [128, 512]) for i in range(3)]
            P.dma(lambda e: e.dma_start(out=n2[:], in_=I["n2"]), writes=["n2"])
            P.dma(lambda e: e.dma_start(out=nf[:], in_=I["nf"]), writes=["nf"])
            P.dve(lambda e: e.memset(ones[:], 1.0), writes=["ones"])
            P.dve(lambda e: e.memset(epsb[:], float(D * EPS)), writes=["epsb"])
            si = 0
            for kc in range(8):
                for hf in range(4):
                    st = wst[si % 2]
                    stt = f"wst{si % 2}"
                    P.dma(lambda e, st=st, kc=kc, hf=hf: e.dma_start(out=st[:], in_=I["w_ff1"][l, kc * 128:(kc + 1) * 128, hf * 1024:(hf + 1) * 1024]), writes=[stt])
                    eng = (P.pool, P.dve)[si % 2]
                    eng(lambda e, st=st, kc=kc, hf=hf: e.tensor_scalar(out=w1[:, kc, hf * 1024:(hf + 1) * 1024], in0=st[:], scalar1=n2[:, l, kc:kc + 1], scalar2=None, op0=ALU.mult),
                        reads=[stt, "n2"], writes=[f"w1_{kc}_{hf}"])
                    si += 1
            for k2 in range(32):
                st = wst[si % 2]
                stt = f"wst{si % 2}"
                P.dma(lambda e, st=st, k2=k2: e.dma_start(out=st[:], in_=I["w_ff2"][l, k2 * 128:(k2 + 1) * 128, :]), writes=[stt])
                if si % 2 == 0:
                    P.pool(lambda e, st=st, k2=k2: e.tensor_copy(out=w2[:, k2, :], in_=st[:]), reads=[stt], writes=[f"w2_{k2}"])
                else:
                    P.act(lambda e, st=st, k2=k2: e.activation(out=w2[:, k2, :], in_=st[:], func=AF.Copy), reads=[stt], writes=[f"w2_{k2}"])
                si += 1
            W1 = [f"w1_{kc}_{hf}" for kc in range(8) for hf in range(4)]
            W2 = [f"w2_{k2}" for k2 in range(32)]

            def load(b):
                t0 = b * TB
                P.dma(lambda e, b=b, t0=t0: e.dma_start(out=xin[b % 2][:], in_=S["xres"][:, t0:t0 + TB].rearrange("(kc p) t -> p kc t", p=128)), writes=[f"xin{b % 2}"])

            def rms(xi, xt, dst_fn, wcol_fn):
                P.act(lambda e: e.activation(out=xsq[:], in_=xi[:], func=AF.Square), reads=[xt], writes=["xsq"])
                for kc in range(8):
                    P.pe(lambda e, kc=kc: e.matmul(ps_ssq[:], lhsT=ones[:], rhs=xsq[:, kc, :], start=(kc == 0), stop=(kc == 7)), reads=["ones", "xsq"], writes=["ps_ssq"])
                P.act(lambda e: e.activation(out=rstd[:], in_=ps_ssq[:], func=AF.Ln, bias=epsb[:, 0:1]), reads=["ps_ssq", "epsb"], writes=["rstd"])
                P.act(lambda e: e.activation(out=rstd[:], in_=rstd[:], func=AF.Exp, scale=-0.5), reads=["rstd"], writes=["rstd"])

            load(0)
            p1 = 0
            p2 = 0
            gi = 0
            for b in range(NB):
                if b + 1 < NB:
                    load(b + 1)
                t0 = b * TB
                xi = xin[b % 2]
                xt = f"xin{b % 2}"
                rms(xi, xt, None, None)
                for kc in range(8):
                    eng = P.dve
                    eng(lambda e, kc=kc, xi=xi: e.scalar_tensor_tensor(out=hn[:, kc, :], in0=xi[:, kc, :], scalar=32.0, in1=rstd[:], op0=ALU.mult, op1=ALU.mult),
                        reads=[xt, "rstd"], writes=[f"hn{kc}"])
                HN = [f"hn{kc}" for kc in range(8)]
                for grp in range(4):
                    u = uu[gi % 2]
                    ut = f"u{gi % 2}"
                    gi += 1
                    for fj in range(8):
                        fc = grp * 8 + fj
                        ps = ps1[p1 % 3]
                        pst = f"ps1_{p1 % 3}"
                        p1 += 1
                        for kc in range(8):
                            P.pe(lambda e, ps=ps, kc=kc, fc=fc: e.matmul(ps[:], lhsT=w1[:, kc, fc * 128:(fc + 1) * 128], rhs=hn[:, kc, :], start=(kc == 0), stop=(kc == 7)),
                                 reads=W1 + HN, writes=[pst])
                        rt = rtmp[p1 % 2]
                        rtt = f"rtmp{p1 % 2}"
                        P.act(lambda e, ps=ps, rt=rt: e.activation(out=rt[:], in_=ps[:], func=AF.Relu), reads=[pst], writes=[rtt])
                        eng = P.dve if fj % 2 == 0 else P.pool
                        eng(lambda e, u=u, fj=fj, rt=rt: e.tensor_tensor(out=u[:, fj, :], in0=rt[:], in1=rt[:], op=ALU.mult), reads=[rtt], writes=[f"{ut}_{fj}"])
                    UT = [f"{ut}_{fj}" for fj in range(8)]
                    for dc in range(8):
                        ps = ps2[p2 % 3]
                        pst = f"ps2_{p2 % 3}"
                        p2 += 1
                        for fj in range(8):
                            P.pe(lambda e, ps=ps, fj=fj, grp=grp, dc=dc, u=u: e.matmul(ps[:], lhsT=w2[:, grp * 8 + fj, dc * 128:(dc + 1) * 128], rhs=u[:, fj, :], start=(fj == 0), stop=(fj == 7)),
                                 reads=W2 + UT, writes=[pst])
                        P.dve(lambda e, ps=ps, dc=dc, xi=xi: e.tensor_tensor(out=xi[:, dc, :], in0=xi[:, dc, :], in1=ps[:], op=ALU.add),
                              reads=[pst, xt] + HN, writes=[xt])
                if not last:
                    P.dma(lambda e, xi=xi, t0=t0: e.dma_start(out=S["xres"][:, t0:t0 + TB].rearrange("(kc p) t -> p kc t", p=128), in_=xi[:]),
                          reads=[xt], writes=["dram_x"])
                else:
                    rms(xi, xt, None, None)
                    for kc in range(8):
                        P.dve(lambda e, kc=kc, xi=xi: e.scalar_tensor_tensor(out=xi[:, kc, :], in0=xi[:, kc, :], scalar=32.0, in1=rstd[:], op0=ALU.mult, op1=ALU.mult),
                              reads=[xt, "rstd"], writes=[xt])
                        P.pool(lambda e, kc=kc, xi=xi: e.tensor_scalar(out=xi[:, kc, :], in0=xi[:, kc, :], scalar1=nf[:, kc:kc + 1], scalar2=None, op0=ALU.mult),
                               reads=[xt, "nf"], writes=[xt])
                    P.dma(lambda e, xi=xi, t0=t0: e.dma_start(out=self.out[:, t0:t0 + TB].rearrange("(kc p) t -> p kc t", p=128), in_=xi[:]),
                          reads=[xt], writes=["dram_out"])
            P.emit()


def _na_bias_tables(rpb_l):
    H = rpb_l.shape[0]
    kl = np.arange(128)
    kr_l, kc = kl // 64, kl % 64
    ql = np.arange(128)
    qr_l, qc = ql // 64, ql % 64

    def tile(i, m):
        kr = 2 * m + kr_l[:, None]
        qr = 2 * i + qr_l[None, :]
        rs = np.clip(qr - 4, 0, 56)
        vr = (kr >= rs) & (kr < rs + 8)
        cs = np.clip(qc[None, :] - 8, 0, 48)
        vc = (kc[:, None] >= cs) & (kc[:, None] < cs + 16)
        ri = np.clip(kr - qr + 7, 0, 14)
        ci = np.clip(kc[:, None] - qc[None, :] + 15, 0, 30)
        vals = rpb_l[:, ri, ci]
        return np.where((vr & vc)[None], vals, np.float32(-30000.0)).astype(np.float32)

    nbi = np.stack([tile(2, m) for m in range(5)], axis=1)
    nbi = np.ascontiguousarray(nbi.transpose(2, 0, 1, 3))
    nbe = []
    for i, ms in ((0, range(4)), (1, range(4)), (30, range(28, 32)), (31, range(28, 32))):
        t = np.stack([tile(i, m) for m in ms], axis=1)
        nbe.append(t.transpose(2, 0, 1, 3))
    nbe = np.ascontiguousarray(np.stack(nbe, axis=0))
    return nbi, nbe


def _prep_shared(norm1_w, conv_w, conv_b, gate_b, mlstm_norm_w, rpb, norm2_w, final_norm_w):
    f = np.float32
    sh = {}
    sh["n1"] = np.ascontiguousarray(norm1_w.reshape(DEPTH, 8, 128).transpose(2, 0, 1)).astype(f)
    sh["n2"] = np.ascontiguousarray(norm2_w.reshape(DEPTH, 8, 128).transpose(2, 0, 1)).astype(f)
    sh["nf"] = np.ascontiguousarray(final_norm_w.reshape(8, 128).T).astype(f)
    sh["cw"] = np.ascontiguousarray(conv_w.reshape(DEPTH, 3, 8, 128).transpose(3, 0, 1, 2)).astype(f)
    sh["cb"] = np.ascontiguousarray(conv_b.reshape(DEPTH, 8, 128).transpose(2, 0, 1)).astype(f)
    gb = np.zeros((36, DEPTH, 2), f)
    g4 = gate_b.reshape(DEPTH, 4, 4)
    gb[0:4, :, 0] = g4[:, 0, :].T
    gb[0:4, :, 1] = g4[:, 1, :].T
    gb[32:36, :, 0] = g4[:, 2, :].T
    gb[32:36, :, 1] = g4[:, 3, :].T
    sh["gb"] = gb
    sh["mnw"] = np.ascontiguousarray(np.broadcast_to(mlstm_norm_w[None], (128, DEPTH, 512))).astype(f)
    nbi, nbe = zip(*[_na_bias_tables(np.asarray(rpb[l], f)) for l in range(DEPTH)])
    sh["nbi"] = np.stack(nbi, 0)
    sh["nbe"] = np.stack(nbe, 0)
    sh["ident"] = np.eye(128, dtype=f)
    sh["trif"] = np.triu(np.ones((128, 128), f))
    sh["trib"] = np.tril(np.ones((128, 128), f))
    return sh


_CACHE = {}


def _get_nc(debug=False, stop_after=None):
    key = (debug, stop_after)
    if key not in _CACHE:
        b = Builder(debug=debug, stop_after=stop_after)
        nc = b.build()
        _CACHE[key] = (nc, b)
    return _CACHE[key]


def run(inputs, debug=False, stop_after=None, cores=8):
    x = np.asarray(inputs["x"], np.float32)
    sh = _prep_shared(*[np.asarray(inputs[k], np.float32) for k in
                        ("norm1_w", "conv_w", "conv_b", "gate_b", "mlstm_norm_w", "rpb", "norm2_w", "final_norm_w")])
    for k in ("w_in", "w_out", "w_ff1", "w_ff2"):
        sh[k] = np.ascontiguousarray(np.asarray(inputs[k], np.float32))
    nc, b = _get_nc(debug, stop_after)
    in_maps = []
    for c in range(cores):
        m = dict(sh)
        m["xT"] = np.ascontiguousarray(x[c].T)
        in_maps.append(m)
    res = run_bass_kernel_spmd(nc, in_maps, core_ids=list(range(cores)))
    return res, b


def kernel(x, norm1_w, w_in, conv_w, conv_b, gate_b, mlstm_norm_w, rpb, w_out,
           norm2_w, w_ff1, w_ff2, final_norm_w):
    inputs = dict(x=x, norm1_w=norm1_w, w_in=w_in, conv_w=conv_w, conv_b=conv_b, gate_b=gate_b,
                  mlstm_norm_w=mlstm_norm_w, rpb=rpb, w_out=w_out, norm2_w=norm2_w, w_ff1=w_ff1,
                  w_ff2=w_ff2, final_norm_w=final_norm_w)
    res, _ = run(inputs)
    out = np.stack([np.ascontiguousarray(r["outT"].T) for r in res.results], axis=0)
    return out.astype(np.float32)
```

```python
import numpy as np
from contextlib import ExitStack
import concourse.bass as bass
import concourse.mybir as mybir
from concourse.bass_utils import run_bass_kernel_spmd

F32 = mybir.dt.float32
BF16 = mybir.dt.bfloat16
AF = mybir.ActivationFunctionType
ALU = mybir.AluOpType
AX = mybir.AxisListType

T = 4096
D = 1024
DEPTH = 2
DIN = 3600
DFF = 4096
NB = 8
TB = 512
NCH = 32
EPS = 1e-6
ALPHA = 128.0 ** -0.5

ENGS = ("pe", "act", "dve", "pool", "sp")
EPOCH = 4000
NDMASEM = 20
SAME_SYNC = True


class Op:
    __slots__ = ("eng", "fn", "deps", "dma", "idx", "sig", "semk", "val", "tiny")

    def __init__(self, eng, fn, dma, tiny=False):
        self.eng = eng
        self.fn = fn
        self.dma = dma
        self.tiny = tiny
        self.deps = set()
        self.sig = False
        self.semk = None
        self.val = 0


class Ctx:
    def __init__(self, nc):
        self.nc = nc
        self.sems = {}
        self.ccount = {e: 0 for e in ENGS}
        self.dval = {}
        self.dk = {e: 0 for e in ENGS}
        self.waited = {e: {} for e in ENGS}
        self.nops = 0
        self.nwaits = 0

    def sem(self, k):
        if k not in self.sems:
            self.sems[k] = self.nc.alloc_semaphore("s_" + "_".join(str(x) for x in k))
        return self.sems[k]


class Prog:
    def __init__(self, ctx):
        self.ctx = ctx
        self.nc = ctx.nc
        self.ops = []
        self.last_w = {}
        self.readers = {}

    def op(self, eng, fn, reads=(), writes=(), dma=False, tiny=False):
        o = Op(eng, fn, dma, tiny)
        o.idx = len(self.ops)
        for r in reads:
            w = self.last_w.get(r)
            if w is not None:
                o.deps.add(w)
        for w_ in writes:
            w = self.last_w.get(w_)
            if w is not None:
                o.deps.add(w)
            for rd in self.readers.get(w_, ()):
                o.deps.add(rd)
        o.deps.discard(o.idx)
        for r in reads:
            self.readers.setdefault(r, []).append(o.idx)
        for w_ in writes:
            self.last_w[w_] = o.idx
            self.readers[w_] = []
        self.ops.append(o)
        return o

    def pe(self, fn, reads=(), writes=()):
        return self.op("pe", fn, reads, writes)

    def act(self, fn, reads=(), writes=(), tiny=False):
        return self.op("act", fn, reads, writes, tiny=tiny)

    def dve(self, fn, reads=(), writes=(), tiny=False):
        return self.op("dve", fn, reads, writes, tiny=tiny)

    def pool(self, fn, reads=(), writes=(), tiny=False):
        return self.op("pool", fn, reads, writes, tiny=tiny)

    def dma(self, fn, reads=(), writes=(), q="sp"):
        return self.op(q, fn, reads, writes, dma=True)

    def _unsynced(self, od, o):
        return (od.eng == o.eng and not od.dma and not o.dma
                and (od.eng == "pe" or not SAME_SYNC or not od.tiny))

    def emit(self):
        ctx = self.ctx
        nc = self.nc
        ops = self.ops
        per = {e: [o for o in ops if o.eng == e] for e in ENGS}
        for o in ops:
            for d in o.deps:
                od = ops[d]
                if od.dma or self._unsynced(od, o):
                    continue
                od.sig = True
        for e in ENGS:
            for o in reversed(per[e]):
                if not o.dma:
                    o.sig = True
                    break
        final = {}
        for e in ENGS:
            for o in per[e]:
                if o.dma:
                    i = ctx.dk[e] % NDMASEM
                    ctx.dk[e] += 1
                    k = ("d", e, i)
                    ctx.dval[k] = ctx.dval.get(k, 0) + 16
                    o.semk = k
                    o.val = ctx.dval[k]
                    final[k] = o.val
                elif o.sig:
                    n = ctx.ccount[e]
                    ctx.ccount[e] += 1
                    k = ("c", e, n // EPOCH)
                    o.semk = k
                    o.val = n % EPOCH + 1
                    final[k] = o.val
        for k in final:
            ctx.sem(k)
        ctx.nops += len(ops)

        def run_engine(e, eng):
            waited = ctx.waited[e]

            def wait(k, v):
                if waited.get(k, 0) >= v:
                    return
                waited[k] = v
                eng.wait_ge(ctx.sems[k], v)
                ctx.nwaits += 1

            for o in per[e]:
                need = {}
                for d in o.deps:
                    od = ops[d]
                    if od.semk is None or self._unsynced(od, o):
                        continue
                    if need.get(od.semk, 0) < od.val:
                        need[od.semk] = od.val
                if o.dma and o.val > 16:
                    if need.get(o.semk, 0) < o.val - 16:
                        need[o.semk] = o.val - 16
                for k, v in need.items():
                    wait(k, v)
                ins = o.fn(eng)
                if o.dma:
                    ins.then_inc(ctx.sems[o.semk], 16)
                elif o.sig:
                    ins.then_inc(ctx.sems[o.semk], 1)
            for k, v in final.items():
                if k[0] == "c" and k[1] == e:
                    if e == "pe" or not SAME_SYNC:
                        pass
                wait(k, v)

        with nc.Block() as block:
            block.tensor(lambda eng: run_engine("pe", eng))
            block.scalar(lambda eng: run_engine("act", eng))
            block.vector(lambda eng: run_engine("dve", eng))
            block.gpsimd(lambda eng: run_engine("pool", eng))
            block.sync(lambda eng: run_engine("sp", eng))


def mk(ap, dims, off=0):
    a = ap.ap
    return bass.AP(ap.tensor, ap.offset + off, [list(a[0])] + [list(d) for d in dims])


class Builder:
    def __init__(self, debug=False, stop_after=None):
        self.debug = debug
        self.stop_after = stop_after
        self.nc = bass.Bass("TRN2", target_bir_lowering=False)
        self.ctx = Ctx(self.nc)
        self.dbg_names = []

    def din(self, name, shape, dt=F32):
        return self.nc.dram_tensor(name, list(shape), dt, kind="ExternalInput").ap()

    def dscr(self, name, shape, dt):
        kind = "ExternalOutput" if self.debug else "Internal"
        if self.debug:
            self.dbg_names.append(name)
        return self.nc.dram_tensor(name, list(shape), dt, kind=kind).ap()

    def build(self):
        nc = self.nc
        I = {}
        I["xT"] = self.din("xT", [D, T])
        I["w_in"] = self.din("w_in", [DEPTH, D, DIN])
        I["w_out"] = self.din("w_out", [DEPTH, D, D])
        I["w_ff1"] = self.din("w_ff1", [DEPTH, D, DFF])
        I["w_ff2"] = self.din("w_ff2", [DEPTH, DFF, D])
        I["n1"] = self.din("n1", [128, DEPTH, 8])
        I["n2"] = self.din("n2", [128, DEPTH, 8])
        I["nf"] = self.din("nf", [128, 8])
        I["cw"] = self.din("cw", [128, DEPTH, 3, 8])
        I["cb"] = self.din("cb", [128, DEPTH, 8])
        I["gb"] = self.din("gb", [36, DEPTH, 2])
        I["mnw"] = self.din("mnw", [128, DEPTH, 512])
        I["nbi"] = self.din("nbi", [DEPTH, 128, 8, 5, 128])
        I["nbe"] = self.din("nbe", [DEPTH, 4, 128, 8, 4, 128])
        I["ident"] = self.din("ident", [128, 128])
        I["trif"] = self.din("trif", [128, 128])
        I["trib"] = self.din("trib", [128, 128])
        self.I = I
        self.out = nc.dram_tensor("outT", [D, T], F32, kind="ExternalOutput").ap()
        S = {}
        S["QT"] = self.dscr("QT", [512, T], BF16)
        S["KT"] = self.dscr("KT", [512, T], BF16)
        S["Ktok"] = self.dscr("Ktok", [T, 512], BF16)
        S["Vtok"] = self.dscr("Vtok", [T, 512], BF16)
        S["Osig"] = self.dscr("Osig", [T, 512], F32)
        S["G"] = self.dscr("G", [16, T], F32)
        S["QnT"] = self.dscr("QnT", [512, T], BF16)
        S["KnT"] = self.dscr("KnT", [512, T], BF16)
        S["Vn"] = self.dscr("Vn", [T, 512], BF16)
        S["YT"] = self.dscr("YT", [D, T], BF16)
        S["Mser"] = self.dscr("Mser", [8, T], F32)
        S["DCs"] = self.dscr("DCs", [8, NCH], F32)
        S["xres"] = self.dscr("xres", [D, T], F32)
        if self.debug:
            S["dbgser"] = self.dscr("dbgser", [128, 4, NCH, 8], F32)
        self.S = S
        phases = []
        for l in range(DEPTH):
            phases += [("A", l), ("B", l), ("C", l), ("D", l), ("E", l)]
        for i, (ph, l) in enumerate(phases):
            getattr(self, "phase_" + ph)(l)
            if self.stop_after is not None and i + 1 >= self.stop_after:
                break
        return nc

    def phase_A(self, l):
        nc, I, S = self.nc, self.I, self.S
        P = Prog(self.ctx)
        xsrc = I["xT"] if l == 0 else S["xres"]
        with ExitStack() as es:
            def sb(name, shape, dt):
                return es.enter_context(nc.sbuf_tensor(f"L{l}" + name, list(shape), dt))

            def psum(name, shape, dt=F32):
                return es.enter_context(nc.psum_tensor(f"L{l}" + name, list(shape), dt))

            wbf = sb("A_wbf", [128, 8, DIN], BF16)
            wst = [sb(f"A_wst{i}", [128, DIN], F32) for i in range(2)]
            n1 = sb("A_n1", [128, DEPTH, 8], F32)
            cw = sb("A_cw", [128, DEPTH, 3, 8], F32)
            cb = sb("A_cb", [128, DEPTH, 8], F32)
            cws = sb("A_cws", [128, 3, 8], F32)
            cbs = sb("A_cbs", [128, 8], F32)
            idf = sb("A_idf", [128, 128], F32)
            idb = sb("A_idb", [128, 128], BF16)
            ones = sb("A_ones", [128, 128], BF16)
            epsb = sb("A_epsb", [128, 1], F32)
            xin = [sb(f"A_xin{i}", [128, 8, 514], F32) for i in range(2)]
            xsq = sb("A_xsq", [128, 8, 514], BF16)
            rstd = sb("A_rstd", [128, 514], F32)
            hn = sb("A_hn", [128, 8, 514], BF16)
            pre = [sb(f"A_pre{i}", [128, 514], F32) for i in range(2)]
            acc = [sb(f"A_acc{i}", [128, 512], F32) for i in range(2)]
            sg = [sb(f"A_sg{i}", [128, 512], F32) for i in range(2)]
            qko = [sb(f"A_qko{i}", [128, 512], BF16) for i in range(3)]
            ktok = sb("A_ktok", [128, 4, 512], BF16)
            vtok = sb("A_vtok", [128, 4, 512], BF16)
            vntok = sb("A_vntok", [128, 4, 512], BF16)
            osig = sb("A_osig", [128, 4, 512], F32)
            gsb = sb("A_gsb", [16, 512], F32)
            ps_ssq = psum("A_ps_ssq", [128, 512])
            ps_h = psum("A_ps_h", [128, 512])
            ps_pre = [psum(f"A_ps_pre{i}", [128, 512]) for i in range(2)]
            ps_tok = [psum(f"A_ps_tok{i}", [128, 512]) for i in range(2)]
            ps_g = psum("A_ps_g", [128, 512])
            ps_t = psum("A_ps_t", [128, 1024], BF16)

            P.dma(lambda e: e.dma_start(out=n1[:], in_=I["n1"]), writes=["n1"])
            P.dma(lambda e: e.dma_start(out=cw[:], in_=I["cw"]), writes=["cw"])
            P.dma(lambda e: e.dma_start(out=cb[:], in_=I["cb"]), writes=["cb"])
            P.dma(lambda e: e.dma_start(out=idf[:], in_=I["ident"]), writes=["idf"])
            P.dve(lambda e: e.tensor_copy(out=idb[:], in_=idf[:]), reads=["idf"], writes=["idb"])
            P.dve(lambda e: e.memset(ones[:], 1.0), writes=["ones"])
            P.dve(lambda e: e.memset(epsb[:], float(D * EPS)), writes=["epsb"], tiny=True)
            P.dve(lambda e: e.tensor_scalar(out=cws[:, :, 0:4], in0=cw[:, l, :, 0:4], scalar1=ALPHA, scalar2=None, op0=ALU.mult),
                  reads=["cw"], writes=["cws"], tiny=True)
            P.dve(lambda e: e.tensor_copy(out=cws[:, :, 4:8], in_=cw[:, l, :, 4:8]), reads=["cw"], writes=["cws"], tiny=True)
            P.dve(lambda e: e.tensor_scalar(out=cbs[:, 0:4], in0=cb[:, l, 0:4], scalar1=ALPHA, scalar2=None, op0=ALU.mult),
                  reads=["cb"], writes=["cbs"], tiny=True)
            P.dve(lambda e: e.tensor_copy(out=cbs[:, 4:8], in_=cb[:, l, 4:8]), reads=["cb"], writes=["cbs"], tiny=True)
            for kc in range(8):
                st = wst[kc % 2]
                P.dma(lambda e, st=st, kc=kc: e.dma_start(out=st[:], in_=I["w_in"][l, kc * 128:(kc + 1) * 128, :]),
                      writes=[f"wst{kc % 2}"])
                half = DIN // 2
                P.pool(lambda e, st=st, kc=kc: e.tensor_scalar(out=wbf[:, kc, 0:half], in0=st[:, 0:half], scalar1=n1[:, l, kc:kc + 1], scalar2=None, op0=ALU.mult),
                       reads=[f"wst{kc % 2}", "n1"], writes=[f"wbf{kc}a"])
                P.dve(lambda e, st=st, kc=kc: e.tensor_scalar(out=wbf[:, kc, half:DIN], in0=st[:, half:DIN], scalar1=n1[:, l, kc:kc + 1], scalar2=None, op0=ALU.mult),
                      reads=[f"wst{kc % 2}", "n1"], writes=[f"wbf{kc}b"])
            WB = [f"wbf{kc}{h}" for kc in range(8) for h in "ab"]

            def load_x(b):
                xi = xin[b % 2]
                t0 = b * TB
                lo = t0 - 1
                hi = t0 + TB + 1
                c0 = 0
                tok = f"xin{b % 2}"
                if b == 0:
                    lo = 0
                    c0 = 1
                    P.dve(lambda e, xi=xi: e.memset(xi[:, :, 0:1], 0.0), writes=[tok], tiny=True)
                if b == NB - 1:
                    hi = T
                    P.dve(lambda e, xi=xi: e.memset(xi[:, :, 513:514], 0.0), writes=[tok], tiny=True)
                n = hi - lo
                src = xsrc[:, lo:hi].rearrange("(kc p) t -> p kc t", p=128)
                P.dma(lambda e, xi=xi, src=src, c0=c0, n=n: e.dma_start(out=xi[:, :, c0:c0 + n], in_=src), writes=[tok])

            load_x(0)
            qi = 0
            for b in range(NB):
                if b + 1 < NB:
                    load_x(b + 1)
                xi = xin[b % 2]
                xt = f"xin{b % 2}"
                t0 = b * TB
                P.act(lambda e, xi=xi: e.activation(out=xsq[:], in_=xi[:], func=AF.Square), reads=[xt], writes=["xsq"])
                for kc in range(8):
                    P.pe(lambda e, kc=kc: e.matmul(ps_ssq[:], lhsT=ones[:], rhs=xsq[:, kc, 1:513], start=(kc == 0), stop=(kc == 7)),
                         reads=["ones", "xsq"], writes=["ps_ssq"])
                for kc in range(8):
                    P.pe(lambda e, kc=kc: e.matmul(ps_h[:, 0:2], lhsT=ones[:], rhs=mk(xsq[:, kc, 0:1], [[513, 2]]), start=(kc == 0), stop=(kc == 7)),
                         reads=["ones", "xsq"], writes=["ps_hs"])
                P.act(lambda e: e.activation(out=rstd[:, 1:513], in_=ps_ssq[:], func=AF.Ln, bias=epsb[:, 0:1]), reads=["ps_ssq", "epsb"], writes=["rstd"])
                P.act(lambda e: e.activation(out=mk(rstd[:, 0:1], [[513, 2]]), in_=ps_h[:, 0:2], func=AF.Ln, bias=epsb[:, 0:1]), reads=["ps_hs", "epsb"], writes=["rstd"], tiny=True)
                P.act(lambda e: e.activation(out=rstd[:], in_=rstd[:], func=AF.Exp, scale=-0.5), reads=["rstd"], writes=["rstd"])
                for kc in range(8):
                    eng = P.dve
                    eng(lambda e, kc=kc, xi=xi: e.scalar_tensor_tensor(out=hn[:, kc, :], in0=xi[:, kc, :], scalar=32.0, in1=rstd[:], op0=ALU.mult, op1=ALU.mult),
                        reads=[xt, "rstd"], writes=[f"hn{kc}"])
                HN = [f"hn{kc}" for kc in range(8)]

                for fc in range(8):
                    pp = ps_pre[qi % 2]
                    ppt = f"ps_pre{qi % 2}"
                    pr = pre[qi % 2]
                    prt = f"pre{qi % 2}"
                    ac = acc[qi % 2]
                    act_ = f"acc{qi % 2}"
                    sgg = sg[qi % 2]
                    sgt = f"sg{qi % 2}"
                    qo = qko[qi % 3]
                    qot = f"qko{qi % 3}"
                    qi += 1
                    c0 = fc * 128
                    for kc in range(8):
                        P.pe(lambda e, kc=kc, pp=pp, c0=c0: e.matmul(pp[:], lhsT=wbf[:, kc, c0:c0 + 128], rhs=hn[:, kc, 1:513], start=(kc == 0), stop=(kc == 7)),
                             reads=WB + HN, writes=[ppt])
                    hc = 8 + 2 * (fc % 2)
                    for kc in range(8):
                        P.pe(lambda e, kc=kc, c0=c0, hc=hc: e.matmul(ps_h[:, hc:hc + 2], lhsT=wbf[:, kc, c0:c0 + 128], rhs=mk(hn[:, kc, 0:1], [[513, 2]]), start=(kc == 0), stop=(kc == 7)),
                             reads=WB + HN, writes=[f"ps_hp{fc % 2}"])
                    P.act(lambda e, pr=pr, pp=pp: e.activation(out=pr[:, 1:513], in_=pp[:], func=AF.Copy), reads=[ppt], writes=[prt])
                    P.act(lambda e, pr=pr, hc=hc: e.activation(out=mk(pr[:, 0:1], [[513, 2]]), in_=ps_h[:, hc:hc + 2], func=AF.Copy),
                          reads=[f"ps_hp{fc % 2}"], writes=[prt], tiny=True)
                    P.pool(lambda e, ac=ac, pr=pr, fc=fc: e.tensor_scalar(out=ac[:], in0=pr[:, 0:512], scalar1=cws[:, 0, fc:fc + 1], scalar2=None, op0=ALU.mult),
                           reads=[prt, "cws"], writes=[act_])
                    P.dve(lambda e, ac=ac, pr=pr, fc=fc: e.scalar_tensor_tensor(out=ac[:], in0=pr[:, 1:513], scalar=cws[:, 1, fc:fc + 1], in1=ac[:], op0=ALU.mult, op1=ALU.add),
                           reads=[prt, "cws", act_], writes=[act_])
                    P.dve(lambda e, ac=ac, pr=pr, fc=fc: e.scalar_tensor_tensor(out=ac[:], in0=pr[:, 2:514], scalar=cws[:, 2, fc:fc + 1], in1=ac[:], op0=ALU.mult, op1=ALU.add),
                           reads=[prt, "cws", act_], writes=[act_])
                    sc = (1.0 / ALPHA) if fc < 4 else 1.0
                    P.act(lambda e, ac=ac, sgg=sgg, fc=fc, sc=sc: e.activation(out=sgg[:], in_=ac[:], func=AF.Sigmoid, bias=cb[:, l, fc:fc + 1], scale=sc),
                          reads=[act_, "cb"], writes=[sgt])
                    P.dve(lambda e, ac=ac, sgg=sgg, fc=fc, qo=qo: e.scalar_tensor_tensor(out=qo[:], in0=ac[:], scalar=cbs[:, fc:fc + 1], in1=sgg[:], op0=ALU.add, op1=ALU.mult),
                          reads=[act_, sgt, "cbs"], writes=[qot])
                    dst = (S["QT"] if fc < 4 else S["KT"])[(fc % 4) * 128:(fc % 4 + 1) * 128, t0:t0 + TB]
                    P.dma(lambda e, qo=qo, dst=dst: e.dma_start(out=dst, in_=qo[:]), reads=[qot], writes=["dram_qk"])
                    if fc >= 4:
                        h = fc - 4
                        half = (h % 2) * 512
                        for tt in range(4):
                            P.pe(lambda e, qo=qo, tt=tt, half=half: e.transpose(ps_t[:, half + tt * 128: half + (tt + 1) * 128], qo[:, tt * 128:(tt + 1) * 128], idb[:]),
                                 reads=[qot, "idb"], writes=[f"ps_t{h % 2}"])
                        P.act(lambda e, h=h, half=half: e.activation(out=ktok[:, :, h * 128:(h + 1) * 128], in_=mk(ps_t[:, half:half + 1], [[128, 4], [1, 128]]), func=AF.Copy),
                              reads=[f"ps_t{h % 2}"], writes=["ktok"])
                P.dma(lambda e, t0=t0: e.dma_start(out=S["Ktok"][t0:t0 + TB, :].rearrange("(tt p) c -> p tt c", p=128), in_=ktok[:]),
                      reads=["ktok"], writes=["dram_ktok"])

                ti = 0
                for (c0, kind) in ((1024, "v"), (1536, "o"), (3088, "vn")):
                    for tt in range(4):
                        pt = ps_tok[ti % 2]
                        ptt = f"ps_tok{ti % 2}"
                        ti += 1
                        for kc in range(8):
                            P.pe(lambda e, kc=kc, pt=pt, tt=tt, c0=c0: e.matmul(pt[:], lhsT=hn[:, kc, 1 + tt * 128:1 + (tt + 1) * 128], rhs=wbf[:, kc, c0:c0 + 512], start=(kc == 0), stop=(kc == 7)),
                                 reads=WB + HN, writes=[ptt])
                        if kind == "v":
                            P.act(lambda e, pt=pt, tt=tt: e.activation(out=vtok[:, tt, :], in_=pt[:], func=AF.Copy), reads=[ptt], writes=["vtok"])
                        elif kind == "vn":
                            P.dve(lambda e, pt=pt, tt=tt: e.tensor_copy(out=vntok[:, tt, :], in_=pt[:]), reads=[ptt], writes=["vntok"])
                        else:
                            P.act(lambda e, pt=pt, tt=tt: e.activation(out=osig[:, tt, :], in_=pt[:], func=AF.Sigmoid), reads=[ptt], writes=["osig"])
                P.dma(lambda e, t0=t0: e.dma_start(out=S["Vtok"][t0:t0 + TB, :].rearrange("(tt p) c -> p tt c", p=128), in_=vtok[:]),
                      reads=["vtok"], writes=["dram_vtok"])
                P.dma(lambda e, t0=t0: e.dma_start(out=S["Vn"][t0:t0 + TB, :].rearrange("(tt p) c -> p tt c", p=128), in_=vntok[:]),
                      reads=["vntok"], writes=["dram_vn"])
                P.dma(lambda e, t0=t0: e.dma_start(out=S["Osig"][t0:t0 + TB, :].rearrange("(tt p) c -> p tt c", p=128), in_=osig[:]),
                      reads=["osig"], writes=["dram_osig"])

                for kc in range(8):
                    P.pe(lambda e, kc=kc: e.matmul(ps_g[0:16, :], lhsT=wbf[:, kc, 2048:2064], rhs=hn[:, kc, 1:513], start=(kc == 0), stop=(kc == 7)),
                         reads=WB + HN, writes=["ps_g"])
                P.dve(lambda e: e.tensor_copy(out=gsb[:], in_=ps_g[0:16, :]), reads=["ps_g"], writes=["gsb"])
                P.dma(lambda e, t0=t0: e.dma_start(out=S["G"][:, t0:t0 + TB], in_=gsb[:]), reads=["gsb"], writes=["dram_g"])

                for j in range(8):
                    pp = ps_pre[qi % 2]
                    ppt = f"ps_pre{qi % 2}"
                    qo = qko[qi % 3]
                    qot = f"qko{qi % 3}"
                    qi += 1
                    c0 = 2064 + j * 128
                    for kc in range(8):
                        P.pe(lambda e, kc=kc, pp=pp, c0=c0: e.matmul(pp[:], lhsT=wbf[:, kc, c0:c0 + 128], rhs=hn[:, kc, 1:513], start=(kc == 0), stop=(kc == 7)),
                             reads=WB + HN, writes=[ppt])
                    sc = 0.125 if j < 4 else 1.0
                    P.act(lambda e, pp=pp, qo=qo, sc=sc: e.activation(out=qo[:], in_=pp[:], func=AF.Copy, scale=sc), reads=[ppt], writes=[qot])
                    dst = (S["QnT"] if j < 4 else S["KnT"])[(j % 4) * 128:(j % 4 + 1) * 128, t0:t0 + TB]
                    P.dma(lambda e, qo=qo, dst=dst: e.dma_start(out=dst, in_=qo[:]), reads=[qot], writes=["dram_qkn"])
            P.emit()

    def phase_B(self, l):
        nc, I, S = self.nc, self.I, self.S
        P = Prog(self.ctx)
        with ExitStack() as es:
            def sb(name, shape, dt):
                return es.enter_context(nc.sbuf_tensor(f"L{l}" + name, list(shape), dt))

            def psum(name, shape, dt=F32):
                return es.enter_context(nc.psum_tensor(f"L{l}" + name, list(shape), dt))

            NP = 36
            idf = sb("B_idf", [128, 128], F32)
            idb = sb("B_idb", [128, 128], BF16)
            trif = sb("B_trif", [128, 128], F32)
            trib = sb("B_trib", [128, 128], F32)
            tokser = sb("B_tokser", [128, 4, NCH, 8], F32)
            dcb = sb("B_dcb", [128, 8, NCH], F32)
            mnw = sb("B_mnw", [128, 512], F32)
            epsb = sb("B_epsb", [128, 1], F32)
            es1 = ExitStack()

            def sb1(name, shape, dt):
                return es1.enter_context(nc.sbuf_tensor(f"L{l}" + name, list(shape), dt))

            ser = {n: sb1("B_" + n, [NP, T], F32) for n in ("Ipre", "F", "Bp", "U", "M", "tmp", "W", "EM", "ES")}
            onesr = sb1("B_onesr", [NP, T], F32)
            gb = sb1("B_gb", [NP, DEPTH, 2], F32)
            ngb = sb1("B_ngb", [NP, 1], F32)
            mend = sb1("B_mend", [NP, NCH], F32)
            mprev = sb1("B_mprev", [NP, NCH], F32)
            dcs = sb1("B_dcs", [NP, NCH], F32)
            ps_ser = es1.enter_context(nc.psum_tensor(f"L{l}B_ps_ser", [128, 512], F32))

            for n in ("Ipre", "F"):
                P.dve(lambda e, n=n: e.memset(ser[n][:], 0.0), writes=[n])
            P.pool(lambda e: e.memset(onesr[:], 1.0), writes=["onesr"])
            P.dve(lambda e: e.memset(epsb[:], float(128 * EPS)), writes=["epsb"], tiny=True)
            P.dma(lambda e: e.dma_start(out=gb[:], in_=I["gb"]), writes=["gb"])
            P.dma(lambda e: e.dma_start(out=idf[:], in_=I["ident"]), writes=["idf"])
            P.dma(lambda e: e.dma_start(out=trif[:], in_=I["trif"]), writes=["trif"])
            P.dma(lambda e: e.dma_start(out=trib[:], in_=I["trib"]), writes=["trib"])
            P.dma(lambda e: e.dma_start(out=mnw[:], in_=I["mnw"][:, l, :]), writes=["mnw"])
            P.dve(lambda e: e.tensor_copy(out=idb[:], in_=idf[:]), reads=["idf"], writes=["idb"])
            P.dve(lambda e: e.tensor_scalar(out=mnw[:], in0=mnw[:], scalar1=float(128.0 ** 0.5), scalar2=None, op0=ALU.mult), reads=["mnw"], writes=["mnw"])
            G = S["G"]
            P.dma(lambda e: e.dma_start(out=ser["Ipre"][0:4, :], in_=G[0:4, :]), writes=["Ipre"])
            P.dma(lambda e: e.dma_start(out=ser["F"][0:4, :], in_=G[4:8, :]), writes=["F"])
            P.dma(lambda e: e.dma_start(out=ser["Ipre"][32:36, :], in_=G[8:12, :]), writes=["Ipre"])
            P.dma(lambda e: e.dma_start(out=ser["F"][32:36, :], in_=G[12:16, :]), writes=["F"])
            P.dve(lambda e: e.tensor_scalar(out=ngb[:], in0=gb[:, l, 1:2], scalar1=-1.0, scalar2=None, op0=ALU.mult), reads=["gb"], writes=["ngb"], tiny=True)
            P.dve(lambda e: e.tensor_scalar(out=ser["Ipre"][:], in0=ser["Ipre"][:], scalar1=gb[:, l, 0:1], scalar2=None, op0=ALU.add),
                  reads=["Ipre", "gb"], writes=["Ipre"])
            P.act(lambda e: e.activation(out=ser["tmp"][:], in_=ser["F"][:], func=AF.Exp, bias=ngb[:, 0:1], scale=-1.0), reads=["F", "ngb"], writes=["tmp"])
            P.act(lambda e: e.activation(out=ser["F"][:], in_=ser["tmp"][:], func=AF.Ln, bias=1.0), reads=["tmp"], writes=["F"])

            def rev(ap):
                a = ap.ap
                return bass.AP(ap.tensor, ap.offset + (a[-1][1] - 1) * a[-1][0], [list(p) for p in a[:-1]] + [[-a[-1][0], a[-1][1]]])

            def scan(out, data, op1, wtok, rtok):
                P.dve(lambda e: e.tensor_tensor_scan(out=out[0:4, :], data0=onesr[0:4, :], data1=data[0:4, :], initial=0.0, op0=ALU.mult, op1=op1),
                      reads=[rtok, "onesr"], writes=[wtok])
                P.dve(lambda e: e.tensor_tensor_scan(out=rev(out[32:36, :]), data0=onesr[32:36, :], data1=rev(data[32:36, :]), initial=0.0, op0=ALU.mult, op1=op1),
                      reads=[rtok, "onesr"], writes=[wtok])

            P.dve(lambda e: e.memset(ser["Bp"][:], 0.0), writes=["Bp"])
            P.pool(lambda e: e.memset(ser["M"][:], 0.0), writes=["M"])
            scan(ser["Bp"], ser["F"], ALU.add, "Bp", "F")
            P.dve(lambda e: e.tensor_tensor(out=ser["U"][:], in0=ser["Ipre"][:], in1=ser["Bp"][:], op=ALU.add), reads=["Ipre", "Bp"], writes=["U"])
            scan(ser["M"], ser["U"], ALU.max, "M", "U")
            P.dve(lambda e: e.tensor_tensor(out=ser["tmp"][:], in0=ser["Bp"][:], in1=ser["M"][:], op=ALU.subtract), reads=["Bp", "M"], writes=["tmp"])
            P.act(lambda e: e.activation(out=ser["EM"][:], in_=ser["tmp"][:], func=AF.Exp), reads=["tmp"], writes=["EM"])
            Mv = ser["M"]
            P.dve(lambda e: e.memset(mprev[:], 0.0), writes=["mprev"], tiny=True)
            P.dve(lambda e: e.memset(mend[:], 0.0), writes=["mend"], tiny=True)
            P.dve(lambda e: e.tensor_copy(out=mend[0:4, :], in_=mk(Mv[0:4, 127:128], [[128, NCH]])), reads=["M"], writes=["mend"], tiny=True)
            P.dve(lambda e: e.tensor_copy(out=mend[32:36, :], in_=mk(Mv[32:36, 0:1], [[128, NCH]])), reads=["M"], writes=["mend"], tiny=True)
            P.dve(lambda e: e.tensor_copy(out=mprev[0:4, 1:NCH], in_=mk(Mv[0:4, 127:128], [[128, NCH - 1]])), reads=["M"], writes=["mprev"], tiny=True)
            P.dve(lambda e: e.tensor_copy(out=mprev[32:36, 0:NCH - 1], in_=mk(Mv[32:36, 128:129], [[128, NCH - 1]])), reads=["M"], writes=["mprev"], tiny=True)
            P.dve(lambda e: e.tensor_tensor(out=ser["tmp"][:].rearrange("p (c t) -> p c t", t=128), in0=mk(mprev[:], [[1, NCH], [0, 128]]),
                                            in1=ser["M"][:].rearrange("p (c t) -> p c t", t=128), op=ALU.subtract),
                  reads=["mprev", "M", "EM"], writes=["tmp"])
            P.act(lambda e: e.activation(out=ser["W"][:], in_=ser["tmp"][:], func=AF.Exp), reads=["tmp"], writes=["W"])
            P.dve(lambda e: e.tensor_tensor(out=ser["tmp"][:].rearrange("p (c t) -> p c t", t=128), in0=ser["U"][:].rearrange("p (c t) -> p c t", t=128),
                                            in1=mk(mend[:], [[1, NCH], [0, 128]]), op=ALU.subtract),
                  reads=["mend", "U", "W"], writes=["tmp"])
            P.act(lambda e: e.activation(out=ser["ES"][:], in_=ser["tmp"][:], func=AF.Exp), reads=["tmp"], writes=["ES"])
            P.dve(lambda e: e.tensor_tensor(out=dcs[:], in0=mprev[:], in1=mend[:], op=ALU.subtract), reads=["mprev", "mend"], writes=["dcs"], tiny=True)
            P.act(lambda e: e.activation(out=dcs[:], in_=dcs[:], func=AF.Exp), reads=["dcs"], writes=["dcs"], tiny=True)
            P.dma(lambda e: e.dma_start(out=S["Mser"][0:4, :], in_=ser["M"][0:4, :]), reads=["M"], writes=["dram_M"])
            P.dma(lambda e: e.dma_start(out=S["Mser"][4:8, :], in_=ser["M"][32:36, :]), reads=["M"], writes=["dram_M"])
            P.dma(lambda e: e.dma_start(out=S["DCs"][0:4, :], in_=dcs[0:4, :]), reads=["dcs"], writes=["dram_DC"])
            P.dma(lambda e: e.dma_start(out=S["DCs"][4:8, :], in_=dcs[32:36, :]), reads=["dcs"], writes=["dram_DC"])
            P.dma(lambda e: e.dma_start(out=dcb[:], in_=bass.AP(S["DCs"].tensor, S["DCs"].offset, [[0, 128], [NCH, 8], [1, NCH]])),
                  reads=["dram_DC"], writes=["dcb"])
            for si, n in enumerate(("U", "W", "EM", "ES")):
                for c0 in range(0, NCH, 8):
                    for c in range(c0, c0 + 8):
                        P.pe(lambda e, n=n, c=c, c0=c0: e.matmul(ps_ser[:, (c - c0) * 64:(c - c0) * 64 + NP], lhsT=ser[n][:, c * 128:(c + 1) * 128], rhs=idf[0:NP, 0:NP], start=True, stop=True),
                             reads=[n, "idf"], writes=["ps_ser"])
                    P.dve(lambda e, si=si, c0=c0: e.tensor_copy(out=tokser[:, si, c0:c0 + 8, 0:4], in_=mk(ps_ser[:, 0:1], [[64, 8], [1, 4]])),
                          reads=["ps_ser"], writes=["tokser"])
                    P.dve(lambda e, si=si, c0=c0: e.tensor_copy(out=tokser[:, si, c0:c0 + 8, 4:8], in_=mk(ps_ser[:, 32:33], [[64, 8], [1, 4]])),
                          reads=["ps_ser"], writes=["tokser"])
            if self.debug:
                P.dma(lambda e: e.dma_start(out=S["dbgser"], in_=tokser[:]), reads=["tokser"], writes=["dram_dbgser"])

            P.emit()
            es1.close()
            P = Prog(self.ctx)

            cst = sb("B_cst", [128, 2, NCH, 4 * 129], BF16)
            c32 = sb("B_c32", [128, 2, 4 * 129], F32)
            kt_ = [sb(f"B_ktok{i}", [128, 512], BF16) for i in range(3)]
            v1_ = [sb(f"B_v1{i}", [128, 4, 129], BF16) for i in range(3)]
            vs_ = [sb(f"B_vs{i}", [128, 4, 129], BF16) for i in range(2)]
            ps_c = [psum(f"B_ps_c{i}", [128, 512]) for i in range(2)]
            P.dve(lambda e: e.memset(c32[:], 0.0), writes=["c32_0", "c32_1"])
            for i in range(3):
                P.pool(lambda e, i=i: e.memset(v1_[i][:, :, 128:129], 1.0), writes=[f"v1{i}"])
            li = 0
            vi = 0
            for d in range(2):
                order = list(range(NCH)) if d == 0 else list(range(NCH - 1, -1, -1))
                for step, c in enumerate(order):
                    P.act(lambda e, d=d, c=c: e.activation(out=cst[:, d, c, :], in_=c32[:, d, :], func=AF.Copy), reads=[f"c32_{d}"], writes=[f"cst{d}_{c}"])
                    if step == NCH - 1:
                        break
                    kt = kt_[li % 3]
                    v1 = v1_[li % 3]
                    ktt = f"ktokb{li % 3}"
                    v1t = f"v1{li % 3}"
                    li += 1
                    P.dma(lambda e, kt=kt, c=c: e.dma_start(out=kt[:], in_=S["Ktok"][c * 128:(c + 1) * 128, :]), writes=[ktt])
                    P.dma(lambda e, v1=v1, c=c: e.dma_start(out=v1[:, :, 0:128], in_=S["Vtok"][c * 128:(c + 1) * 128, :].rearrange("p (h x) -> p h x", h=4)), writes=[v1t])
                    vs = vs_[vi % 2]
                    vst = f"vs{vi % 2}"
                    vi += 1
                    P.dve(lambda e, vs=vs, v1=v1, c=c, d=d: e.tensor_tensor(out=vs[:], in0=v1[:], in1=mk(tokser[:, 3, c, d * 4:d * 4 + 1], [[1, 4], [0, 129]]), op=ALU.mult),
                          reads=[v1t, "tokser"], writes=[vst])
                    for hp in range(2):
                        pc = ps_c[hp]
                        for hh in range(2):
                            h = hp * 2 + hh
                            P.pe(lambda e, pc=pc, hh=hh, h=h, kt=kt, vs=vs: e.matmul(pc[:, hh * 129:(hh + 1) * 129], lhsT=kt[:, h * 128:(h + 1) * 128], rhs=vs[:, h, :], start=True, stop=True),
                                 reads=[ktt, vst], writes=[f"ps_c{hp}"])
                        for hh in range(2):
                            h = hp * 2 + hh
                            P.dve(lambda e, pc=pc, hh=hh, h=h, d=d, c=c: e.scalar_tensor_tensor(out=c32[:, d, h * 129:(h + 1) * 129], in0=c32[:, d, h * 129:(h + 1) * 129],
                                                                                     scalar=dcb[:, d * 4 + h, c:c + 1], in1=pc[:, hh * 129:(hh + 1) * 129], op0=ALU.mult, op1=ALU.add),
                                  reads=[f"ps_c{hp}", "dcb", f"c32_{d}"], writes=[f"c32_{d}"])

            qT_ = [sb(f"B_qT{i}", [128, 4, 128], BF16) for i in range(2)]
            kT_ = [sb(f"B_kT{i}", [128, 4, 128], BF16) for i in range(2)]
            mb_ = [sb(f"B_mb{i}", [128, 8, 128], F32) for i in range(2)]
            og_ = [sb(f"B_og{i}", [128, 512], F32) for i in range(2)]
            gg = sb("B_gg", [128, 512], F32)
            sm_ = [sb(f"B_sm{i}", [128, 2, 128], F32) for i in range(2)]
            ee_ = [sb(f"B_ee{i}", [128, 2, 128], F32) for i in range(2)]
            pp_ = [sb(f"B_pp{i}", [128, 2, 128], BF16) for i in range(2)]
            tin_ = [sb(f"B_tin{i}", [128, 2, 129], F32) for i in range(2)]
            num_ = [sb(f"B_num{i}", [128, 2, 129], F32) for i in range(2)]
            dd = sb("B_dd", [128, 2], F32)
            hall = sb("B_hall", [128, 512], F32)
            hsq = sb("B_hsq", [128, 512], F32)
            ss = sb("B_ss", [128, 4], F32)
            ybf = sb("B_ybf", [128, 512], BF16)
            yT_ = [sb(f"B_yT{i}", [128, 4, 128], BF16) for i in range(2)]
            ps_s = [psum(f"B_ps_s{i}", [128, 512]) for i in range(2)]
            ps_o = [psum(f"B_ps_o{i}", [128, 4, 128]) for i in range(2)]
            ps_y = psum("B_ps_y", [128, 1024], BF16)
            ui = 0
            for c in range(NCH):
                s = c % 2
                qT, kT, mb, og = qT_[s], kT_[s], mb_[s], og_[s]
                kt = kt_[c % 3]
                v1 = v1_[c % 3]
                ktt = f"ktokb{c % 3}"
                v1t = f"v1{c % 3}"
                tk = c * 128
                P.dma(lambda e, qT=qT, tk=tk: e.dma_start(out=qT[:], in_=S["QT"][:, tk:tk + 128].rearrange("(h p) t -> p h t", p=128)), writes=[f"qT{s}"])
                P.dma(lambda e, kT=kT, tk=tk: e.dma_start(out=kT[:], in_=S["KT"][:, tk:tk + 128].rearrange("(h p) t -> p h t", p=128)), writes=[f"kT{s}"])
                P.dma(lambda e, mb=mb, tk=tk: e.dma_start(out=mb[:], in_=bass.AP(S["Mser"].tensor, S["Mser"].offset + tk, [[0, 128], [T, 8], [1, 128]])), writes=[f"mb{s}"])
                P.dma(lambda e, og=og, tk=tk: e.dma_start(out=og[:], in_=S["Osig"][tk:tk + 128, :]), writes=[f"og{s}"])
                P.dma(lambda e, v1=v1, tk=tk: e.dma_start(out=v1[:, :, 0:128], in_=S["Vtok"][tk:tk + 128, :].rearrange("p (h x) -> p h x", h=4)), writes=[v1t])
                P.pool(lambda e, og=og: e.tensor_tensor(out=gg[:], in0=og[:], in1=mnw[:], op=ALU.mult),
                       reads=[f"og{s}", "mnw"], writes=["gg"])
                for h in range(4):
                    u = ui % 2
                    ui += 1
                    ps, po = ps_s[u], ps_o[u]
                    sm, ee, pp, tin, num = sm_[u], ee_[u], pp_[u], tin_[u], num_[u]
                    P.pe(lambda e, ps=ps, kT=kT, qT=qT, h=h: e.matmul(ps[:, 0:128], lhsT=kT[:, h, :], rhs=qT[:, h, :], start=True, stop=True),
                         reads=[f"kT{s}", f"qT{s}"], writes=[f"ps_s{u}"])
                    P.dve(lambda e, ps=ps, sm=sm: e.tensor_tensor(out=sm[:, 0, :], in0=ps[:, 0:128], in1=trif[:], op=ALU.mult), reads=[f"ps_s{u}", "trif"], writes=[f"sm{u}"])
                    P.dve(lambda e, ps=ps, sm=sm: e.tensor_tensor(out=sm[:, 1, :], in0=ps[:, 0:128], in1=trib[:], op=ALU.mult), reads=[f"ps_s{u}", "trib"], writes=[f"sm{u}"])
                    for d in range(2):
                        P.act(lambda e, ee=ee, mb=mb, d=d, h=h, c=c: e.activation(out=ee[:, d, :], in_=mb[:, d * 4 + h, :], func=AF.Exp, bias=tokser[:, 0, c, d * 4 + h:d * 4 + h + 1], scale=-1.0),
                              reads=[f"mb{s}", "tokser"], writes=[f"ee{u}"])
                    P.pool(lambda e, pp=pp, sm=sm, ee=ee: e.tensor_tensor(out=pp[:], in0=sm[:], in1=ee[:], op=ALU.mult), reads=[f"sm{u}", f"ee{u}"], writes=[f"pp{u}"])
                    for d in range(2):
                        P.pe(lambda e, po=po, qT=qT, h=h, d=d, c=c: e.matmul(po[:, 2 * d, :], lhsT=qT[:, h, :], rhs=cst[:, d, c, h * 129:h * 129 + 128], start=True, stop=True),
                             reads=[f"qT{s}", f"cst{d}_{c}"], writes=[f"ps_o{u}"])
                        P.pe(lambda e, po=po, pp=pp, v1=v1, h=h, d=d: e.matmul(po[:, 2 * d + 1, :], lhsT=pp[:, d, :], rhs=v1[:, h, 0:128], start=True, stop=True),
                             reads=[f"pp{u}", v1t], writes=[f"ps_o{u}"])
                        P.pe(lambda e, ps=ps, qT=qT, h=h, d=d, c=c: e.matmul(ps[:, 256 + 2 * d:257 + 2 * d], lhsT=qT[:, h, :], rhs=cst[:, d, c, h * 129 + 128:h * 129 + 129], start=True, stop=True),
                             reads=[f"qT{s}", f"cst{d}_{c}"], writes=[f"ps_d{u}"])
                        P.pe(lambda e, ps=ps, pp=pp, v1=v1, h=h, d=d: e.matmul(ps[:, 257 + 2 * d:258 + 2 * d], lhsT=pp[:, d, :], rhs=v1[:, h, 128:129], start=True, stop=True),
                             reads=[f"pp{u}", v1t], writes=[f"ps_d{u}"])
                    for d in range(2):
                        wcol = tokser[:, 1, c, d * 4 + h:d * 4 + h + 1]
                        P.act(lambda e, tin=tin, po=po, d=d, wcol=wcol: e.activation(out=tin[:, d, 0:128], in_=po[:, 2 * d, :], func=AF.Copy, scale=wcol),
                              reads=[f"ps_o{u}", "tokser"], writes=[f"tin{u}"])
                        P.act(lambda e, tin=tin, ps=ps, d=d, wcol=wcol: e.activation(out=tin[:, d, 128:129], in_=ps[:, 256 + 2 * d:257 + 2 * d], func=AF.Copy, scale=wcol),
                              reads=[f"ps_d{u}", "tokser"], writes=[f"tin{u}"], tiny=True)
                        P.dve(lambda e, num=num, tin=tin, po=po, d=d: e.tensor_tensor(out=num[:, d, 0:128], in0=tin[:, d, 0:128], in1=po[:, 2 * d + 1, :], op=ALU.add),
                              reads=[f"tin{u}", f"ps_o{u}"], writes=[f"num{u}"])
                        P.dve(lambda e, num=num, tin=tin, ps=ps, d=d: e.tensor_tensor(out=num[:, d, 128:129], in0=tin[:, d, 128:129], in1=ps[:, 257 + 2 * d:258 + 2 * d], op=ALU.add),
                              reads=[f"tin{u}", f"ps_d{u}"], writes=[f"num{u}"], tiny=True)
                        P.dve(lambda e, num=num, d=d: e.scalar_tensor_tensor(out=dd[:, d:d + 1], in0=num[:, d, 128:129], scalar=-1.0, in1=num[:, d, 128:129], op0=ALU.mult, op1=ALU.max),
                              reads=[f"num{u}"], writes=["dd"], tiny=True)
                        P.dve(lambda e, d=d, h=h, c=c: e.tensor_tensor(out=dd[:, d:d + 1], in0=dd[:, d:d + 1], in1=tokser[:, 2, c, d * 4 + h:d * 4 + h + 1], op=ALU.max),
                              reads=["dd", "tokser"], writes=["dd"], tiny=True)
                    P.dve(lambda e: e.reciprocal(out=dd[:], in_=dd[:]), reads=["dd"], writes=["dd"], tiny=True)
                    P.dve(lambda e, num=num, h=h: e.tensor_scalar(out=hall[:, h * 128:(h + 1) * 128], in0=num[:, 0, 0:128], scalar1=dd[:, 0:1], scalar2=None, op0=ALU.mult),
                          reads=[f"num{u}", "dd"], writes=["hall"])
                    P.dve(lambda e, num=num, h=h: e.scalar_tensor_tensor(out=hall[:, h * 128:(h + 1) * 128], in0=num[:, 1, 0:128], scalar=dd[:, 1:2], in1=hall[:, h * 128:(h + 1) * 128], op0=ALU.mult, op1=ALU.add),
                          reads=[f"num{u}", "dd", "hall"], writes=["hall"])
                P.pool(lambda e: e.tensor_tensor(out=hsq[:], in0=hall[:], in1=hall[:], op=ALU.mult), reads=["hall"], writes=["hsq"])
                P.dve(lambda e: e.tensor_reduce(out=ss[:], in_=hsq[:].rearrange("p (h x) -> p h x", h=4), axis=AX.X, op=ALU.add), reads=["hsq"], writes=["ss"], tiny=True)
                P.act(lambda e: e.activation(out=ss[:], in_=ss[:], func=AF.Ln, bias=epsb[:, 0:1]), reads=["ss", "epsb"], writes=["ss"], tiny=True)
                P.act(lambda e: e.activation(out=ss[:], in_=ss[:], func=AF.Exp, scale=-0.5), reads=["ss"], writes=["ss"], tiny=True)
                for h in range(4):
                    P.dve(lambda e, h=h: e.scalar_tensor_tensor(out=ybf[:, h * 128:(h + 1) * 128], in0=hall[:, h * 128:(h + 1) * 128], scalar=ss[:, h:h + 1], in1=gg[:, h * 128:(h + 1) * 128], op0=ALU.mult, op1=ALU.mult),
                          reads=["hall", "ss", "gg"], writes=["ybf"])
                yT = yT_[s]
                for h in range(4):
                    P.pe(lambda e, h=h, s=s: e.transpose(ps_y[:, s * 512 + h * 128: s * 512 + (h + 1) * 128], ybf[:, h * 128:(h + 1) * 128], idb[:]),
                         reads=["ybf", "idb"], writes=[f"ps_y{s}"])
                P.act(lambda e, yT=yT, s=s: e.activation(out=yT[:], in_=ps_y[:, s * 512:(s + 1) * 512].rearrange("p (h t) -> p h t", h=4), func=AF.Copy),
                      reads=[f"ps_y{s}"], writes=[f"yT{s}"])
                P.dma(lambda e, yT=yT, tk=tk: e.dma_start(out=S["YT"][0:512, tk:tk + 128].rearrange("(h p) t -> p h t", p=128), in_=yT[:]),
                      reads=[f"yT{s}"], writes=["dram_yt"])
            P.emit()

    def phase_C(self, l):
        nc, I, S = self.nc, self.I, self.S
        P = Prog(self.ctx)
        with ExitStack() as es:
            def sb(name, shape, dt):
                return es.enter_context(nc.sbuf_tensor(f"L{l}" + name, list(shape), dt))

            def psum(name, shape, dt=F32):
                return es.enter_context(nc.psum_tensor(f"L{l}" + name, list(shape), dt))

            qn = sb("C_qn", [128, 4, T], BF16)
            kn = sb("C_kn", [128, 4, T], BF16)
            vn = sb("C_vn", [128, 32, 8, 65], BF16)
            nbi = sb("C_nbi", [128, 8, 5, 128], F32)
            nbe = sb("C_nbe", [128, 8, 4, 128], F32)
            idf = sb("C_idf", [128, 128], F32)
            idb = sb("C_idb", [128, 128], BF16)
            sbb_ = [sb(f"C_sb{i}", [128, 5, 128], F32) for i in range(2)]
            pt_ = [sb(f"C_pt{i}", [128, 5, 128], BF16) for i in range(2)]
            rd = sb("C_rd", [128, 8], F32)
            yn = sb("C_yn", [128, 8, 64], BF16)
            yT_ = [sb(f"C_yT{i}", [128, 4, 128], BF16) for i in range(2)]
            ps_s = [psum(f"C_ps_s{i}", [128, 1024]) for i in range(2)]
            ps_o = psum("C_ps_o", [128, 512])
            ps_y = psum("C_ps_y", [128, 1024], BF16)
            P.dma(lambda e: e.dma_start(out=idf[:], in_=I["ident"]), writes=["idf"])
            P.dve(lambda e: e.tensor_copy(out=idb[:], in_=idf[:]), reads=["idf"], writes=["idb"])
            P.dma(lambda e: e.dma_start(out=nbi[:], in_=I["nbi"][l]), writes=["nbi"])
            for g in range(4):
                P.dma(lambda e, g=g: e.dma_start(out=qn[:, g, :], in_=S["QnT"][g * 128:(g + 1) * 128, :]), writes=["qn"])
                P.dma(lambda e, g=g: e.dma_start(out=kn[:, g, :], in_=S["KnT"][g * 128:(g + 1) * 128, :]), writes=["kn"])
            P.pool(lambda e: e.memset(vn[:, :, :, 64:65], 1.0), writes=["vn"])
            for m in range(32):
                P.dma(lambda e, m=m: e.dma_start(out=vn[:, m, :, 0:64], in_=S["Vn"][m * 128:(m + 1) * 128, :].rearrange("p (h x) -> p h x", h=8)), writes=["vn"])
            ui = 0
            for i in range(32):
                if i < 2:
                    tiles = [0, 1, 2, 3]
                    eidx = i
                elif i >= 30:
                    tiles = [28, 29, 30, 31]
                    eidx = i - 28
                else:
                    tiles = [i - 2, i - 1, i, i + 1, i + 2]
                    eidx = None
                if eidx is not None:
                    P.dma(lambda e, eidx=eidx: e.dma_start(out=nbe[:], in_=I["nbe"][l, eidx]), writes=["nbe"])
                nt = len(tiles)
                for h in range(8):
                    g, hp = h // 2, (h % 2) * 64
                    u = ui % 2
                    ui += 1
                    ps = ps_s[u]
                    sbb, pt = sbb_[u], pt_[u]
                    for j, m in enumerate(tiles):
                        P.pe(lambda e, ps=ps, j=j, m=m, g=g, hp=hp, i=i: e.matmul(ps[:, j * 128:(j + 1) * 128], lhsT=kn[hp:hp + 64, g, m * 128:(m + 1) * 128], rhs=qn[hp:hp + 64, g, i * 128:(i + 1) * 128], start=True, stop=True),
                             reads=["kn", "qn"], writes=[f"ps_s{u}"])
                    bias = nbi[:, h, :, :] if eidx is None else nbe[:, h, :, :]
                    btok = "nbi" if eidx is None else "nbe"
                    P.dve(lambda e, ps=ps, sbb=sbb, bias=bias, nt=nt: e.tensor_tensor(out=sbb[:, 0:nt, :], in0=ps[:, 0:nt * 128].rearrange("p (j x) -> p j x", x=128), in1=bias, op=ALU.add),
                          reads=[f"ps_s{u}", btok], writes=[f"sb{u}"])
                    P.act(lambda e, sbb=sbb, pt=pt, nt=nt: e.activation(out=pt[:, 0:nt, :], in_=sbb[:, 0:nt, :], func=AF.Exp), reads=[f"sb{u}"], writes=[f"pt{u}"])
                    so = (ui % 4) * 128
                    sot = f"ps_o{ui % 4}"
                    for j, m in enumerate(tiles):
                        P.pe(lambda e, pt=pt, j=j, m=m, h=h, nt=nt, so=so: e.matmul(ps_o[:, so:so + 65], lhsT=pt[:, j, :], rhs=vn[:, m, h, :], start=(j == 0), stop=(j == nt - 1)),
                             reads=[f"pt{u}", "vn"], writes=[sot])
                    P.dve(lambda e, h=h, so=so: e.reciprocal(out=rd[:, h:h + 1], in_=ps_o[:, so + 64:so + 65]), reads=[sot], writes=[f"rd{h}"], tiny=True)
                    P.act(lambda e, h=h, so=so: e.activation(out=yn[:, h, :], in_=ps_o[:, so:so + 64], func=AF.Copy, scale=rd[:, h:h + 1]), reads=[sot, f"rd{h}"], writes=["yn"])
                s = i % 2
                yT = yT_[s]
                for g in range(4):
                    P.pe(lambda e, g=g, s=s: e.transpose(ps_y[:, s * 512 + g * 128:s * 512 + (g + 1) * 128], yn[:, 2 * g:2 * g + 2, :].rearrange("p a b -> p (a b)"), idb[:]),
                         reads=["yn", "idb"], writes=[f"ps_y{s}"])
                P.dve(lambda e, yT=yT, s=s: e.tensor_copy(out=yT[:], in_=ps_y[:, s * 512:(s + 1) * 512].rearrange("p (h t) -> p h t", h=4)),
                      reads=[f"ps_y{s}"], writes=[f"yT{s}"])
                P.dma(lambda e, yT=yT, i=i: e.dma_start(out=S["YT"][512:1024, i * 128:(i + 1) * 128].rearrange("(h p) t -> p h t", p=128), in_=yT[:]),
                      reads=[f"yT{s}"], writes=["dram_yt"])
            P.emit()

    def phase_D(self, l):
        nc, I, S = self.nc, self.I, self.S
        P = Prog(self.ctx)
        xsrc = I["xT"] if l == 0 else S["xres"]
        with ExitStack() as es:
            def sb(name, shape, dt):
                return es.enter_context(nc.sbuf_tensor(f"L{l}" + name, list(shape), dt))

            def psum(name, shape, dt=F32):
                return es.enter_context(nc.psum_tensor(f"L{l}" + name, list(shape), dt))

            wbf = sb("D_wbf", [128, 8, D], BF16)
            wst = [sb(f"D_wst{i}", [128, D], F32) for i in range(2)]
            xin = [sb(f"D_xin{i}", [128, 8, TB], F32) for i in range(2)]
            yin = [sb(f"D_yin{i}", [128, 8, TB], BF16) for i in range(2)]
            pso = [psum(f"D_ps{i}", [128, 512]) for i in range(4)]
            for kc in range(8):
                st = wst[kc % 2]
                P.dma(lambda e, st=st, kc=kc: e.dma_start(out=st[:], in_=I["w_out"][l, kc * 128:(kc + 1) * 128, :]), writes=[f"wst{kc % 2}"])
                eng = P.pool if kc % 2 == 0 else P.act
                if kc % 2 == 0:
                    P.pool(lambda e, st=st, kc=kc: e.tensor_copy(out=wbf[:, kc, :], in_=st[:]), reads=[f"wst{kc % 2}"], writes=[f"wbf{kc}"])
                else:
                    P.act(lambda e, st=st, kc=kc: e.activation(out=wbf[:, kc, :], in_=st[:], func=AF.Copy), reads=[f"wst{kc % 2}"], writes=[f"wbf{kc}"])
            WB = [f"wbf{kc}" for kc in range(8)]

            def load(b):
                t0 = b * TB
                P.dma(lambda e, b=b, t0=t0: e.dma_start(out=xin[b % 2][:], in_=xsrc[:, t0:t0 + TB].rearrange("(kc p) t -> p kc t", p=128)), writes=[f"xin{b % 2}"])
                P.dma(lambda e, b=b, t0=t0: e.dma_start(out=yin[b % 2][:], in_=S["YT"][:, t0:t0 + TB].rearrange("(kc p) t -> p kc t", p=128)), writes=[f"yin{b % 2}"])

            load(0)
            pi = 0
            for b in range(NB):
                if b + 1 < NB:
                    load(b + 1)
                t0 = b * TB
                xi, yi = xin[b % 2], yin[b % 2]
                for dc in range(8):
                    ps = pso[pi % 4]
                    pst = f"ps{pi % 4}"
                    pi += 1
                    for kc in range(8):
                        P.pe(lambda e, ps=ps, kc=kc, dc=dc, yi=yi: e.matmul(ps[:], lhsT=wbf[:, kc, dc * 128:(dc + 1) * 128], rhs=yi[:, kc, :], start=(kc == 0), stop=(kc == 7)),
                             reads=WB + [f"yin{b % 2}"], writes=[pst])
                    P.dve(lambda e, ps=ps, dc=dc, xi=xi: e.tensor_tensor(out=xi[:, dc, :], in0=xi[:, dc, :], in1=ps[:], op=ALU.add),
                          reads=[pst, f"xin{b % 2}"], writes=[f"xin{b % 2}"])
                P.dma(lambda e, xi=xi, t0=t0: e.dma_start(out=S["xres"][:, t0:t0 + TB].rearrange("(kc p) t -> p kc t", p=128), in_=xi[:]),
                      reads=[f"xin{b % 2}"], writes=["dram_x"])
            P.emit()

    def phase_E(self, l):
        nc, I, S = self.nc, self.I, self.S
        P = Prog(self.ctx)
        last = (l == DEPTH - 1)
        with ExitStack() as es:
            def sb(name, shape, dt):
                return es.enter_context(nc.sbuf_tensor(f"L{l}" + name, list(shape), dt))

            def psum(name, shape, dt=F32):
                return es.enter_context(nc.psum_tensor(f"L{l}" + name, list(shape), dt))

            w1 = sb("E_w1", [128, 8, DFF], BF16)
            w2 = sb("E_w2", [128, 32, D], BF16)
            wst = [sb(f"E_wst{i}", [128, 1024], F32) for i in range(2)]
            n2 = sb("E_n2", [128, DEPTH, 8], F32)
            nf = sb("E_nf", [128, 8], F32)
            ones = sb("E_ones", [128, 128], BF16)
            epsb = sb("E_epsb", [128, 1], F32)
            xin = [sb(f"E_xin{i}", [128, 8, TB], F32) for i in range(2)]
            xsq = sb("E_xsq", [128, 8, TB], BF16)
            rstd = sb("E_rstd", [128, TB], F32)
            hn = sb("E_hn", [128, 8, TB], BF16)
            uu = [sb(f"E_u{i}", [128, 8, TB], BF16) for i in range(2)]
            rtmp = [sb(f"E_rtmp{i}", [128, TB], F32) for i in range(2)]
            ps_ssq = psum("E_ps_ssq", [128, 512])
            ps1 = [psum(f"E_ps1_{i}", [128, 512]) for i in range(3)]
            ps2 = [psum(f"E_ps2_{i}", [128, 512]) for i in range(3)]
            P.dma(lambda e: e.dma_start(out=n2[:], in_=I["n2"]), writes=["n2"])
            P.dma(lambda e: e.dma_start(out=nf[:], in_=I["nf"]), writes=["nf"])
            P.dve(lambda e: e.memset(ones[:], 1.0), writes=["ones"])
            P.dve(lambda e: e.memset(epsb[:], float(D * EPS)), writes=["epsb"], tiny=True)
            si = 0
            for kc in range(8):
                for hf in range(4):
                    st = wst[si % 2]
                    stt = f"wst{si % 2}"
                    P.dma(lambda e, st=st, kc=kc, hf=hf: e.dma_start(out=st[:], in_=I["w_ff1"][l, kc * 128:(kc + 1) * 128, hf * 1024:(hf + 1) * 1024]), writes=[stt])
                    eng = (P.pool, P.dve)[si % 2]
                    eng(lambda e, st=st, kc=kc, hf=hf: e.tensor_scalar(out=w1[:, kc, hf * 1024:(hf + 1) * 1024], in0=st[:], scalar1=n2[:, l, kc:kc + 1], scalar2=None, op0=ALU.mult),
                        reads=[stt, "n2"], writes=[f"w1_{kc}_{hf}"])
                    si += 1
            for k2 in range(32):
                st = wst[si % 2]
                stt = f"wst{si % 2}"
                P.dma(lambda e, st=st, k2=k2: e.dma_start(out=st[:], in_=I["w_ff2"][l, k2 * 128:(k2 + 1) * 128, :]), writes=[stt])
                if si % 2 == 0:
                    P.pool(lambda e, st=st, k2=k2: e.tensor_copy(out=w2[:, k2, :], in_=st[:]), reads=[stt], writes=[f"w2_{k2}"])
                else:
                    P.act(lambda e, st=st, k2=k2: e.activation(out=w2[:, k2, :], in_=st[:], func=AF.Copy), reads=[stt], writes=[f"w2_{k2}"])
                si += 1
            W1 = [f"w1_{kc}_{hf}" for kc in range(8) for hf in range(4)]
            W2 = [f"w2_{k2}" for k2 in range(32)]

            def load(b):
                t0 = b * TB
                P.dma(lambda e, b=b, t0=t0: e.dma_start(out=xin[b % 2][:], in_=S["xres"][:, t0:t0 + TB].rearrange("(kc p) t -> p kc t", p=128)), writes=[f"xin{b % 2}"])

            def rms(xi, xt, dst_fn, wcol_fn):
                P.act(lambda e: e.activation(out=xsq[:], in_=xi[:], func=AF.Square), reads=[xt], writes=["xsq"])
                for kc in range(8):
                    P.pe(lambda e, kc=kc: e.matmul(ps_ssq[:], lhsT=ones[:], rhs=xsq[:, kc, :], start=(kc == 0), stop=(kc == 7)), reads=["ones", "xsq"], writes=["ps_ssq"])
                P.act(lambda e: e.activation(out=rstd[:], in_=ps_ssq[:], func=AF.Ln, bias=epsb[:, 0:1]), reads=["ps_ssq", "epsb"], writes=["rstd"])
                P.act(lambda e: e.activation(out=rstd[:], in_=rstd[:], func=AF.Exp, scale=-0.5), reads=["rstd"], writes=["rstd"])

            load(0)
            p1 = 0
            p2 = 0
            gi = 0
            for b in range(NB):
                if b + 1 < NB:
                    load(b + 1)
                t0 = b * TB
                xi = xin[b % 2]
                xt = f"xin{b % 2}"
                rms(xi, xt, None, None)
                for kc in range(8):
                    eng = P.dve
                    eng(lambda e, kc=kc, xi=xi: e.scalar_tensor_tensor(out=hn[:, kc, :], in0=xi[:, kc, :], scalar=32.0, in1=rstd[:], op0=ALU.mult, op1=ALU.mult),
                        reads=[xt, "rstd"], writes=[f"hn{kc}"])
                HN = [f"hn{kc}" for kc in range(8)]
                for grp in range(4):
                    u = uu[gi % 2]
                    ut = f"u{gi % 2}"
                    gi += 1
                    for fj in range(8):
                        fc = grp * 8 + fj
                        ps = ps1[p1 % 3]
                        pst = f"ps1_{p1 % 3}"
                        p1 += 1
                        for kc in range(8):
                            P.pe(lambda e, ps=ps, kc=kc, fc=fc: e.matmul(ps[:], lhsT=w1[:, kc, fc * 128:(fc + 1) * 128], rhs=hn[:, kc, :], start=(kc == 0), stop=(kc == 7)),
                                 reads=W1 + HN, writes=[pst])
                        rt = rtmp[p1 % 2]
                        rtt = f"rtmp{p1 % 2}"
                        P.act(lambda e, ps=ps, rt=rt: e.activation(out=rt[:], in_=ps[:], func=AF.Relu), reads=[pst], writes=[rtt])
                        eng = P.dve if fj % 2 == 0 else P.pool
                        eng(lambda e, u=u, fj=fj, rt=rt: e.tensor_tensor(out=u[:, fj, :], in0=rt[:], in1=rt[:], op=ALU.mult), reads=[rtt], writes=[f"{ut}_{fj}"])
                    UT = [f"{ut}_{fj}" for fj in range(8)]
                    for dc in range(8):
                        ps = ps2[p2 % 3]
                        pst = f"ps2_{p2 % 3}"
                        p2 += 1
                        for fj in range(8):
                            P.pe(lambda e, ps=ps, fj=fj, grp=grp, dc=dc, u=u: e.matmul(ps[:], lhsT=w2[:, grp * 8 + fj, dc * 128:(dc + 1) * 128], rhs=u[:, fj, :], start=(fj == 0), stop=(fj == 7)),
                                 reads=W2 + UT, writes=[pst])
                        P.dve(lambda e, ps=ps, dc=dc, xi=xi: e.tensor_tensor(out=xi[:, dc, :], in0=xi[:, dc, :], in1=ps[:], op=ALU.add),
                              reads=[pst, xt] + HN, writes=[xt])
                if not last:
                    P.dma(lambda e, xi=xi, t0=t0: e.dma_start(out=S["xres"][:, t0:t0 + TB].rearrange("(kc p) t -> p kc t", p=128), in_=xi[:]),
                          reads=[xt], writes=["dram_x"])
                else:
                    rms(xi, xt, None, None)
                    for kc in range(8):
                        P.dve(lambda e, kc=kc, xi=xi: e.scalar_tensor_tensor(out=xi[:, kc, :], in0=xi[:, kc, :], scalar=32.0, in1=rstd[:], op0=ALU.mult, op1=ALU.mult),
                              reads=[xt, "rstd"], writes=[xt])
                        P.pool(lambda e, kc=kc, xi=xi: e.tensor_scalar(out=xi[:, kc, :], in0=xi[:, kc, :], scalar1=nf[:, kc:kc + 1], scalar2=None, op0=ALU.mult),
                               reads=[xt, "nf"], writes=[xt])
                    P.dma(lambda e, xi=xi, t0=t0: e.dma_start(out=self.out[:, t0:t0 + TB].rearrange("(kc p) t -> p kc t", p=128), in_=xi[:]),
                          reads=[xt], writes=["dram_out"])
            P.emit()


def _na_bias_tables(rpb_l):
    H = rpb_l.shape[0]
    kl = np.arange(128)
    kr_l, kc = kl // 64, kl % 64
    ql = np.arange(128)
    qr_l, qc = ql // 64, ql % 64

    def tile(i, m):
        kr = 2 * m + kr_l[:, None]
        qr = 2 * i + qr_l[None, :]
        rs = np.clip(qr - 4, 0, 56)
        vr = (kr >= rs) & (kr < rs + 8)
        cs = np.clip(qc[None, :] - 8, 0, 48)
        vc = (kc[:, None] >= cs) & (kc[:, None] < cs + 16)
        ri = np.clip(kr - qr + 7, 0, 14)
        ci = np.clip(kc[:, None] - qc[None, :] + 15, 0, 30)
        vals = rpb_l[:, ri, ci]
        return np.where((vr & vc)[None], vals, np.float32(-30000.0)).astype(np.float32)

    nbi = np.stack([tile(2, m) for m in range(5)], axis=1)
    nbi = np.ascontiguousarray(nbi.transpose(2, 0, 1, 3))
    nbe = []
    for i, ms in ((0, range(4)), (1, range(4)), (30, range(28, 32)), (31, range(28, 32))):
        t = np.stack([tile(i, m) for m in ms], axis=1)
        nbe.append(t.transpose(2, 0, 1, 3))
    nbe = np.ascontiguousarray(np.stack(nbe, axis=0))
    return nbi, nbe


def _prep_shared(norm1_w, conv_w, conv_b, gate_b, mlstm_norm_w, rpb, norm2_w, final_norm_w):
    f = np.float32
    sh = {}
    sh["n1"] = np.ascontiguousarray(norm1_w.reshape(DEPTH, 8, 128).transpose(2, 0, 1)).astype(f)
    sh["n2"] = np.ascontiguousarray(norm2_w.reshape(DEPTH, 8, 128).transpose(2, 0, 1)).astype(f)
    sh["nf"] = np.ascontiguousarray(final_norm_w.reshape(8, 128).T).astype(f)
    sh["cw"] = np.ascontiguousarray(conv_w.reshape(DEPTH, 3, 8, 128).transpose(3, 0, 1, 2)).astype(f)
    sh["cb"] = np.ascontiguousarray(conv_b.reshape(DEPTH, 8, 128).transpose(2, 0, 1)).astype(f)
    gb = np.zeros((36, DEPTH, 2), f)
    g4 = gate_b.reshape(DEPTH, 4, 4)
    gb[0:4, :, 0] = g4[:, 0, :].T
    gb[0:4, :, 1] = g4[:, 1, :].T
    gb[32:36, :, 0] = g4[:, 2, :].T
    gb[32:36, :, 1] = g4[:, 3, :].T
    sh["gb"] = gb
    sh["mnw"] = np.ascontiguousarray(np.broadcast_to(mlstm_norm_w[None], (128, DEPTH, 512))).astype(f)
    nbi, nbe = zip(*[_na_bias_tables(np.asarray(rpb[l], f)) for l in range(DEPTH)])
    sh["nbi"] = np.stack(nbi, 0)
    sh["nbe"] = np.stack(nbe, 0)
    sh["ident"] = np.eye(128, dtype=f)
    sh["trif"] = np.triu(np.ones((128, 128), f))
    sh["trib"] = np.tril(np.ones((128, 128), f))
    return sh


_CACHE = {}


def _get_nc(debug=False, stop_after=None):
    key = (debug, stop_after)
    if key not in _CACHE:
        b = Builder(debug=debug, stop_after=stop_after)
        nc = b.build()
        _CACHE[key] = (nc, b)
    return _CACHE[key]


def run(inputs, debug=False, stop_after=None, cores=8):
    x = np.asarray(inputs["x"], np.float32)
    sh = _prep_shared(*[np.asarray(inputs[k], np.float32) for k in
                        ("norm1_w", "conv_w", "conv_b", "gate_b", "mlstm_norm_w", "rpb", "norm2_w", "final_norm_w")])
    for k in ("w_in", "w_out", "w_ff1", "w_ff2"):
        sh[k] = np.ascontiguousarray(np.asarray(inputs[k], np.float32))
    nc, b = _get_nc(debug, stop_after)
    in_maps = []
    for c in range(cores):
        m = dict(sh)
        m["xT"] = np.ascontiguousarray(x[c].T)
        in_maps.append(m)
    res = run_bass_kernel_spmd(nc, in_maps, core_ids=list(range(cores)))
    return res, b


def kernel(x, norm1_w, w_in, conv_w, conv_b, gate_b, mlstm_norm_w, rpb, w_out,
           norm2_w, w_ff1, w_ff2, final_norm_w):
    inputs = dict(x=x, norm1_w=norm1_w, w_in=w_in, conv_w=conv_w, conv_b=conv_b, gate_b=gate_b,
                  mlstm_norm_w=mlstm_norm_w, rpb=rpb, w_out=w_out, norm2_w=norm2_w, w_ff1=w_ff1,
                  w_ff2=w_ff2, final_norm_w=final_norm_w)
    res, _ = run(inputs)
    out = np.stack([np.ascontiguousarray(r["outT"].T) for r in res.results], axis=0)
    return out.astype(np.float32)
```

```python
import numpy as np
from contextlib import ExitStack
import concourse.bass as bass
import concourse.mybir as mybir
from concourse.bass_utils import run_bass_kernel_spmd

F32 = mybir.dt.float32
BF16 = mybir.dt.bfloat16
AF = mybir.ActivationFunctionType
ALU = mybir.AluOpType
AX = mybir.AxisListType

T = 4096
D = 1024
DEPTH = 2
DIN = 3600
DFF = 4096
NB = 8
TB = 512
NCH = 32
EPS = 1e-6
ALPHA = 128.0 ** -0.5

ENGS = ("pe", "act", "dve", "pool", "sp")
EPOCH = 4000
NDMASEM = 20
SAME_SYNC = True


class Op:
    __slots__ = ("eng", "fn", "deps", "dma", "idx", "sig", "semk", "val", "tiny")

    def __init__(self, eng, fn, dma, tiny=False):
        self.eng = eng
        self.fn = fn
        self.dma = dma
        self.tiny = tiny
        self.deps = set()
        self.sig = False
        self.semk = None
        self.val = 0


class Ctx:
    def __init__(self, nc):
        self.nc = nc
        self.sems = {}
        self.ccount = {e: 0 for e in ENGS}
        self.dval = {}
        self.dk = {e: 0 for e in ENGS}
        self.waited = {e: {} for e in ENGS}
        self.nops = 0
        self.nwaits = 0

    def sem(self, k):
        if k not in self.sems:
            self.sems[k] = self.nc.alloc_semaphore("s_" + "_".join(str(x) for x in k))
        return self.sems[k]


class Prog:
    def __init__(self, ctx):
        self.ctx = ctx
        self.nc = ctx.nc
        self.ops = []
        self.last_w = {}
        self.readers = {}

    def op(self, eng, fn, reads=(), writes=(), dma=False, tiny=False):
        o = Op(eng, fn, dma, tiny)
        o.idx = len(self.ops)
        for r in reads:
            w = self.last_w.get(r)
            if w is not None:
                o.deps.add(w)
        for w_ in writes:
            w = self.last_w.get(w_)
            if w is not None:
                o.deps.add(w)
            for rd in self.readers.get(w_, ()):
                o.deps.add(rd)
        o.deps.discard(o.idx)
        for r in reads:
            self.readers.setdefault(r, []).append(o.idx)
        for w_ in writes:
            self.last_w[w_] = o.idx
            self.readers[w_] = []
        self.ops.append(o)
        return o

    def pe(self, fn, reads=(), writes=()):
        return self.op("pe", fn, reads, writes)

    def act(self, fn, reads=(), writes=(), tiny=False):
        return self.op("act", fn, reads, writes, tiny=tiny)

    def dve(self, fn, reads=(), writes=(), tiny=False):
        return self.op("dve", fn, reads, writes, tiny=tiny)

    def pool(self, fn, reads=(), writes=(), tiny=False):
        return self.op("pool", fn, reads, writes, tiny=tiny)

    def dma(self, fn, reads=(), writes=(), q="sp"):
        return self.op(q, fn, reads, writes, dma=True)

    def _unsynced(self, od, o):
        return (od.eng == o.eng and not od.dma and not o.dma
                and (od.eng == "pe" or not SAME_SYNC or not od.tiny))

    def emit(self):
        ctx = self.ctx
        nc = self.nc
        ops = self.ops
        per = {e: [o for o in ops if o.eng == e] for e in ENGS}
        for o in ops:
            for d in o.deps:
                od = ops[d]
                if od.dma or self._unsynced(od, o):
                    continue
                od.sig = True
        for e in ENGS:
            for o in reversed(per[e]):
                if not o.dma:
                    o.sig = True
                    break
        final = {}
        for e in ENGS:
            for o in per[e]:
                if o.dma:
                    i = ctx.dk[e] % NDMASEM
                    ctx.dk[e] += 1
                    k = ("d", e, i)
                    ctx.dval[k] = ctx.dval.get(k, 0) + 16
                    o.semk = k
                    o.val = ctx.dval[k]
                    final[k] = o.val
                elif o.sig:
                    n = ctx.ccount[e]
                    ctx.ccount[e] += 1
                    k = ("c", e, n // EPOCH)
                    o.semk = k
                    o.val = n % EPOCH + 1
                    final[k] = o.val
        for k in final:
            ctx.sem(k)
        ctx.nops += len(ops)

        def run_engine(e, eng):
            waited = ctx.waited[e]

            def wait(k, v):
                if waited.get(k, 0) >= v:
                    return
                waited[k] = v
                eng.wait_ge(ctx.sems[k], v)
                ctx.nwaits += 1

            for o in per[e]:
                need = {}
                for d in o.deps:
                    od = ops[d]
                    if od.semk is None or self._unsynced(od, o):
                        continue
                    if need.get(od.semk, 0) < od.val:
                        need[od.semk] = od.val
                if o.dma and o.val > 16:
                    if need.get(o.semk, 0) < o.val - 16:
                        need[o.semk] = o.val - 16
                for k, v in need.items():
                    wait(k, v)
                ins = o.fn(eng)
                if o.dma:
                    ins.then_inc(ctx.sems[o.semk], 16)
                elif o.sig:
                    ins.then_inc(ctx.sems[o.semk], 1)
            for k, v in final.items():
                if k[0] == "c" and k[1] == e:
                    if e == "pe" or not SAME_SYNC:
                        pass
                wait(k, v)

        with nc.Block() as block:
            block.tensor(lambda eng: run_engine("pe", eng))
            block.scalar(lambda eng: run_engine("act", eng))
            block.vector(lambda eng: run_engine("dve", eng))
            block.gpsimd(lambda eng: run_engine("pool", eng))
            block.sync(lambda eng: run_engine("sp", eng))


def mk(ap, dims, off=0):
    a = ap.ap
    return bass.AP(ap.tensor, ap.offset + off, [list(a[0])] + [list(d) for d in dims])


class Builder:
    def __init__(self, debug=False, stop_after=None):
        self.debug = debug
        self.stop_after = stop_after
        self.nc = bass.Bass("TRN2", target_bir_lowering=False)
        self.ctx = Ctx(self.nc)
        self.dbg_names = []

    def din(self, name, shape, dt=F32):
        return self.nc.dram_tensor(name, list(shape), dt, kind="ExternalInput").ap()

    def dscr(self, name, shape, dt):
        kind = "ExternalOutput" if self.debug else "Internal"
        if self.debug:
            self.dbg_names.append(name)
        return self.nc.dram_tensor(name, list(shape), dt, kind=kind).ap()

    def build(self):
        nc = self.nc
        I = {}
        I["xT"] = self.din("xT", [D, T])
        I["w_in"] = self.din("w_in", [DEPTH, D, DIN])
        I["w_out"] = self.din("w_out", [DEPTH, D, D])
        I["w_ff1"] = self.din("w_ff1", [DEPTH, D, DFF])
        I["w_ff2"] = self.din("w_ff2", [DEPTH, DFF, D])
        I["n1"] = self.din("n1", [128, DEPTH, 8])
        I["n2"] = self.din("n2", [128, DEPTH, 8])
        I["nf"] = self.din("nf", [128, 8])
        I["cw"] = self.din("cw", [128, DEPTH, 3, 8])
        I["cb"] = self.din("cb", [128, DEPTH, 8])
        I["gb"] = self.din("gb", [36, DEPTH, 2])
        I["mnw"] = self.din("mnw", [128, DEPTH, 512])
        I["nbi"] = self.din("nbi", [DEPTH, 128, 8, 5, 128])
        I["nbe"] = self.din("nbe", [DEPTH, 4, 128, 8, 4, 128])
        I["ident"] = self.din("ident", [128, 128])
        I["trif"] = self.din("trif", [128, 128])
        I["trib"] = self.din("trib", [128, 128])
        self.I = I
        self.out = nc.dram_tensor("outT", [D, T], F32, kind="ExternalOutput").ap()
        S = {}
        S["QT"] = self.dscr("QT", [512, T], BF16)
        S["KT"] = self.dscr("KT", [512, T], BF16)
        S["Ktok"] = self.dscr("Ktok", [T, 512], BF16)
        S["Vtok"] = self.dscr("Vtok", [T, 512], BF16)
        S["Osig"] = self.dscr("Osig", [T, 512], F32)
        S["G"] = self.dscr("G", [16, T], F32)
        S["QnT"] = self.dscr("QnT", [512, T], BF16)
        S["KnT"] = self.dscr("KnT", [512, T], BF16)
        S["Vn"] = self.dscr("Vn", [T, 512], BF16)
        S["YT"] = self.dscr("YT", [D, T], BF16)
        S["Mser"] = self.dscr("Mser", [8, T], F32)
        S["Wser"] = self.dscr("Wser", [8, T], F32)
        S["DCs"] = self.dscr("DCs", [8, NCH], F32)
        S["xres"] = self.dscr("xres", [D, T], F32)
        if self.debug:
            S["dbgser"] = self.dscr("dbgser", [128, 4, NCH, 8], F32)
        self.S = S
        phases = []
        for l in range(DEPTH):
            phases += [("A", l), ("B", l), ("C", l), ("D", l), ("E", l)]
        for i, (ph, l) in enumerate(phases):
            getattr(self, "phase_" + ph)(l)
            if self.stop_after is not None and i + 1 >= self.stop_after:
                break
        return nc

    def phase_A(self, l):
        nc, I, S = self.nc, self.I, self.S
        P = Prog(self.ctx)
        xsrc = I["xT"] if l == 0 else S["xres"]
        with ExitStack() as es:
            def sb(name, shape, dt):
                return es.enter_context(nc.sbuf_tensor(f"L{l}" + name, list(shape), dt))

            def psum(name, shape, dt=F32):
                return es.enter_context(nc.psum_tensor(f"L{l}" + name, list(shape), dt))

            wbf = sb("A_wbf", [128, 8, DIN], BF16)
            wst = [sb(f"A_wst{i}", [128, DIN], F32) for i in range(2)]
            n1 = sb("A_n1", [128, DEPTH, 8], F32)
            cw = sb("A_cw", [128, DEPTH, 3, 8], F32)
            cb = sb("A_cb", [128, DEPTH, 8], F32)
            cws = sb("A_cws", [128, 3, 8], F32)
            cbs = sb("A_cbs", [128, 8], F32)
            idf = sb("A_idf", [128, 128], F32)
            idb = sb("A_idb", [128, 128], BF16)
            ones = sb("A_ones", [128, 128], BF16)
            epsb = sb("A_epsb", [128, 1], F32)
            xin = [sb(f"A_xin{i}", [128, 8, 514], F32) for i in range(2)]
            xsq = sb("A_xsq", [128, 8, 514], BF16)
            rstd = sb("A_rstd", [128, 514], F32)
            hn = sb("A_hn", [128, 8, 514], BF16)
            pre = [sb(f"A_pre{i}", [128, 514], F32) for i in range(2)]
            acc = [sb(f"A_acc{i}", [128, 512], F32) for i in range(2)]
            sg = [sb(f"A_sg{i}", [128, 512], F32) for i in range(2)]
            qko = [sb(f"A_qko{i}", [128, 512], BF16) for i in range(3)]
            ktok = sb("A_ktok", [128, 4, 512], BF16)
            vtok = sb("A_vtok", [128, 4, 512], BF16)
            vntok = sb("A_vntok", [128, 4, 512], BF16)
            osig = sb("A_osig", [128, 4, 512], F32)
            gsb = sb("A_gsb", [16, 512], F32)
            ps_ssq = psum("A_ps_ssq", [128, 512])
            ps_h = psum("A_ps_h", [128, 512])
            ps_pre = [psum(f"A_ps_pre{i}", [128, 512]) for i in range(2)]
            ps_tok = [psum(f"A_ps_tok{i}", [128, 512]) for i in range(2)]
            ps_g = psum("A_ps_g", [128, 512])
            ps_t = psum("A_ps_t", [128, 1024], BF16)

            P.dma(lambda e: e.dma_start(out=n1[:], in_=I["n1"]), writes=["n1"])
            P.dma(lambda e: e.dma_start(out=cw[:], in_=I["cw"]), writes=["cw"])
            P.dma(lambda e: e.dma_start(out=cb[:], in_=I["cb"]), writes=["cb"])
            P.dma(lambda e: e.dma_start(out=idf[:], in_=I["ident"]), writes=["idf"])
            P.dve(lambda e: e.tensor_copy(out=idb[:], in_=idf[:]), reads=["idf"], writes=["idb"])
            P.dve(lambda e: e.memset(ones[:], 1.0), writes=["ones"])
            P.dve(lambda e: e.memset(epsb[:], float(D * EPS)), writes=["epsb"], tiny=True)
            P.dve(lambda e: e.tensor_scalar(out=cws[:, :, 0:4], in0=cw[:, l, :, 0:4], scalar1=ALPHA, scalar2=None, op0=ALU.mult),
                  reads=["cw"], writes=["cws"], tiny=True)
            P.dve(lambda e: e.tensor_copy(out=cws[:, :, 4:8], in_=cw[:, l, :, 4:8]), reads=["cw"], writes=["cws"], tiny=True)
            P.dve(lambda e: e.tensor_scalar(out=cbs[:, 0:4], in0=cb[:, l, 0:4], scalar1=ALPHA, scalar2=None, op0=ALU.mult),
                  reads=["cb"], writes=["cbs"], tiny=True)
            P.dve(lambda e: e.tensor_copy(out=cbs[:, 4:8], in_=cb[:, l, 4:8]), reads=["cb"], writes=["cbs"], tiny=True)
            for kc in range(8):
                st = wst[kc % 2]
                P.dma(lambda e, st=st, kc=kc: e.dma_start(out=st[:], in_=I["w_in"][l, kc * 128:(kc + 1) * 128, :]),
                      writes=[f"wst{kc % 2}"])
                half = DIN // 2
                P.pool(lambda e, st=st, kc=kc: e.tensor_scalar(out=wbf[:, kc, 0:half], in0=st[:, 0:half], scalar1=n1[:, l, kc:kc + 1], scalar2=None, op0=ALU.mult),
                       reads=[f"wst{kc % 2}", "n1"], writes=[f"wbf{kc}a"])
                P.dve(lambda e, st=st, kc=kc: e.tensor_scalar(out=wbf[:, kc, half:DIN], in0=st[:, half:DIN], scalar1=n1[:, l, kc:kc + 1], scalar2=None, op0=ALU.mult),
                      reads=[f"wst{kc % 2}", "n1"], writes=[f"wbf{kc}b"])
            WB = [f"wbf{kc}{h}" for kc in range(8) for h in "ab"]

            def load_x(b):
                xi = xin[b % 2]
                t0 = b * TB
                lo = t0 - 1
                hi = t0 + TB + 1
                c0 = 0
                tok = f"xin{b % 2}"
                if b == 0:
                    lo = 0
                    c0 = 1
                    P.dve(lambda e, xi=xi: e.memset(xi[:, :, 0:1], 0.0), writes=[tok], tiny=True)
                if b == NB - 1:
                    hi = T
                    P.dve(lambda e, xi=xi: e.memset(xi[:, :, 513:514], 0.0), writes=[tok], tiny=True)
                n = hi - lo
                src = xsrc[:, lo:hi].rearrange("(kc p) t -> p kc t", p=128)
                P.dma(lambda e, xi=xi, src=src, c0=c0, n=n: e.dma_start(out=xi[:, :, c0:c0 + n], in_=src), writes=[tok])

            load_x(0)
            qi = 0
            for b in range(NB):
                if b + 1 < NB:
                    load_x(b + 1)
                xi = xin[b % 2]
                xt = f"xin{b % 2}"
                t0 = b * TB
                P.act(lambda e, xi=xi: e.activation(out=xsq[:], in_=xi[:], func=AF.Square), reads=[xt], writes=["xsq"])
                for kc in range(8):
                    P.pe(lambda e, kc=kc: e.matmul(ps_ssq[:], lhsT=ones[:], rhs=xsq[:, kc, 1:513], start=(kc == 0), stop=(kc == 7)),
                         reads=["ones", "xsq"], writes=["ps_ssq"])
                for kc in range(8):
                    P.pe(lambda e, kc=kc: e.matmul(ps_h[:, 0:2], lhsT=ones[:], rhs=mk(xsq[:, kc, 0:1], [[513, 2]]), start=(kc == 0), stop=(kc == 7)),
                         reads=["ones", "xsq"], writes=["ps_hs"])
                P.act(lambda e: e.activation(out=rstd[:, 1:513], in_=ps_ssq[:], func=AF.Ln, bias=epsb[:, 0:1]), reads=["ps_ssq", "epsb"], writes=["rstd"])
                P.act(lambda e: e.activation(out=mk(rstd[:, 0:1], [[513, 2]]), in_=ps_h[:, 0:2], func=AF.Ln, bias=epsb[:, 0:1]), reads=["ps_hs", "epsb"], writes=["rstd"], tiny=True)
                P.act(lambda e: e.activation(out=rstd[:], in_=rstd[:], func=AF.Exp, scale=-0.5), reads=["rstd"], writes=["rstd"])
                for kc in range(8):
                    eng = P.dve
                    eng(lambda e, kc=kc, xi=xi: e.scalar_tensor_tensor(out=hn[:, kc, :], in0=xi[:, kc, :], scalar=32.0, in1=rstd[:], op0=ALU.mult, op1=ALU.mult),
                        reads=[xt, "rstd"], writes=[f"hn{kc}"])
                HN = [f"hn{kc}" for kc in range(8)]

                for fc in range(8):
                    pp = ps_pre[qi % 2]
                    ppt = f"ps_pre{qi % 2}"
                    pr = pre[qi % 2]
                    prt = f"pre{qi % 2}"
                    ac = acc[qi % 2]
                    act_ = f"acc{qi % 2}"
                    sgg = sg[qi % 2]
                    sgt = f"sg{qi % 2}"
                    qo = qko[qi % 3]
                    qot = f"qko{qi % 3}"
                    qi += 1
                    c0 = fc * 128
                    for kc in range(8):
                        P.pe(lambda e, kc=kc, pp=pp, c0=c0: e.matmul(pp[:], lhsT=wbf[:, kc, c0:c0 + 128], rhs=hn[:, kc, 1:513], start=(kc == 0), stop=(kc == 7)),
                             reads=WB + HN, writes=[ppt])
                    hc = 8 + 2 * (fc % 2)
                    for kc in range(8):
                        P.pe(lambda e, kc=kc, c0=c0, hc=hc: e.matmul(ps_h[:, hc:hc + 2], lhsT=wbf[:, kc, c0:c0 + 128], rhs=mk(hn[:, kc, 0:1], [[513, 2]]), start=(kc == 0), stop=(kc == 7)),
                             reads=WB + HN, writes=[f"ps_hp{fc % 2}"])
                    P.act(lambda e, pr=pr, pp=pp: e.activation(out=pr[:, 1:513], in_=pp[:], func=AF.Copy), reads=[ppt], writes=[prt])
                    P.act(lambda e, pr=pr, hc=hc: e.activation(out=mk(pr[:, 0:1], [[513, 2]]), in_=ps_h[:, hc:hc + 2], func=AF.Copy),
                          reads=[f"ps_hp{fc % 2}"], writes=[prt], tiny=True)
                    P.pool(lambda e, ac=ac, pr=pr, fc=fc: e.tensor_scalar(out=ac[:], in0=pr[:, 0:512], scalar1=cws[:, 0, fc:fc + 1], scalar2=None, op0=ALU.mult),
                           reads=[prt, "cws"], writes=[act_])
                    P.dve(lambda e, ac=ac, pr=pr, fc=fc: e.scalar_tensor_tensor(out=ac[:], in0=pr[:, 1:513], scalar=cws[:, 1, fc:fc + 1], in1=ac[:], op0=ALU.mult, op1=ALU.add),
                           reads=[prt, "cws", act_], writes=[act_])
                    P.dve(lambda e, ac=ac, pr=pr, fc=fc: e.scalar_tensor_tensor(out=ac[:], in0=pr[:, 2:514], scalar=cws[:, 2, fc:fc + 1], in1=ac[:], op0=ALU.mult, op1=ALU.add),
                           reads=[prt, "cws", act_], writes=[act_])
                    sc = (1.0 / ALPHA) if fc < 4 else 1.0
                    P.act(lambda e, ac=ac, sgg=sgg, fc=fc, sc=sc: e.activation(out=sgg[:], in_=ac[:], func=AF.Sigmoid, bias=cb[:, l, fc:fc + 1], scale=sc),
                          reads=[act_, "cb"], writes=[sgt])
                    P.dve(lambda e, ac=ac, sgg=sgg, fc=fc, qo=qo: e.scalar_tensor_tensor(out=qo[:], in0=ac[:], scalar=cbs[:, fc:fc + 1], in1=sgg[:], op0=ALU.add, op1=ALU.mult),
                          reads=[act_, sgt, "cbs"], writes=[qot])
                    dst = (S["QT"] if fc < 4 else S["KT"])[(fc % 4) * 128:(fc % 4 + 1) * 128, t0:t0 + TB]
                    P.dma(lambda e, qo=qo, dst=dst: e.dma_start(out=dst, in_=qo[:]), reads=[qot], writes=["dram_qk"])
                    if fc >= 4:
                        h = fc - 4
                        half = (h % 2) * 512
                        for tt in range(4):
                            P.pe(lambda e, qo=qo, tt=tt, half=half: e.transpose(ps_t[:, half + tt * 128: half + (tt + 1) * 128], qo[:, tt * 128:(tt + 1) * 128], idb[:]),
                                 reads=[qot, "idb"], writes=[f"ps_t{h % 2}"])
                        P.act(lambda e, h=h, half=half: e.activation(out=ktok[:, :, h * 128:(h + 1) * 128], in_=mk(ps_t[:, half:half + 1], [[128, 4], [1, 128]]), func=AF.Copy),
                              reads=[f"ps_t{h % 2}"], writes=["ktok"])
                P.dma(lambda e, t0=t0: e.dma_start(out=S["Ktok"][t0:t0 + TB, :].rearrange("(tt p) c -> p tt c", p=128), in_=ktok[:]),
                      reads=["ktok"], writes=["dram_ktok"])

                ti = 0
                for (c0, kind) in ((1024, "v"), (1536, "o"), (3088, "vn")):
                    for tt in range(4):
                        pt = ps_tok[ti % 2]
                        ptt = f"ps_tok{ti % 2}"
                        ti += 1
                        for kc in range(8):
                            P.pe(lambda e, kc=kc, pt=pt, tt=tt, c0=c0: e.matmul(pt[:], lhsT=hn[:, kc, 1 + tt * 128:1 + (tt + 1) * 128], rhs=wbf[:, kc, c0:c0 + 512], start=(kc == 0), stop=(kc == 7)),
                                 reads=WB + HN, writes=[ptt])
                        if kind == "v":
                            P.act(lambda e, pt=pt, tt=tt: e.activation(out=vtok[:, tt, :], in_=pt[:], func=AF.Copy), reads=[ptt], writes=["vtok"])
                        elif kind == "vn":
                            P.dve(lambda e, pt=pt, tt=tt: e.tensor_copy(out=vntok[:, tt, :], in_=pt[:]), reads=[ptt], writes=["vntok"])
                        else:
                            P.act(lambda e, pt=pt, tt=tt: e.activation(out=osig[:, tt, :], in_=pt[:], func=AF.Sigmoid), reads=[ptt], writes=["osig"])
                P.dma(lambda e, t0=t0: e.dma_start(out=S["Vtok"][t0:t0 + TB, :].rearrange("(tt p) c -> p tt c", p=128), in_=vtok[:]),
                      reads=["vtok"], writes=["dram_vtok"])
                P.dma(lambda e, t0=t0: e.dma_start(out=S["Vn"][t0:t0 + TB, :].rearrange("(tt p) c -> p tt c", p=128), in_=vntok[:]),
                      reads=["vntok"], writes=["dram_vn"])
                P.dma(lambda e, t0=t0: e.dma_start(out=S["Osig"][t0:t0 + TB, :].rearrange("(tt p) c -> p tt c", p=128), in_=osig[:]),
                      reads=["osig"], writes=["dram_osig"])

                for kc in range(8):
                    P.pe(lambda e, kc=kc: e.matmul(ps_g[0:16, :], lhsT=wbf[:, kc, 2048:2064], rhs=hn[:, kc, 1:513], start=(kc == 0), stop=(kc == 7)),
                         reads=WB + HN, writes=["ps_g"])
                P.dve(lambda e: e.tensor_copy(out=gsb[:], in_=ps_g[0:16, :]), reads=["ps_g"], writes=["gsb"])
                P.dma(lambda e, t0=t0: e.dma_start(out=S["G"][:, t0:t0 + TB], in_=gsb[:]), reads=["gsb"], writes=["dram_g"])

                for j in range(8):
                    pp = ps_pre[qi % 2]
                    ppt = f"ps_pre{qi % 2}"
                    qo = qko[qi % 3]
                    qot = f"qko{qi % 3}"
                    qi += 1
                    c0 = 2064 + j * 128
                    for kc in range(8):
                        P.pe(lambda e, kc=kc, pp=pp, c0=c0: e.matmul(pp[:], lhsT=wbf[:, kc, c0:c0 + 128], rhs=hn[:, kc, 1:513], start=(kc == 0), stop=(kc == 7)),
                             reads=WB + HN, writes=[ppt])
                    sc = 0.125 if j < 4 else 1.0
                    P.act(lambda e, pp=pp, qo=qo, sc=sc: e.activation(out=qo[:], in_=pp[:], func=AF.Copy, scale=sc), reads=[ppt], writes=[qot])
                    dst = (S["QnT"] if j < 4 else S["KnT"])[(j % 4) * 128:(j % 4 + 1) * 128, t0:t0 + TB]
                    P.dma(lambda e, qo=qo, dst=dst: e.dma_start(out=dst, in_=qo[:]), reads=[qot], writes=["dram_qkn"])
            P.emit()

    def phase_B(self, l):
        nc, I, S = self.nc, self.I, self.S
        P = Prog(self.ctx)
        with ExitStack() as es:
            def sb(name, shape, dt):
                return es.enter_context(nc.sbuf_tensor(f"L{l}" + name, list(shape), dt))

            def psum(name, shape, dt=F32):
                return es.enter_context(nc.psum_tensor(f"L{l}" + name, list(shape), dt))

            NP = 36
            idf = sb("B_idf", [128, 128], F32)
            idb = sb("B_idb", [128, 128], BF16)
            trif = sb("B_trif", [128, 128], F32)
            trib = sb("B_trib", [128, 128], F32)
            tokser = sb("B_tokser", [128, 4, NCH, 8], F32)
            dcb = sb("B_dcb", [128, 8, NCH], F32)
            mnw = sb("B_mnw", [128, 512], F32)
            epsb = sb("B_epsb", [128, 1], F32)
            es1 = ExitStack()

            def sb1(name, shape, dt):
                return es1.enter_context(nc.sbuf_tensor(f"L{l}" + name, list(shape), dt))

            ser = {n: sb1("B_" + n, [NP, T], F32) for n in ("Ipre", "F", "Bp", "U", "M", "tmp", "W", "EM", "ES")}
            onesr = sb1("B_onesr", [NP, T], F32)
            gb = sb1("B_gb", [NP, DEPTH, 2], F32)
            ngb = sb1("B_ngb", [NP, 1], F32)
            mend = sb1("B_mend", [NP, NCH], F32)
            mprev = sb1("B_mprev", [NP, NCH], F32)
            dcs = sb1("B_dcs", [NP, NCH], F32)
            ps_ser = es1.enter_context(nc.psum_tensor(f"L{l}B_ps_ser", [128, 512], F32))

            for n in ("Ipre", "F"):
                P.dve(lambda e, n=n: e.memset(ser[n][:], 0.0), writes=[n])
            P.pool(lambda e: e.memset(onesr[:], 1.0), writes=["onesr"])
            P.dve(lambda e: e.memset(epsb[:], float(128 * EPS)), writes=["epsb"], tiny=True)
            P.dma(lambda e: e.dma_start(out=gb[:], in_=I["gb"]), writes=["gb"])
            P.dma(lambda e: e.dma_start(out=idf[:], in_=I["ident"]), writes=["idf"])
            P.dma(lambda e: e.dma_start(out=trif[:], in_=I["trif"]), writes=["trif"])
            P.dma(lambda e: e.dma_start(out=trib[:], in_=I["trib"]), writes=["trib"])
            P.dma(lambda e: e.dma_start(out=mnw[:], in_=I["mnw"][:, l, :]), writes=["mnw"])
            P.dve(lambda e: e.tensor_copy(out=idb[:], in_=idf[:]), reads=["idf"], writes=["idb"])
            P.dve(lambda e: e.tensor_scalar(out=mnw[:], in0=mnw[:], scalar1=float(128.0 ** 0.5), scalar2=None, op0=ALU.mult), reads=["mnw"], writes=["mnw"])
            G = S["G"]
            P.dma(lambda e: e.dma_start(out=ser["Ipre"][0:4, :], in_=G[0:4, :]), writes=["Ipre"])
            P.dma(lambda e: e.dma_start(out=ser["F"][0:4, :], in_=G[4:8, :]), writes=["F"])
            P.dma(lambda e: e.dma_start(out=ser["Ipre"][32:36, :], in_=G[8:12, :]), writes=["Ipre"])
            P.dma(lambda e: e.dma_start(out=ser["F"][32:36, :], in_=G[12:16, :]), writes=["F"])
            P.dve(lambda e: e.tensor_scalar(out=ngb[:], in0=gb[:, l, 1:2], scalar1=-1.0, scalar2=None, op0=ALU.mult), reads=["gb"], writes=["ngb"], tiny=True)
            P.dve(lambda e: e.tensor_scalar(out=ser["Ipre"][:], in0=ser["Ipre"][:], scalar1=gb[:, l, 0:1], scalar2=None, op0=ALU.add),
                  reads=["Ipre", "gb"], writes=["Ipre"])
            P.act(lambda e: e.activation(out=ser["tmp"][:], in_=ser["F"][:], func=AF.Exp, bias=ngb[:, 0:1], scale=-1.0), reads=["F", "ngb"], writes=["tmp"])
            P.act(lambda e: e.activation(out=ser["F"][:], in_=ser["tmp"][:], func=AF.Ln, bias=1.0), reads=["tmp"], writes=["F"])

            def rev(ap):
                a = ap.ap
                return bass.AP(ap.tensor, ap.offset + (a[-1][1] - 1) * a[-1][0], [list(p) for p in a[:-1]] + [[-a[-1][0], a[-1][1]]])

            def scan(out, data, op1, wtok, rtok):
                P.dve(lambda e: e.tensor_tensor_scan(out=out[0:4, :], data0=onesr[0:4, :], data1=data[0:4, :], initial=0.0, op0=ALU.mult, op1=op1),
                      reads=[rtok, "onesr"], writes=[wtok])
                P.dve(lambda e: e.tensor_tensor_scan(out=rev(out[32:36, :]), data0=onesr[32:36, :], data1=rev(data[32:36, :]), initial=0.0, op0=ALU.mult, op1=op1),
                      reads=[rtok, "onesr"], writes=[wtok])

            P.dve(lambda e: e.memset(ser["Bp"][:], 0.0), writes=["Bp"])
            P.pool(lambda e: e.memset(ser["M"][:], 0.0), writes=["M"])
            scan(ser["Bp"], ser["F"], ALU.add, "Bp", "F")
            P.dve(lambda e: e.tensor_tensor(out=ser["U"][:], in0=ser["Ipre"][:], in1=ser["Bp"][:], op=ALU.add), reads=["Ipre", "Bp"], writes=["U"])
            scan(ser["M"], ser["U"], ALU.max, "M", "U")
            P.dve(lambda e: e.tensor_tensor(out=ser["tmp"][:], in0=ser["Bp"][:], in1=ser["M"][:], op=ALU.subtract), reads=["Bp", "M"], writes=["tmp"])
            P.act(lambda e: e.activation(out=ser["EM"][:], in_=ser["tmp"][:], func=AF.Exp), reads=["tmp"], writes=["EM"])
            Mv = ser["M"]
            P.dve(lambda e: e.memset(mprev[:], 0.0), writes=["mprev"], tiny=True)
            P.dve(lambda e: e.memset(mend[:], 0.0), writes=["mend"], tiny=True)
            P.dve(lambda e: e.tensor_copy(out=mend[0:4, :], in_=mk(Mv[0:4, 127:128], [[128, NCH]])), reads=["M"], writes=["mend"], tiny=True)
            P.dve(lambda e: e.tensor_copy(out=mend[32:36, :], in_=mk(Mv[32:36, 0:1], [[128, NCH]])), reads=["M"], writes=["mend"], tiny=True)
            P.dve(lambda e: e.tensor_copy(out=mprev[0:4, 1:NCH], in_=mk(Mv[0:4, 127:128], [[128, NCH - 1]])), reads=["M"], writes=["mprev"], tiny=True)
            P.dve(lambda e: e.tensor_copy(out=mprev[32:36, 0:NCH - 1], in_=mk(Mv[32:36, 128:129], [[128, NCH - 1]])), reads=["M"], writes=["mprev"], tiny=True)
            P.dve(lambda e: e.tensor_tensor(out=ser["tmp"][:].rearrange("p (c t) -> p c t", t=128), in0=mk(mprev[:], [[1, NCH], [0, 128]]),
                                            in1=ser["M"][:].rearrange("p (c t) -> p c t", t=128), op=ALU.subtract),
                  reads=["mprev", "M", "EM"], writes=["tmp"])
            P.act(lambda e: e.activation(out=ser["W"][:], in_=ser["tmp"][:], func=AF.Exp), reads=["tmp"], writes=["W"])
            P.dve(lambda e: e.tensor_tensor(out=ser["tmp"][:].rearrange("p (c t) -> p c t", t=128), in0=ser["U"][:].rearrange("p (c t) -> p c t", t=128),
                                            in1=mk(mend[:], [[1, NCH], [0, 128]]), op=ALU.subtract),
                  reads=["mend", "U", "W"], writes=["tmp"])
            P.act(lambda e: e.activation(out=ser["ES"][:], in_=ser["tmp"][:], func=AF.Exp), reads=["tmp"], writes=["ES"])
            P.dve(lambda e: e.tensor_tensor(out=dcs[:], in0=mprev[:], in1=mend[:], op=ALU.subtract), reads=["mprev", "mend"], writes=["dcs"], tiny=True)
            P.act(lambda e: e.activation(out=dcs[:], in_=dcs[:], func=AF.Exp), reads=["dcs"], writes=["dcs"], tiny=True)
            P.dma(lambda e: e.dma_start(out=S["Mser"][0:4, :], in_=ser["M"][0:4, :]), reads=["M"], writes=["dram_M"])
            P.dma(lambda e: e.dma_start(out=S["Mser"][4:8, :], in_=ser["M"][32:36, :]), reads=["M"], writes=["dram_M"])
            P.dma(lambda e: e.dma_start(out=S["Wser"][0:4, :], in_=ser["W"][0:4, :]), reads=["W"], writes=["dram_W"])
            P.dma(lambda e: e.dma_start(out=S["Wser"][4:8, :], in_=ser["W"][32:36, :]), reads=["W"], writes=["dram_W"])
            P.dma(lambda e: e.dma_start(out=S["DCs"][0:4, :], in_=dcs[0:4, :]), reads=["dcs"], writes=["dram_DC"])
            P.dma(lambda e: e.dma_start(out=S["DCs"][4:8, :], in_=dcs[32:36, :]), reads=["dcs"], writes=["dram_DC"])
            P.dma(lambda e: e.dma_start(out=dcb[:], in_=bass.AP(S["DCs"].tensor, S["DCs"].offset, [[0, 128], [NCH, 8], [1, NCH]])),
                  reads=["dram_DC"], writes=["dcb"])
            for si, n in enumerate(("U", "W", "EM", "ES")):
                for c0 in range(0, NCH, 8):
                    for c in range(c0, c0 + 8):
                        P.pe(lambda e, n=n, c=c, c0=c0: e.matmul(ps_ser[:, (c - c0) * 64:(c - c0) * 64 + NP], lhsT=ser[n][:, c * 128:(c + 1) * 128], rhs=idf[0:NP, 0:NP], start=True, stop=True),
                             reads=[n, "idf"], writes=["ps_ser"])
                    P.dve(lambda e, si=si, c0=c0: e.tensor_copy(out=tokser[:, si, c0:c0 + 8, 0:4], in_=mk(ps_ser[:, 0:1], [[64, 8], [1, 4]])),
                          reads=["ps_ser"], writes=["tokser"])
                    P.dve(lambda e, si=si, c0=c0: e.tensor_copy(out=tokser[:, si, c0:c0 + 8, 4:8], in_=mk(ps_ser[:, 32:33], [[64, 8], [1, 4]])),
                          reads=["ps_ser"], writes=["tokser"])
            if self.debug:
                P.dma(lambda e: e.dma_start(out=S["dbgser"], in_=tokser[:]), reads=["tokser"], writes=["dram_dbgser"])

            P.emit()
            es1.close()

            cst = sb("B_cst", [128, 2, NCH, 4 * 129], BF16)
            c32 = sb("B_c32", [128, 2, 4 * 129], F32)
            v1_ = [sb(f"B_v1{i}", [128, 4, 129], BF16) for i in range(4)]
            P = Prog(self.ctx)
            es2 = ExitStack()
            kt_ = [es2.enter_context(nc.sbuf_tensor(f"L{l}B_ktok{i}", [128, 512], BF16)) for i in range(4)]
            vs_ = [es2.enter_context(nc.sbuf_tensor(f"L{l}B_vs{i}", [128, 4, 129], BF16)) for i in range(4)]
            ps_c = [[es2.enter_context(nc.psum_tensor(f"L{l}B_ps_c{d}{hp}", [128, 512], F32)) for hp in range(2)] for d in range(2)]
            P.dve(lambda e: e.memset(c32[:], 0.0), writes=[f"c32_{d}_{h}" for d in range(2) for h in range(4)])
            for i in range(4):
                P.pool(lambda e, i=i: e.memset(v1_[i][:, :, 128:129], 1.0), writes=[f"v1{i}"])
            li = 0
            for step in range(NCH):
                for d in range(2):
                    c = step if d == 0 else NCH - 1 - step
                    P.act(lambda e, d=d, c=c: e.activation(out=cst[:, d, c, :], in_=c32[:, d, :], func=AF.Copy),
                          reads=[f"c32_{d}_{h}" for h in range(4)], writes=[f"cst{d}_{c}"])
                    if step == NCH - 1:
                        continue
                    kt = kt_[li % 4]
                    v1 = v1_[li % 4]
                    vs = vs_[li % 4]
                    ktt = f"ktokb{li % 4}"
                    v1t = f"v1{li % 4}"
                    vst = f"vs{li % 4}"
                    li += 1
                    P.dma(lambda e, kt=kt, c=c: e.dma_start(out=kt[:], in_=S["Ktok"][c * 128:(c + 1) * 128, :]), writes=[ktt])
                    P.dma(lambda e, v1=v1, c=c: e.dma_start(out=v1[:, :, 0:128], in_=S["Vtok"][c * 128:(c + 1) * 128, :].rearrange("p (h x) -> p h x", h=4)), writes=[v1t])
                    P.dve(lambda e, vs=vs, v1=v1, c=c, d=d: e.tensor_tensor(out=vs[:], in0=v1[:], in1=mk(tokser[:, 3, c, d * 4:d * 4 + 1], [[1, 4], [0, 129]]), op=ALU.mult),
                           reads=[v1t, "tokser"], writes=[vst])
                    for hp in range(2):
                        pc = ps_c[d][hp]
                        for hh in range(2):
                            h = hp * 2 + hh
                            P.pe(lambda e, pc=pc, hh=hh, h=h, kt=kt, vs=vs: e.matmul(pc[:, hh * 129:(hh + 1) * 129], lhsT=kt[:, h * 128:(h + 1) * 128], rhs=vs[:, h, :], start=True, stop=True),
                                 reads=[ktt, vst], writes=[f"ps_c{d}{hp}{hh}"])
                        for hh in range(2):
                            h = hp * 2 + hh
                            P.dve(lambda e, pc=pc, hh=hh, h=h, d=d, c=c: e.scalar_tensor_tensor(out=c32[:, d, h * 129:(h + 1) * 129], in0=c32[:, d, h * 129:(h + 1) * 129],
                                                                                     scalar=dcb[:, d * 4 + h, c:c + 1], in1=pc[:, hh * 129:(hh + 1) * 129], op0=ALU.mult, op1=ALU.add),
                                  reads=[f"ps_c{d}{hp}{hh}", "dcb", f"c32_{d}_{h}"], writes=[f"c32_{d}_{h}"])
            P.emit()
            es2.close()

            import os
            if os.environ.get("BSTOP") == "2":
                return
            P = Prog(self.ctx)
            two = range(2)
            qT_ = [sb(f"B_qT{i}", [128, 4, 128], BF16) for i in two]
            kT_ = [sb(f"B_kT{i}", [128, 4, 128], BF16) for i in two]
            mb_ = [sb(f"B_mb{i}", [128, 8, 128], F32) for i in two]
            wb_ = [sb(f"B_wb{i}", [128, 8, 128], F32) for i in two]
            og_ = [sb(f"B_og{i}", [128, 512], F32) for i in two]
            gg_ = [sb(f"B_gg{i}", [128, 512], F32) for i in two]
            sm_ = [sb(f"B_sm{i}", [128, 8, 128], F32) for i in two]
            ee_ = [sb(f"B_ee{i}", [128, 8, 128], F32) for i in two]
            pp_ = [sb(f"B_pp{i}", [128, 8, 128], BF16) for i in two]
            qw_ = [sb(f"B_qw{i}", [128, 8, 128], BF16) for i in two]
            hs_ = [sb(f"B_hs{i}", [128, 8, 128], F32) for i in two]
            hall_ = [sb(f"B_hall{i}", [128, 512], F32) for i in two]
            hsq_ = [sb(f"B_hsq{i}", [128, 512], F32) for i in two]
            dd_ = [sb(f"B_dd{i}", [128, 8], F32) for i in two]
            ss_ = [sb(f"B_ss{i}", [128, 4], F32) for i in two]
            ybf_ = [sb(f"B_ybf{i}", [128, 512], BF16) for i in two]
            yT_ = [sb(f"B_yT{i}", [128, 4, 128], BF16) for i in two]
            ps_s = [psum(f"B_ps_s{i}", [128, 512]) for i in two]
            ps_o = [psum(f"B_ps_o{i}", [128, 1024]) for i in two]
            ps_d = psum("B_ps_d", [128, 512])
            ps_y = psum("B_ps_y", [128, 1024], BF16)
            for i in range(4):
                P.pool(lambda e, i=i: e.memset(v1_[i][:, :, 128:129], 1.0), writes=[f"v1{i}"])
            tri4 = sb("B_tri4", [128, 8, 128], F32)
            for i in range(4):
                P.pool(lambda e, i=i: e.tensor_copy(out=tri4[:, i, :], in_=trif[:]), writes=["tri4"])
                P.pool(lambda e, i=i: e.tensor_copy(out=tri4[:, 4 + i, :], in_=trib[:]), writes=["tri4"])

            def b3_load(c):
                s = c % 2
                tk = c * 128
                P.dma(lambda e: e.dma_start(out=qT_[s][:], in_=S["QT"][:, tk:tk + 128].rearrange("(h p) t -> p h t", p=128)), writes=[f"qT{s}"])
                P.dma(lambda e: e.dma_start(out=kT_[s][:], in_=S["KT"][:, tk:tk + 128].rearrange("(h p) t -> p h t", p=128)), writes=[f"kT{s}"])
                P.dma(lambda e: e.dma_start(out=mb_[s][:], in_=bass.AP(S["Mser"].tensor, S["Mser"].offset + tk, [[0, 128], [T, 8], [1, 128]])), writes=[f"mb{s}"])
                P.dma(lambda e: e.dma_start(out=wb_[s][:], in_=bass.AP(S["Wser"].tensor, S["Wser"].offset + tk, [[0, 128], [T, 8], [1, 128]])), writes=[f"wb{s}"])
                P.dma(lambda e: e.dma_start(out=og_[s][:], in_=S["Osig"][tk:tk + 128, :]), writes=[f"og{s}"])
                P.dma(lambda e: e.dma_start(out=v1_[c % 4][:, :, 0:128], in_=S["Vtok"][tk:tk + 128, :].rearrange("p (h x) -> p h x", h=4)), writes=[f"v1{c % 4}"])

            b3_load(0)
            for c in range(NCH):
                if c + 1 < NCH:
                    b3_load(c + 1)
                s = c % 2
                tk = c * 128
                qT, kT, mb, wb, og, gg = qT_[s], kT_[s], mb_[s], wb_[s], og_[s], gg_[s]
                sm, ee, pp, qw, hs, hall, hsq, dd, ss, ybf, yT = sm_[s], ee_[s], pp_[s], qw_[s], hs_[s], hall_[s], hsq_[s], dd_[s], ss_[s], ybf_[s], yT_[s]
                ps, po = ps_s[s], ps_o[s]
                v1 = v1_[c % 4]
                v1t = f"v1{c % 4}"
                for h in range(4):
                    P.pe(lambda e, ps=ps, kT=kT, qT=qT, h=h: e.matmul(ps[:, h * 128:(h + 1) * 128], lhsT=kT[:, h, :], rhs=qT[:, h, :], start=True, stop=True),
                         reads=[f"kT{s}", f"qT{s}"], writes=[f"ps_s{s}"])
                psv = ps[:].rearrange("p (h t) -> p h t", h=4)
                P.dve(lambda e, sm=sm, psv=psv: e.tensor_tensor(out=sm[:, 0:4, :], in0=psv, in1=tri4[:, 0:4, :], op=ALU.mult),
                      reads=[f"ps_s{s}", "tri4"], writes=[f"smf{s}"])
                P.dve(lambda e, sm=sm, psv=psv: e.tensor_tensor(out=sm[:, 4:8, :], in0=psv, in1=tri4[:, 4:8, :], op=ALU.mult),
                      reads=[f"ps_s{s}", "tri4"], writes=[f"smb{s}"])
                for r in range(8):
                    P.act(lambda e, ee=ee, mb=mb, r=r, c=c: e.activation(out=ee[:, r, :], in_=mb[:, r, :], func=AF.Exp, bias=tokser[:, 0, c, r:r + 1], scale=-1.0),
                          reads=[f"mb{s}", "tokser"], writes=[f"ee{s}"])
                P.dve(lambda e, pp=pp, ee=ee, sm=sm: e.scalar_tensor_tensor(out=pp[:], in0=ee[:], scalar=1.0, in1=sm[:], op0=ALU.min, op1=ALU.mult),
                      reads=[f"ee{s}", f"smf{s}", f"smb{s}"], writes=[f"pp{s}"])
                for d in range(2):
                    P.pool(lambda e, qw=qw, qT=qT, wb=wb, d=d: e.tensor_tensor(out=qw[:, d * 4:(d + 1) * 4, :], in0=qT[:], in1=wb[:, d * 4:(d + 1) * 4, :], op=ALU.mult),
                           reads=[f"qT{s}", f"wb{s}"], writes=[f"qw{s}"])
                P.pool(lambda e, gg=gg, og=og: e.tensor_tensor(out=gg[:], in0=og[:], in1=mnw[:], op=ALU.mult), reads=[f"og{s}", "mnw"], writes=[f"gg{s}"])
                for r in range(8):
                    d, h = r // 4, r % 4
                    P.pe(lambda e, po=po, qw=qw, r=r, d=d, h=h, c=c: e.matmul(po[:, r * 128:(r + 1) * 128], lhsT=qw[:, r, :], rhs=cst[:, d, c, h * 129:h * 129 + 128], start=True, stop=False),
                         reads=[f"qw{s}", f"cst{d}_{c}"], writes=[f"ps_o{s}"])
                    P.pe(lambda e, po=po, pp=pp, v1=v1, r=r, h=h: e.matmul(po[:, r * 128:(r + 1) * 128], lhsT=pp[:, r, :], rhs=v1[:, h, 0:128], start=False, stop=True),
                         reads=[f"pp{s}", v1t], writes=[f"ps_o{s}"])
                    P.pe(lambda e, qw=qw, r=r, d=d, h=h, c=c, s=s: e.matmul(ps_d[:, s * 8 + r:s * 8 + r + 1], lhsT=qw[:, r, :], rhs=cst[:, d, c, h * 129 + 128:h * 129 + 129], start=True, stop=False),
                         reads=[f"qw{s}", f"cst{d}_{c}"], writes=[f"ps_d{s}"])
                    P.pe(lambda e, pp=pp, v1=v1, r=r, h=h, s=s: e.matmul(ps_d[:, s * 8 + r:s * 8 + r + 1], lhsT=pp[:, r, :], rhs=v1[:, h, 128:129], start=False, stop=True),
                         reads=[f"pp{s}", v1t], writes=[f"ps_d{s}"])
                P.act(lambda e, dd=dd, s=s: e.activation(out=dd[:], in_=ps_d[:, s * 8:s * 8 + 8], func=AF.Copy), reads=[f"ps_d{s}"], writes=[f"dd{s}"], tiny=True)
                P.dve(lambda e, dd=dd: e.scalar_tensor_tensor(out=dd[:], in0=dd[:], scalar=-1.0, in1=dd[:], op0=ALU.mult, op1=ALU.max), reads=[f"dd{s}"], writes=[f"dd{s}"], tiny=True)
                P.dve(lambda e, dd=dd, c=c: e.tensor_tensor(out=dd[:], in0=dd[:], in1=tokser[:, 2, c, :], op=ALU.max), reads=[f"dd{s}", "tokser"], writes=[f"dd{s}"], tiny=True)
                P.dve(lambda e, dd=dd: e.reciprocal(out=dd[:], in_=dd[:]), reads=[f"dd{s}"], writes=[f"dd{s}"], tiny=True)
                P.act(lambda e, hs=hs, po=po: e.activation(out=hs[:], in_=po[:].rearrange("p (r t) -> p r t", r=8), func=AF.Copy), reads=[f"ps_o{s}"], writes=[f"hs{s}"])
                P.dve(lambda e, hs=hs, dd=dd: e.tensor_tensor(out=hs[:], in0=hs[:], in1=mk(dd[:, 0:1], [[1, 8], [0, 128]]), op=ALU.mult),
                      reads=[f"hs{s}", f"dd{s}"], writes=[f"hs{s}"])
                P.pool(lambda e, hall=hall, hs=hs: e.tensor_tensor(out=hall[:].rearrange("p (h x) -> p h x", h=4), in0=hs[:, 0:4, :], in1=hs[:, 4:8, :], op=ALU.add),
                       reads=[f"hs{s}"], writes=[f"hall{s}"])
                P.pool(lambda e, hall=hall, hsq=hsq: e.tensor_tensor(out=hsq[:], in0=hall[:], in1=hall[:], op=ALU.mult), reads=[f"hall{s}"], writes=[f"hsq{s}"])
                P.dve(lambda e, ss=ss, hsq=hsq: e.tensor_reduce(out=ss[:], in_=hsq[:].rearrange("p (h x) -> p h x", h=4), axis=AX.X, op=ALU.add), reads=[f"hsq{s}"], writes=[f"ss{s}"], tiny=True)
                P.act(lambda e, ss=ss: e.activation(out=ss[:], in_=ss[:], func=AF.Ln, bias=epsb[:, 0:1]), reads=[f"ss{s}", "epsb"], writes=[f"ss{s}"], tiny=True)
                P.act(lambda e, ss=ss: e.activation(out=ss[:], in_=ss[:], func=AF.Exp, scale=-0.5), reads=[f"ss{s}"], writes=[f"ss{s}"], tiny=True)
                P.dve(lambda e, hsq=hsq, hall=hall, ss=ss: e.tensor_tensor(out=hsq[:].rearrange("p (h x) -> p h x", h=4), in0=hall[:].rearrange("p (h x) -> p h x", h=4), in1=mk(ss[:, 0:1], [[1, 4], [0, 128]]), op=ALU.mult),
                      reads=[f"hall{s}", f"ss{s}", f"hsq{s}"], writes=[f"hsq{s}"])
                P.pool(lambda e, ybf=ybf, hsq=hsq, gg=gg: e.tensor_tensor(out=ybf[:], in0=hsq[:], in1=gg[:], op=ALU.mult), reads=[f"hsq{s}", f"gg{s}"], writes=[f"ybf{s}"])
                for h in range(4):
                    P.pe(lambda e, h=h, s=s, ybf=ybf: e.transpose(ps_y[:, s * 512 + h * 128: s * 512 + (h + 1) * 128], ybf[:, h * 128:(h + 1) * 128], idb[:]),
                         reads=[f"ybf{s}", "idb"], writes=[f"ps_y{s}"])
                P.act(lambda e, yT=yT, s=s: e.activation(out=yT[:], in_=ps_y[:, s * 512:(s + 1) * 512].rearrange("p (h t) -> p h t", h=4), func=AF.Copy),
                      reads=[f"ps_y{s}"], writes=[f"yT{s}"])
                P.dma(lambda e, yT=yT, tk=tk: e.dma_start(out=S["YT"][0:512, tk:tk + 128].rearrange("(h p) t -> p h t", p=128), in_=yT[:]),
                      reads=[f"yT{s}"], writes=["dram_yt"])
            P.emit()

    def phase_C(self, l):
        nc, I, S = self.nc, self.I, self.S
        P = Prog(self.ctx)
        with ExitStack() as es:
            def sb(name, shape, dt):
                return es.enter_context(nc.sbuf_tensor(f"L{l}" + name, list(shape), dt))

            def psum(name, shape, dt=F32):
                return es.enter_context(nc.psum_tensor(f"L{l}" + name, list(shape), dt))

            qn = sb("C_qn", [128, 4, T], BF16)
            kn = sb("C_kn", [128, 4, T], BF16)
            vn = sb("C_vn", [128, 32, 8, 65], BF16)
            nbi = sb("C_nbi", [128, 8, 5, 128], F32)
            nbe = sb("C_nbe", [128, 8, 4, 128], F32)
            idf = sb("C_idf", [128, 128], F32)
            idb = sb("C_idb", [128, 128], BF16)
            sbb_ = [sb(f"C_sb{i}", [128, 5, 128], F32) for i in range(2)]
            pt_ = [sb(f"C_pt{i}", [128, 5, 128], BF16) for i in range(2)]
            rd = sb("C_rd", [128, 8], F32)
            yn = sb("C_yn", [128, 8, 64], BF16)
            yT_ = [sb(f"C_yT{i}", [128, 4, 128], BF16) for i in range(2)]
            ps_s = [psum(f"C_ps_s{i}", [128, 1024]) for i in range(2)]
            ps_o = psum("C_ps_o", [128, 512])
            ps_y = psum("C_ps_y", [128, 1024], BF16)
            P.dma(lambda e: e.dma_start(out=idf[:], in_=I["ident"]), writes=["idf"])
            P.dve(lambda e: e.tensor_copy(out=idb[:], in_=idf[:]), reads=["idf"], writes=["idb"])
            P.dma(lambda e: e.dma_start(out=nbi[:], in_=I["nbi"][l]), writes=["nbi"])
            for g in range(4):
                P.dma(lambda e, g=g: e.dma_start(out=qn[:, g, :], in_=S["QnT"][g * 128:(g + 1) * 128, :]), writes=["qn"])
                P.dma(lambda e, g=g: e.dma_start(out=kn[:, g, :], in_=S["KnT"][g * 128:(g + 1) * 128, :]), writes=["kn"])
            P.pool(lambda e: e.memset(vn[:, :, :, 64:65], 1.0), writes=["vn"])
            for m in range(32):
                P.dma(lambda e, m=m: e.dma_start(out=vn[:, m, :, 0:64], in_=S["Vn"][m * 128:(m + 1) * 128, :].rearrange("p (h x) -> p h x", h=8)), writes=["vn"])
            ui = 0
            for i in range(32):
                if i < 2:
                    tiles = [0, 1, 2, 3]
                    eidx = i
                elif i >= 30:
                    tiles = [28, 29, 30, 31]
                    eidx = i - 28
                else:
                    tiles = [i - 2, i - 1, i, i + 1, i + 2]
                    eidx = None
                if eidx is not None:
                    P.dma(lambda e, eidx=eidx: e.dma_start(out=nbe[:], in_=I["nbe"][l, eidx]), writes=["nbe"])
                nt = len(tiles)
                for h in range(8):
                    g, hp = h // 2, (h % 2) * 64
                    u = ui % 2
                    ui += 1
                    ps = ps_s[u]
                    sbb, pt = sbb_[u], pt_[u]
                    for j, m in enumerate(tiles):
                        P.pe(lambda e, ps=ps, j=j, m=m, g=g, hp=hp, i=i: e.matmul(ps[:, j * 128:(j + 1) * 128], lhsT=kn[hp:hp + 64, g, m * 128:(m + 1) * 128], rhs=qn[hp:hp + 64, g, i * 128:(i + 1) * 128], start=True, stop=True),
                             reads=["kn", "qn"], writes=[f"ps_s{u}"])
                    bias = nbi[:, h, :, :] if eidx is None else nbe[:, h, :, :]
                    btok = "nbi" if eidx is None else "nbe"
                    P.dve(lambda e, ps=ps, sbb=sbb, bias=bias, nt=nt: e.tensor_tensor(out=sbb[:, 0:nt, :], in0=ps[:, 0:nt * 128].rearrange("p (j x) -> p j x", x=128), in1=bias, op=ALU.add),
                          reads=[f"ps_s{u}", btok], writes=[f"sb{u}"])
                    P.act(lambda e, sbb=sbb, pt=pt, nt=nt: e.activation(out=pt[:, 0:nt, :], in_=sbb[:, 0:nt, :], func=AF.Exp), reads=[f"sb{u}"], writes=[f"pt{u}"])
                    so = (ui % 4) * 128
                    sot = f"ps_o{ui % 4}"
                    for j, m in enumerate(tiles):
                        P.pe(lambda e, pt=pt, j=j, m=m, h=h, nt=nt, so=so: e.matmul(ps_o[:, so:so + 65], lhsT=pt[:, j, :], rhs=vn[:, m, h, :], start=(j == 0), stop=(j == nt - 1)),
                             reads=[f"pt{u}", "vn"], writes=[sot])
                    P.dve(lambda e, h=h, so=so: e.reciprocal(out=rd[:, h:h + 1], in_=ps_o[:, so + 64:so + 65]), reads=[sot], writes=[f"rd{h}"], tiny=True)
                    P.act(lambda e, h=h, so=so: e.activation(out=yn[:, h, :], in_=ps_o[:, so:so + 64], func=AF.Copy, scale=rd[:, h:h + 1]), reads=[sot, f"rd{h}"], writes=["yn"])
                s = i % 2
                yT = yT_[s]
                for g in range(4):
                    P.pe(lambda e, g=g, s=s: e.transpose(ps_y[:, s * 512 + g * 128:s * 512 + (g + 1) * 128], yn[:, 2 * g:2 * g + 2, :].rearrange("p a b -> p (a b)"), idb[:]),
                         reads=["yn", "idb"], writes=[f"ps_y{s}"])
                P.dve(lambda e, yT=yT, s=s: e.tensor_copy(out=yT[:], in_=ps_y[:, s * 512:(s + 1) * 512].rearrange("p (h t) -> p h t", h=4)),
                      reads=[f"ps_y{s}"], writes=[f"yT{s}"])
                P.dma(lambda e, yT=yT, i=i: e.dma_start(out=S["YT"][512:1024, i * 128:(i + 1) * 128].rearrange("(h p) t -> p h t", p=128), in_=yT[:]),
                      reads=[f"yT{s}"], writes=["dram_yt"])
            P.emit()

    def phase_D(self, l):
        nc, I, S = self.nc, self.I, self.S
        P = Prog(self.ctx)
        xsrc = I["xT"] if l == 0 else S["xres"]
        with ExitStack() as es:
            def sb(name, shape, dt):
                return es.enter_context(nc.sbuf_tensor(f"L{l}" + name, list(shape), dt))

            def psum(name, shape, dt=F32):
                return es.enter_context(nc.psum_tensor(f"L{l}" + name, list(shape), dt))

            wbf = sb("D_wbf", [128, 8, D], BF16)
            wst = [sb(f"D_wst{i}", [128, D], F32) for i in range(2)]
            xin = [sb(f"D_xin{i}", [128, 8, TB], F32) for i in range(2)]
            yin = [sb(f"D_yin{i}", [128, 8, TB], BF16) for i in range(2)]
            pso = [psum(f"D_ps{i}", [128, 512]) for i in range(4)]
            for kc in range(8):
                st = wst[kc % 2]
                P.dma(lambda e, st=st, kc=kc: e.dma_start(out=st[:], in_=I["w_out"][l, kc * 128:(kc + 1) * 128, :]), writes=[f"wst{kc % 2}"])
                eng = P.pool if kc % 2 == 0 else P.act
                if kc % 2 == 0:
                    P.pool(lambda e, st=st, kc=kc: e.tensor_copy(out=wbf[:, kc, :], in_=st[:]), reads=[f"wst{kc % 2}"], writes=[f"wbf{kc}"])
                else:
                    P.act(lambda e, st=st, kc=kc: e.activation(out=wbf[:, kc, :], in_=st[:], func=AF.Copy), reads=[f"wst{kc % 2}"], writes=[f"wbf{kc}"])
            WB = [f"wbf{kc}" for kc in range(8)]

            def load(b):
                t0 = b * TB
                P.dma(lambda e, b=b, t0=t0: e.dma_start(out=xin[b % 2][:], in_=xsrc[:, t0:t0 + TB].rearrange("(kc p) t -> p kc t", p=128)), writes=[f"xin{b % 2}"])
                P.dma(lambda e, b=b, t0=t0: e.dma_start(out=yin[b % 2][:], in_=S["YT"][:, t0:t0 + TB].rearrange("(kc p) t -> p kc t", p=128)), writes=[f"yin{b % 2}"])

            load(0)
            pi = 0
            for b in range(NB):
                if b + 1 < NB:
                    load(b + 1)
                t0 = b * TB
                xi, yi = xin[b % 2], yin[b % 2]
                for dc in range(8):
                    ps = pso[pi % 4]
                    pst = f"ps{pi % 4}"
                    pi += 1
                    for kc in range(8):
                        P.pe(lambda e, ps=ps, kc=kc, dc=dc, yi=yi: e.matmul(ps[:], lhsT=wbf[:, kc, dc * 128:(dc + 1) * 128], rhs=yi[:, kc, :], start=(kc == 0), stop=(kc == 7)),
                             reads=WB + [f"yin{b % 2}"], writes=[pst])
                    P.dve(lambda e, ps=ps, dc=dc, xi=xi: e.tensor_tensor(out=xi[:, dc, :], in0=xi[:, dc, :], in1=ps[:], op=ALU.add),
                          reads=[pst, f"xin{b % 2}"], writes=[f"xin{b % 2}"])
                P.dma(lambda e, xi=xi, t0=t0: e.dma_start(out=S["xres"][:, t0:t0 + TB].rearrange("(kc p) t -> p kc t", p=128), in_=xi[:]),
                      reads=[f"xin{b % 2}"], writes=["dram_x"])
            P.emit()

    def phase_E(self, l):
        nc, I, S = self.nc, self.I, self.S
        P = Prog(self.ctx)
        last = (l == DEPTH - 1)
        with ExitStack() as es:
            def sb(name, shape, dt):
                return es.enter_context(nc.sbuf_tensor(f"L{l}" + name, list(shape), dt))

            def psum(name, shape, dt=F32):
                return es.enter_context(nc.psum_tensor(f"L{l}" + name, list(shape), dt))

            w1 = sb("E_w1", [128, 8, DFF], BF16)
            w2 = sb("E_w2", [128, 32, D], BF16)
            wst = [sb(f"E_wst{i}", [128, 1024], F32) for i in range(2)]
            n2 = sb("E_n2", [128, DEPTH, 8], F32)
            nf = sb("E_nf", [128, 8], F32)
            ones = sb("E_ones", [128, 128], BF16)
            epsb = sb("E_epsb", [128, 1], F32)
            xin = [sb(f"E_xin{i}", [128, 8, TB], F32) for i in range(2)]
            xsq = sb("E_xsq", [128, 8, TB], BF16)
            rstd = sb("E_rstd", [128, TB], F32)
            hn = sb("E_hn", [128, 8, TB], BF16)
            uu = [sb(f"E_u{i}", [128, 8, TB], BF16) for i in range(2)]
            rtmp = [sb(f"E_rtmp{i}", [128, TB], F32) for i in range(2)]
            ps_ssq = psum("E_ps_ssq", [128, 512])
            ps1 = [psum(f"E_ps1_{i}", [128, 512]) for i in range(3)]
            ps2 = [psum(f"E_ps2_{i}", [128, 512]) for i in range(3)]
            P.dma(lambda e: e.dma_start(out=n2[:], in_=I["n2"]), writes=["n2"])
            P.dma(lambda e: e.dma_start(out=nf[:], in_=I["nf"]), writes=["nf"])
            P.dve(lambda e: e.memset(ones[:], 1.0), writes=["ones"])
            P.dve(lambda e: e.memset(epsb[:], float(D * EPS)), writes=["epsb"], tiny=True)
            si = 0
            for kc in range(8):
                for hf in range(4):
                    st = wst[si % 2]
                    stt = f"wst{si % 2}"
                    P.dma(lambda e, st=st, kc=kc, hf=hf: e.dma_start(out=st[:], in_=I["w_ff1"][l, kc * 128:(kc + 1) * 128, hf * 1024:(hf + 1) * 1024]), writes=[stt])
                    eng = (P.pool, P.dve)[si % 2]
                    eng(lambda e, st=st, kc=kc, hf=hf: e.tensor_scalar(out=w1[:, kc, hf * 1024:(hf + 1) * 1024], in0=st[:], scalar1=n2[:, l, kc:kc + 1], scalar2=None, op0=ALU.mult),
                        reads=[stt, "n2"], writes=[f"w1_{kc}_{hf}"])
                    si += 1
            for k2 in range(32):
                st = wst[si % 2]
                stt = f"wst{si % 2}"
                P.dma(lambda e, st=st, k2=k2: e.dma_start(out=st[:], in_=I["w_ff2"][l, k2 * 128:(k2 + 1) * 128, :]), writes=[stt])
                if si % 2 == 0:
                    P.pool(lambda e, st=st, k2=k2: e.tensor_copy(out=w2[:, k2, :], in_=st[:]), reads=[stt], writes=[f"w2_{k2}"])
                else:
                    P.act(lambda e, st=st, k2=k2: e.activation(out=w2[:, k2, :], in_=st[:], func=AF.Copy), reads=[stt], writes=[f"w2_{k2}"])
                si += 1
            W1 = [f"w1_{kc}_{hf}" for kc in range(8) for hf in range(4)]
            W2 = [f"w2_{k2}" for k2 in range(32)]

            def load(b):
                t0 = b * TB
                P.dma(lambda e, b=b, t0=t0: e.dma_start(out=xin[b % 2][:], in_=S["xres"][:, t0:t0 + TB].rearrange("(kc p) t -> p kc t", p=128)), writes=[f"xin{b % 2}"])

            def rms(xi, xt, dst_fn, wcol_fn):
                P.act(lambda e: e.activation(out=xsq[:], in_=xi[:], func=AF.Square), reads=[xt], writes=["xsq"])
                for kc in range(8):
                    P.pe(lambda e, kc=kc: e.matmul(ps_ssq[:], lhsT=ones[:], rhs=xsq[:, kc, :], start=(kc == 0), stop=(kc == 7)), reads=["ones", "xsq"], writes=["ps_ssq"])
                P.act(lambda e: e.activation(out=rstd[:], in_=ps_ssq[:], func=AF.Ln, bias=epsb[:, 0:1]), reads=["ps_ssq", "epsb"], writes=["rstd"])
                P.act(lambda e: e.activation(out=rstd[:], in_=rstd[:], func=AF.Exp, scale=-0.5), reads=["rstd"], writes=["rstd"])

            load(0)
            p1 = 0
            p2 = 0
            gi = 0
            for b in range(NB):
                if b + 1 < NB:
                    load(b + 1)
                t0 = b * TB
                xi = xin[b % 2]
                xt = f"xin{b % 2}"
                rms(xi, xt, None, None)
                for kc in range(8):
                    eng = P.dve
                    eng(lambda e, kc=kc, xi=xi: e.scalar_tensor_tensor(out=hn[:, kc, :], in0=xi[:, kc, :], scalar=32.0, in1=rstd[:], op0=ALU.mult, op1=ALU.mult),
                        reads=[xt, "rstd"], writes=[f"hn{kc}"])
                HN = [f"hn{kc}" for kc in range(8)]
                for grp in range(4):
                    u = uu[gi % 2]
                    ut = f"u{gi % 2}"
                    gi += 1
                    for fj in range(8):
                        fc = grp * 8 + fj
                        ps = ps1[p1 % 3]
                        pst = f"ps1_{p1 % 3}"
                        p1 += 1
                        for kc in range(8):
                            P.pe(lambda e, ps=ps, kc=kc, fc=fc: e.matmul(ps[:], lhsT=w1[:, kc, fc * 128:(fc + 1) * 128], rhs=hn[:, kc, :], start=(kc == 0), stop=(kc == 7)),
                                 reads=W1 + HN, writes=[pst])
                        rt = rtmp[p1 % 2]
                        rtt = f"rtmp{p1 % 2}"
                        P.act(lambda e, ps=ps, rt=rt: e.activation(out=rt[:], in_=ps[:], func=AF.Relu), reads=[pst], writes=[rtt])
                        eng = P.dve if fj % 2 == 0 else P.pool
                        eng(lambda e, u=u, fj=fj, rt=rt: e.tensor_tensor(out=u[:, fj, :], in0=rt[:], in1=rt[:], op=ALU.mult), reads=[rtt], writes=[f"{ut}_{fj}"])
                    UT = [f"{ut}_{fj}" for fj in range(8)]
                    for dc in range(8):
                        ps = ps2[p2 % 3]
                        pst = f"ps2_{p2 % 3}"
                        p2 += 1
                        for fj in range(8):
                            P.pe(lambda e, ps=ps, fj=fj, grp=grp, dc=dc, u=u: e.matmul(ps[:], lhsT=w2[:, grp * 8 + fj, dc * 128:(dc + 1) * 128], rhs=u[:, fj, :], start=(fj == 0), stop=(fj == 7)),
                                 reads=W2 + UT, writes=[pst])
                        P.dve(lambda e, ps=ps, dc=dc, xi=xi: e.tensor_tensor(out=xi[:, dc, :], in0=xi[:, dc, :], in1=ps[:], op=ALU.add),
                              reads=[pst, xt] + HN, writes=[xt])
                if not last:
                    P.dma(lambda e, xi=xi, t0=t0: e.dma_start(out=S["xres"][:, t0:t0 + TB].rearrange("(kc p) t -> p kc t", p=128), in_=xi[:]),
                          reads=[xt], writes=["dram_x"])
                else:
                    rms(xi, xt, None, None)
                    for kc in range(8):
                        P.dve(lambda e, kc=kc, xi=xi: e.scalar_tensor_tensor(out=xi[:, kc, :], in0=xi[:, kc, :], scalar=32.0, in1=rstd[:], op0=ALU.mult, op1=ALU.mult),
                              reads=[xt, "rstd"], writes=[xt])
                        P.pool(lambda e, kc=kc, xi=xi: e.tensor_scalar(out=xi[:, kc, :], in0=xi[:, kc, :], scalar1=nf[:, kc:kc + 1], scalar2=None, op0=ALU.mult),
                               reads=[xt, "nf"], writes=[xt])
                    P.dma(lambda e, xi=xi, t0=t0: e.dma_start(out=self.out[:, t0:t0 + TB].rearrange("(kc p) t -> p kc t", p=128), in_=xi[:]),
                          reads=[xt], writes=["dram_out"])
            P.emit()


def _na_bias_tables(rpb_l):
    H = rpb_l.shape[0]
    kl = np.arange(128)
    kr_l, kc = kl // 64, kl % 64
    ql = np.arange(128)
    qr_l, qc = ql // 64, ql % 64

    def tile(i, m):
        kr = 2 * m + kr_l[:, None]
        qr = 2 * i + qr_l[None, :]
        rs = np.clip(qr - 4, 0, 56)
        vr = (kr >= rs) & (kr < rs + 8)
        cs = np.clip(qc[None, :] - 8, 0, 48)
        vc = (kc[:, None] >= cs) & (kc[:, None] < cs + 16)
        ri = np.clip(kr - qr + 7, 0, 14)
        ci = np.clip(kc[:, None] - qc[None, :] + 15, 0, 30)
        vals = rpb_l[:, ri, ci]
        return np.where((vr & vc)[None], vals, np.float32(-30000.0)).astype(np.float32)

    nbi = np.stack([tile(2, m) for m in range(5)], axis=1)
    nbi = np.ascontiguousarray(nbi.transpose(2, 0, 1, 3))
    nbe = []
    for i, ms in ((0, range(4)), (1, range(4)), (30, range(28, 32)), (31, range(28, 32))):
        t = np.stack([tile(i, m) for m in ms], axis=1)
        nbe.append(t.transpose(2, 0, 1, 3))
    nbe = np.ascontiguousarray(np.stack(nbe, axis=0))
    return nbi, nbe


def _prep_shared(norm1_w, conv_w, conv_b, gate_b, mlstm_norm_w, rpb, norm2_w, final_norm_w):
    f = np.float32
    sh = {}
    sh["n1"] = np.ascontiguousarray(norm1_w.reshape(DEPTH, 8, 128).transpose(2, 0, 1)).astype(f)
    sh["n2"] = np.ascontiguousarray(norm2_w.reshape(DEPTH, 8, 128).transpose(2, 0, 1)).astype(f)
    sh["nf"] = np.ascontiguousarray(final_norm_w.reshape(8, 128).T).astype(f)
    sh["cw"] = np.ascontiguousarray(conv_w.reshape(DEPTH, 3, 8, 128).transpose(3, 0, 1, 2)).astype(f)
    sh["cb"] = np.ascontiguousarray(conv_b.reshape(DEPTH, 8, 128).transpose(2, 0, 1)).astype(f)
    gb = np.zeros((36, DEPTH, 2), f)
    g4 = gate_b.reshape(DEPTH, 4, 4)
    gb[0:4, :, 0] = g4[:, 0, :].T
    gb[0:4, :, 1] = g4[:, 1, :].T
    gb[32:36, :, 0] = g4[:, 2, :].T
    gb[32:36, :, 1] = g4[:, 3, :].T
    sh["gb"] = gb
    sh["mnw"] = np.ascontiguousarray(np.broadcast_to(mlstm_norm_w[None], (128, DEPTH, 512))).astype(f)
    nbi, nbe = zip(*[_na_bias_tables(np.asarray(rpb[l], f)) for l in range(DEPTH)])
    sh["nbi"] = np.stack(nbi, 0)
    sh["nbe"] = np.stack(nbe, 0)
    sh["ident"] = np.eye(128, dtype=f)
    sh["trif"] = np.triu(np.ones((128, 128), f))
    sh["trib"] = np.tril(np.ones((128, 128), f))
    return sh


_CACHE = {}


def _get_nc(debug=False, stop_after=None):
    key = (debug, stop_after)
    if key not in _CACHE:
        b = Builder(debug=debug, stop_after=stop_after)
        nc = b.build()
        _CACHE[key] = (nc, b)
    return _CACHE[key]


def run(inputs, debug=False, stop_after=None, cores=8):
    x = np.asarray(inputs["x"], np.float32)
    sh = _prep_shared(*[np.asarray(inputs[k], np.float32) for k in
                        ("norm1_w", "conv_w", "conv_b", "gate_b", "mlstm_norm_w", "rpb", "norm2_w", "final_norm_w")])
    for k in ("w_in", "w_out", "w_ff1", "w_ff2"):
        sh[k] = np.ascontiguousarray(np.asarray(inputs[k], np.float32))
    nc, b = _get_nc(debug, stop_after)
    in_maps = []
    for c in range(cores):
        m = dict(sh)
        m["xT"] = np.ascontiguousarray(x[c].T)
        in_maps.append(m)
    res = run_bass_kernel_spmd(nc, in_maps, core_ids=list(range(cores)))
    return res, b


def kernel(x, norm1_w, w_in, conv_w, conv_b, gate_b, mlstm_norm_w, rpb, w_out,
           norm2_w, w_ff1, w_ff2, final_norm_w):
    inputs = dict(x=x, norm1_w=norm1_w, w_in=w_in, conv_w=conv_w, conv_b=conv_b, gate_b=gate_b,
                  mlstm_norm_w=mlstm_norm_w, rpb=rpb, w_out=w_out, norm2_w=norm2_w, w_ff1=w_ff1,
                  w_ff2=w_ff2, final_norm_w=final_norm_w)
    res, _ = run(inputs)
    out = np.stack([np.ascontiguousarray(r["outT"].T) for r in res.results], axis=0)
    return out.astype(np.float32)
```

```python
import numpy as np
from contextlib import ExitStack
import concourse.bass as bass
import concourse.mybir as mybir
from concourse.bass_utils import run_bass_kernel_spmd

F32 = mybir.dt.float32
BF16 = mybir.dt.bfloat16
AF = mybir.ActivationFunctionType
ALU = mybir.AluOpType
AX = mybir.AxisListType

T = 4096
D = 1024
DEPTH = 2
DIN = 3600
DFF = 4096
NB = 8
TB = 512
NCH = 32
EPS = 1e-6
ALPHA = 128.0 ** -0.5

ENGS = ("pe", "act", "dve", "pool", "sp")
EPOCH = 4000
NDMASEM = 20
SAME_SYNC = True


class Op:
    __slots__ = ("eng", "fn", "deps", "dma", "idx", "sig", "semk", "val", "tiny")

    def __init__(self, eng, fn, dma, tiny=False):
        self.eng = eng
        self.fn = fn
        self.dma = dma
        self.tiny = tiny
        self.deps = set()
        self.sig = False
        self.semk = None
        self.val = 0


class Ctx:
    def __init__(self, nc):
        self.nc = nc
        self.sems = {}
        self.ccount = {e: 0 for e in ENGS}
        self.dval = {}
        self.dk = {e: 0 for e in ENGS}
        self.waited = {e: {} for e in ENGS}
        self.nops = 0
        self.nwaits = 0

    def sem(self, k):
        if k not in self.sems:
            self.sems[k] = self.nc.alloc_semaphore("s_" + "_".join(str(x) for x in k))
        return self.sems[k]


class Prog:
    def __init__(self, ctx):
        self.ctx = ctx
        self.nc = ctx.nc
        self.ops = []
        self.last_w = {}
        self.readers = {}

    def op(self, eng, fn, reads=(), writes=(), dma=False, tiny=False):
        o = Op(eng, fn, dma, tiny)
        o.idx = len(self.ops)
        for r in reads:
            w = self.last_w.get(r)
            if w is not None:
                o.deps.add(w)
        for w_ in writes:
            w = self.last_w.get(w_)
            if w is not None:
                o.deps.add(w)
            for rd in self.readers.get(w_, ()):
                o.deps.add(rd)
        o.deps.discard(o.idx)
        for r in reads:
            self.readers.setdefault(r, []).append(o.idx)
        for w_ in writes:
            self.last_w[w_] = o.idx
            self.readers[w_] = []
        self.ops.append(o)
        return o

    def pe(self, fn, reads=(), writes=()):
        return self.op("pe", fn, reads, writes)

    def act(self, fn, reads=(), writes=(), tiny=False):
        return self.op("act", fn, reads, writes, tiny=tiny)

    def dve(self, fn, reads=(), writes=(), tiny=False):
        return self.op("dve", fn, reads, writes, tiny=tiny)

    def pool(self, fn, reads=(), writes=(), tiny=False):
        return self.op("pool", fn, reads, writes, tiny=tiny)

    def dma(self, fn, reads=(), writes=(), q="sp"):
        return self.op(q, fn, reads, writes, dma=True)

    def _unsynced(self, od, o):
        return (od.eng == o.eng and not od.dma and not o.dma
                and (od.eng == "pe" or not SAME_SYNC or not od.tiny))

    def emit(self):
        ctx = self.ctx
        nc = self.nc
        ops = self.ops
        per = {e: [o for o in ops if o.eng == e] for e in ENGS}
        for o in ops:
            for d in o.deps:
                od = ops[d]
                if od.dma or self._unsynced(od, o):
                    continue
                od.sig = True
        for e in ENGS:
            for o in reversed(per[e]):
                if not o.dma:
                    o.sig = True
                    break
        final = {}
        for e in ENGS:
            for o in per[e]:
                if o.dma:
                    i = ctx.dk[e] % NDMASEM
                    ctx.dk[e] += 1
                    k = ("d", e, i)
                    ctx.dval[k] = ctx.dval.get(k, 0) + 16
                    o.semk = k
                    o.val = ctx.dval[k]
                    final[k] = o.val
                elif o.sig:
                    n = ctx.ccount[e]
                    ctx.ccount[e] += 1
                    k = ("c", e, n // EPOCH)
                    o.semk = k
                    o.val = n % EPOCH + 1
                    final[k] = o.val
        for k in final:
            ctx.sem(k)
        ctx.nops += len(ops)

        def run_engine(e, eng):
            waited = ctx.waited[e]

            def wait(k, v):
                if waited.get(k, 0) >= v:
                    return
                waited[k] = v
                eng.wait_ge(ctx.sems[k], v)
                ctx.nwaits += 1

            for o in per[e]:
                need = {}
                for d in o.deps:
                    od = ops[d]
                    if od.semk is None or self._unsynced(od, o):
                        continue
                    if need.get(od.semk, 0) < od.val:
                        need[od.semk] = od.val
                if o.dma and o.val > 16:
                    if need.get(o.semk, 0) < o.val - 16:
                        need[o.semk] = o.val - 16
                for k, v in need.items():
                    wait(k, v)
                ins = o.fn(eng)
                if o.dma:
                    ins.then_inc(ctx.sems[o.semk], 16)
                elif o.sig:
                    ins.then_inc(ctx.sems[o.semk], 1)
            for k, v in final.items():
                if k[0] == "c" and k[1] == e:
                    if e == "pe" or not SAME_SYNC:
                        pass
                wait(k, v)

        with nc.Block() as block:
            block.tensor(lambda eng: run_engine("pe", eng))
            block.scalar(lambda eng: run_engine("act", eng))
            block.vector(lambda eng: run_engine("dve", eng))
            block.gpsimd(lambda eng: run_engine("pool", eng))
            block.sync(lambda eng: run_engine("sp", eng))


def mk(ap, dims, off=0):
    a = ap.ap
    return bass.AP(ap.tensor, ap.offset + off, [list(a[0])] + [list(d) for d in dims])


class Builder:
    def __init__(self, debug=False, stop_after=None):
        self.debug = debug
        self.stop_after = stop_after
        self.nc = bass.Bass("TRN2", target_bir_lowering=False)
        self.ctx = Ctx(self.nc)
        self.dbg_names = []

    def din(self, name, shape, dt=F32):
        return self.nc.dram_tensor(name, list(shape), dt, kind="ExternalInput").ap()

    def dscr(self, name, shape, dt):
        kind = "ExternalOutput" if self.debug else "Internal"
        if self.debug:
            self.dbg_names.append(name)
        return self.nc.dram_tensor(name, list(shape), dt, kind=kind).ap()

    def build(self):
        nc = self.nc
        I = {}
        I["xT"] = self.din("xT", [D, T])
        I["w_in"] = self.din("w_in", [DEPTH, D, DIN])
        I["w_out"] = self.din("w_out", [DEPTH, D, D])
        I["w_ff1"] = self.din("w_ff1", [DEPTH, D, DFF])
        I["w_ff2"] = self.din("w_ff2", [DEPTH, DFF, D])
        I["n1"] = self.din("n1", [128, DEPTH, 8])
        I["n2"] = self.din("n2", [128, DEPTH, 8])
        I["nf"] = self.din("nf", [128, 8])
        I["cw"] = self.din("cw", [128, DEPTH, 3, 8])
        I["cb"] = self.din("cb", [128, DEPTH, 8])
        I["gb"] = self.din("gb", [36, DEPTH, 2])
        I["mnw"] = self.din("mnw", [128, DEPTH, 512])
        I["nbi"] = self.din("nbi", [DEPTH, 128, 8, 5, 128])
        I["nbe"] = self.din("nbe", [DEPTH, 4, 128, 8, 4, 128])
        I["ident"] = self.din("ident", [128, 128])
        I["trif"] = self.din("trif", [128, 128])
        I["trib"] = self.din("trib", [128, 128])
        self.I = I
        self.out = nc.dram_tensor("outT", [D, T], F32, kind="ExternalOutput").ap()
        S = {}
        S["QT"] = self.dscr("QT", [512, T], BF16)
        S["KT"] = self.dscr("KT", [512, T], BF16)
        S["Ktok"] = self.dscr("Ktok", [T, 512], BF16)
        S["Vtok"] = self.dscr("Vtok", [T, 512], BF16)
        S["Osig"] = self.dscr("Osig", [T, 512], F32)
        S["G"] = self.dscr("G", [16, T], F32)
        S["QnT"] = self.dscr("QnT", [512, T], BF16)
        S["KnT"] = self.dscr("KnT", [512, T], BF16)
        S["Vn"] = self.dscr("Vn", [T, 512], BF16)
        S["YT"] = self.dscr("YT", [D, T], BF16)
        S["Mser"] = self.dscr("Mser", [8, T], F32)
        S["Wser"] = self.dscr("Wser", [8, T], F32)
        S["DCs"] = self.dscr("DCs", [8, NCH], F32)
        S["xres"] = self.dscr("xres", [D, T], F32)
        if self.debug:
            S["dbgser"] = self.dscr("dbgser", [128, 4, NCH, 8], F32)
        self.S = S
        phases = []
        for l in range(DEPTH):
            phases += [("A", l), ("B", l), ("C", l), ("D", l), ("E", l)]
        for i, (ph, l) in enumerate(phases):
            getattr(self, "phase_" + ph)(l)
            if self.stop_after is not None and i + 1 >= self.stop_after:
                break
        return nc

    def phase_A(self, l):
        nc, I, S = self.nc, self.I, self.S
        P = Prog(self.ctx)
        xsrc = I["xT"] if l == 0 else S["xres"]
        with ExitStack() as es:
            def sb(name, shape, dt):
                return es.enter_context(nc.sbuf_tensor(f"L{l}" + name, list(shape), dt))

            def psum(name, shape, dt=F32):
                return es.enter_context(nc.psum_tensor(f"L{l}" + name, list(shape), dt))

            wbf = sb("A_wbf", [128, 8, DIN], BF16)
            wst = [sb(f"A_wst{i}", [128, DIN], F32) for i in range(2)]
            n1 = sb("A_n1", [128, DEPTH, 8], F32)
            cw = sb("A_cw", [128, DEPTH, 3, 8], F32)
            cb = sb("A_cb", [128, DEPTH, 8], F32)
            cws = sb("A_cws", [128, 3, 8], F32)
            cbs = sb("A_cbs", [128, 8], F32)
            idf = sb("A_idf", [128, 128], F32)
            idb = sb("A_idb", [128, 128], BF16)
            ones = sb("A_ones", [128, 128], BF16)
            epsb = sb("A_epsb", [128, 1], F32)
            xin = [sb(f"A_xin{i}", [128, 8, 514], F32) for i in range(2)]
            xsq = sb("A_xsq", [128, 8, 514], BF16)
            rstd = sb("A_rstd", [128, 514], F32)
            hn = sb("A_hn", [128, 8, 514], BF16)
            pre = [sb(f"A_pre{i}", [128, 514], F32) for i in range(2)]
            acc = [sb(f"A_acc{i}", [128, 512], F32) for i in range(2)]
            sg = [sb(f"A_sg{i}", [128, 512], F32) for i in range(2)]
            qko = [sb(f"A_qko{i}", [128, 512], BF16) for i in range(3)]
            ktok = sb("A_ktok", [128, 4, 512], BF16)
            vtok = sb("A_vtok", [128, 4, 512], BF16)
            vntok = sb("A_vntok", [128, 4, 512], BF16)
            osig = sb("A_osig", [128, 4, 512], F32)
            gsb = sb("A_gsb", [16, 512], F32)
            ps_ssq = psum("A_ps_ssq", [128, 512])
            ps_h = psum("A_ps_h", [128, 512])
            ps_pre = [psum(f"A_ps_pre{i}", [128, 512]) for i in range(2)]
            ps_tok = [psum(f"A_ps_tok{i}", [128, 512]) for i in range(2)]
            ps_g = psum("A_ps_g", [128, 512])
            ps_t = psum("A_ps_t", [128, 1024], BF16)

            P.dma(lambda e: e.dma_start(out=n1[:], in_=I["n1"]), writes=["n1"])
            P.dma(lambda e: e.dma_start(out=cw[:], in_=I["cw"]), writes=["cw"])
            P.dma(lambda e: e.dma_start(out=cb[:], in_=I["cb"]), writes=["cb"])
            P.dma(lambda e: e.dma_start(out=idf[:], in_=I["ident"]), writes=["idf"])
            P.dve(lambda e: e.tensor_copy(out=idb[:], in_=idf[:]), reads=["idf"], writes=["idb"])
            P.dve(lambda e: e.memset(ones[:], 1.0), writes=["ones"])
            P.dve(lambda e: e.memset(epsb[:], float(D * EPS)), writes=["epsb"], tiny=True)
            P.dve(lambda e: e.tensor_scalar(out=cws[:, :, 0:4], in0=cw[:, l, :, 0:4], scalar1=ALPHA, scalar2=None, op0=ALU.mult),
                  reads=["cw"], writes=["cws"], tiny=True)
            P.dve(lambda e: e.tensor_copy(out=cws[:, :, 4:8], in_=cw[:, l, :, 4:8]), reads=["cw"], writes=["cws"], tiny=True)
            P.dve(lambda e: e.tensor_scalar(out=cbs[:, 0:4], in0=cb[:, l, 0:4], scalar1=ALPHA, scalar2=None, op0=ALU.mult),
                  reads=["cb"], writes=["cbs"], tiny=True)
            P.dve(lambda e: e.tensor_copy(out=cbs[:, 4:8], in_=cb[:, l, 4:8]), reads=["cb"], writes=["cbs"], tiny=True)
            for kc in range(8):
                st = wst[kc % 2]
                P.dma(lambda e, st=st, kc=kc: e.dma_start(out=st[:], in_=I["w_in"][l, kc * 128:(kc + 1) * 128, :]),
                      writes=[f"wst{kc % 2}"])
                half = DIN // 2
                P.pool(lambda e, st=st, kc=kc: e.tensor_scalar(out=wbf[:, kc, 0:half], in0=st[:, 0:half], scalar1=n1[:, l, kc:kc + 1], scalar2=None, op0=ALU.mult),
                       reads=[f"wst{kc % 2}", "n1"], writes=[f"wbf{kc}a"])
                P.dve(lambda e, st=st, kc=kc: e.tensor_scalar(out=wbf[:, kc, half:DIN], in0=st[:, half:DIN], scalar1=n1[:, l, kc:kc + 1], scalar2=None, op0=ALU.mult),
                      reads=[f"wst{kc % 2}", "n1"], writes=[f"wbf{kc}b"])
            WB = [f"wbf{kc}{h}" for kc in range(8) for h in "ab"]

            def load_x(b):
                xi = xin[b % 2]
                t0 = b * TB
                lo = t0 - 1
                hi = t0 + TB + 1
                c0 = 0
                tok = f"xin{b % 2}"
                if b == 0:
                    lo = 0
                    c0 = 1
                    P.dve(lambda e, xi=xi: e.memset(xi[:, :, 0:1], 0.0), writes=[tok], tiny=True)
                if b == NB - 1:
                    hi = T
                    P.dve(lambda e, xi=xi: e.memset(xi[:, :, 513:514], 0.0), writes=[tok], tiny=True)
                n = hi - lo
                src = xsrc[:, lo:hi].rearrange("(kc p) t -> p kc t", p=128)
                P.dma(lambda e, xi=xi, src=src, c0=c0, n=n: e.dma_start(out=xi[:, :, c0:c0 + n], in_=src), writes=[tok])

            load_x(0)
            qi = 0
            for b in range(NB):
                if b + 1 < NB:
                    load_x(b + 1)
                xi = xin[b % 2]
                xt = f"xin{b % 2}"
                t0 = b * TB
                P.act(lambda e, xi=xi: e.activation(out=xsq[:], in_=xi[:], func=AF.Square), reads=[xt], writes=["xsq"])
                for kc in range(8):
                    P.pe(lambda e, kc=kc: e.matmul(ps_ssq[:], lhsT=ones[:], rhs=xsq[:, kc, 1:513], start=(kc == 0), stop=(kc == 7)),
                         reads=["ones", "xsq"], writes=["ps_ssq"])
                for kc in range(8):
                    P.pe(lambda e, kc=kc: e.matmul(ps_h[:, 0:2], lhsT=ones[:], rhs=mk(xsq[:, kc, 0:1], [[513, 2]]), start=(kc == 0), stop=(kc == 7)),
                         reads=["ones", "xsq"], writes=["ps_hs"])
                P.act(lambda e: e.activation(out=rstd[:, 1:513], in_=ps_ssq[:], func=AF.Ln, bias=epsb[:, 0:1]), reads=["ps_ssq", "epsb"], writes=["rstd"])
                P.act(lambda e: e.activation(out=mk(rstd[:, 0:1], [[513, 2]]), in_=ps_h[:, 0:2], func=AF.Ln, bias=epsb[:, 0:1]), reads=["ps_hs", "epsb"], writes=["rstd"], tiny=True)
                P.act(lambda e: e.activation(out=rstd[:], in_=rstd[:], func=AF.Exp, scale=-0.5), reads=["rstd"], writes=["rstd"])
                for kc in range(8):
                    eng = P.dve
                    eng(lambda e, kc=kc, xi=xi: e.scalar_tensor_tensor(out=hn[:, kc, :], in0=xi[:, kc, :], scalar=32.0, in1=rstd[:], op0=ALU.mult, op1=ALU.mult),
                        reads=[xt, "rstd"], writes=[f"hn{kc}"])
                HN = [f"hn{kc}" for kc in range(8)]

                for fc in range(8):
                    pp = ps_pre[qi % 2]
                    ppt = f"ps_pre{qi % 2}"
                    pr = pre[qi % 2]
                    prt = f"pre{qi % 2}"
                    ac = acc[qi % 2]
                    act_ = f"acc{qi % 2}"
                    sgg = sg[qi % 2]
                    sgt = f"sg{qi % 2}"
                    qo = qko[qi % 3]
                    qot = f"qko{qi % 3}"
                    qi += 1
                    c0 = fc * 128
                    for kc in range(8):
                        P.pe(lambda e, kc=kc, pp=pp, c0=c0: e.matmul(pp[:], lhsT=wbf[:, kc, c0:c0 + 128], rhs=hn[:, kc, 1:513], start=(kc == 0), stop=(kc == 7)),
                             reads=WB + HN, writes=[ppt])
                    hc = 8 + 2 * (fc % 2)
                    for kc in range(8):
                        P.pe(lambda e, kc=kc, c0=c0, hc=hc: e.matmul(ps_h[:, hc:hc + 2], lhsT=wbf[:, kc, c0:c0 + 128], rhs=mk(hn[:, kc, 0:1], [[513, 2]]), start=(kc == 0), stop=(kc == 7)),
                             reads=WB + HN, writes=[f"ps_hp{fc % 2}"])
                    P.act(lambda e, pr=pr, pp=pp: e.activation(out=pr[:, 1:513], in_=pp[:], func=AF.Copy), reads=[ppt], writes=[prt])
                    P.act(lambda e, pr=pr, hc=hc: e.activation(out=mk(pr[:, 0:1], [[513, 2]]), in_=ps_h[:, hc:hc + 2], func=AF.Copy),
                          reads=[f"ps_hp{fc % 2}"], writes=[prt], tiny=True)
                    P.pool(lambda e, ac=ac, pr=pr, fc=fc: e.tensor_scalar(out=ac[:], in0=pr[:, 0:512], scalar1=cws[:, 0, fc:fc + 1], scalar2=None, op0=ALU.mult),
                           reads=[prt, "cws"], writes=[act_])
                    P.dve(lambda e, ac=ac, pr=pr, fc=fc: e.scalar_tensor_tensor(out=ac[:], in0=pr[:, 1:513], scalar=cws[:, 1, fc:fc + 1], in1=ac[:], op0=ALU.mult, op1=ALU.add),
                           reads=[prt, "cws", act_], writes=[act_])
                    P.dve(lambda e, ac=ac, pr=pr, fc=fc: e.scalar_tensor_tensor(out=ac[:], in0=pr[:, 2:514], scalar=cws[:, 2, fc:fc + 1], in1=ac[:], op0=ALU.mult, op1=ALU.add),
                           reads=[prt, "cws", act_], writes=[act_])
                    sc = (1.0 / ALPHA) if fc < 4 else 1.0
                    P.act(lambda e, ac=ac, sgg=sgg, fc=fc, sc=sc: e.activation(out=sgg[:], in_=ac[:], func=AF.Sigmoid, bias=cb[:, l, fc:fc + 1], scale=sc),
                          reads=[act_, "cb"], writes=[sgt])
                    P.dve(lambda e, ac=ac, sgg=sgg, fc=fc, qo=qo: e.scalar_tensor_tensor(out=qo[:], in0=ac[:], scalar=cbs[:, fc:fc + 1], in1=sgg[:], op0=ALU.add, op1=ALU.mult),
                          reads=[act_, sgt, "cbs"], writes=[qot])
                    dst = (S["QT"] if fc < 4 else S["KT"])[(fc % 4) * 128:(fc % 4 + 1) * 128, t0:t0 + TB]
                    P.dma(lambda e, qo=qo, dst=dst: e.dma_start(out=dst, in_=qo[:]), reads=[qot], writes=["dram_qk"])
                    if fc >= 4:
                        h = fc - 4
                        half = (h % 2) * 512
                        for tt in range(4):
                            P.pe(lambda e, qo=qo, tt=tt, half=half: e.transpose(ps_t[:, half + tt * 128: half + (tt + 1) * 128], qo[:, tt * 128:(tt + 1) * 128], idb[:]),
                                 reads=[qot, "idb"], writes=[f"ps_t{h % 2}"])
                        P.act(lambda e, h=h, half=half: e.activation(out=ktok[:, :, h * 128:(h + 1) * 128], in_=mk(ps_t[:, half:half + 1], [[128, 4], [1, 128]]), func=AF.Copy),
                              reads=[f"ps_t{h % 2}"], writes=["ktok"])
                P.dma(lambda e, t0=t0: e.dma_start(out=S["Ktok"][t0:t0 + TB, :].rearrange("(tt p) c -> p tt c", p=128), in_=ktok[:]),
                      reads=["ktok"], writes=["dram_ktok"])

                ti = 0
                for (c0, kind) in ((1024, "v"), (1536, "o"), (3088, "vn")):
                    for tt in range(4):
                        pt = ps_tok[ti % 2]
                        ptt = f"ps_tok{ti % 2}"
                        ti += 1
                        for kc in range(8):
                            P.pe(lambda e, kc=kc, pt=pt, tt=tt, c0=c0: e.matmul(pt[:], lhsT=hn[:, kc, 1 + tt * 128:1 + (tt + 1) * 128], rhs=wbf[:, kc, c0:c0 + 512], start=(kc == 0), stop=(kc == 7)),
                                 reads=WB + HN, writes=[ptt])
                        if kind == "v":
                            P.act(lambda e, pt=pt, tt=tt: e.activation(out=vtok[:, tt, :], in_=pt[:], func=AF.Copy), reads=[ptt], writes=["vtok"])
                        elif kind == "vn":
                            P.dve(lambda e, pt=pt, tt=tt: e.tensor_copy(out=vntok[:, tt, :], in_=pt[:]), reads=[ptt], writes=["vntok"])
                        else:
                            P.act(lambda e, pt=pt, tt=tt: e.activation(out=osig[:, tt, :], in_=pt[:], func=AF.Sigmoid), reads=[ptt], writes=["osig"])
                P.dma(lambda e, t0=t0: e.dma_start(out=S["Vtok"][t0:t0 + TB, :].rearrange("(tt p) c -> p tt c", p=128), in_=vtok[:]),
                      reads=["vtok"], writes=["dram_vtok"])
                P.dma(lambda e, t0=t0: e.dma_start(out=S["Vn"][t0:t0 + TB, :].rearrange("(tt p) c -> p tt c", p=128), in_=vntok[:]),
                      reads=["vntok"], writes=["dram_vn"])
                P.dma(lambda e, t0=t0: e.dma_start(out=S["Osig"][t0:t0 + TB, :].rearrange("(tt p) c -> p tt c", p=128), in_=osig[:]),
                      reads=["osig"], writes=["dram_osig"])

                for kc in range(8):
                    P.pe(lambda e, kc=kc: e.matmul(ps_g[0:16, :], lhsT=wbf[:, kc, 2048:2064], rhs=hn[:, kc, 1:513], start=(kc == 0), stop=(kc == 7)),
                         reads=WB + HN, writes=["ps_g"])
                P.dve(lambda e: e.tensor_copy(out=gsb[:], in_=ps_g[0:16, :]), reads=["ps_g"], writes=["gsb"])
                P.dma(lambda e, t0=t0: e.dma_start(out=S["G"][:, t0:t0 + TB], in_=gsb[:]), reads=["gsb"], writes=["dram_g"])

                for j in range(8):
                    pp = ps_pre[qi % 2]
                    ppt = f"ps_pre{qi % 2}"
                    qo = qko[qi % 3]
                    qot = f"qko{qi % 3}"
                    qi += 1
                    c0 = 2064 + j * 128
                    for kc in range(8):
                        P.pe(lambda e, kc=kc, pp=pp, c0=c0: e.matmul(pp[:], lhsT=wbf[:, kc, c0:c0 + 128], rhs=hn[:, kc, 1:513], start=(kc == 0), stop=(kc == 7)),
                             reads=WB + HN, writes=[ppt])
                    sc = 0.125 if j < 4 else 1.0
                    P.act(lambda e, pp=pp, qo=qo, sc=sc: e.activation(out=qo[:], in_=pp[:], func=AF.Copy, scale=sc), reads=[ppt], writes=[qot])
                    dst = (S["QnT"] if j < 4 else S["KnT"])[(j % 4) * 128:(j % 4 + 1) * 128, t0:t0 + TB]
                    P.dma(lambda e, qo=qo, dst=dst: e.dma_start(out=dst, in_=qo[:]), reads=[qot], writes=["dram_qkn"])
            P.emit()

    def phase_B(self, l):
        nc, I, S = self.nc, self.I, self.S
        P = Prog(self.ctx)
        with ExitStack() as es:
            def sb(name, shape, dt):
                return es.enter_context(nc.sbuf_tensor(f"L{l}" + name, list(shape), dt))

            def psum(name, shape, dt=F32):
                return es.enter_context(nc.psum_tensor(f"L{l}" + name, list(shape), dt))

            NP = 36
            idf = sb("B_idf", [128, 128], F32)
            idb = sb("B_idb", [128, 128], BF16)
            trif = sb("B_trif", [128, 128], F32)
            trib = sb("B_trib", [128, 128], F32)
            tokser = sb("B_tokser", [128, 4, NCH, 8], F32)
            dcb = sb("B_dcb", [128, 8, NCH], F32)
            mnw = sb("B_mnw", [128, 512], F32)
            epsb = sb("B_epsb", [128, 1], F32)
            es1 = ExitStack()

            def sb1(name, shape, dt):
                return es1.enter_context(nc.sbuf_tensor(f"L{l}" + name, list(shape), dt))

            ser = {n: sb1("B_" + n, [NP, T], F32) for n in ("Ipre", "F", "Bp", "U", "M", "tmp", "W", "EM", "ES")}
            onesr = sb1("B_onesr", [NP, T], F32)
            gb = sb1("B_gb", [NP, DEPTH, 2], F32)
            ngb = sb1("B_ngb", [NP, 1], F32)
            mend = sb1("B_mend", [NP, NCH], F32)
            mprev = sb1("B_mprev", [NP, NCH], F32)
            dcs = sb1("B_dcs", [NP, NCH], F32)
            ps_ser = es1.enter_context(nc.psum_tensor(f"L{l}B_ps_ser", [128, 512], F32))

            for n in ("Ipre", "F"):
                P.dve(lambda e, n=n: e.memset(ser[n][:], 0.0), writes=[n])
            P.pool(lambda e: e.memset(onesr[:], 1.0), writes=["onesr"])
            P.dve(lambda e: e.memset(epsb[:], float(128 * EPS)), writes=["epsb"], tiny=True)
            P.dma(lambda e: e.dma_start(out=gb[:], in_=I["gb"]), writes=["gb"])
            P.dma(lambda e: e.dma_start(out=idf[:], in_=I["ident"]), writes=["idf"])
            P.dma(lambda e: e.dma_start(out=trif[:], in_=I["trif"]), writes=["trif"])
            P.dma(lambda e: e.dma_start(out=trib[:], in_=I["trib"]), writes=["trib"])
            P.dma(lambda e: e.dma_start(out=mnw[:], in_=I["mnw"][:, l, :]), writes=["mnw"])
            P.dve(lambda e: e.tensor_copy(out=idb[:], in_=idf[:]), reads=["idf"], writes=["idb"])
            P.dve(lambda e: e.tensor_scalar(out=mnw[:], in0=mnw[:], scalar1=float(128.0 ** 0.5), scalar2=None, op0=ALU.mult), reads=["mnw"], writes=["mnw"])
            G = S["G"]
            P.dma(lambda e: e.dma_start(out=ser["Ipre"][0:4, :], in_=G[0:4, :]), writes=["Ipre"])
            P.dma(lambda e: e.dma_start(out=ser["F"][0:4, :], in_=G[4:8, :]), writes=["F"])
            P.dma(lambda e: e.dma_start(out=ser["Ipre"][32:36, :], in_=G[8:12, :]), writes=["Ipre"])
            P.dma(lambda e: e.dma_start(out=ser["F"][32:36, :], in_=G[12:16, :]), writes=["F"])
            P.dve(lambda e: e.tensor_scalar(out=ngb[:], in0=gb[:, l, 1:2], scalar1=-1.0, scalar2=None, op0=ALU.mult), reads=["gb"], writes=["ngb"], tiny=True)
            P.dve(lambda e: e.tensor_scalar(out=ser["Ipre"][:], in0=ser["Ipre"][:], scalar1=gb[:, l, 0:1], scalar2=None, op0=ALU.add),
                  reads=["Ipre", "gb"], writes=["Ipre"])
            P.act(lambda e: e.activation(out=ser["tmp"][:], in_=ser["F"][:], func=AF.Exp, bias=ngb[:, 0:1], scale=-1.0), reads=["F", "ngb"], writes=["tmp"])
            P.act(lambda e: e.activation(out=ser["F"][:], in_=ser["tmp"][:], func=AF.Ln, bias=1.0), reads=["tmp"], writes=["F"])

            def rev(ap):
                a = ap.ap
                return bass.AP(ap.tensor, ap.offset + (a[-1][1] - 1) * a[-1][0], [list(p) for p in a[:-1]] + [[-a[-1][0], a[-1][1]]])

            def scan(out, data, op1, wtok, rtok):
                P.dve(lambda e: e.tensor_tensor_scan(out=out[0:4, :], data0=onesr[0:4, :], data1=data[0:4, :], initial=0.0, op0=ALU.mult, op1=op1),
                      reads=[rtok, "onesr"], writes=[wtok])
                P.dve(lambda e: e.tensor_tensor_scan(out=rev(out[32:36, :]), data0=onesr[32:36, :], data1=rev(data[32:36, :]), initial=0.0, op0=ALU.mult, op1=op1),
                      reads=[rtok, "onesr"], writes=[wtok])

            P.dve(lambda e: e.memset(ser["Bp"][:], 0.0), writes=["Bp"])
            P.pool(lambda e: e.memset(ser["M"][:], 0.0), writes=["M"])
            scan(ser["Bp"], ser["F"], ALU.add, "Bp", "F")
            P.dve(lambda e: e.tensor_tensor(out=ser["U"][:], in0=ser["Ipre"][:], in1=ser["Bp"][:], op=ALU.add), reads=["Ipre", "Bp"], writes=["U"])
            scan(ser["M"], ser["U"], ALU.max, "M", "U")
            P.dve(lambda e: e.tensor_tensor(out=ser["tmp"][:], in0=ser["Bp"][:], in1=ser["M"][:], op=ALU.subtract), reads=["Bp", "M"], writes=["tmp"])
            P.act(lambda e: e.activation(out=ser["EM"][:], in_=ser["tmp"][:], func=AF.Exp), reads=["tmp"], writes=["EM"])
            Mv = ser["M"]
            P.dve(lambda e: e.memset(mprev[:], 0.0), writes=["mprev"], tiny=True)
            P.dve(lambda e: e.memset(mend[:], 0.0), writes=["mend"], tiny=True)
            P.dve(lambda e: e.tensor_copy(out=mend[0:4, :], in_=mk(Mv[0:4, 127:128], [[128, NCH]])), reads=["M"], writes=["mend"], tiny=True)
            P.dve(lambda e: e.tensor_copy(out=mend[32:36, :], in_=mk(Mv[32:36, 0:1], [[128, NCH]])), reads=["M"], writes=["mend"], tiny=True)
            P.dve(lambda e: e.tensor_copy(out=mprev[0:4, 1:NCH], in_=mk(Mv[0:4, 127:128], [[128, NCH - 1]])), reads=["M"], writes=["mprev"], tiny=True)
            P.dve(lambda e: e.tensor_copy(out=mprev[32:36, 0:NCH - 1], in_=mk(Mv[32:36, 128:129], [[128, NCH - 1]])), reads=["M"], writes=["mprev"], tiny=True)
            P.dve(lambda e: e.tensor_tensor(out=ser["tmp"][:].rearrange("p (c t) -> p c t", t=128), in0=mk(mprev[:], [[1, NCH], [0, 128]]),
                                            in1=ser["M"][:].rearrange("p (c t) -> p c t", t=128), op=ALU.subtract),
                  reads=["mprev", "M", "EM"], writes=["tmp"])
            P.act(lambda e: e.activation(out=ser["W"][:], in_=ser["tmp"][:], func=AF.Exp), reads=["tmp"], writes=["W"])
            P.dve(lambda e: e.tensor_tensor(out=ser["tmp"][:].rearrange("p (c t) -> p c t", t=128), in0=ser["U"][:].rearrange("p (c t) -> p c t", t=128),
                                            in1=mk(mend[:], [[1, NCH], [0, 128]]), op=ALU.subtract),
                  reads=["mend", "U", "W"], writes=["tmp"])
            P.act(lambda e: e.activation(out=ser["ES"][:], in_=ser["tmp"][:], func=AF.Exp), reads=["tmp"], writes=["ES"])
            P.dve(lambda e: e.tensor_tensor(out=dcs[:], in0=mprev[:], in1=mend[:], op=ALU.subtract), reads=["mprev", "mend"], writes=["dcs"], tiny=True)
            P.act(lambda e: e.activation(out=dcs[:], in_=dcs[:], func=AF.Exp), reads=["dcs"], writes=["dcs"], tiny=True)
            P.dma(lambda e: e.dma_start(out=S["Mser"][0:4, :], in_=ser["M"][0:4, :]), reads=["M"], writes=["dram_M"])
            P.dma(lambda e: e.dma_start(out=S["Mser"][4:8, :], in_=ser["M"][32:36, :]), reads=["M"], writes=["dram_M"])
            P.dma(lambda e: e.dma_start(out=S["Wser"][0:4, :], in_=ser["W"][0:4, :]), reads=["W"], writes=["dram_W"])
            P.dma(lambda e: e.dma_start(out=S["Wser"][4:8, :], in_=ser["W"][32:36, :]), reads=["W"], writes=["dram_W"])
            P.dma(lambda e: e.dma_start(out=S["DCs"][0:4, :], in_=dcs[0:4, :]), reads=["dcs"], writes=["dram_DC"])
            P.dma(lambda e: e.dma_start(out=S["DCs"][4:8, :], in_=dcs[32:36, :]), reads=["dcs"], writes=["dram_DC"])
            P.dma(lambda e: e.dma_start(out=dcb[:], in_=bass.AP(S["DCs"].tensor, S["DCs"].offset, [[0, 128], [NCH, 8], [1, NCH]])),
                  reads=["dram_DC"], writes=["dcb"])
            for si, n in enumerate(("U", "W", "EM", "ES")):
                for c0 in range(0, NCH, 8):
                    for c in range(c0, c0 + 8):
                        P.pe(lambda e, n=n, c=c, c0=c0: e.matmul(ps_ser[:, (c - c0) * 64:(c - c0) * 64 + NP], lhsT=ser[n][:, c * 128:(c + 1) * 128], rhs=idf[0:NP, 0:NP], start=True, stop=True),
                             reads=[n, "idf"], writes=["ps_ser"])
                    P.dve(lambda e, si=si, c0=c0: e.tensor_copy(out=tokser[:, si, c0:c0 + 8, 0:4], in_=mk(ps_ser[:, 0:1], [[64, 8], [1, 4]])),
                          reads=["ps_ser"], writes=["tokser"])
                    P.dve(lambda e, si=si, c0=c0: e.tensor_copy(out=tokser[:, si, c0:c0 + 8, 4:8], in_=mk(ps_ser[:, 32:33], [[64, 8], [1, 4]])),
                          reads=["ps_ser"], writes=["tokser"])
            if self.debug:
                P.dma(lambda e: e.dma_start(out=S["dbgser"], in_=tokser[:]), reads=["tokser"], writes=["dram_dbgser"])

            P.emit()
            es1.close()

            cst = sb("B_cst", [128, 2, NCH, 4 * 129], BF16)
            c32 = sb("B_c32", [128, 2, 4 * 129], F32)
            v1_ = [sb(f"B_v1{i}", [128, 4, 129], BF16) for i in range(4)]
            P = Prog(self.ctx)
            es2 = ExitStack()
            kt_ = [es2.enter_context(nc.sbuf_tensor(f"L{l}B_ktok{i}", [128, 512], BF16)) for i in range(4)]
            vs_ = [es2.enter_context(nc.sbuf_tensor(f"L{l}B_vs{i}", [128, 4, 129], BF16)) for i in range(4)]
            ps_c = [[es2.enter_context(nc.psum_tensor(f"L{l}B_ps_c{d}{hp}", [128, 512], F32)) for hp in range(2)] for d in range(2)]
            P.dve(lambda e: e.memset(c32[:], 0.0), writes=[f"c32_{d}_{h}" for d in range(2) for h in range(4)])
            for i in range(4):
                P.pool(lambda e, i=i: e.memset(v1_[i][:, :, 128:129], 1.0), writes=[f"v1{i}"])
            li = 0
            for step in range(NCH):
                for d in range(2):
                    c = step if d == 0 else NCH - 1 - step
                    P.act(lambda e, d=d, c=c: e.activation(out=cst[:, d, c, :], in_=c32[:, d, :], func=AF.Copy),
                          reads=[f"c32_{d}_{h}" for h in range(4)], writes=[f"cst{d}_{c}"])
                    if step == NCH - 1:
                        continue
                    kt = kt_[li % 4]
                    v1 = v1_[li % 4]
                    vs = vs_[li % 4]
                    ktt = f"ktokb{li % 4}"
                    v1t = f"v1{li % 4}"
                    vst = f"vs{li % 4}"
                    li += 1
                    P.dma(lambda e, kt=kt, c=c: e.dma_start(out=kt[:], in_=S["Ktok"][c * 128:(c + 1) * 128, :]), writes=[ktt])
                    P.dma(lambda e, v1=v1, c=c: e.dma_start(out=v1[:, :, 0:128], in_=S["Vtok"][c * 128:(c + 1) * 128, :].rearrange("p (h x) -> p h x", h=4)), writes=[v1t])
                    P.dve(lambda e, vs=vs, v1=v1, c=c, d=d: e.tensor_tensor(out=vs[:], in0=v1[:], in1=mk(tokser[:, 3, c, d * 4:d * 4 + 1], [[1, 4], [0, 129]]), op=ALU.mult),
                           reads=[v1t, "tokser"], writes=[vst])
                    for hp in range(2):
                        pc = ps_c[d][hp]
                        for hh in range(2):
                            h = hp * 2 + hh
                            P.pe(lambda e, pc=pc, hh=hh, h=h, kt=kt, vs=vs: e.matmul(pc[:, hh * 129:(hh + 1) * 129], lhsT=kt[:, h * 128:(h + 1) * 128], rhs=vs[:, h, :], start=True, stop=True),
                                 reads=[ktt, vst], writes=[f"ps_c{d}{hp}{hh}"])
                        for hh in range(2):
                            h = hp * 2 + hh
                            P.dve(lambda e, pc=pc, hh=hh, h=h, d=d, c=c: e.scalar_tensor_tensor(out=c32[:, d, h * 129:(h + 1) * 129], in0=c32[:, d, h * 129:(h + 1) * 129],
                                                                                     scalar=dcb[:, d * 4 + h, c:c + 1], in1=pc[:, hh * 129:(hh + 1) * 129], op0=ALU.mult, op1=ALU.add),
                                  reads=[f"ps_c{d}{hp}{hh}", "dcb", f"c32_{d}_{h}"], writes=[f"c32_{d}_{h}"])
            P.emit()
            es2.close()

            import os
            if os.environ.get("BSTOP") == "2":
                return
            P = Prog(self.ctx)
            two = range(2)
            qT_ = [sb(f"B_qT{i}", [128, 4, 128], BF16) for i in two]
            kT_ = [sb(f"B_kT{i}", [128, 4, 128], BF16) for i in two]
            mb_ = [sb(f"B_mb{i}", [128, 8, 128], F32) for i in two]
            wb_ = [sb(f"B_wb{i}", [128, 8, 128], F32) for i in two]
            og_ = [sb(f"B_og{i}", [128, 512], F32) for i in two]
            gg_ = [sb(f"B_gg{i}", [128, 512], F32) for i in two]
            sm_ = [sb(f"B_sm{i}", [128, 8, 128], F32) for i in two]
            ee_ = [sb(f"B_ee{i}", [128, 8, 128], F32) for i in two]
            pp_ = [sb(f"B_pp{i}", [128, 8, 128], BF16) for i in two]
            qw_ = [sb(f"B_qw{i}", [128, 8, 128], BF16) for i in two]
            hs_ = [sb(f"B_hs{i}", [128, 8, 128], F32) for i in two]
            hall_ = [sb(f"B_hall{i}", [128, 512], F32) for i in two]
            hsq_ = [sb(f"B_hsq{i}", [128, 512], F32) for i in two]
            dd_ = [sb(f"B_dd{i}", [128, 8], F32) for i in two]
            ss_ = [sb(f"B_ss{i}", [128, 4], F32) for i in two]
            ybf_ = [sb(f"B_ybf{i}", [128, 512], BF16) for i in two]
            yT_ = [sb(f"B_yT{i}", [128, 4, 128], BF16) for i in two]
            ps_s = [psum(f"B_ps_s{i}", [128, 512]) for i in two]
            ps_o = [psum(f"B_ps_o{i}", [128, 1024]) for i in two]
            ps_d = psum("B_ps_d", [128, 512])
            ps_y = psum("B_ps_y", [128, 1024], BF16)
            for i in range(4):
                P.pool(lambda e, i=i: e.memset(v1_[i][:, :, 128:129], 1.0), writes=[f"v1{i}"])
            tri4 = sb("B_tri4", [128, 8, 128], F32)
            for i in range(4):
                P.pool(lambda e, i=i: e.tensor_copy(out=tri4[:, i, :], in_=trif[:]), writes=["tri4"])
                P.pool(lambda e, i=i: e.tensor_copy(out=tri4[:, 4 + i, :], in_=trib[:]), writes=["tri4"])

            def b3_load(c):
                s = c % 2
                tk = c * 128
                P.dma(lambda e: e.dma_start(out=qT_[s][:], in_=S["QT"][:, tk:tk + 128].rearrange("(h p) t -> p h t", p=128)), writes=[f"qT{s}"])
                P.dma(lambda e: e.dma_start(out=kT_[s][:], in_=S["KT"][:, tk:tk + 128].rearrange("(h p) t -> p h t", p=128)), writes=[f"kT{s}"])
                P.dma(lambda e: e.dma_start(out=mb_[s][:], in_=bass.AP(S["Mser"].tensor, S["Mser"].offset + tk, [[0, 128], [T, 8], [1, 128]])), writes=[f"mb{s}"])
                P.dma(lambda e: e.dma_start(out=wb_[s][:], in_=bass.AP(S["Wser"].tensor, S["Wser"].offset + tk, [[0, 128], [T, 8], [1, 128]])), writes=[f"wb{s}"])
                P.dma(lambda e: e.dma_start(out=og_[s][:], in_=S["Osig"][tk:tk + 128, :]), writes=[f"og{s}"])
                P.dma(lambda e: e.dma_start(out=v1_[c % 4][:, :, 0:128], in_=S["Vtok"][tk:tk + 128, :].rearrange("p (h x) -> p h x", h=4)), writes=[f"v1{c % 4}"])

            def b3_stage(c, stage):
                s = c % 2
                tk = c * 128
                qT, kT, mb, wb, og, gg = qT_[s], kT_[s], mb_[s], wb_[s], og_[s], gg_[s]
                sm, ee, pp, qw, hs, hall, hsq, dd, ss, ybf, yT = sm_[s], ee_[s], pp_[s], qw_[s], hs_[s], hall_[s], hsq_[s], dd_[s], ss_[s], ybf_[s], yT_[s]
                ps, po = ps_s[s], ps_o[s]
                v1 = v1_[c % 4]
                v1t = f"v1{c % 4}"
                if stage == 1:
                    for h in range(4):
                        P.pe(lambda e, ps=ps, kT=kT, qT=qT, h=h: e.matmul(ps[:, h * 128:(h + 1) * 128], lhsT=kT[:, h, :], rhs=qT[:, h, :], start=True, stop=True),
                             reads=[f"kT{s}", f"qT{s}"], writes=[f"ps_s{s}"])
                    psv = ps[:].rearrange("p (h t) -> p h t", h=4)
                    P.dve(lambda e, sm=sm, psv=psv: e.tensor_tensor(out=sm[:, 0:4, :], in0=psv, in1=tri4[:, 0:4, :], op=ALU.mult),
                          reads=[f"ps_s{s}", "tri4"], writes=[f"smf{s}"])
                    P.dve(lambda e, sm=sm, psv=psv: e.tensor_tensor(out=sm[:, 4:8, :], in0=psv, in1=tri4[:, 4:8, :], op=ALU.mult),
                          reads=[f"ps_s{s}", "tri4"], writes=[f"smb{s}"])
                    for r in range(8):
                        P.act(lambda e, ee=ee, mb=mb, r=r, c=c: e.activation(out=ee[:, r, :], in_=mb[:, r, :], func=AF.Exp, bias=tokser[:, 0, c, r:r + 1], scale=-1.0),
                              reads=[f"mb{s}", "tokser"], writes=[f"ee{s}"])
                    P.dve(lambda e, pp=pp, ee=ee, sm=sm: e.scalar_tensor_tensor(out=pp[:], in0=ee[:], scalar=1.0, in1=sm[:], op0=ALU.min, op1=ALU.mult),
                          reads=[f"ee{s}", f"smf{s}", f"smb{s}"], writes=[f"pp{s}"])
                    for d in range(2):
                        P.pool(lambda e, qw=qw, qT=qT, wb=wb, d=d: e.tensor_tensor(out=qw[:, d * 4:(d + 1) * 4, :], in0=qT[:], in1=wb[:, d * 4:(d + 1) * 4, :], op=ALU.mult),
                               reads=[f"qT{s}", f"wb{s}"], writes=[f"qw{s}"])
                    P.pool(lambda e, gg=gg, og=og: e.tensor_tensor(out=gg[:], in0=og[:], in1=mnw[:], op=ALU.mult), reads=[f"og{s}", "mnw"], writes=[f"gg{s}"])
                    for r in range(8):
                        d, h = r // 4, r % 4
                        P.pe(lambda e, po=po, qw=qw, r=r, d=d, h=h, c=c: e.matmul(po[:, r * 128:(r + 1) * 128], lhsT=qw[:, r, :], rhs=cst[:, d, c, h * 129:h * 129 + 128], start=True, stop=False),
                             reads=[f"qw{s}", f"cst{d}_{c}"], writes=[f"ps_o{s}"])
                        P.pe(lambda e, po=po, pp=pp, v1=v1, r=r, h=h: e.matmul(po[:, r * 128:(r + 1) * 128], lhsT=pp[:, r, :], rhs=v1[:, h, 0:128], start=False, stop=True),
                             reads=[f"pp{s}", v1t], writes=[f"ps_o{s}"])
                        P.pe(lambda e, qw=qw, r=r, d=d, h=h, c=c, s=s: e.matmul(ps_d[:, s * 8 + r:s * 8 + r + 1], lhsT=qw[:, r, :], rhs=cst[:, d, c, h * 129 + 128:h * 129 + 129], start=True, stop=False),
                             reads=[f"qw{s}", f"cst{d}_{c}"], writes=[f"ps_d{s}"])
                        P.pe(lambda e, pp=pp, v1=v1, r=r, h=h, s=s: e.matmul(ps_d[:, s * 8 + r:s * 8 + r + 1], lhsT=pp[:, r, :], rhs=v1[:, h, 128:129], start=False, stop=True),
                             reads=[f"pp{s}", v1t], writes=[f"ps_d{s}"])
                else:
                    P.act(lambda e, dd=dd, s=s: e.activation(out=dd[:], in_=ps_d[:, s * 8:s * 8 + 8], func=AF.Copy), reads=[f"ps_d{s}"], writes=[f"dd{s}"], tiny=True)
                    P.dve(lambda e, dd=dd: e.scalar_tensor_tensor(out=dd[:], in0=dd[:], scalar=-1.0, in1=dd[:], op0=ALU.mult, op1=ALU.max), reads=[f"dd{s}"], writes=[f"dd{s}"], tiny=True)
                    P.dve(lambda e, dd=dd, c=c: e.tensor_tensor(out=dd[:], in0=dd[:], in1=tokser[:, 2, c, :], op=ALU.max), reads=[f"dd{s}", "tokser"], writes=[f"dd{s}"], tiny=True)
                    P.dve(lambda e, dd=dd: e.reciprocal(out=dd[:], in_=dd[:]), reads=[f"dd{s}"], writes=[f"dd{s}"], tiny=True)
                    P.act(lambda e, hs=hs, po=po: e.activation(out=hs[:], in_=po[:].rearrange("p (r t) -> p r t", r=8), func=AF.Copy), reads=[f"ps_o{s}"], writes=[f"hs{s}"])
                    P.dve(lambda e, hs=hs, dd=dd: e.tensor_tensor(out=hs[:], in0=hs[:], in1=mk(dd[:, 0:1], [[1, 8], [0, 128]]), op=ALU.mult),
                          reads=[f"hs{s}", f"dd{s}"], writes=[f"hs{s}"])
                    P.pool(lambda e, hall=hall, hs=hs: e.tensor_tensor(out=hall[:].rearrange("p (h x) -> p h x", h=4), in0=hs[:, 0:4, :], in1=hs[:, 4:8, :], op=ALU.add),
                           reads=[f"hs{s}"], writes=[f"hall{s}"])
                    P.pool(lambda e, hall=hall, hsq=hsq: e.tensor_tensor(out=hsq[:], in0=hall[:], in1=hall[:], op=ALU.mult), reads=[f"hall{s}"], writes=[f"hsq{s}"])
                    P.dve(lambda e, ss=ss, hsq=hsq: e.tensor_reduce(out=ss[:], in_=hsq[:].rearrange("p (h x) -> p h x", h=4), axis=AX.X, op=ALU.add), reads=[f"hsq{s}"], writes=[f"ss{s}"], tiny=True)
                    P.act(lambda e, ss=ss: e.activation(out=ss[:], in_=ss[:], func=AF.Ln, bias=epsb[:, 0:1]), reads=[f"ss{s}", "epsb"], writes=[f"ss{s}"], tiny=True)
                    P.act(lambda e, ss=ss: e.activation(out=ss[:], in_=ss[:], func=AF.Exp, scale=-0.5), reads=[f"ss{s}"], writes=[f"ss{s}"], tiny=True)
                    P.dve(lambda e, hsq=hsq, hall=hall, ss=ss: e.tensor_tensor(out=hsq[:].rearrange("p (h x) -> p h x", h=4), in0=hall[:].rearrange("p (h x) -> p h x", h=4), in1=mk(ss[:, 0:1], [[1, 4], [0, 128]]), op=ALU.mult),
                          reads=[f"hall{s}", f"ss{s}", f"hsq{s}"], writes=[f"hsq{s}"])
                    P.pool(lambda e, ybf=ybf, hsq=hsq, gg=gg: e.tensor_tensor(out=ybf[:], in0=hsq[:], in1=gg[:], op=ALU.mult), reads=[f"hsq{s}", f"gg{s}"], writes=[f"ybf{s}"])
                    for h in range(4):
                        P.pe(lambda e, h=h, s=s, ybf=ybf: e.transpose(ps_y[:, s * 512 + h * 128: s * 512 + (h + 1) * 128], ybf[:, h * 128:(h + 1) * 128], idb[:]),
                             reads=[f"ybf{s}", "idb"], writes=[f"ps_y{s}"])
                    P.act(lambda e, yT=yT, s=s: e.activation(out=yT[:], in_=ps_y[:, s * 512:(s + 1) * 512].rearrange("p (h t) -> p h t", h=4), func=AF.Copy),
                          reads=[f"ps_y{s}"], writes=[f"yT{s}"])
                    P.dma(lambda e, yT=yT, tk=tk: e.dma_start(out=S["YT"][0:512, tk:tk + 128].rearrange("(h p) t -> p h t", p=128), in_=yT[:]),
                          reads=[f"yT{s}"], writes=["dram_yt"])

            b3_load(0)
            b3_load(1)
            b3_stage(0, 1)
            for c in range(NCH):
                if c + 1 < NCH:
                    b3_stage(c + 1, 1)
                if c + 2 < NCH:
                    b3_load(c + 2)
                b3_stage(c, 2)
            P.emit()

    def phase_C(self, l):
        nc, I, S = self.nc, self.I, self.S
        P = Prog(self.ctx)
        with ExitStack() as es:
            def sb(name, shape, dt):
                return es.enter_context(nc.sbuf_tensor(f"L{l}" + name, list(shape), dt))

            def psum(name, shape, dt=F32):
                return es.enter_context(nc.psum_tensor(f"L{l}" + name, list(shape), dt))

            qn = sb("C_qn", [128, 4, T], BF16)
            kn = sb("C_kn", [128, 4, T], BF16)
            vn = sb("C_vn", [128, 32, 8, 65], BF16)
            nbi = sb("C_nbi", [128, 8, 5, 128], F32)
            nbe_ = [sb(f"C_nbe{i}", [128, 8, 4, 128], F32) for i in range(2)]
            idf = sb("C_idf", [128, 128], F32)
            idb = sb("C_idb", [128, 128], BF16)
            sbb_ = [sb(f"C_sb{i}", [128, 5, 128], F32) for i in range(3)]
            pt_ = [sb(f"C_pt{i}", [128, 5, 128], BF16) for i in range(3)]
            rd = sb("C_rd", [128, 8], F32)
            yn_ = [sb(f"C_yn{i}", [128, 8, 64], BF16) for i in range(2)]
            yT_ = [sb(f"C_yT{i}", [128, 4, 128], BF16) for i in range(2)]
            ps_s = [psum(f"C_ps_s{i}", [128, 1024]) for i in range(3)]
            ps_o = psum("C_ps_o", [128, 512])
            ps_y = psum("C_ps_y", [128, 1024], BF16)
            P.dma(lambda e: e.dma_start(out=idf[:], in_=I["ident"]), writes=["idf"])
            P.dve(lambda e: e.tensor_copy(out=idb[:], in_=idf[:]), reads=["idf"], writes=["idb"])
            P.dma(lambda e: e.dma_start(out=nbi[:], in_=I["nbi"][l]), writes=["nbi"])
            for g in range(4):
                P.dma(lambda e, g=g: e.dma_start(out=qn[:, g, :], in_=S["QnT"][g * 128:(g + 1) * 128, :]), writes=["qn"])
                P.dma(lambda e, g=g: e.dma_start(out=kn[:, g, :], in_=S["KnT"][g * 128:(g + 1) * 128, :]), writes=["kn"])
            P.pool(lambda e: e.memset(vn[:, :, :, 64:65], 1.0), writes=["vn"])
            for m in range(32):
                P.dma(lambda e, m=m: e.dma_start(out=vn[:, m, :, 0:64], in_=S["Vn"][m * 128:(m + 1) * 128, :].rearrange("p (h x) -> p h x", h=8)), writes=["vn"])
            def blk(i):
                if i < 2:
                    return [0, 1, 2, 3], i
                if i >= 30:
                    return [28, 29, 30, 31], i - 28
                return [i - 2, i - 1, i, i + 1, i + 2], None

            import os
            NU = int(os.environ.get("CNU", 32 * 8))

            def st_S(u):
                i, h = u // 8, u % 8
                tiles, eidx = blk(i)
                g, hp = h // 2, (h % 2) * 64
                ps = ps_s[u % 3]
                for j, m in enumerate(tiles):
                    P.pe(lambda e, ps=ps, j=j, m=m, g=g, hp=hp, i=i: e.matmul(ps[:, j * 128:(j + 1) * 128], lhsT=kn[hp:hp + 64, g, m * 128:(m + 1) * 128], rhs=qn[hp:hp + 64, g, i * 128:(i + 1) * 128], start=True, stop=True),
                         reads=["kn", "qn"], writes=[f"ps_s{u % 3}"])

            def st_E(u):
                i, h = u // 8, u % 8
                tiles, eidx = blk(i)
                nt = len(tiles)
                ps = ps_s[u % 3]
                sbb, pt = sbb_[u % 3], pt_[u % 3]
                if eidx is not None and h == 0:
                    P.dma(lambda e, eidx=eidx, i=i: e.dma_start(out=nbe_[i % 2][:], in_=I["nbe"][l, eidx]), writes=[f"nbe{i % 2}"])
                bias = nbi[:, h, :, :] if eidx is None else nbe_[i % 2][:, h, :, :]
                btok = "nbi" if eidx is None else f"nbe{i % 2}"
                P.dve(lambda e, ps=ps, sbb=sbb, bias=bias, nt=nt: e.tensor_tensor(out=sbb[:, 0:nt, :], in0=ps[:, 0:nt * 128].rearrange("p (j x) -> p j x", x=128), in1=bias, op=ALU.add),
                      reads=[f"ps_s{u % 3}", btok], writes=[f"sb{u % 3}"])
                P.act(lambda e, sbb=sbb, pt=pt, nt=nt: e.activation(out=pt[:, 0:nt, :], in_=sbb[:, 0:nt, :], func=AF.Exp), reads=[f"sb{u % 3}"], writes=[f"pt{u % 3}"])

            def st_PV(u):
                i, h = u // 8, u % 8
                tiles, eidx = blk(i)
                nt = len(tiles)
                pt = pt_[u % 3]
                so = (u % 4) * 128
                sot = f"ps_o{u % 4}"
                for j, m in enumerate(tiles):
                    P.pe(lambda e, pt=pt, j=j, m=m, h=h, nt=nt, so=so: e.matmul(ps_o[:, so:so + 65], lhsT=pt[:, j, :], rhs=vn[:, m, h, :], start=(j == 0), stop=(j == nt - 1)),
                         reads=[f"pt{u % 3}", "vn"], writes=[sot])
                P.dve(lambda e, h=h, so=so: e.reciprocal(out=rd[:, h:h + 1], in_=ps_o[:, so + 64:so + 65]), reads=[sot], writes=[f"rd{h}"], tiny=True)
                P.act(lambda e, h=h, so=so, i=i: e.activation(out=yn_[i % 2][:, h, :], in_=ps_o[:, so:so + 64], func=AF.Copy, scale=rd[:, h:h + 1]), reads=[sot, f"rd{h}"], writes=[f"yn{i % 2}"])
                if h == 7:
                    sl = i % 2
                    yT = yT_[sl]
                    for g in range(4):
                        P.pe(lambda e, g=g, sl=sl: e.transpose(ps_y[:, sl * 512 + g * 128:sl * 512 + (g + 1) * 128], yn_[sl][:, 2 * g:2 * g + 2, :].rearrange("p a b -> p (a b)"), idb[:]),
                             reads=[f"yn{sl}", "idb"], writes=[f"ps_y{sl}"])
                    P.dve(lambda e, yT=yT, sl=sl: e.tensor_copy(out=yT[:], in_=ps_y[:, sl * 512:(sl + 1) * 512].rearrange("p (h t) -> p h t", h=4)),
                          reads=[f"ps_y{sl}"], writes=[f"yT{sl}"])
                    P.dma(lambda e, yT=yT, i=i: e.dma_start(out=S["YT"][512:1024, i * 128:(i + 1) * 128].rearrange("(h p) t -> p h t", p=128), in_=yT[:]),
                          reads=[f"yT{sl}"], writes=["dram_yt"])

            CD = int(os.environ.get("CDEPTH", 1))
            if CD == 0:
                for u in range(NU):
                    st_S(u)
                    st_E(u)
                    st_PV(u)
            elif CD == 1:
                st_S(0)
                for u in range(NU):
                    if u + 1 < NU:
                        st_S(u + 1)
                    st_E(u)
                    st_PV(u)
            else:
                st_S(0)
                st_S(1)
                st_E(0)
                for u in range(NU):
                    if u + 2 < NU:
                        st_S(u + 2)
                    if u + 1 < NU:
                        st_E(u + 1)
                    st_PV(u)
            P.emit()

    def phase_D(self, l):
        nc, I, S = self.nc, self.I, self.S
        P = Prog(self.ctx)
        xsrc = I["xT"] if l == 0 else S["xres"]
        self.esE = ExitStack()
        esE = self.esE
        w1 = esE.enter_context(nc.sbuf_tensor(f"L{l}E_w1", [128, 8, DFF], BF16))
        w2 = esE.enter_context(nc.sbuf_tensor(f"L{l}E_w2", [128, 32, D], BF16))
        n2 = esE.enter_context(nc.sbuf_tensor(f"L{l}E_n2", [128, DEPTH, 8], F32))
        self.Ew = (w1, w2, n2)
        with ExitStack() as es:
            def sb(name, shape, dt):
                return es.enter_context(nc.sbuf_tensor(f"L{l}" + name, list(shape), dt))

            def psum(name, shape, dt=F32):
                return es.enter_context(nc.psum_tensor(f"L{l}" + name, list(shape), dt))

            wbf = sb("D_wbf", [128, 8, D], BF16)
            wst = [sb(f"D_wst{i}", [128, 512], F32) for i in range(2)]
            stE = [sb(f"D_stE{i}", [128, 1024], F32) for i in range(2)]
            xin = [sb(f"D_xin{i}", [128, 8, TB], F32) for i in range(2)]
            yin = [sb(f"D_yin{i}", [128, 8, TB], BF16) for i in range(2)]
            pso = [psum(f"D_ps{i}", [128, 512]) for i in range(4)]
            P.dma(lambda e: e.dma_start(out=n2[:], in_=I["n2"]), writes=["n2"])
            wi = 0
            for kc in range(8):
                for hf in range(2):
                    st = wst[wi % 2]
                    stt = f"wst{wi % 2}"
                    P.dma(lambda e, st=st, kc=kc, hf=hf: e.dma_start(out=st[:], in_=I["w_out"][l, kc * 128:(kc + 1) * 128, hf * 512:(hf + 1) * 512]), writes=[stt])
                    if wi % 2 == 0:
                        P.pool(lambda e, st=st, kc=kc, hf=hf: e.tensor_copy(out=wbf[:, kc, hf * 512:(hf + 1) * 512], in_=st[:]), reads=[stt], writes=[f"wbf{kc}_{hf}"])
                    else:
                        P.act(lambda e, st=st, kc=kc, hf=hf: e.activation(out=wbf[:, kc, hf * 512:(hf + 1) * 512], in_=st[:], func=AF.Copy), reads=[stt], writes=[f"wbf{kc}_{hf}"])
                    wi += 1
            WB = [f"wbf{kc}_{hf}" for kc in range(8) for hf in range(2)]

            def prefetch(pc):
                st = stE[pc % 2]
                stt = f"stE{pc % 2}"
                if pc < 32:
                    kc, hf = pc // 4, pc % 4
                    P.dma(lambda e: e.dma_start(out=st[:], in_=I["w_ff1"][l, kc * 128:(kc + 1) * 128, hf * 1024:(hf + 1) * 1024]), writes=[stt])
                    if pc % 2 == 0:
                        P.pool(lambda e: e.tensor_scalar(out=w1[:, kc, hf * 1024:(hf + 1) * 1024], in0=st[:], scalar1=n2[:, l, kc:kc + 1], scalar2=None, op0=ALU.mult),
                               reads=[stt, "n2"], writes=[f"w1_{pc}"])
                    else:
                        P.act(lambda e: e.activation(out=w1[:, kc, hf * 1024:(hf + 1) * 1024], in_=st[:], func=AF.Copy, scale=n2[:, l, kc:kc + 1]),
                              reads=[stt, "n2"], writes=[f"w1_{pc}"])
                else:
                    k2 = pc - 32
                    P.dma(lambda e: e.dma_start(out=st[:], in_=I["w_ff2"][l, k2 * 128:(k2 + 1) * 128, :]), writes=[stt])
                    if pc % 2 == 0:
                        P.pool(lambda e: e.tensor_copy(out=w2[:, k2, :], in_=st[:]), reads=[stt], writes=[f"w2_{k2}"])
                    else:
                        P.act(lambda e: e.activation(out=w2[:, k2, :], in_=st[:], func=AF.Copy), reads=[stt], writes=[f"w2_{k2}"])

            def load(b):
                t0 = b * TB
                P.dma(lambda e, b=b, t0=t0: e.dma_start(out=xin[b % 2][:], in_=xsrc[:, t0:t0 + TB].rearrange("(kc p) t -> p kc t", p=128)), writes=[f"xin{b % 2}"])
                P.dma(lambda e, b=b, t0=t0: e.dma_start(out=yin[b % 2][:], in_=S["YT"][:, t0:t0 + TB].rearrange("(kc p) t -> p kc t", p=128)), writes=[f"yin{b % 2}"])

            load(0)
            pi = 0
            for b in range(NB):
                if b + 1 < NB:
                    load(b + 1)
                for pc in range(8 * b, 8 * b + 8):
                    prefetch(pc)
                t0 = b * TB
                xi, yi = xin[b % 2], yin[b % 2]
                for dc in range(8):
                    ps = pso[pi % 4]
                    pst = f"ps{pi % 4}"
                    pi += 1
                    for kc in range(8):
                        P.pe(lambda e, ps=ps, kc=kc, dc=dc, yi=yi: e.matmul(ps[:], lhsT=wbf[:, kc, dc * 128:(dc + 1) * 128], rhs=yi[:, kc, :], start=(kc == 0), stop=(kc == 7)),
                             reads=WB + [f"yin{b % 2}"], writes=[pst])
                    P.dve(lambda e, ps=ps, dc=dc, xi=xi: e.tensor_tensor(out=xi[:, dc, :], in0=xi[:, dc, :], in1=ps[:], op=ALU.add),
                          reads=[pst, f"xin{b % 2}"], writes=[f"xin{b % 2}"])
                P.dma(lambda e, xi=xi, t0=t0: e.dma_start(out=S["xres"][:, t0:t0 + TB].rearrange("(kc p) t -> p kc t", p=128), in_=xi[:]),
                      reads=[f"xin{b % 2}"], writes=["dram_x"])
            P.emit()

    def phase_E(self, l):
        nc, I, S = self.nc, self.I, self.S
        P = Prog(self.ctx)
        last = (l == DEPTH - 1)
        w1, w2, n2 = self.Ew
        with ExitStack() as es:
            def sb(name, shape, dt):
                return es.enter_context(nc.sbuf_tensor(f"L{l}" + name, list(shape), dt))

            def psum(name, shape, dt=F32):
                return es.enter_context(nc.psum_tensor(f"L{l}" + name, list(shape), dt))

            nf = sb("E_nf", [128, 8], F32)
            ones = sb("E_ones", [128, 128], BF16)
            epsb = sb("E_epsb", [128, 1], F32)
            xin = [sb(f"E_xin{i}", [128, 8, TB], F32) for i in range(2)]
            xsq = sb("E_xsq", [128, 8, TB], BF16)
            rstd = sb("E_rstd", [128, TB], F32)
            hn = sb("E_hn", [128, 8, TB], BF16)
            uu = [sb(f"E_u{i}", [128, 8, TB], BF16) for i in range(2)]
            rtmp = [sb(f"E_rtmp{i}", [128, TB], F32) for i in range(2)]
            ps_ssq = psum("E_ps_ssq", [128, 512])
            ps1 = [psum(f"E_ps1_{i}", [128, 512]) for i in range(3)]
            ps2 = [psum(f"E_ps2_{i}", [128, 512]) for i in range(3)]
            P.dma(lambda e: e.dma_start(out=nf[:], in_=I["nf"]), writes=["nf"])
            P.dve(lambda e: e.memset(ones[:], 1.0), writes=["ones"])
            P.dve(lambda e: e.memset(epsb[:], float(D * EPS)), writes=["epsb"], tiny=True)
            cnt = {"p1": 0, "p2": 0}
            HN = [f"hn{kc}" for kc in range(8)]

            def load(b):
                t0 = b * TB
                P.dma(lambda e, b=b, t0=t0: e.dma_start(out=xin[b % 2][:], in_=S["xres"][:, t0:t0 + TB].rearrange("(kc p) t -> p kc t", p=128)), writes=[f"xin{b % 2}"])

            def rms(xi, xt):
                P.act(lambda e: e.activation(out=xsq[:], in_=xi[:], func=AF.Square), reads=[xt], writes=["xsq"])
                for kc in range(8):
                    P.pe(lambda e, kc=kc: e.matmul(ps_ssq[:], lhsT=ones[:], rhs=xsq[:, kc, :], start=(kc == 0), stop=(kc == 7)), reads=["ones", "xsq"], writes=["ps_ssq"])
                P.act(lambda e: e.activation(out=rstd[:], in_=ps_ssq[:], func=AF.Ln, bias=epsb[:, 0:1]), reads=["ps_ssq", "epsb"], writes=["rstd"])
                P.act(lambda e: e.activation(out=rstd[:], in_=rstd[:], func=AF.Exp, scale=-0.5), reads=["rstd"], writes=["rstd"])

            def prep(b):
                xi = xin[b % 2]
                xt = f"xin{b % 2}"
                rms(xi, xt)
                for kc in range(8):
                    P.dve(lambda e, kc=kc, xi=xi: e.scalar_tensor_tensor(out=hn[:, kc, :], in0=xi[:, kc, :], scalar=32.0, in1=rstd[:], op0=ALU.mult, op1=ALU.mult),
                          reads=[xt, "rstd"], writes=[f"hn{kc}"])

            def S1(b, grp):
                gi = (b * 4 + grp) % 2
                u = uu[gi]
                ut = f"u{gi}"
                for fj in range(8):
                    fc = grp * 8 + fj
                    ps = ps1[cnt["p1"] % 3]
                    pst = f"ps1_{cnt['p1'] % 3}"
                    rt = rtmp[cnt["p1"] % 2]
                    rtt = f"rtmp{cnt['p1'] % 2}"
                    cnt["p1"] += 1
                    for kc in range(8):
                        P.pe(lambda e, ps=ps, kc=kc, fc=fc: e.matmul(ps[:], lhsT=w1[:, kc, fc * 128:(fc + 1) * 128], rhs=hn[:, kc, :], start=(kc == 0), stop=(kc == 7)),
                             reads=HN, writes=[pst])
                    P.act(lambda e, ps=ps, rt=rt: e.activation(out=rt[:], in_=ps[:], func=AF.Relu), reads=[pst], writes=[rtt])
                    eng = P.dve if fj % 2 == 0 else P.pool
                    eng(lambda e, u=u, fj=fj, rt=rt: e.tensor_tensor(out=u[:, fj, :], in0=rt[:], in1=rt[:], op=ALU.mult), reads=[rtt], writes=[f"{ut}_{fj}"])

            def S2(b, grp):
                gi = (b * 4 + grp) % 2
                u = uu[gi]
                ut = f"u{gi}"
                xi = xin[b % 2]
                xt = f"xin{b % 2}"
                UT = [f"{ut}_{fj}" for fj in range(8)]
                for dc in range(8):
                    ps = ps2[cnt["p2"] % 3]
                    pst = f"ps2_{cnt['p2'] % 3}"
                    cnt["p2"] += 1
                    for fj in range(8):
                        P.pe(lambda e, ps=ps, fj=fj, grp=grp, dc=dc, u=u: e.matmul(ps[:], lhsT=w2[:, grp * 8 + fj, dc * 128:(dc + 1) * 128], rhs=u[:, fj, :], start=(fj == 0), stop=(fj == 7)),
                             reads=UT, writes=[pst])
                    P.dve(lambda e, ps=ps, dc=dc, xi=xi: e.tensor_tensor(out=xi[:, dc, :], in0=xi[:, dc, :], in1=ps[:], op=ALU.add),
                          reads=[pst, xt], writes=[xt])

            def fin(b):
                t0 = b * TB
                xi = xin[b % 2]
                xt = f"xin{b % 2}"
                if not last:
                    P.dma(lambda e, xi=xi, t0=t0: e.dma_start(out=S["xres"][:, t0:t0 + TB].rearrange("(kc p) t -> p kc t", p=128), in_=xi[:]),
                          reads=[xt], writes=["dram_x"])
                else:
                    rms(xi, xt)
                    for kc in range(8):
                        P.dve(lambda e, kc=kc, xi=xi: e.scalar_tensor_tensor(out=xi[:, kc, :], in0=xi[:, kc, :], scalar=32.0, in1=rstd[:], op0=ALU.mult, op1=ALU.mult),
                              reads=[xt, "rstd"], writes=[xt])
                        P.pool(lambda e, kc=kc, xi=xi: e.tensor_scalar(out=xi[:, kc, :], in0=xi[:, kc, :], scalar1=nf[:, kc:kc + 1], scalar2=None, op0=ALU.mult),
                               reads=[xt, "nf"], writes=[xt])
                    P.dma(lambda e, xi=xi, t0=t0: e.dma_start(out=self.out[:, t0:t0 + TB].rearrange("(kc p) t -> p kc t", p=128), in_=xi[:]),
                          reads=[xt], writes=["dram_out"])

            load(0)
            prep(0)
            S1(0, 0)
            for b in range(NB):
                if b + 1 < NB:
                    load(b + 1)
                for grp in range(4):
                    if grp < 3:
                        S1(b, grp + 1)
                    elif b + 1 < NB:
                        prep(b + 1)
                        S1(b + 1, 0)
                    S2(b, grp)
                fin(b)
            P.emit()
        self.esE.close()


def _na_bias_tables(rpb_l):
    H = rpb_l.shape[0]
    kl = np.arange(128)
    kr_l, kc = kl // 64, kl % 64
    ql = np.arange(128)
    qr_l, qc = ql // 64, ql % 64

    def tile(i, m):
        kr = 2 * m + kr_l[:, None]
        qr = 2 * i + qr_l[None, :]
        rs = np.clip(qr - 4, 0, 56)
        vr = (kr >= rs) & (kr < rs + 8)
        cs = np.clip(qc[None, :] - 8, 0, 48)
        vc = (kc[:, None] >= cs) & (kc[:, None] < cs + 16)
        ri = np.clip(kr - qr + 7, 0, 14)
        ci = np.clip(kc[:, None] - qc[None, :] + 15, 0, 30)
        vals = rpb_l[:, ri, ci]
        return np.where((vr & vc)[None], vals, np.float32(-30000.0)).astype(np.float32)

    nbi = np.stack([tile(2, m) for m in range(5)], axis=1)
    nbi = np.ascontiguousarray(nbi.transpose(2, 0, 1, 3))
    nbe = []
    for i, ms in ((0, range(4)), (1, range(4)), (30, range(28, 32)), (31, range(28, 32))):
        t = np.stack([tile(i, m) for m in ms], axis=1)
        nbe.append(t.transpose(2, 0, 1, 3))
    nbe = np.ascontiguousarray(np.stack(nbe, axis=0))
    return nbi, nbe


def _prep_shared(norm1_w, conv_w, conv_b, gate_b, mlstm_norm_w, rpb, norm2_w, final_norm_w):
    f = np.float32
    sh = {}
    sh["n1"] = np.ascontiguousarray(norm1_w.reshape(DEPTH, 8, 128).transpose(2, 0, 1)).astype(f)
    sh["n2"] = np.ascontiguousarray(norm2_w.reshape(DEPTH, 8, 128).transpose(2, 0, 1)).astype(f)
    sh["nf"] = np.ascontiguousarray(final_norm_w.reshape(8, 128).T).astype(f)
    sh["cw"] = np.ascontiguousarray(conv_w.reshape(DEPTH, 3, 8, 128).transpose(3, 0, 1, 2)).astype(f)
    sh["cb"] = np.ascontiguousarray(conv_b.reshape(DEPTH, 8, 128).transpose(2, 0, 1)).astype(f)
    gb = np.zeros((36, DEPTH, 2), f)
    g4 = gate_b.reshape(DEPTH, 4, 4)
    gb[0:4, :, 0] = g4[:, 0, :].T
    gb[0:4, :, 1] = g4[:, 1, :].T
    gb[32:36, :, 0] = g4[:, 2, :].T
    gb[32:36, :, 1] = g4[:, 3, :].T
    sh["gb"] = gb
    sh["mnw"] = np.ascontiguousarray(np.broadcast_to(mlstm_norm_w[None], (128, DEPTH, 512))).astype(f)
    nbi, nbe = zip(*[_na_bias_tables(np.asarray(rpb[l], f)) for l in range(DEPTH)])
    sh["nbi"] = np.stack(nbi, 0)
    sh["nbe"] = np.stack(nbe, 0)
    sh["ident"] = np.eye(128, dtype=f)
    sh["trif"] = np.triu(np.ones((128, 128), f))
    sh["trib"] = np.tril(np.ones((128, 128), f))
    return sh


_CACHE = {}


def _get_nc(debug=False, stop_after=None):
    key = (debug, stop_after)
    if key not in _CACHE:
        b = Builder(debug=debug, stop_after=stop_after)
        nc = b.build()
        _CACHE[key] = (nc, b)
    return _CACHE[key]


def run(inputs, debug=False, stop_after=None, cores=8):
    x = np.asarray(inputs["x"], np.float32)
    sh = _prep_shared(*[np.asarray(inputs[k], np.float32) for k in
                        ("norm1_w", "conv_w", "conv_b", "gate_b", "mlstm_norm_w", "rpb", "norm2_w", "final_norm_w")])
    for k in ("w_in", "w_out", "w_ff1", "w_ff2"):
        sh[k] = np.ascontiguousarray(np.asarray(inputs[k], np.float32))
    nc, b = _get_nc(debug, stop_after)
    in_maps = []
    for c in range(cores):
        m = dict(sh)
        m["xT"] = np.ascontiguousarray(x[c].T)
        in_maps.append(m)
    res = run_bass_kernel_spmd(nc, in_maps, core_ids=list(range(cores)))
    return res, b


def kernel(x, norm1_w, w_in, conv_w, conv_b, gate_b, mlstm_norm_w, rpb, w_out,
           norm2_w, w_ff1, w_ff2, final_norm_w):
    inputs = dict(x=x, norm1_w=norm1_w, w_in=w_in, conv_w=conv_w, conv_b=conv_b, gate_b=gate_b,
                  mlstm_norm_w=mlstm_norm_w, rpb=rpb, w_out=w_out, norm2_w=norm2_w, w_ff1=w_ff1,
                  w_ff2=w_ff2, final_norm_w=final_norm_w)
    res, _ = run(inputs)
    out = np.stack([np.ascontiguousarray(r["outT"].T) for r in res.results], axis=0)
    return out.astype(np.float32)
```

```python
import numpy as np
from contextlib import ExitStack
import concourse.bass as bass
import concourse.mybir as mybir
from concourse.bass_utils import run_bass_kernel_spmd

F32 = mybir.dt.float32
BF16 = mybir.dt.bfloat16
AF = mybir.ActivationFunctionType
ALU = mybir.AluOpType
AX = mybir.AxisListType

T = 4096
D = 1024
DEPTH = 2
DIN = 3600
DFF = 4096
NB = 8
TB = 512
NCH = 32
EPS = 1e-6
ALPHA = 128.0 ** -0.5

ENGS = ("pe", "act", "dve", "pool", "sp")
EPOCH = 4000
NDMASEM = 20
SAME_SYNC = True


class Op:
    __slots__ = ("eng", "fn", "deps", "dma", "idx", "sig", "semk", "val", "tiny")

    def __init__(self, eng, fn, dma, tiny=False):
        self.eng = eng
        self.fn = fn
        self.dma = dma
        self.tiny = tiny
        self.deps = set()
        self.sig = False
        self.semk = None
        self.val = 0


class Ctx:
    def __init__(self, nc):
        self.nc = nc
        self.sems = {}
        self.ccount = {e: 0 for e in ENGS}
        self.dval = {}
        self.dk = {e: 0 for e in ENGS}
        self.waited = {e: {} for e in ENGS}
        self.nops = 0
        self.nwaits = 0

    def sem(self, k):
        if k not in self.sems:
            self.sems[k] = self.nc.alloc_semaphore("s_" + "_".join(str(x) for x in k))
        return self.sems[k]


class Prog:
    def __init__(self, ctx):
        self.ctx = ctx
        self.nc = ctx.nc
        self.ops = []
        self.last_w = {}
        self.readers = {}

    def op(self, eng, fn, reads=(), writes=(), dma=False, tiny=False):
        o = Op(eng, fn, dma, tiny)
        o.idx = len(self.ops)
        for r in reads:
            w = self.last_w.get(r)
            if w is not None:
                o.deps.add(w)
        for w_ in writes:
            w = self.last_w.get(w_)
            if w is not None:
                o.deps.add(w)
            for rd in self.readers.get(w_, ()):
                o.deps.add(rd)
        o.deps.discard(o.idx)
        for r in reads:
            self.readers.setdefault(r, []).append(o.idx)
        for w_ in writes:
            self.last_w[w_] = o.idx
            self.readers[w_] = []
        self.ops.append(o)
        return o

    def pe(self, fn, reads=(), writes=()):
        return self.op("pe", fn, reads, writes)

    def act(self, fn, reads=(), writes=(), tiny=False):
        return self.op("act", fn, reads, writes, tiny=tiny)

    def dve(self, fn, reads=(), writes=(), tiny=False):
        return self.op("dve", fn, reads, writes, tiny=tiny)

    def pool(self, fn, reads=(), writes=(), tiny=False):
        return self.op("pool", fn, reads, writes, tiny=tiny)

    def dma(self, fn, reads=(), writes=(), q="sp"):
        return self.op(q, fn, reads, writes, dma=True)

    def _unsynced(self, od, o):
        return (od.eng == o.eng and not od.dma and not o.dma
                and (od.eng == "pe" or not SAME_SYNC or not od.tiny))

    def emit(self):
        ctx = self.ctx
        nc = self.nc
        ops = self.ops
        per = {e: [o for o in ops if o.eng == e] for e in ENGS}
        for o in ops:
            for d in o.deps:
                od = ops[d]
                if od.dma or self._unsynced(od, o):
                    continue
                od.sig = True
        for e in ENGS:
            for o in reversed(per[e]):
                if not o.dma:
                    o.sig = True
                    break
        final = {}
        for e in ENGS:
            for o in per[e]:
                if o.dma:
                    i = ctx.dk[e] % NDMASEM
                    ctx.dk[e] += 1
                    k = ("d", e, i)
                    ctx.dval[k] = ctx.dval.get(k, 0) + 16
                    o.semk = k
                    o.val = ctx.dval[k]
                    final[k] = o.val
                elif o.sig:
                    n = ctx.ccount[e]
                    ctx.ccount[e] += 1
                    k = ("c", e, n // EPOCH)
                    o.semk = k
                    o.val = n % EPOCH + 1
                    final[k] = o.val
        for k in final:
            ctx.sem(k)
        ctx.nops += len(ops)

        def run_engine(e, eng):
            waited = ctx.waited[e]

            def wait(k, v):
                if waited.get(k, 0) >= v:
                    return
                waited[k] = v
                eng.wait_ge(ctx.sems[k], v)
                ctx.nwaits += 1

            for o in per[e]:
                need = {}
                for d in o.deps:
                    od = ops[d]
                    if od.semk is None or self._unsynced(od, o):
                        continue
                    if need.get(od.semk, 0) < od.val:
                        need[od.semk] = od.val
                if o.dma and o.val > 16:
                    if need.get(o.semk, 0) < o.val - 16:
                        need[o.semk] = o.val - 16
                for k, v in need.items():
                    wait(k, v)
                ins = o.fn(eng)
                if o.dma:
                    ins.then_inc(ctx.sems[o.semk], 16)
                elif o.sig:
                    ins.then_inc(ctx.sems[o.semk], 1)
            for k, v in final.items():
                if k[0] == "c" and k[1] == e:
                    if e == "pe" or not SAME_SYNC:
                        pass
                wait(k, v)

        with nc.Block() as block:
            block.tensor(lambda eng: run_engine("pe", eng))
            block.scalar(lambda eng: run_engine("act", eng))
            block.vector(lambda eng: run_engine("dve", eng))
            block.gpsimd(lambda eng: run_engine("pool", eng))
            block.sync(lambda eng: run_engine("sp", eng))


def mk(ap, dims, off=0):
    a = ap.ap
    return bass.AP(ap.tensor, ap.offset + off, [list(a[0])] + [list(d) for d in dims])


class Builder:
    def __init__(self, debug=False, stop_after=None):
        self.debug = debug
        self.stop_after = stop_after
        self.nc = bass.Bass("TRN2", target_bir_lowering=False)
        self.ctx = Ctx(self.nc)
        self.dbg_names = []

    def din(self, name, shape, dt=F32):
        return self.nc.dram_tensor(name, list(shape), dt, kind="ExternalInput").ap()

    def dscr(self, name, shape, dt):
        kind = "ExternalOutput" if self.debug else "Internal"
        if self.debug:
            self.dbg_names.append(name)
        return self.nc.dram_tensor(name, list(shape), dt, kind=kind).ap()

    def build(self):
        nc = self.nc
        I = {}
        I["xT"] = self.din("xT", [D, T])
        I["w_in"] = self.din("w_in", [DEPTH, D, DIN])
        I["w_out"] = self.din("w_out", [DEPTH, D, D])
        I["w_ff1"] = self.din("w_ff1", [DEPTH, D, DFF])
        I["w_ff2"] = self.din("w_ff2", [DEPTH, DFF, D])
        I["n1"] = self.din("n1", [128, DEPTH, 8])
        I["n2"] = self.din("n2", [128, DEPTH, 8])
        I["nf"] = self.din("nf", [128, 8])
        I["cw"] = self.din("cw", [128, DEPTH, 3, 8])
        I["cb"] = self.din("cb", [128, DEPTH, 8])
        I["gb"] = self.din("gb", [36, DEPTH, 2])
        I["mnw"] = self.din("mnw", [128, DEPTH, 512])
        I["nbi"] = self.din("nbi", [DEPTH, 128, 8, 5, 128])
        I["nbe"] = self.din("nbe", [DEPTH, 4, 128, 8, 4, 128])
        I["ident"] = self.din("ident", [128, 128])
        I["trif"] = self.din("trif", [128, 128])
        I["trib"] = self.din("trib", [128, 128])
        self.I = I
        self.out = nc.dram_tensor("outT", [D, T], F32, kind="ExternalOutput").ap()
        S = {}
        S["QT"] = self.dscr("QT", [512, T], BF16)
        S["KT"] = self.dscr("KT", [512, T], BF16)
        S["Ktok"] = self.dscr("Ktok", [T, 512], BF16)
        S["Vtok"] = self.dscr("Vtok", [T, 512], BF16)
        S["Osig"] = self.dscr("Osig", [T, 512], F32)
        S["G"] = self.dscr("G", [16, T], F32)
        S["QnT"] = self.dscr("QnT", [512, T], BF16)
        S["KnT"] = self.dscr("KnT", [512, T], BF16)
        S["Vn"] = self.dscr("Vn", [T, 512], BF16)
        S["YT"] = self.dscr("YT", [D, T], BF16)
        S["Mser"] = self.dscr("Mser", [8, T], F32)
        S["Wser"] = self.dscr("Wser", [8, T], F32)
        S["DCs"] = self.dscr("DCs", [8, NCH], F32)
        S["xres"] = self.dscr("xres", [D, T], F32)
        if self.debug:
            S["dbgser"] = self.dscr("dbgser", [128, 4, NCH, 8], F32)
        self.S = S
        phases = []
        for l in range(DEPTH):
            phases += [("A", l), ("B", l), ("C", l), ("D", l), ("E", l)]
        for i, (ph, l) in enumerate(phases):
            getattr(self, "phase_" + ph)(l)
            if self.stop_after is not None and i + 1 >= self.stop_after:
                break
        return nc

    def phase_A(self, l):
        nc, I, S = self.nc, self.I, self.S
        P = Prog(self.ctx)
        xsrc = I["xT"] if l == 0 else S["xres"]
        with ExitStack() as es:
            def sb(name, shape, dt):
                return es.enter_context(nc.sbuf_tensor(f"L{l}" + name, list(shape), dt))

            def psum(name, shape, dt=F32):
                return es.enter_context(nc.psum_tensor(f"L{l}" + name, list(shape), dt))

            wbf = sb("A_wbf", [128, 8, DIN], BF16)
            wst = [sb(f"A_wst{i}", [128, DIN], F32) for i in range(2)]
            n1 = sb("A_n1", [128, DEPTH, 8], F32)
            cw = sb("A_cw", [128, DEPTH, 3, 8], F32)
            cb = sb("A_cb", [128, DEPTH, 8], F32)
            cws = sb("A_cws", [128, 3, 8], F32)
            cbs = sb("A_cbs", [128, 8], F32)
            idf = sb("A_idf", [128, 128], F32)
            idb = sb("A_idb", [128, 128], BF16)
            ones = sb("A_ones", [128, 128], BF16)
            epsb = sb("A_epsb", [128, 1], F32)
            xin = [sb(f"A_xin{i}", [128, 8, 514], F32) for i in range(2)]
            xsq = sb("A_xsq", [128, 8, 514], BF16)
            rstd = sb("A_rstd", [128, 514], F32)
            hn_ = [sb(f"A_hn{i}", [128, 8, 514], BF16) for i in range(2)]
            pre = [sb(f"A_pre{i}", [128, 514], F32) for i in range(2)]
            acc = [sb(f"A_acc{i}", [128, 512], F32) for i in range(2)]
            sg = [sb(f"A_sg{i}", [128, 512], F32) for i in range(2)]
            qko = [sb(f"A_qko{i}", [128, 512], BF16) for i in range(3)]
            ktok = sb("A_ktok", [128, 4, 512], BF16)
            vtok = sb("A_vtok", [128, 4, 512], BF16)
            vntok = sb("A_vntok", [128, 4, 512], BF16)
            osig = sb("A_osig", [128, 4, 512], F32)
            gsb = sb("A_gsb", [16, 512], F32)
            ps_ssq = psum("A_ps_ssq", [128, 512])
            ps_h = psum("A_ps_h", [128, 512])
            ps_pre = [psum(f"A_ps_pre{i}", [128, 512]) for i in range(2)]
            ps_tok = [psum(f"A_ps_tok{i}", [128, 512]) for i in range(2)]
            ps_g = psum("A_ps_g", [128, 512])
            ps_t = psum("A_ps_t", [128, 1024], BF16)

            P.dma(lambda e: e.dma_start(out=n1[:], in_=I["n1"]), writes=["n1"])
            P.dma(lambda e: e.dma_start(out=cw[:], in_=I["cw"]), writes=["cw"])
            P.dma(lambda e: e.dma_start(out=cb[:], in_=I["cb"]), writes=["cb"])
            P.dma(lambda e: e.dma_start(out=idf[:], in_=I["ident"]), writes=["idf"])
            P.dve(lambda e: e.tensor_copy(out=idb[:], in_=idf[:]), reads=["idf"], writes=["idb"])
            P.dve(lambda e: e.memset(ones[:], 1.0), writes=["ones"])
            P.dve(lambda e: e.memset(epsb[:], float(D * EPS)), writes=["epsb"], tiny=True)
            P.dve(lambda e: e.tensor_scalar(out=cws[:, :, 0:4], in0=cw[:, l, :, 0:4], scalar1=ALPHA, scalar2=None, op0=ALU.mult),
                  reads=["cw"], writes=["cws"], tiny=True)
            P.dve(lambda e: e.tensor_copy(out=cws[:, :, 4:8], in_=cw[:, l, :, 4:8]), reads=["cw"], writes=["cws"], tiny=True)
            P.dve(lambda e: e.tensor_scalar(out=cbs[:, 0:4], in0=cb[:, l, 0:4], scalar1=ALPHA, scalar2=None, op0=ALU.mult),
                  reads=["cb"], writes=["cbs"], tiny=True)
            P.dve(lambda e: e.tensor_copy(out=cbs[:, 4:8], in_=cb[:, l, 4:8]), reads=["cb"], writes=["cbs"], tiny=True)
            for kc in range(8):
                st = wst[kc % 2]
                P.dma(lambda e, st=st, kc=kc: e.dma_start(out=st[:], in_=I["w_in"][l, kc * 128:(kc + 1) * 128, :]),
                      writes=[f"wst{kc % 2}"])
                half = DIN // 2
                P.pool(lambda e, st=st, kc=kc: e.tensor_scalar(out=wbf[:, kc, 0:half], in0=st[:, 0:half], scalar1=n1[:, l, kc:kc + 1], scalar2=None, op0=ALU.mult),
                       reads=[f"wst{kc % 2}", "n1"], writes=[f"wbf{kc}a"])
                P.dve(lambda e, st=st, kc=kc: e.tensor_scalar(out=wbf[:, kc, half:DIN], in0=st[:, half:DIN], scalar1=n1[:, l, kc:kc + 1], scalar2=None, op0=ALU.mult),
                      reads=[f"wst{kc % 2}", "n1"], writes=[f"wbf{kc}b"])
            WB = [f"wbf{kc}{h}" for kc in range(8) for h in "ab"]

            def load_x(b):
                xi = xin[b % 2]
                t0 = b * TB
                lo = t0 - 1
                hi = t0 + TB + 1
                c0 = 0
                tok = f"xin{b % 2}"
                if b == 0:
                    lo = 0
                    c0 = 1
                    P.dve(lambda e, xi=xi: e.memset(xi[:, :, 0:1], 0.0), writes=[tok], tiny=True)
                if b == NB - 1:
                    hi = T
                    P.dve(lambda e, xi=xi: e.memset(xi[:, :, 513:514], 0.0), writes=[tok], tiny=True)
                n = hi - lo
                src = xsrc[:, lo:hi].rearrange("(kc p) t -> p kc t", p=128)
                P.dma(lambda e, xi=xi, src=src, c0=c0, n=n: e.dma_start(out=xi[:, :, c0:c0 + n], in_=src), writes=[tok])

            def prep(b):
                xi = xin[b % 2]
                xt = f"xin{b % 2}"
                hn = hn_[b % 2]
                P.act(lambda e, xi=xi: e.activation(out=xsq[:], in_=xi[:], func=AF.Square), reads=[xt], writes=["xsq"])
                for kc in range(8):
                    P.pe(lambda e, kc=kc: e.matmul(ps_ssq[:], lhsT=ones[:], rhs=xsq[:, kc, 1:513], start=(kc == 0), stop=(kc == 7)),
                         reads=["ones", "xsq"], writes=["ps_ssq"])
                for kc in range(8):
                    P.pe(lambda e, kc=kc: e.matmul(ps_h[:, 0:2], lhsT=ones[:], rhs=mk(xsq[:, kc, 0:1], [[513, 2]]), start=(kc == 0), stop=(kc == 7)),
                         reads=["ones", "xsq"], writes=["ps_hs"])
                P.act(lambda e: e.activation(out=rstd[:, 1:513], in_=ps_ssq[:], func=AF.Ln, bias=epsb[:, 0:1]), reads=["ps_ssq", "epsb"], writes=["rstd"])
                P.act(lambda e: e.activation(out=mk(rstd[:, 0:1], [[513, 2]]), in_=ps_h[:, 0:2], func=AF.Ln, bias=epsb[:, 0:1]), reads=["ps_hs", "epsb"], writes=["rstd"], tiny=True)
                P.act(lambda e: e.activation(out=rstd[:], in_=rstd[:], func=AF.Exp, scale=-0.5), reads=["rstd"], writes=["rstd"])
                for kc in range(8):
                    P.dve(lambda e, kc=kc, xi=xi, hn=hn: e.scalar_tensor_tensor(out=hn[:, kc, :], in0=xi[:, kc, :], scalar=32.0, in1=rstd[:], op0=ALU.mult, op1=ALU.mult),
                          reads=[xt, "rstd"], writes=[f"hn{b % 2}_{kc}"])

            load_x(0)
            load_x(1)
            prep(0)
            qi = 0
            pending = []
            def main(b):
                nonlocal qi
                hn = hn_[b % 2]
                t0 = b * TB
                HN = [f"hn{b % 2}_{kc}" for kc in range(8)]

                for fc in range(8):
                    pp = ps_pre[qi % 2]
                    ppt = f"ps_pre{qi % 2}"
                    pr = pre[qi % 2]
                    prt = f"pre{qi % 2}"
                    ac = acc[qi % 2]
                    act_ = f"acc{qi % 2}"
                    sgg = sg[qi % 2]
                    sgt = f"sg{qi % 2}"
                    qo = qko[qi % 3]
                    qot = f"qko{qi % 3}"
                    qi += 1
                    c0 = fc * 128
                    for kc in range(8):
                        P.pe(lambda e, kc=kc, pp=pp, c0=c0: e.matmul(pp[:], lhsT=wbf[:, kc, c0:c0 + 128], rhs=hn[:, kc, 1:513], start=(kc == 0), stop=(kc == 7)),
                             reads=WB + HN, writes=[ppt])
                    hc = 8 + 2 * (fc % 2)
                    for kc in range(8):
                        P.pe(lambda e, kc=kc, c0=c0, hc=hc: e.matmul(ps_h[:, hc:hc + 2], lhsT=wbf[:, kc, c0:c0 + 128], rhs=mk(hn[:, kc, 0:1], [[513, 2]]), start=(kc == 0), stop=(kc == 7)),
                             reads=WB + HN, writes=[f"ps_hp{fc % 2}"])
                    P.act(lambda e, pr=pr, pp=pp: e.activation(out=pr[:, 1:513], in_=pp[:], func=AF.Copy), reads=[ppt], writes=[prt])
                    P.act(lambda e, pr=pr, hc=hc: e.activation(out=mk(pr[:, 0:1], [[513, 2]]), in_=ps_h[:, hc:hc + 2], func=AF.Copy),
                          reads=[f"ps_hp{fc % 2}"], writes=[prt], tiny=True)
                    def post(fc=fc, pr=pr, prt=prt, ac=ac, act_=act_, sgg=sgg, sgt=sgt, qo=qo, qot=qot, t0=t0):
                        P.pool(lambda e, ac=ac, pr=pr, fc=fc: e.tensor_scalar(out=ac[:], in0=pr[:, 0:512], scalar1=cws[:, 0, fc:fc + 1], scalar2=None, op0=ALU.mult),
                               reads=[prt, "cws"], writes=[act_])
                        P.dve(lambda e, ac=ac, pr=pr, fc=fc: e.scalar_tensor_tensor(out=ac[:], in0=pr[:, 1:513], scalar=cws[:, 1, fc:fc + 1], in1=ac[:], op0=ALU.mult, op1=ALU.add),
                               reads=[prt, "cws", act_], writes=[act_])
                        P.dve(lambda e, ac=ac, pr=pr, fc=fc: e.scalar_tensor_tensor(out=ac[:], in0=pr[:, 2:514], scalar=cws[:, 2, fc:fc + 1], in1=ac[:], op0=ALU.mult, op1=ALU.add),
                               reads=[prt, "cws", act_], writes=[act_])
                        sc = (1.0 / ALPHA) if fc < 4 else 1.0
                        P.act(lambda e, ac=ac, sgg=sgg, fc=fc, sc=sc: e.activation(out=sgg[:], in_=ac[:], func=AF.Sigmoid, bias=cb[:, l, fc:fc + 1], scale=sc),
                              reads=[act_, "cb"], writes=[sgt])
                        P.dve(lambda e, ac=ac, sgg=sgg, fc=fc, qo=qo: e.scalar_tensor_tensor(out=qo[:], in0=ac[:], scalar=cbs[:, fc:fc + 1], in1=sgg[:], op0=ALU.add, op1=ALU.mult),
                              reads=[act_, sgt, "cbs"], writes=[qot])
                        dst = (S["QT"] if fc < 4 else S["KT"])[(fc % 4) * 128:(fc % 4 + 1) * 128, t0:t0 + TB]
                        P.dma(lambda e, qo=qo, dst=dst: e.dma_start(out=dst, in_=qo[:]), reads=[qot], writes=["dram_qk"])
                        if fc >= 4:
                            h = fc - 4
                            half = (h % 2) * 512
                            for tt in range(4):
                                P.pe(lambda e, qo=qo, tt=tt, half=half: e.transpose(ps_t[:, half + tt * 128: half + (tt + 1) * 128], qo[:, tt * 128:(tt + 1) * 128], idb[:]),
                                     reads=[qot, "idb"], writes=[f"ps_t{h % 2}"])
                            P.act(lambda e, h=h, half=half: e.activation(out=ktok[:, :, h * 128:(h + 1) * 128], in_=mk(ps_t[:, half:half + 1], [[128, 4], [1, 128]]), func=AF.Copy),
                                  reads=[f"ps_t{h % 2}"], writes=["ktok"])
                    if pending:
                        pending.pop()()
                    pending.append(post)

                ti = 0
                for (c0, kind) in ((1024, "v"), (1536, "o"), (3088, "vn")):
                    for tt in range(4):
                        pt = ps_tok[ti % 2]
                        ptt = f"ps_tok{ti % 2}"
                        ti += 1
                        for kc in range(8):
                            P.pe(lambda e, kc=kc, pt=pt, tt=tt, c0=c0: e.matmul(pt[:], lhsT=hn[:, kc, 1 + tt * 128:1 + (tt + 1) * 128], rhs=wbf[:, kc, c0:c0 + 512], start=(kc == 0), stop=(kc == 7)),
                                 reads=WB + HN, writes=[ptt])
                        if pending:
                            pending.pop()()
                            P.dma(lambda e, t0=t0: e.dma_start(out=S["Ktok"][t0:t0 + TB, :].rearrange("(tt p) c -> p tt c", p=128), in_=ktok[:]),
                                  reads=["ktok"], writes=["dram_ktok"])
                            if b + 1 < NB:
                                prep(b + 1)
                            if b + 2 < NB:
                                load_x(b + 2)
                        if kind == "v":
                            P.act(lambda e, pt=pt, tt=tt: e.activation(out=vtok[:, tt, :], in_=pt[:], func=AF.Copy), reads=[ptt], writes=["vtok"])
                        elif kind == "vn":
                            P.dve(lambda e, pt=pt, tt=tt: e.tensor_copy(out=vntok[:, tt, :], in_=pt[:]), reads=[ptt], writes=["vntok"])
                        else:
                            P.act(lambda e, pt=pt, tt=tt: e.activation(out=osig[:, tt, :], in_=pt[:], func=AF.Sigmoid), reads=[ptt], writes=["osig"])
                P.dma(lambda e, t0=t0: e.dma_start(out=S["Vtok"][t0:t0 + TB, :].rearrange("(tt p) c -> p tt c", p=128), in_=vtok[:]),
                      reads=["vtok"], writes=["dram_vtok"])
                P.dma(lambda e, t0=t0: e.dma_start(out=S["Vn"][t0:t0 + TB, :].rearrange("(tt p) c -> p tt c", p=128), in_=vntok[:]),
                      reads=["vntok"], writes=["dram_vn"])
                P.dma(lambda e, t0=t0: e.dma_start(out=S["Osig"][t0:t0 + TB, :].rearrange("(tt p) c -> p tt c", p=128), in_=osig[:]),
                      reads=["osig"], writes=["dram_osig"])

                for kc in range(8):
                    P.pe(lambda e, kc=kc: e.matmul(ps_g[0:16, :], lhsT=wbf[:, kc, 2048:2064], rhs=hn[:, kc, 1:513], start=(kc == 0), stop=(kc == 7)),
                         reads=WB + HN, writes=["ps_g"])
                P.dve(lambda e: e.tensor_copy(out=gsb[:], in_=ps_g[0:16, :]), reads=["ps_g"], writes=["gsb"])
                P.dma(lambda e, t0=t0: e.dma_start(out=S["G"][:, t0:t0 + TB], in_=gsb[:]), reads=["gsb"], writes=["dram_g"])

                for j in range(8):
                    pp = ps_pre[qi % 2]
                    ppt = f"ps_pre{qi % 2}"
                    qo = qko[qi % 3]
                    qot = f"qko{qi % 3}"
                    qi += 1
                    c0 = 2064 + j * 128
                    for kc in range(8):
                        P.pe(lambda e, kc=kc, pp=pp, c0=c0: e.matmul(pp[:], lhsT=wbf[:, kc, c0:c0 + 128], rhs=hn[:, kc, 1:513], start=(kc == 0), stop=(kc == 7)),
                             reads=WB + HN, writes=[ppt])
                    sc = 0.125 if j < 4 else 1.0
                    P.act(lambda e, pp=pp, qo=qo, sc=sc: e.activation(out=qo[:], in_=pp[:], func=AF.Copy, scale=sc), reads=[ppt], writes=[qot])
                    dst = (S["QnT"] if j < 4 else S["KnT"])[(j % 4) * 128:(j % 4 + 1) * 128, t0:t0 + TB]
                    P.dma(lambda e, qo=qo, dst=dst: e.dma_start(out=dst, in_=qo[:]), reads=[qot], writes=["dram_qkn"])
            for b in range(NB):
                main(b)
            P.emit()

    def phase_B(self, l):
        nc, I, S = self.nc, self.I, self.S
        P = Prog(self.ctx)
        with ExitStack() as es:
            def sb(name, shape, dt):
                return es.enter_context(nc.sbuf_tensor(f"L{l}" + name, list(shape), dt))

            def psum(name, shape, dt=F32):
                return es.enter_context(nc.psum_tensor(f"L{l}" + name, list(shape), dt))

            NP = 36
            idf = sb("B_idf", [128, 128], F32)
            idb = sb("B_idb", [128, 128], BF16)
            trif = sb("B_trif", [128, 128], F32)
            trib = sb("B_trib", [128, 128], F32)
            tokser = sb("B_tokser", [128, 4, NCH, 8], F32)
            dcb = sb("B_dcb", [128, 8, NCH], F32)
            mnw = sb("B_mnw", [128, 512], F32)
            epsb = sb("B_epsb", [128, 1], F32)
            es1 = ExitStack()

            def sb1(name, shape, dt):
                return es1.enter_context(nc.sbuf_tensor(f"L{l}" + name, list(shape), dt))

            ser = {n: sb1("B_" + n, [NP, T], F32) for n in ("Ipre", "F", "Bp", "U", "M", "tmp", "W", "EM", "ES")}
            onesr = sb1("B_onesr", [NP, T], F32)
            gb = sb1("B_gb", [NP, DEPTH, 2], F32)
            ngb = sb1("B_ngb", [NP, 1], F32)
            mend = sb1("B_mend", [NP, NCH], F32)
            mprev = sb1("B_mprev", [NP, NCH], F32)
            dcs = sb1("B_dcs", [NP, NCH], F32)
            ps_ser = es1.enter_context(nc.psum_tensor(f"L{l}B_ps_ser", [128, 512], F32))

            for n in ("Ipre", "F"):
                P.dve(lambda e, n=n: e.memset(ser[n][:], 0.0), writes=[n])
            P.pool(lambda e: e.memset(onesr[:], 1.0), writes=["onesr"])
            P.dve(lambda e: e.memset(epsb[:], float(128 * EPS)), writes=["epsb"], tiny=True)
            P.dma(lambda e: e.dma_start(out=gb[:], in_=I["gb"]), writes=["gb"])
            P.dma(lambda e: e.dma_start(out=idf[:], in_=I["ident"]), writes=["idf"])
            P.dma(lambda e: e.dma_start(out=trif[:], in_=I["trif"]), writes=["trif"])
            P.dma(lambda e: e.dma_start(out=trib[:], in_=I["trib"]), writes=["trib"])
            P.dma(lambda e: e.dma_start(out=mnw[:], in_=I["mnw"][:, l, :]), writes=["mnw"])
            P.dve(lambda e: e.tensor_copy(out=idb[:], in_=idf[:]), reads=["idf"], writes=["idb"])
            P.dve(lambda e: e.tensor_scalar(out=mnw[:], in0=mnw[:], scalar1=float(128.0 ** 0.5), scalar2=None, op0=ALU.mult), reads=["mnw"], writes=["mnw"])
            G = S["G"]
            P.dma(lambda e: e.dma_start(out=ser["Ipre"][0:4, :], in_=G[0:4, :]), writes=["Ipre"])
            P.dma(lambda e: e.dma_start(out=ser["F"][0:4, :], in_=G[4:8, :]), writes=["F"])
            P.dma(lambda e: e.dma_start(out=ser["Ipre"][32:36, :], in_=G[8:12, :]), writes=["Ipre"])
            P.dma(lambda e: e.dma_start(out=ser["F"][32:36, :], in_=G[12:16, :]), writes=["F"])
            P.dve(lambda e: e.tensor_scalar(out=ngb[:], in0=gb[:, l, 1:2], scalar1=-1.0, scalar2=None, op0=ALU.mult), reads=["gb"], writes=["ngb"], tiny=True)
            P.dve(lambda e: e.tensor_scalar(out=ser["Ipre"][:], in0=ser["Ipre"][:], scalar1=gb[:, l, 0:1], scalar2=None, op0=ALU.add),
                  reads=["Ipre", "gb"], writes=["Ipre"])
            P.act(lambda e: e.activation(out=ser["tmp"][:], in_=ser["F"][:], func=AF.Exp, bias=ngb[:, 0:1], scale=-1.0), reads=["F", "ngb"], writes=["tmp"])
            P.act(lambda e: e.activation(out=ser["F"][:], in_=ser["tmp"][:], func=AF.Ln, bias=1.0), reads=["tmp"], writes=["F"])

            def rev(ap):
                a = ap.ap
                return bass.AP(ap.tensor, ap.offset + (a[-1][1] - 1) * a[-1][0], [list(p) for p in a[:-1]] + [[-a[-1][0], a[-1][1]]])

            def scan(out, data, op1, wtok, rtok):
                P.dve(lambda e: e.tensor_tensor_scan(out=out[0:4, :], data0=onesr[0:4, :], data1=data[0:4, :], initial=0.0, op0=ALU.mult, op1=op1),
                      reads=[rtok, "onesr"], writes=[wtok])
                P.dve(lambda e: e.tensor_tensor_scan(out=rev(out[32:36, :]), data0=onesr[32:36, :], data1=rev(data[32:36, :]), initial=0.0, op0=ALU.mult, op1=op1),
                      reads=[rtok, "onesr"], writes=[wtok])

            P.dve(lambda e: e.memset(ser["Bp"][:], 0.0), writes=["Bp"])
            P.pool(lambda e: e.memset(ser["M"][:], 0.0), writes=["M"])
            scan(ser["Bp"], ser["F"], ALU.add, "Bp", "F")
            P.dve(lambda e: e.tensor_tensor(out=ser["U"][:], in0=ser["Ipre"][:], in1=ser["Bp"][:], op=ALU.add), reads=["Ipre", "Bp"], writes=["U"])
            scan(ser["M"], ser["U"], ALU.max, "M", "U")
            P.dve(lambda e: e.tensor_tensor(out=ser["tmp"][:], in0=ser["Bp"][:], in1=ser["M"][:], op=ALU.subtract), reads=["Bp", "M"], writes=["tmp"])
            P.act(lambda e: e.activation(out=ser["EM"][:], in_=ser["tmp"][:], func=AF.Exp), reads=["tmp"], writes=["EM"])
            Mv = ser["M"]
            P.dve(lambda e: e.memset(mprev[:], 0.0), writes=["mprev"], tiny=True)
            P.dve(lambda e: e.memset(mend[:], 0.0), writes=["mend"], tiny=True)
            P.dve(lambda e: e.tensor_copy(out=mend[0:4, :], in_=mk(Mv[0:4, 127:128], [[128, NCH]])), reads=["M"], writes=["mend"], tiny=True)
            P.dve(lambda e: e.tensor_copy(out=mend[32:36, :], in_=mk(Mv[32:36, 0:1], [[128, NCH]])), reads=["M"], writes=["mend"], tiny=True)
            P.dve(lambda e: e.tensor_copy(out=mprev[0:4, 1:NCH], in_=mk(Mv[0:4, 127:128], [[128, NCH - 1]])), reads=["M"], writes=["mprev"], tiny=True)
            P.dve(lambda e: e.tensor_copy(out=mprev[32:36, 0:NCH - 1], in_=mk(Mv[32:36, 128:129], [[128, NCH - 1]])), reads=["M"], writes=["mprev"], tiny=True)
            P.dve(lambda e: e.tensor_tensor(out=ser["tmp"][:].rearrange("p (c t) -> p c t", t=128), in0=mk(mprev[:], [[1, NCH], [0, 128]]),
                                            in1=ser["M"][:].rearrange("p (c t) -> p c t", t=128), op=ALU.subtract),
                  reads=["mprev", "M", "EM"], writes=["tmp"])
            P.act(lambda e: e.activation(out=ser["W"][:], in_=ser["tmp"][:], func=AF.Exp), reads=["tmp"], writes=["W"])
            P.dve(lambda e: e.tensor_tensor(out=ser["tmp"][:].rearrange("p (c t) -> p c t", t=128), in0=ser["U"][:].rearrange("p (c t) -> p c t", t=128),
                                            in1=mk(mend[:], [[1, NCH], [0, 128]]), op=ALU.subtract),
                  reads=["mend", "U", "W"], writes=["tmp"])
            P.act(lambda e: e.activation(out=ser["ES"][:], in_=ser["tmp"][:], func=AF.Exp), reads=["tmp"], writes=["ES"])
            P.dve(lambda e: e.tensor_tensor(out=dcs[:], in0=mprev[:], in1=mend[:], op=ALU.subtract), reads=["mprev", "mend"], writes=["dcs"], tiny=True)
            P.act(lambda e: e.activation(out=dcs[:], in_=dcs[:], func=AF.Exp), reads=["dcs"], writes=["dcs"], tiny=True)
            P.dma(lambda e: e.dma_start(out=S["Mser"][0:4, :], in_=ser["M"][0:4, :]), reads=["M"], writes=["dram_M"])
            P.dma(lambda e: e.dma_start(out=S["Mser"][4:8, :], in_=ser["M"][32:36, :]), reads=["M"], writes=["dram_M"])
            P.dma(lambda e: e.dma_start(out=S["Wser"][0:4, :], in_=ser["W"][0:4, :]), reads=["W"], writes=["dram_W"])
            P.dma(lambda e: e.dma_start(out=S["Wser"][4:8, :], in_=ser["W"][32:36, :]), reads=["W"], writes=["dram_W"])
            P.dma(lambda e: e.dma_start(out=S["DCs"][0:4, :], in_=dcs[0:4, :]), reads=["dcs"], writes=["dram_DC"])
            P.dma(lambda e: e.dma_start(out=S["DCs"][4:8, :], in_=dcs[32:36, :]), reads=["dcs"], writes=["dram_DC"])
            P.dma(lambda e: e.dma_start(out=dcb[:], in_=bass.AP(S["DCs"].tensor, S["DCs"].offset, [[0, 128], [NCH, 8], [1, NCH]])),
                  reads=["dram_DC"], writes=["dcb"])
            for si, n in enumerate(("U", "W", "EM", "ES")):
                for c0 in range(0, NCH, 8):
                    for c in range(c0, c0 + 8):
                        P.pe(lambda e, n=n, c=c, c0=c0: e.matmul(ps_ser[:, (c - c0) * 64:(c - c0) * 64 + NP], lhsT=ser[n][:, c * 128:(c + 1) * 128], rhs=idf[0:NP, 0:NP], start=True, stop=True),
                             reads=[n, "idf"], writes=["ps_ser"])
                    P.dve(lambda e, si=si, c0=c0: e.tensor_copy(out=tokser[:, si, c0:c0 + 8, 0:4], in_=mk(ps_ser[:, 0:1], [[64, 8], [1, 4]])),
                          reads=["ps_ser"], writes=["tokser"])
                    P.dve(lambda e, si=si, c0=c0: e.tensor_copy(out=tokser[:, si, c0:c0 + 8, 4:8], in_=mk(ps_ser[:, 32:33], [[64, 8], [1, 4]])),
                          reads=["ps_ser"], writes=["tokser"])
            if self.debug:
                P.dma(lambda e: e.dma_start(out=S["dbgser"], in_=tokser[:]), reads=["tokser"], writes=["dram_dbgser"])

            P.emit()
            es1.close()

            cst = sb("B_cst", [128, 2, NCH, 4 * 129], BF16)
            c32 = sb("B_c32", [128, 2, 4 * 129], F32)
            v1_ = [sb(f"B_v1{i}", [128, 4, 129], BF16) for i in range(4)]
            P = Prog(self.ctx)
            es2 = ExitStack()
            kt_ = [es2.enter_context(nc.sbuf_tensor(f"L{l}B_ktok{i}", [128, 512], BF16)) for i in range(4)]
            vs_ = [es2.enter_context(nc.sbuf_tensor(f"L{l}B_vs{i}", [128, 4, 129], BF16)) for i in range(4)]
            ps_c = [[es2.enter_context(nc.psum_tensor(f"L{l}B_ps_c{d}{hp}", [128, 512], F32)) for hp in range(2)] for d in range(2)]
            P.dve(lambda e: e.memset(c32[:], 0.0), writes=[f"c32_{d}_{h}" for d in range(2) for h in range(4)])
            for i in range(4):
                P.pool(lambda e, i=i: e.memset(v1_[i][:, :, 128:129], 1.0), writes=[f"v1{i}"])
            li = 0
            for step in range(NCH):
                for d in range(2):
                    c = step if d == 0 else NCH - 1 - step
                    P.act(lambda e, d=d, c=c: e.activation(out=cst[:, d, c, :], in_=c32[:, d, :], func=AF.Copy),
                          reads=[f"c32_{d}_{h}" for h in range(4)], writes=[f"cst{d}_{c}"])
                    if step == NCH - 1:
                        continue
                    kt = kt_[li % 4]
                    v1 = v1_[li % 4]
                    vs = vs_[li % 4]
                    ktt = f"ktokb{li % 4}"
                    v1t = f"v1{li % 4}"
                    vst = f"vs{li % 4}"
                    li += 1
                    P.dma(lambda e, kt=kt, c=c: e.dma_start(out=kt[:], in_=S["Ktok"][c * 128:(c + 1) * 128, :]), writes=[ktt])
                    P.dma(lambda e, v1=v1, c=c: e.dma_start(out=v1[:, :, 0:128], in_=S["Vtok"][c * 128:(c + 1) * 128, :].rearrange("p (h x) -> p h x", h=4)), writes=[v1t])
                    P.dve(lambda e, vs=vs, v1=v1, c=c, d=d: e.tensor_tensor(out=vs[:], in0=v1[:], in1=mk(tokser[:, 3, c, d * 4:d * 4 + 1], [[1, 4], [0, 129]]), op=ALU.mult),
                           reads=[v1t, "tokser"], writes=[vst])
                    for hp in range(2):
                        pc = ps_c[d][hp]
                        for hh in range(2):
                            h = hp * 2 + hh
                            P.pe(lambda e, pc=pc, hh=hh, h=h, kt=kt, vs=vs: e.matmul(pc[:, hh * 129:(hh + 1) * 129], lhsT=kt[:, h * 128:(h + 1) * 128], rhs=vs[:, h, :], start=True, stop=True),
                                 reads=[ktt, vst], writes=[f"ps_c{d}{hp}{hh}"])
                        for hh in range(2):
                            h = hp * 2 + hh
                            P.dve(lambda e, pc=pc, hh=hh, h=h, d=d, c=c: e.scalar_tensor_tensor(out=c32[:, d, h * 129:(h + 1) * 129], in0=c32[:, d, h * 129:(h + 1) * 129],
                                                                                     scalar=dcb[:, d * 4 + h, c:c + 1], in1=pc[:, hh * 129:(hh + 1) * 129], op0=ALU.mult, op1=ALU.add),
                                  reads=[f"ps_c{d}{hp}{hh}", "dcb", f"c32_{d}_{h}"], writes=[f"c32_{d}_{h}"])
            P.emit()
            es2.close()

            import os
            if os.environ.get("BSTOP") == "2":
                return
            P = Prog(self.ctx)
            two = range(2)
            qT_ = [sb(f"B_qT{i}", [128, 4, 128], BF16) for i in two]
            kT_ = [sb(f"B_kT{i}", [128, 4, 128], BF16) for i in two]
            mb_ = [sb(f"B_mb{i}", [128, 8, 128], F32) for i in two]
            wb_ = [sb(f"B_wb{i}", [128, 8, 128], F32) for i in two]
            og_ = [sb(f"B_og{i}", [128, 512], F32) for i in two]
            gg_ = [sb(f"B_gg{i}", [128, 512], F32) for i in range(3)]
            sm_ = [sb(f"B_sm{i}", [128, 8, 128], F32) for i in two]
            ee_ = [sb(f"B_ee{i}", [128, 8, 128], F32) for i in two]
            pp_ = [sb(f"B_pp{i}", [128, 8, 128], BF16) for i in two]
            qw_ = [sb(f"B_qw{i}", [128, 8, 128], BF16) for i in two]
            hs_ = [sb(f"B_hs{i}", [128, 8, 128], F32) for i in two]
            hall_ = [sb(f"B_hall{i}", [128, 512], F32) for i in two]
            hsq_ = [sb(f"B_hsq{i}", [128, 512], F32) for i in two]
            dd_ = [sb(f"B_dd{i}", [128, 8], F32) for i in two]
            ss_ = [sb(f"B_ss{i}", [128, 4], F32) for i in two]
            ybf_ = [sb(f"B_ybf{i}", [128, 512], BF16) for i in two]
            yT_ = [sb(f"B_yT{i}", [128, 4, 128], BF16) for i in two]
            ps_s = [psum(f"B_ps_s{i}", [128, 512]) for i in two]
            ps_o = [psum(f"B_ps_o{i}", [128, 1024]) for i in two]
            ps_d = psum("B_ps_d", [128, 512])
            ps_y = psum("B_ps_y", [128, 1024], BF16)
            for i in range(4):
                P.pool(lambda e, i=i: e.memset(v1_[i][:, :, 128:129], 1.0), writes=[f"v1{i}"])
            tri4 = sb("B_tri4", [128, 8, 128], F32)
            for i in range(4):
                P.pool(lambda e, i=i: e.tensor_copy(out=tri4[:, i, :], in_=trif[:]), writes=["tri4"])
                P.pool(lambda e, i=i: e.tensor_copy(out=tri4[:, 4 + i, :], in_=trib[:]), writes=["tri4"])

            def b3_load(c):
                s = c % 2
                tk = c * 128
                P.dma(lambda e: e.dma_start(out=qT_[s][:], in_=S["QT"][:, tk:tk + 128].rearrange("(h p) t -> p h t", p=128)), writes=[f"qT{s}"])
                P.dma(lambda e: e.dma_start(out=kT_[s][:], in_=S["KT"][:, tk:tk + 128].rearrange("(h p) t -> p h t", p=128)), writes=[f"kT{s}"])
                P.dma(lambda e: e.dma_start(out=mb_[s][:], in_=bass.AP(S["Mser"].tensor, S["Mser"].offset + tk, [[0, 128], [T, 8], [1, 128]])), writes=[f"mb{s}"])
                P.dma(lambda e: e.dma_start(out=wb_[s][:], in_=bass.AP(S["Wser"].tensor, S["Wser"].offset + tk, [[0, 128], [T, 8], [1, 128]])), writes=[f"wb{s}"])
                P.dma(lambda e: e.dma_start(out=og_[s][:], in_=S["Osig"][tk:tk + 128, :]), writes=[f"og{s}"])
                P.dma(lambda e: e.dma_start(out=v1_[c % 4][:, :, 0:128], in_=S["Vtok"][tk:tk + 128, :].rearrange("p (h x) -> p h x", h=4)), writes=[f"v1{c % 4}"])

            def b3_stage(c, stage):
                s = c % 2
                tk = c * 128
                qT, kT, mb, wb, og, gg = qT_[s], kT_[s], mb_[s], wb_[s], og_[s], gg_[c % 3]
                sm, ee, pp, qw, hs, hall, hsq, dd, ss, ybf, yT = sm_[s], ee_[s], pp_[s], qw_[s], hs_[s], hall_[s], hsq_[s], dd_[s], ss_[s], ybf_[s], yT_[s]
                ps, po = ps_s[s], ps_o[s]
                v1 = v1_[c % 4]
                v1t = f"v1{c % 4}"
                if stage == 1:
                    for h in range(4):
                        P.pe(lambda e, ps=ps, kT=kT, qT=qT, h=h: e.matmul(ps[:, h * 128:(h + 1) * 128], lhsT=kT[:, h, :], rhs=qT[:, h, :], start=True, stop=True),
                             reads=[f"kT{s}", f"qT{s}"], writes=[f"ps_s{s}"])
                    psv = ps[:].rearrange("p (h t) -> p h t", h=4)
                    P.dve(lambda e, sm=sm, psv=psv: e.tensor_tensor(out=sm[:, 0:4, :], in0=psv, in1=tri4[:, 0:4, :], op=ALU.mult),
                          reads=[f"ps_s{s}", "tri4"], writes=[f"smf{s}"])
                    P.dve(lambda e, sm=sm, psv=psv: e.tensor_tensor(out=sm[:, 4:8, :], in0=psv, in1=tri4[:, 4:8, :], op=ALU.mult),
                          reads=[f"ps_s{s}", "tri4"], writes=[f"smb{s}"])
                    for r in range(8):
                        P.act(lambda e, ee=ee, mb=mb, r=r, c=c: e.activation(out=ee[:, r, :], in_=mb[:, r, :], func=AF.Exp, bias=tokser[:, 0, c, r:r + 1], scale=-1.0),
                              reads=[f"mb{s}", "tokser"], writes=[f"ee{s}"])
                    P.dve(lambda e, pp=pp, ee=ee, sm=sm: e.scalar_tensor_tensor(out=pp[:], in0=ee[:], scalar=1.0, in1=sm[:], op0=ALU.min, op1=ALU.mult),
                          reads=[f"ee{s}", f"smf{s}", f"smb{s}"], writes=[f"pp{s}"])
                    for d in range(2):
                        P.pool(lambda e, qw=qw, qT=qT, wb=wb, d=d: e.tensor_tensor(out=qw[:, d * 4:(d + 1) * 4, :], in0=qT[:], in1=wb[:, d * 4:(d + 1) * 4, :], op=ALU.mult),
                               reads=[f"qT{s}", f"wb{s}"], writes=[f"qw{s}"])
                    P.pool(lambda e, gg=gg, og=og: e.tensor_tensor(out=gg[:], in0=og[:], in1=mnw[:], op=ALU.mult), reads=[f"og{s}", "mnw"], writes=[f"gg{c % 3}"])
                    for r in range(8):
                        d, h = r // 4, r % 4
                        P.pe(lambda e, po=po, qw=qw, r=r, d=d, h=h, c=c: e.matmul(po[:, r * 128:(r + 1) * 128], lhsT=qw[:, r, :], rhs=cst[:, d, c, h * 129:h * 129 + 128], start=True, stop=False),
                             reads=[f"qw{s}", f"cst{d}_{c}"], writes=[f"ps_o{s}"])
                        P.pe(lambda e, po=po, pp=pp, v1=v1, r=r, h=h: e.matmul(po[:, r * 128:(r + 1) * 128], lhsT=pp[:, r, :], rhs=v1[:, h, 0:128], start=False, stop=True),
                             reads=[f"pp{s}", v1t], writes=[f"ps_o{s}"])
                        P.pe(lambda e, qw=qw, r=r, d=d, h=h, c=c, s=s: e.matmul(ps_d[:, s * 8 + r:s * 8 + r + 1], lhsT=qw[:, r, :], rhs=cst[:, d, c, h * 129 + 128:h * 129 + 129], start=True, stop=False),
                             reads=[f"qw{s}", f"cst{d}_{c}"], writes=[f"ps_d{s}"])
                        P.pe(lambda e, pp=pp, v1=v1, r=r, h=h, s=s: e.matmul(ps_d[:, s * 8 + r:s * 8 + r + 1], lhsT=pp[:, r, :], rhs=v1[:, h, 128:129], start=False, stop=True),
                             reads=[f"pp{s}", v1t], writes=[f"ps_d{s}"])
                if stage == 2:
                    P.act(lambda e, dd=dd, s=s: e.activation(out=dd[:], in_=ps_d[:, s * 8:s * 8 + 8], func=AF.Copy), reads=[f"ps_d{s}"], writes=[f"dd{s}"], tiny=True)
                    P.dve(lambda e, dd=dd: e.scalar_tensor_tensor(out=dd[:], in0=dd[:], scalar=-1.0, in1=dd[:], op0=ALU.mult, op1=ALU.max), reads=[f"dd{s}"], writes=[f"dd{s}"], tiny=True)
                    P.dve(lambda e, dd=dd, c=c: e.tensor_tensor(out=dd[:], in0=dd[:], in1=tokser[:, 2, c, :], op=ALU.max), reads=[f"dd{s}", "tokser"], writes=[f"dd{s}"], tiny=True)
                    P.dve(lambda e, dd=dd: e.reciprocal(out=dd[:], in_=dd[:]), reads=[f"dd{s}"], writes=[f"dd{s}"], tiny=True)
                    P.act(lambda e, hs=hs, po=po: e.activation(out=hs[:], in_=po[:].rearrange("p (r t) -> p r t", r=8), func=AF.Copy), reads=[f"ps_o{s}"], writes=[f"hs{s}"])
                    P.dve(lambda e, hs=hs, dd=dd: e.tensor_tensor(out=hs[:], in0=hs[:], in1=mk(dd[:, 0:1], [[1, 8], [0, 128]]), op=ALU.mult),
                          reads=[f"hs{s}", f"dd{s}"], writes=[f"hs{s}"])
                    P.pool(lambda e, hall=hall, hs=hs: e.tensor_tensor(out=hall[:].rearrange("p (h x) -> p h x", h=4), in0=hs[:, 0:4, :], in1=hs[:, 4:8, :], op=ALU.add),
                           reads=[f"hs{s}"], writes=[f"hall{s}"])
                if stage == 3:
                    P.pool(lambda e, hall=hall, hsq=hsq: e.tensor_tensor(out=hsq[:], in0=hall[:], in1=hall[:], op=ALU.mult), reads=[f"hall{s}"], writes=[f"hsq{s}"])
                    P.dve(lambda e, ss=ss, hsq=hsq: e.tensor_reduce(out=ss[:], in_=hsq[:].rearrange("p (h x) -> p h x", h=4), axis=AX.X, op=ALU.add), reads=[f"hsq{s}"], writes=[f"ss{s}"], tiny=True)
                    P.act(lambda e, ss=ss: e.activation(out=ss[:], in_=ss[:], func=AF.Ln, bias=epsb[:, 0:1]), reads=[f"ss{s}", "epsb"], writes=[f"ss{s}"], tiny=True)
                    P.act(lambda e, ss=ss: e.activation(out=ss[:], in_=ss[:], func=AF.Exp, scale=-0.5), reads=[f"ss{s}"], writes=[f"ss{s}"], tiny=True)
                    P.dve(lambda e, hsq=hsq, hall=hall, ss=ss: e.tensor_tensor(out=hsq[:].rearrange("p (h x) -> p h x", h=4), in0=hall[:].rearrange("p (h x) -> p h x", h=4), in1=mk(ss[:, 0:1], [[1, 4], [0, 128]]), op=ALU.mult),
                          reads=[f"hall{s}", f"ss{s}", f"hsq{s}"], writes=[f"hsq{s}"])
                    P.pool(lambda e, ybf=ybf, hsq=hsq, gg=gg: e.tensor_tensor(out=ybf[:], in0=hsq[:], in1=gg[:], op=ALU.mult), reads=[f"hsq{s}", f"gg{c % 3}"], writes=[f"ybf{s}"])
                    for h in range(4):
                        P.pe(lambda e, h=h, s=s, ybf=ybf: e.transpose(ps_y[:, s * 512 + h * 128: s * 512 + (h + 1) * 128], ybf[:, h * 128:(h + 1) * 128], idb[:]),
                             reads=[f"ybf{s}", "idb"], writes=[f"ps_y{s}"])
                    P.act(lambda e, yT=yT, s=s: e.activation(out=yT[:], in_=ps_y[:, s * 512:(s + 1) * 512].rearrange("p (h t) -> p h t", h=4), func=AF.Copy),
                          reads=[f"ps_y{s}"], writes=[f"yT{s}"])
                    P.dma(lambda e, yT=yT, tk=tk: e.dma_start(out=S["YT"][0:512, tk:tk + 128].rearrange("(h p) t -> p h t", p=128), in_=yT[:]),
                          reads=[f"yT{s}"], writes=["dram_yt"])

            import os
            B3D = int(os.environ.get("B3DEPTH", 2))
            if B3D == 2:
                b3_load(0)
                b3_load(1)
                b3_stage(0, 1)
                for c in range(NCH):
                    if c + 1 < NCH:
                        b3_stage(c + 1, 1)
                    if c + 2 < NCH:
                        b3_load(c + 2)
                    b3_stage(c, 2)
                    b3_stage(c, 3)
            else:
                b3_load(0)
                b3_load(1)
                b3_stage(0, 1)
                b3_stage(1, 1)
                b3_load(2)
                b3_stage(0, 2)
                for c in range(NCH):
                    if c + 2 < NCH:
                        b3_stage(c + 2, 1)
                    if c + 3 < NCH:
                        b3_load(c + 3)
                    if c + 1 < NCH:
                        b3_stage(c + 1, 2)
                    b3_stage(c, 3)
            P.emit()

    def phase_C(self, l):
        nc, I, S = self.nc, self.I, self.S
        P = Prog(self.ctx)
        with ExitStack() as es:
            def sb(name, shape, dt):
                return es.enter_context(nc.sbuf_tensor(f"L{l}" + name, list(shape), dt))

            def psum(name, shape, dt=F32):
                return es.enter_context(nc.psum_tensor(f"L{l}" + name, list(shape), dt))

            qn = sb("C_qn", [128, 4, T], BF16)
            kn = sb("C_kn", [128, 4, T], BF16)
            vn = sb("C_vn", [128, 32, 8, 65], BF16)
            nbi = sb("C_nbi", [128, 8, 5, 128], F32)
            nbe_ = [sb(f"C_nbe{i}", [128, 8, 4, 128], F32) for i in range(2)]
            idf = sb("C_idf", [128, 128], F32)
            idb = sb("C_idb", [128, 128], BF16)
            sbb_ = [sb(f"C_sb{i}", [128, 5, 128], F32) for i in range(3)]
            pt_ = [sb(f"C_pt{i}", [128, 5, 128], BF16) for i in range(3)]
            rd = sb("C_rd", [128, 8], F32)
            yn_ = [sb(f"C_yn{i}", [128, 8, 64], BF16) for i in range(2)]
            yT_ = [sb(f"C_yT{i}", [128, 4, 128], BF16) for i in range(2)]
            ps_s = [psum(f"C_ps_s{i}", [128, 1024]) for i in range(3)]
            ps_o = psum("C_ps_o", [128, 512])
            ps_y = psum("C_ps_y", [128, 1024], BF16)
            P.dma(lambda e: e.dma_start(out=idf[:], in_=I["ident"]), writes=["idf"])
            P.dve(lambda e: e.tensor_copy(out=idb[:], in_=idf[:]), reads=["idf"], writes=["idb"])
            P.dma(lambda e: e.dma_start(out=nbi[:], in_=I["nbi"][l]), writes=["nbi"])
            for g in range(4):
                P.dma(lambda e, g=g: e.dma_start(out=qn[:, g, :], in_=S["QnT"][g * 128:(g + 1) * 128, :]), writes=["qn"])
                P.dma(lambda e, g=g: e.dma_start(out=kn[:, g, :], in_=S["KnT"][g * 128:(g + 1) * 128, :]), writes=["kn"])
            P.pool(lambda e: e.memset(vn[:, :, :, 64:65], 1.0), writes=["vn"])
            for m in range(32):
                P.dma(lambda e, m=m: e.dma_start(out=vn[:, m, :, 0:64], in_=S["Vn"][m * 128:(m + 1) * 128, :].rearrange("p (h x) -> p h x", h=8)), writes=["vn"])
            def blk(i):
                if i < 2:
                    return [0, 1, 2, 3], i
                if i >= 30:
                    return [28, 29, 30, 31], i - 28
                return [i - 2, i - 1, i, i + 1, i + 2], None

            import os
            NU = int(os.environ.get("CNU", 32 * 8))

            def st_S(u):
                i, h = u // 8, u % 8
                tiles, eidx = blk(i)
                g, hp = h // 2, (h % 2) * 64
                ps = ps_s[u % 3]
                for j, m in enumerate(tiles):
                    P.pe(lambda e, ps=ps, j=j, m=m, g=g, hp=hp, i=i: e.matmul(ps[:, j * 128:(j + 1) * 128], lhsT=kn[hp:hp + 64, g, m * 128:(m + 1) * 128], rhs=qn[hp:hp + 64, g, i * 128:(i + 1) * 128], start=True, stop=True),
                         reads=["kn", "qn"], writes=[f"ps_s{u % 3}"])

            def st_E(u):
                i, h = u // 8, u % 8
                tiles, eidx = blk(i)
                nt = len(tiles)
                ps = ps_s[u % 3]
                sbb, pt = sbb_[u % 3], pt_[u % 3]
                if eidx is not None and h == 0:
                    P.dma(lambda e, eidx=eidx, i=i: e.dma_start(out=nbe_[i % 2][:], in_=I["nbe"][l, eidx]), writes=[f"nbe{i % 2}"])
                bias = nbi[:, h, :, :] if eidx is None else nbe_[i % 2][:, h, :, :]
                btok = "nbi" if eidx is None else f"nbe{i % 2}"
                P.dve(lambda e, ps=ps, sbb=sbb, bias=bias, nt=nt: e.tensor_tensor(out=sbb[:, 0:nt, :], in0=ps[:, 0:nt * 128].rearrange("p (j x) -> p j x", x=128), in1=bias, op=ALU.add),
                      reads=[f"ps_s{u % 3}", btok], writes=[f"sb{u % 3}"])
                P.act(lambda e, sbb=sbb, pt=pt, nt=nt: e.activation(out=pt[:, 0:nt, :], in_=sbb[:, 0:nt, :], func=AF.Exp), reads=[f"sb{u % 3}"], writes=[f"pt{u % 3}"])

            def st_PV(u):
                i, h = u // 8, u % 8
                tiles, eidx = blk(i)
                nt = len(tiles)
                pt = pt_[u % 3]
                so = (u % 4) * 128
                sot = f"ps_o{u % 4}"
                for j, m in enumerate(tiles):
                    P.pe(lambda e, pt=pt, j=j, m=m, h=h, nt=nt, so=so: e.matmul(ps_o[:, so:so + 65], lhsT=pt[:, j, :], rhs=vn[:, m, h, :], start=(j == 0), stop=(j == nt - 1)),
                         reads=[f"pt{u % 3}", "vn"], writes=[sot])
                P.dve(lambda e, h=h, so=so: e.reciprocal(out=rd[:, h:h + 1], in_=ps_o[:, so + 64:so + 65]), reads=[sot], writes=[f"rd{h}"], tiny=True)
                P.act(lambda e, h=h, so=so, i=i: e.activation(out=yn_[i % 2][:, h, :], in_=ps_o[:, so:so + 64], func=AF.Copy, scale=rd[:, h:h + 1]), reads=[sot, f"rd{h}"], writes=[f"yn{i % 2}"])
                if h == 7:
                    sl = i % 2
                    yT = yT_[sl]
                    for g in range(4):
                        P.pe(lambda e, g=g, sl=sl: e.transpose(ps_y[:, sl * 512 + g * 128:sl * 512 + (g + 1) * 128], yn_[sl][:, 2 * g:2 * g + 2, :].rearrange("p a b -> p (a b)"), idb[:]),
                             reads=[f"yn{sl}", "idb"], writes=[f"ps_y{sl}"])
                    P.dve(lambda e, yT=yT, sl=sl: e.tensor_copy(out=yT[:], in_=ps_y[:, sl * 512:(sl + 1) * 512].rearrange("p (h t) -> p h t", h=4)),
                          reads=[f"ps_y{sl}"], writes=[f"yT{sl}"])
                    P.dma(lambda e, yT=yT, i=i: e.dma_start(out=S["YT"][512:1024, i * 128:(i + 1) * 128].rearrange("(h p) t -> p h t", p=128), in_=yT[:]),
                          reads=[f"yT{sl}"], writes=["dram_yt"])

            CD = int(os.environ.get("CDEPTH", 1))
            if CD == 0:
                for u in range(NU):
                    st_S(u)
                    st_E(u)
                    st_PV(u)
            elif CD == 1:
                st_S(0)
                for u in range(NU):
                    if u + 1 < NU:
                        st_S(u + 1)
                    st_E(u)
                    st_PV(u)
            else:
                st_S(0)
                st_S(1)
                st_E(0)
                for u in range(NU):
                    if u + 2 < NU:
                        st_S(u + 2)
                    if u + 1 < NU:
                        st_E(u + 1)
                    st_PV(u)
            P.emit()

    def phase_D(self, l):
        nc, I, S = self.nc, self.I, self.S
        P = Prog(self.ctx)
        xsrc = I["xT"] if l == 0 else S["xres"]
        self.esE = ExitStack()
        esE = self.esE
        w1 = esE.enter_context(nc.sbuf_tensor(f"L{l}E_w1", [128, 8, DFF], BF16))
        w2 = esE.enter_context(nc.sbuf_tensor(f"L{l}E_w2", [128, 32, D], BF16))
        n2 = esE.enter_context(nc.sbuf_tensor(f"L{l}E_n2", [128, DEPTH, 8], F32))
        self.Ew = (w1, w2, n2)
        with ExitStack() as es:
            def sb(name, shape, dt):
                return es.enter_context(nc.sbuf_tensor(f"L{l}" + name, list(shape), dt))

            def psum(name, shape, dt=F32):
                return es.enter_context(nc.psum_tensor(f"L{l}" + name, list(shape), dt))

            wbf = sb("D_wbf", [128, 8, D], BF16)
            wst = [sb(f"D_wst{i}", [128, 512], F32) for i in range(2)]
            stE = [sb(f"D_stE{i}", [128, 1024], F32) for i in range(2)]
            xin = [sb(f"D_xin{i}", [128, 8, TB], F32) for i in range(2)]
            yin = [sb(f"D_yin{i}", [128, 8, TB], BF16) for i in range(2)]
            pso = [psum(f"D_ps{i}", [128, 512]) for i in range(4)]
            P.dma(lambda e: e.dma_start(out=n2[:], in_=I["n2"]), writes=["n2"])
            wi = 0
            for kc in range(8):
                for hf in range(2):
                    st = wst[wi % 2]
                    stt = f"wst{wi % 2}"
                    P.dma(lambda e, st=st, kc=kc, hf=hf: e.dma_start(out=st[:], in_=I["w_out"][l, kc * 128:(kc + 1) * 128, hf * 512:(hf + 1) * 512]), writes=[stt])
                    if wi % 2 == 0:
                        P.pool(lambda e, st=st, kc=kc, hf=hf: e.tensor_copy(out=wbf[:, kc, hf * 512:(hf + 1) * 512], in_=st[:]), reads=[stt], writes=[f"wbf{kc}_{hf}"])
                    else:
                        P.act(lambda e, st=st, kc=kc, hf=hf: e.activation(out=wbf[:, kc, hf * 512:(hf + 1) * 512], in_=st[:], func=AF.Copy), reads=[stt], writes=[f"wbf{kc}_{hf}"])
                    wi += 1
            WB = [f"wbf{kc}_{hf}" for kc in range(8) for hf in range(2)]

            def prefetch(pc):
                st = stE[pc % 2]
                stt = f"stE{pc % 2}"
                if pc < 32:
                    kc, hf = pc // 4, pc % 4
                    P.dma(lambda e: e.dma_start(out=st[:], in_=I["w_ff1"][l, kc * 128:(kc + 1) * 128, hf * 1024:(hf + 1) * 1024]), writes=[stt])
                    if pc % 2 == 0:
                        P.pool(lambda e: e.tensor_scalar(out=w1[:, kc, hf * 1024:(hf + 1) * 1024], in0=st[:], scalar1=n2[:, l, kc:kc + 1], scalar2=None, op0=ALU.mult),
                               reads=[stt, "n2"], writes=[f"w1_{pc}"])
                    else:
                        P.act(lambda e: e.activation(out=w1[:, kc, hf * 1024:(hf + 1) * 1024], in_=st[:], func=AF.Copy, scale=n2[:, l, kc:kc + 1]),
                              reads=[stt, "n2"], writes=[f"w1_{pc}"])
                else:
                    k2 = pc - 32
                    P.dma(lambda e: e.dma_start(out=st[:], in_=I["w_ff2"][l, k2 * 128:(k2 + 1) * 128, :]), writes=[stt])
                    if pc % 2 == 0:
                        P.pool(lambda e: e.tensor_copy(out=w2[:, k2, :], in_=st[:]), reads=[stt], writes=[f"w2_{k2}"])
                    else:
                        P.act(lambda e: e.activation(out=w2[:, k2, :], in_=st[:], func=AF.Copy), reads=[stt], writes=[f"w2_{k2}"])

            def load(b):
                t0 = b * TB
                P.dma(lambda e, b=b, t0=t0: e.dma_start(out=xin[b % 2][:], in_=xsrc[:, t0:t0 + TB].rearrange("(kc p) t -> p kc t", p=128)), writes=[f"xin{b % 2}"])
                P.dma(lambda e, b=b, t0=t0: e.dma_start(out=yin[b % 2][:], in_=S["YT"][:, t0:t0 + TB].rearrange("(kc p) t -> p kc t", p=128)), writes=[f"yin{b % 2}"])

            load(0)
            pi = 0
            for b in range(NB):
                if b + 1 < NB:
                    load(b + 1)
                for pc in range(8 * b, 8 * b + 8):
                    prefetch(pc)
                t0 = b * TB
                xi, yi = xin[b % 2], yin[b % 2]
                for dc in range(8):
                    ps = pso[pi % 4]
                    pst = f"ps{pi % 4}"
                    pi += 1
                    for kc in range(8):
                        P.pe(lambda e, ps=ps, kc=kc, dc=dc, yi=yi: e.matmul(ps[:], lhsT=wbf[:, kc, dc * 128:(dc + 1) * 128], rhs=yi[:, kc, :], start=(kc == 0), stop=(kc == 7)),
                             reads=WB + [f"yin{b % 2}"], writes=[pst])
                    P.dve(lambda e, ps=ps, dc=dc, xi=xi: e.tensor_tensor(out=xi[:, dc, :], in0=xi[:, dc, :], in1=ps[:], op=ALU.add),
                          reads=[pst, f"xin{b % 2}"], writes=[f"xin{b % 2}"])
                P.dma(lambda e, xi=xi, t0=t0: e.dma_start(out=S["xres"][:, t0:t0 + TB].rearrange("(kc p) t -> p kc t", p=128), in_=xi[:]),
                      reads=[f"xin{b % 2}"], writes=["dram_x"])
            P.emit()

    def phase_E(self, l):
        nc, I, S = self.nc, self.I, self.S
        P = Prog(self.ctx)
        last = (l == DEPTH - 1)
        w1, w2, n2 = self.Ew
        with ExitStack() as es:
            def sb(name, shape, dt):
                return es.enter_context(nc.sbuf_tensor(f"L{l}" + name, list(shape), dt))

            def psum(name, shape, dt=F32):
                return es.enter_context(nc.psum_tensor(f"L{l}" + name, list(shape), dt))

            nf = sb("E_nf", [128, 8], F32)
            ones = sb("E_ones", [128, 128], BF16)
            epsb = sb("E_epsb", [128, 1], F32)
            xin = [sb(f"E_xin{i}", [128, 8, TB], F32) for i in range(2)]
            xsq = sb("E_xsq", [128, 8, TB], BF16)
            rstd = sb("E_rstd", [128, TB], F32)
            hn = sb("E_hn", [128, 8, TB], BF16)
            uu = [sb(f"E_u{i}", [128, 8, TB], BF16) for i in range(2)]
            rtmp = [sb(f"E_rtmp{i}", [128, TB], F32) for i in range(2)]
            ps_ssq = psum("E_ps_ssq", [128, 512])
            ps1 = [psum(f"E_ps1_{i}", [128, 512]) for i in range(3)]
            ps2 = [psum(f"E_ps2_{i}", [128, 512]) for i in range(3)]
            P.dma(lambda e: e.dma_start(out=nf[:], in_=I["nf"]), writes=["nf"])
            P.dve(lambda e: e.tensor_scalar(out=nf[:], in0=nf[:], scalar1=32.0, scalar2=None, op0=ALU.mult), reads=["nf"], writes=["nf"], tiny=True)
            P.dve(lambda e: e.memset(ones[:], 1.0), writes=["ones"])
            P.dve(lambda e: e.memset(epsb[:], float(D * EPS)), writes=["epsb"], tiny=True)
            cnt = {"p1": 0, "p2": 0}
            HN = [f"hn{kc}" for kc in range(8)]

            def load(b):
                t0 = b * TB
                P.dma(lambda e, b=b, t0=t0: e.dma_start(out=xin[b % 2][:], in_=S["xres"][:, t0:t0 + TB].rearrange("(kc p) t -> p kc t", p=128)), writes=[f"xin{b % 2}"])

            def rms(xi, xt):
                P.act(lambda e: e.activation(out=xsq[:], in_=xi[:], func=AF.Square), reads=[xt], writes=["xsq"])
                for kc in range(8):
                    P.pe(lambda e, kc=kc: e.matmul(ps_ssq[:], lhsT=ones[:], rhs=xsq[:, kc, :], start=(kc == 0), stop=(kc == 7)), reads=["ones", "xsq"], writes=["ps_ssq"])
                P.act(lambda e: e.activation(out=rstd[:], in_=ps_ssq[:], func=AF.Ln, bias=epsb[:, 0:1]), reads=["ps_ssq", "epsb"], writes=["rstd"])
                P.act(lambda e: e.activation(out=rstd[:], in_=rstd[:], func=AF.Exp, scale=-0.5), reads=["rstd"], writes=["rstd"])

            def prep(b):
                xi = xin[b % 2]
                xt = f"xin{b % 2}"
                rms(xi, xt)
                for kc in range(8):
                    P.dve(lambda e, kc=kc, xi=xi: e.scalar_tensor_tensor(out=hn[:, kc, :], in0=xi[:, kc, :], scalar=32.0, in1=rstd[:], op0=ALU.mult, op1=ALU.mult),
                          reads=[xt, "rstd"], writes=[f"hn{kc}"])

            def S1(b, grp):
                gi = (b * 4 + grp) % 2
                u = uu[gi]
                ut = f"u{gi}"
                for fj in range(8):
                    fc = grp * 8 + fj
                    ps = ps1[cnt["p1"] % 3]
                    pst = f"ps1_{cnt['p1'] % 3}"
                    rt = rtmp[cnt["p1"] % 2]
                    rtt = f"rtmp{cnt['p1'] % 2}"
                    cnt["p1"] += 1
                    for kc in range(8):
                        P.pe(lambda e, ps=ps, kc=kc, fc=fc: e.matmul(ps[:], lhsT=w1[:, kc, fc * 128:(fc + 1) * 128], rhs=hn[:, kc, :], start=(kc == 0), stop=(kc == 7)),
                             reads=HN, writes=[pst])
                    P.act(lambda e, ps=ps, rt=rt: e.activation(out=rt[:], in_=ps[:], func=AF.Relu), reads=[pst], writes=[rtt])
                    eng = P.dve if fj % 2 == 0 else P.pool
                    eng(lambda e, u=u, fj=fj, rt=rt: e.tensor_tensor(out=u[:, fj, :], in0=rt[:], in1=rt[:], op=ALU.mult), reads=[rtt], writes=[f"{ut}_{fj}"])

            def S2(b, grp):
                gi = (b * 4 + grp) % 2
                u = uu[gi]
                ut = f"u{gi}"
                xi = xin[b % 2]
                xt = f"xin{b % 2}"
                UT = [f"{ut}_{fj}" for fj in range(8)]
                for dc in range(8):
                    ps = ps2[cnt["p2"] % 3]
                    pst = f"ps2_{cnt['p2'] % 3}"
                    cnt["p2"] += 1
                    for fj in range(8):
                        P.pe(lambda e, ps=ps, fj=fj, grp=grp, dc=dc, u=u: e.matmul(ps[:], lhsT=w2[:, grp * 8 + fj, dc * 128:(dc + 1) * 128], rhs=u[:, fj, :], start=(fj == 0), stop=(fj == 7)),
                             reads=UT, writes=[pst])
                    P.dve(lambda e, ps=ps, dc=dc, xi=xi: e.tensor_tensor(out=xi[:, dc, :], in0=xi[:, dc, :], in1=ps[:], op=ALU.add),
                          reads=[pst, xt], writes=[xt])

            def fin(b):
                t0 = b * TB
                xi = xin[b % 2]
                xt = f"xin{b % 2}"
                if not last:
                    P.dma(lambda e, xi=xi, t0=t0: e.dma_start(out=S["xres"][:, t0:t0 + TB].rearrange("(kc p) t -> p kc t", p=128), in_=xi[:]),
                          reads=[xt], writes=["dram_x"])
                else:
                    rms(xi, xt)
                    for kc in range(8):
                        P.dve(lambda e, kc=kc, xi=xi: e.scalar_tensor_tensor(out=xi[:, kc, :], in0=xi[:, kc, :], scalar=nf[:, kc:kc + 1], in1=rstd[:], op0=ALU.mult, op1=ALU.mult),
                              reads=[xt, "rstd", "nf"], writes=[xt])
                    P.dma(lambda e, xi=xi, t0=t0: e.dma_start(out=self.out[:, t0:t0 + TB].rearrange("(kc p) t -> p kc t", p=128), in_=xi[:]),
                          reads=[xt], writes=["dram_out"])

            load(0)
            prep(0)
            S1(0, 0)
            for b in range(NB):
                if b + 1 < NB:
                    load(b + 1)
                for grp in range(4):
                    if grp < 3:
                        S1(b, grp + 1)
                    elif b + 1 < NB:
                        prep(b + 1)
                        S1(b + 1, 0)
                    S2(b, grp)
                fin(b)
            P.emit()
        self.esE.close()


def _na_bias_tables(rpb_l):
    H = rpb_l.shape[0]
    kl = np.arange(128)
    kr_l, kc = kl // 64, kl % 64
    ql = np.arange(128)
    qr_l, qc = ql // 64, ql % 64

    def tile(i, m):
        kr = 2 * m + kr_l[:, None]
        qr = 2 * i + qr_l[None, :]
        rs = np.clip(qr - 4, 0, 56)
        vr = (kr >= rs) & (kr < rs + 8)
        cs = np.clip(qc[None, :] - 8, 0, 48)
        vc = (kc[:, None] >= cs) & (kc[:, None] < cs + 16)
        ri = np.clip(kr - qr + 7, 0, 14)
        ci = np.clip(kc[:, None] - qc[None, :] + 15, 0, 30)
        vals = rpb_l[:, ri, ci]
        return np.where((vr & vc)[None], vals, np.float32(-30000.0)).astype(np.float32)

    nbi = np.stack([tile(2, m) for m in range(5)], axis=1)
    nbi = np.ascontiguousarray(nbi.transpose(2, 0, 1, 3))
    nbe = []
    for i, ms in ((0, range(4)), (1, range(4)), (30, range(28, 32)), (31, range(28, 32))):
        t = np.stack([tile(i, m) for m in ms], axis=1)
        nbe.append(t.transpose(2, 0, 1, 3))
    nbe = np.ascontiguousarray(np.stack(nbe, axis=0))
    return nbi, nbe


def _prep_shared(norm1_w, conv_w, conv_b, gate_b, mlstm_norm_w, rpb, norm2_w, final_norm_w):
    f = np.float32
    sh = {}
    sh["n1"] = np.ascontiguousarray(norm1_w.reshape(DEPTH, 8, 128).transpose(2, 0, 1)).astype(f)
    sh["n2"] = np.ascontiguousarray(norm2_w.reshape(DEPTH, 8, 128).transpose(2, 0, 1)).astype(f)
    sh["nf"] = np.ascontiguousarray(final_norm_w.reshape(8, 128).T).astype(f)
    sh["cw"] = np.ascontiguousarray(conv_w.reshape(DEPTH, 3, 8, 128).transpose(3, 0, 1, 2)).astype(f)
    sh["cb"] = np.ascontiguousarray(conv_b.reshape(DEPTH, 8, 128).transpose(2, 0, 1)).astype(f)
    gb = np.zeros((36, DEPTH, 2), f)
    g4 = gate_b.reshape(DEPTH, 4, 4)
    gb[0:4, :, 0] = g4[:, 0, :].T
    gb[0:4, :, 1] = g4[:, 1, :].T
    gb[32:36, :, 0] = g4[:, 2, :].T
    gb[32:36, :, 1] = g4[:, 3, :].T
    sh["gb"] = gb
    sh["mnw"] = np.ascontiguousarray(np.broadcast_to(mlstm_norm_w[None], (128, DEPTH, 512))).astype(f)
    nbi, nbe = zip(*[_na_bias_tables(np.asarray(rpb[l], f)) for l in range(DEPTH)])
    sh["nbi"] = np.stack(nbi, 0)
    sh["nbe"] = np.stack(nbe, 0)
    sh["ident"] = np.eye(128, dtype=f)
    sh["trif"] = np.triu(np.ones((128, 128), f))
    sh["trib"] = np.tril(np.ones((128, 128), f))
    return sh


_CACHE = {}


def _get_nc(debug=False, stop_after=None):
    key = (debug, stop_after)
    if key not in _CACHE:
        b = Builder(debug=debug, stop_after=stop_after)
        nc = b.build()
        _CACHE[key] = (nc, b)
    return _CACHE[key]


def run(inputs, debug=False, stop_after=None, cores=8):
    x = np.asarray(inputs["x"], np.float32)
    sh = _prep_shared(*[np.asarray(inputs[k], np.float32) for k in
                        ("norm1_w", "conv_w", "conv_b", "gate_b", "mlstm_norm_w", "rpb", "norm2_w", "final_norm_w")])
    for k in ("w_in", "w_out", "w_ff1", "w_ff2"):
        sh[k] = np.ascontiguousarray(np.asarray(inputs[k], np.float32))
    nc, b = _get_nc(debug, stop_after)
    in_maps = []
    for c in range(cores):
        m = dict(sh)
        m["xT"] = np.ascontiguousarray(x[c].T)
        in_maps.append(m)
    res = run_bass_kernel_spmd(nc, in_maps, core_ids=list(range(cores)))
    return res, b


def kernel(x, norm1_w, w_in, conv_w, conv_b, gate_b, mlstm_norm_w, rpb, w_out,
           norm2_w, w_ff1, w_ff2, final_norm_w):
    inputs = dict(x=x, norm1_w=norm1_w, w_in=w_in, conv_w=conv_w, conv_b=conv_b, gate_b=gate_b,
                  mlstm_norm_w=mlstm_norm_w, rpb=rpb, w_out=w_out, norm2_w=norm2_w, w_ff1=w_ff1,
                  w_ff2=w_ff2, final_norm_w=final_norm_w)
    res, _ = run(inputs)
    out = np.stack([np.ascontiguousarray(r["outT"].T) for r in res.results], axis=0)
    return out.astype(np.float32)
```

```python
import numpy as np
from contextlib import ExitStack
import concourse.bass as bass
import concourse.mybir as mybir
from concourse.bass_utils import run_bass_kernel_spmd

F32 = mybir.dt.float32
BF16 = mybir.dt.bfloat16
AF = mybir.ActivationFunctionType
ALU = mybir.AluOpType
AX = mybir.AxisListType

T = 4096
D = 1024
DEPTH = 2
DIN = 3600
DFF = 4096
NB = 8
TB = 512
NCH = 32
EPS = 1e-6
ALPHA = 128.0 ** -0.5

ENGS = ("pe", "act", "dve", "pool", "sp")
EPOCH = 4000
NDMASEM = 8
SAME_SYNC = True


class Op:
    __slots__ = ("eng", "fn", "deps", "dma", "idx", "sig", "semk", "val", "tiny")

    def __init__(self, eng, fn, dma, tiny=False):
        self.eng = eng
        self.fn = fn
        self.dma = dma
        self.tiny = tiny
        self.deps = set()
        self.sig = False
        self.semk = None
        self.val = 0


class Ctx:
    def __init__(self, nc):
        self.nc = nc
        self.sems = {}
        self.ccount = {e: 0 for e in ENGS}
        self.dval = {}
        self.dk = {e: 0 for e in ENGS}
        self.waited = {e: {} for e in ENGS}
        self.nops = 0
        self.nwaits = 0

    def sem(self, k):
        if k not in self.sems:
            self.sems[k] = self.nc.alloc_semaphore("s_" + "_".join(str(x) for x in k))
        return self.sems[k]


class Prog:
    def __init__(self, ctx):
        self.ctx = ctx
        self.nc = ctx.nc
        self.ops = []
        self.last_w = {}
        self.readers = {}

    def op(self, eng, fn, reads=(), writes=(), dma=False, tiny=False):
        o = Op(eng, fn, dma, tiny)
        o.idx = len(self.ops)
        for r in reads:
            w = self.last_w.get(r)
            if w is not None:
                o.deps.add(w)
        for w_ in writes:
            w = self.last_w.get(w_)
            if w is not None:
                o.deps.add(w)
            for rd in self.readers.get(w_, ()):
                o.deps.add(rd)
        o.deps.discard(o.idx)
        for r in reads:
            self.readers.setdefault(r, []).append(o.idx)
        for w_ in writes:
            self.last_w[w_] = o.idx
            self.readers[w_] = []
        self.ops.append(o)
        return o

    def pe(self, fn, reads=(), writes=()):
        return self.op("pe", fn, reads, writes)

    def act(self, fn, reads=(), writes=(), tiny=False):
        return self.op("act", fn, reads, writes, tiny=tiny)

    def dve(self, fn, reads=(), writes=(), tiny=False):
        return self.op("dve", fn, reads, writes, tiny=tiny)

    def pool(self, fn, reads=(), writes=(), tiny=False):
        return self.op("pool", fn, reads, writes, tiny=tiny)

    def dma(self, fn, reads=(), writes=(), q="sp"):
        return self.op(q, fn, reads, writes, dma=True)

    def _unsynced(self, od, o):
        return (od.eng == o.eng and not od.dma and not o.dma
                and (od.eng == "pe" or not SAME_SYNC or not od.tiny))

    def emit(self):
        ctx = self.ctx
        nc = self.nc
        ops = self.ops
        per = {e: [o for o in ops if o.eng == e] for e in ENGS}
        for o in ops:
            for d in o.deps:
                od = ops[d]
                if od.dma or self._unsynced(od, o):
                    continue
                od.sig = True
        for e in ENGS:
            for o in reversed(per[e]):
                if not o.dma:
                    o.sig = True
                    break
        final = {}
        for e in ENGS:
            for o in per[e]:
                if o.dma:
                    i = ctx.dk[e] % NDMASEM
                    ctx.dk[e] += 1
                    k = ("d", e, i)
                    ctx.dval[k] = ctx.dval.get(k, 0) + 16
                    o.semk = k
                    o.val = ctx.dval[k]
                    final[k] = o.val
                elif o.sig:
                    n = ctx.ccount[e]
                    ctx.ccount[e] += 1
                    k = ("c", e, n // EPOCH)
                    o.semk = k
                    o.val = n % EPOCH + 1
                    final[k] = o.val
        for k in final:
            ctx.sem(k)
        ctx.nops += len(ops)

        def run_engine(e, eng):
            waited = ctx.waited[e]

            def wait(k, v):
                if waited.get(k, 0) >= v:
                    return
                waited[k] = v
                eng.wait_ge(ctx.sems[k], v)
                ctx.nwaits += 1

            for o in per[e]:
                need = {}
                for d in o.deps:
                    od = ops[d]
                    if od.semk is None or self._unsynced(od, o):
                        continue
                    if need.get(od.semk, 0) < od.val:
                        need[od.semk] = od.val
                if o.dma and o.val > 16:
                    if need.get(o.semk, 0) < o.val - 16:
                        need[o.semk] = o.val - 16
                for k, v in need.items():
                    wait(k, v)
                ins = o.fn(eng)
                if o.dma:
                    ins.then_inc(ctx.sems[o.semk], 16)
                elif o.sig:
                    ins.then_inc(ctx.sems[o.semk], 1)
            for k, v in final.items():
                if k[0] == "c" and k[1] == e:
                    if e == "pe" or not SAME_SYNC:
                        pass
                wait(k, v)

        with nc.Block() as block:
            block.tensor(lambda eng: run_engine("pe", eng))
            block.scalar(lambda eng: run_engine("act", eng))
            block.vector(lambda eng: run_engine("dve", eng))
            block.gpsimd(lambda eng: run_engine("pool", eng))
            block.sync(lambda eng: run_engine("sp", eng))


def mk(ap, dims, off=0):
    a = ap.ap
    return bass.AP(ap.tensor, ap.offset + off, [list(a[0])] + [list(d) for d in dims])


class Builder:
    def __init__(self, debug=False, stop_after=None):
        self.debug = debug
        self.stop_after = stop_after
        self.nc = bass.Bass("TRN2", target_bir_lowering=False)
        self.ctx = Ctx(self.nc)
        self.dbg_names = []

    def din(self, name, shape, dt=F32):
        return self.nc.dram_tensor(name, list(shape), dt, kind="ExternalInput").ap()

    def dscr(self, name, shape, dt):
        kind = "ExternalOutput" if self.debug else "Internal"
        if self.debug:
            self.dbg_names.append(name)
        return self.nc.dram_tensor(name, list(shape), dt, kind=kind).ap()

    def build(self):
        nc = self.nc
        I = {}
        I["xT"] = self.din("xT", [D, T])
        I["w_in"] = self.din("w_in", [DEPTH, D, DIN])
        I["w_out"] = self.din("w_out", [DEPTH, D, D])
        I["w_ff1"] = self.din("w_ff1", [DEPTH, D, DFF])
        I["w_ff2"] = self.din("w_ff2", [DEPTH, DFF, D])
        I["n1"] = self.din("n1", [128, DEPTH, 8])
        I["n2"] = self.din("n2", [128, DEPTH, 8])
        I["nf"] = self.din("nf", [128, 8])
        I["cw"] = self.din("cw", [128, DEPTH, 3, 8])
        I["cb"] = self.din("cb", [128, DEPTH, 8])
        I["gb"] = self.din("gb", [36, DEPTH, 2])
        I["mnw"] = self.din("mnw", [128, DEPTH, 512])
        I["nbi"] = self.din("nbi", [DEPTH, 128, 8, 5, 128])
        I["nbe"] = self.din("nbe", [DEPTH, 4, 128, 8, 4, 128])
        I["ident"] = self.din("ident", [128, 128])
        I["trif"] = self.din("trif", [128, 128])
        I["trib"] = self.din("trib", [128, 128])
        self.I = I
        self.out = nc.dram_tensor("outT", [D, T], F32, kind="ExternalOutput").ap()
        S = {}
        S["QT"] = self.dscr("QT", [512, T], BF16)
        S["KT"] = self.dscr("KT", [512, T], BF16)
        S["Ktok"] = self.dscr("Ktok", [T, 512], BF16)
        S["Vtok"] = self.dscr("Vtok", [T, 512], BF16)
        S["Osig"] = self.dscr("Osig", [T, 512], F32)
        S["G"] = self.dscr("G", [16, T], F32)
        S["QnT"] = self.dscr("QnT", [512, T], BF16)
        S["KnT"] = self.dscr("KnT", [512, T], BF16)
        S["Vn"] = self.dscr("Vn", [T, 512], BF16)
        S["YT"] = self.dscr("YT", [D, T], BF16)
        S["Mser"] = self.dscr("Mser", [8, T], F32)
        S["Wser"] = self.dscr("Wser", [8, T], F32)
        S["DCs"] = self.dscr("DCs", [8, NCH], F32)
        S["xres"] = self.dscr("xres", [D, T], F32)
        if self.debug:
            S["dbgser"] = self.dscr("dbgser", [128, 4, NCH, 8], F32)
        self.S = S
        phases = []
        for l in range(DEPTH):
            phases += [("A", l), ("B", l), ("C", l), ("D", l), ("E", l)]
        for i, (ph, l) in enumerate(phases):
            getattr(self, "phase_" + ph)(l)
            if self.stop_after is not None and i + 1 >= self.stop_after:
                break
        return nc

    def phase_A(self, l):
        nc, I, S = self.nc, self.I, self.S
        P = Prog(self.ctx)
        xsrc = I["xT"] if l == 0 else S["xres"]
        with ExitStack() as es:
            def sb(name, shape, dt):
                return es.enter_context(nc.sbuf_tensor(f"L{l}" + name, list(shape), dt))

            def psum(name, shape, dt=F32):
                return es.enter_context(nc.psum_tensor(f"L{l}" + name, list(shape), dt))

            wbf = sb("A_wbf", [128, 8, DIN], BF16)
            wst = [sb(f"A_wst{i}", [128, DIN], F32) for i in range(2)]
            n1 = sb("A_n1", [128, DEPTH, 8], F32)
            cw = sb("A_cw", [128, DEPTH, 3, 8], F32)
            cb = sb("A_cb", [128, DEPTH, 8], F32)
            cws = sb("A_cws", [128, 3, 8], F32)
            cbs = sb("A_cbs", [128, 8], F32)
            idf = sb("A_idf", [128, 128], F32)
            idb = sb("A_idb", [128, 128], BF16)
            ones = sb("A_ones", [128, 128], BF16)
            epsb = sb("A_epsb", [128, 1], F32)
            xin = [sb(f"A_xin{i}", [128, 8, 514], F32) for i in range(2)]
            xsq = sb("A_xsq", [128, 8, 514], BF16)
            rstd = sb("A_rstd", [128, 514], F32)
            hn_ = [sb(f"A_hn{i}", [128, 8, 514], BF16) for i in range(2)]
            pre = [sb(f"A_pre{i}", [128, 514], F32) for i in range(2)]
            acc = [sb(f"A_acc{i}", [128, 512], F32) for i in range(2)]
            sg = [sb(f"A_sg{i}", [128, 512], F32) for i in range(2)]
            qko = [sb(f"A_qko{i}", [128, 512], BF16) for i in range(3)]
            ktok = sb("A_ktok", [128, 4, 512], BF16)
            vtok = sb("A_vtok", [128, 4, 512], BF16)
            vntok = sb("A_vntok", [128, 4, 512], BF16)
            osig = sb("A_osig", [128, 4, 512], F32)
            gsb = sb("A_gsb", [16, 512], F32)
            ps_ssq = psum("A_ps_ssq", [128, 512])
            ps_h = psum("A_ps_h", [128, 512])
            ps_pre = [psum(f"A_ps_pre{i}", [128, 512]) for i in range(2)]
            ps_tok = [psum(f"A_ps_tok{i}", [128, 512]) for i in range(2)]
            ps_g = psum("A_ps_g", [128, 512])
            ps_t = psum("A_ps_t", [128, 1024], BF16)

            P.dma(lambda e: e.dma_start(out=n1[:], in_=I["n1"]), writes=["n1"])
            P.dma(lambda e: e.dma_start(out=cw[:], in_=I["cw"]), writes=["cw"])
            P.dma(lambda e: e.dma_start(out=cb[:], in_=I["cb"]), writes=["cb"])
            P.dma(lambda e: e.dma_start(out=idf[:], in_=I["ident"]), writes=["idf"])
            P.dve(lambda e: e.tensor_copy(out=idb[:], in_=idf[:]), reads=["idf"], writes=["idb"])
            P.dve(lambda e: e.memset(ones[:], 1.0), writes=["ones"])
            P.dve(lambda e: e.memset(epsb[:], float(D * EPS)), writes=["epsb"], tiny=True)
            P.dve(lambda e: e.tensor_scalar(out=cws[:, :, 0:4], in0=cw[:, l, :, 0:4], scalar1=ALPHA, scalar2=None, op0=ALU.mult),
                  reads=["cw"], writes=["cws"], tiny=True)
            P.dve(lambda e: e.tensor_copy(out=cws[:, :, 4:8], in_=cw[:, l, :, 4:8]), reads=["cw"], writes=["cws"], tiny=True)
            P.dve(lambda e: e.tensor_scalar(out=cbs[:, 0:4], in0=cb[:, l, 0:4], scalar1=ALPHA, scalar2=None, op0=ALU.mult),
                  reads=["cb"], writes=["cbs"], tiny=True)
            P.dve(lambda e: e.tensor_copy(out=cbs[:, 4:8], in_=cb[:, l, 4:8]), reads=["cb"], writes=["cbs"], tiny=True)
            for kc in range(8):
                st = wst[kc % 2]
                half = DIN // 2
                P.dma(lambda e, st=st, kc=kc: e.dma_start(out=st[:, 0:half], in_=I["w_in"][l, kc * 128:(kc + 1) * 128, 0:half]),
                      writes=[f"wst{kc % 2}a"])
                P.dma(lambda e, st=st, kc=kc: e.dma_start(out=st[:, half:DIN], in_=I["w_in"][l, kc * 128:(kc + 1) * 128, half:DIN]),
                      writes=[f"wst{kc % 2}b"])
                P.pool(lambda e, st=st, kc=kc: e.tensor_scalar(out=wbf[:, kc, 0:half], in0=st[:, 0:half], scalar1=n1[:, l, kc:kc + 1], scalar2=None, op0=ALU.mult),
                       reads=[f"wst{kc % 2}a", "n1"], writes=[f"wbf{kc}a"])
                P.dve(lambda e, st=st, kc=kc: e.tensor_scalar(out=wbf[:, kc, half:DIN], in0=st[:, half:DIN], scalar1=n1[:, l, kc:kc + 1], scalar2=None, op0=ALU.mult),
                      reads=[f"wst{kc % 2}b", "n1"], writes=[f"wbf{kc}b"])
            WB = [f"wbf{kc}{h}" for kc in range(8) for h in "ab"]

            def load_x(b):
                xi = xin[b % 2]
                t0 = b * TB
                lo = t0 - 1
                hi = t0 + TB + 1
                c0 = 0
                tok = f"xin{b % 2}"
                if b == 0:
                    lo = 0
                    c0 = 1
                    P.dve(lambda e, xi=xi: e.memset(xi[:, :, 0:1], 0.0), writes=[tok], tiny=True)
                if b == NB - 1:
                    hi = T
                    P.dve(lambda e, xi=xi: e.memset(xi[:, :, 513:514], 0.0), writes=[tok], tiny=True)
                n = hi - lo
                src = xsrc[:, lo:hi].rearrange("(kc p) t -> p kc t", p=128)
                P.dma(lambda e, xi=xi, src=src, c0=c0, n=n: e.dma_start(out=xi[:, :, c0:c0 + n], in_=src), writes=[tok])

            def prep(b):
                xi = xin[b % 2]
                xt = f"xin{b % 2}"
                hn = hn_[b % 2]
                P.act(lambda e, xi=xi: e.activation(out=xsq[:], in_=xi[:], func=AF.Square), reads=[xt], writes=["xsq"])
                for kc in range(8):
                    P.pe(lambda e, kc=kc: e.matmul(ps_ssq[:], lhsT=ones[:], rhs=xsq[:, kc, 1:513], start=(kc == 0), stop=(kc == 7)),
                         reads=["ones", "xsq"], writes=["ps_ssq"])
                for kc in range(8):
                    P.pe(lambda e, kc=kc: e.matmul(ps_h[:, 0:2], lhsT=ones[:], rhs=mk(xsq[:, kc, 0:1], [[513, 2]]), start=(kc == 0), stop=(kc == 7)),
                         reads=["ones", "xsq"], writes=["ps_hs"])
                P.act(lambda e: e.activation(out=rstd[:, 1:513], in_=ps_ssq[:], func=AF.Ln, bias=epsb[:, 0:1]), reads=["ps_ssq", "epsb"], writes=["rstd"])
                P.act(lambda e: e.activation(out=mk(rstd[:, 0:1], [[513, 2]]), in_=ps_h[:, 0:2], func=AF.Ln, bias=epsb[:, 0:1]), reads=["ps_hs", "epsb"], writes=["rstd"], tiny=True)
                P.act(lambda e: e.activation(out=rstd[:], in_=rstd[:], func=AF.Exp, scale=-0.5), reads=["rstd"], writes=["rstd"])
                for kc in range(8):
                    P.dve(lambda e, kc=kc, xi=xi, hn=hn: e.scalar_tensor_tensor(out=hn[:, kc, :], in0=xi[:, kc, :], scalar=32.0, in1=rstd[:], op0=ALU.mult, op1=ALU.mult),
                          reads=[xt, "rstd"], writes=[f"hn{b % 2}_{kc}"])

            load_x(0)
            load_x(1)
            prep(0)
            qi = 0
            pending = []
            def main(b):
                nonlocal qi
                hn = hn_[b % 2]
                t0 = b * TB
                HN = [f"hn{b % 2}_{kc}" for kc in range(8)]

                for fc in range(8):
                    pp = ps_pre[qi % 2]
                    ppt = f"ps_pre{qi % 2}"
                    pr = pre[qi % 2]
                    prt = f"pre{qi % 2}"
                    ac = acc[qi % 2]
                    act_ = f"acc{qi % 2}"
                    sgg = sg[qi % 2]
                    sgt = f"sg{qi % 2}"
                    qo = qko[qi % 3]
                    qot = f"qko{qi % 3}"
                    qi += 1
                    c0 = fc * 128
                    for kc in range(8):
                        P.pe(lambda e, kc=kc, pp=pp, c0=c0: e.matmul(pp[:], lhsT=wbf[:, kc, c0:c0 + 128], rhs=hn[:, kc, 1:513], start=(kc == 0), stop=(kc == 7)),
                             reads=WB + HN, writes=[ppt])
                    hc = 8 + 2 * (fc % 2)
                    for kc in range(8):
                        P.pe(lambda e, kc=kc, c0=c0, hc=hc: e.matmul(ps_h[:, hc:hc + 2], lhsT=wbf[:, kc, c0:c0 + 128], rhs=mk(hn[:, kc, 0:1], [[513, 2]]), start=(kc == 0), stop=(kc == 7)),
                             reads=WB + HN, writes=[f"ps_hp{fc % 2}"])
                    P.act(lambda e, pr=pr, pp=pp: e.activation(out=pr[:, 1:513], in_=pp[:], func=AF.Copy), reads=[ppt], writes=[prt])
                    P.act(lambda e, pr=pr, hc=hc: e.activation(out=mk(pr[:, 0:1], [[513, 2]]), in_=ps_h[:, hc:hc + 2], func=AF.Copy),
                          reads=[f"ps_hp{fc % 2}"], writes=[prt], tiny=True)
                    def post(fc=fc, pr=pr, prt=prt, ac=ac, act_=act_, sgg=sgg, sgt=sgt, qo=qo, qot=qot, t0=t0):
                        P.pool(lambda e, ac=ac, pr=pr, fc=fc: e.tensor_scalar(out=ac[:], in0=pr[:, 0:512], scalar1=cws[:, 0, fc:fc + 1], scalar2=None, op0=ALU.mult),
                               reads=[prt, "cws"], writes=[act_])
                        P.dve(lambda e, ac=ac, pr=pr, fc=fc: e.scalar_tensor_tensor(out=ac[:], in0=pr[:, 1:513], scalar=cws[:, 1, fc:fc + 1], in1=ac[:], op0=ALU.mult, op1=ALU.add),
                               reads=[prt, "cws", act_], writes=[act_])
                        P.dve(lambda e, ac=ac, pr=pr, fc=fc: e.scalar_tensor_tensor(out=ac[:], in0=pr[:, 2:514], scalar=cws[:, 2, fc:fc + 1], in1=ac[:], op0=ALU.mult, op1=ALU.add),
                               reads=[prt, "cws", act_], writes=[act_])
                        sc = (1.0 / ALPHA) if fc < 4 else 1.0
                        P.act(lambda e, ac=ac, sgg=sgg, fc=fc, sc=sc: e.activation(out=sgg[:], in_=ac[:], func=AF.Sigmoid, bias=cb[:, l, fc:fc + 1], scale=sc),
                              reads=[act_, "cb"], writes=[sgt])
                        P.dve(lambda e, ac=ac, sgg=sgg, fc=fc, qo=qo: e.scalar_tensor_tensor(out=qo[:], in0=ac[:], scalar=cbs[:, fc:fc + 1], in1=sgg[:], op0=ALU.add, op1=ALU.mult),
                              reads=[act_, sgt, "cbs"], writes=[qot])
                        dst = (S["QT"] if fc < 4 else S["KT"])[(fc % 4) * 128:(fc % 4 + 1) * 128, t0:t0 + TB]
                        P.dma(lambda e, qo=qo, dst=dst: e.dma_start(out=dst, in_=qo[:]), reads=[qot], writes=["dram_qk"])
                        if fc >= 4:
                            h = fc - 4
                            half = (h % 2) * 512
                            for tt in range(4):
                                P.pe(lambda e, qo=qo, tt=tt, half=half: e.transpose(ps_t[:, half + tt * 128: half + (tt + 1) * 128], qo[:, tt * 128:(tt + 1) * 128], idb[:]),
                                     reads=[qot, "idb"], writes=[f"ps_t{h % 2}"])
                            P.act(lambda e, h=h, half=half: e.activation(out=ktok[:, :, h * 128:(h + 1) * 128], in_=mk(ps_t[:, half:half + 1], [[128, 4], [1, 128]]), func=AF.Copy),
                                  reads=[f"ps_t{h % 2}"], writes=["ktok"])
                    if pending:
                        pending.pop()()
                    pending.append(post)

                ti = 0
                for (c0, kind) in ((1024, "v"), (1536, "o"), (3088, "vn")):
                    for tt in range(4):
                        pt = ps_tok[ti % 2]
                        ptt = f"ps_tok{ti % 2}"
                        ti += 1
                        for kc in range(8):
                            P.pe(lambda e, kc=kc, pt=pt, tt=tt, c0=c0: e.matmul(pt[:], lhsT=hn[:, kc, 1 + tt * 128:1 + (tt + 1) * 128], rhs=wbf[:, kc, c0:c0 + 512], start=(kc == 0), stop=(kc == 7)),
                                 reads=WB + HN, writes=[ptt])
                        if pending:
                            pending.pop()()
                            P.dma(lambda e, t0=t0: e.dma_start(out=S["Ktok"][t0:t0 + TB, :].rearrange("(tt p) c -> p tt c", p=128), in_=ktok[:]),
                                  reads=["ktok"], writes=["dram_ktok"])
                            if b + 1 < NB:
                                prep(b + 1)
                            if b + 2 < NB:
                                load_x(b + 2)
                        if kind == "v":
                            P.act(lambda e, pt=pt, tt=tt: e.activation(out=vtok[:, tt, :], in_=pt[:], func=AF.Copy), reads=[ptt], writes=["vtok"])
                        elif kind == "vn":
                            P.dve(lambda e, pt=pt, tt=tt: e.tensor_copy(out=vntok[:, tt, :], in_=pt[:]), reads=[ptt], writes=["vntok"])
                        else:
                            P.act(lambda e, pt=pt, tt=tt: e.activation(out=osig[:, tt, :], in_=pt[:], func=AF.Sigmoid), reads=[ptt], writes=["osig"])
                P.dma(lambda e, t0=t0: e.dma_start(out=S["Vtok"][t0:t0 + TB, :].rearrange("(tt p) c -> p tt c", p=128), in_=vtok[:]),
                      reads=["vtok"], writes=["dram_vtok"])
                P.dma(lambda e, t0=t0: e.dma_start(out=S["Vn"][t0:t0 + TB, :].rearrange("(tt p) c -> p tt c", p=128), in_=vntok[:]),
                      reads=["vntok"], writes=["dram_vn"])
                P.dma(lambda e, t0=t0: e.dma_start(out=S["Osig"][t0:t0 + TB, :].rearrange("(tt p) c -> p tt c", p=128), in_=osig[:]),
                      reads=["osig"], writes=["dram_osig"])

                for kc in range(8):
                    P.pe(lambda e, kc=kc: e.matmul(ps_g[0:16, :], lhsT=wbf[:, kc, 2048:2064], rhs=hn[:, kc, 1:513], start=(kc == 0), stop=(kc == 7)),
                         reads=WB + HN, writes=["ps_g"])
                P.dve(lambda e: e.tensor_copy(out=gsb[:], in_=ps_g[0:16, :]), reads=["ps_g"], writes=["gsb"])
                P.dma(lambda e, t0=t0: e.dma_start(out=S["G"][:, t0:t0 + TB], in_=gsb[:]), reads=["gsb"], writes=["dram_g"])

                for j in range(8):
                    pp = ps_pre[qi % 2]
                    ppt = f"ps_pre{qi % 2}"
                    qo = qko[qi % 3]
                    qot = f"qko{qi % 3}"
                    qi += 1
                    c0 = 2064 + j * 128
                    for kc in range(8):
                        P.pe(lambda e, kc=kc, pp=pp, c0=c0: e.matmul(pp[:], lhsT=wbf[:, kc, c0:c0 + 128], rhs=hn[:, kc, 1:513], start=(kc == 0), stop=(kc == 7)),
                             reads=WB + HN, writes=[ppt])
                    sc = 0.125 if j < 4 else 1.0
                    P.act(lambda e, pp=pp, qo=qo, sc=sc: e.activation(out=qo[:], in_=pp[:], func=AF.Copy, scale=sc), reads=[ppt], writes=[qot])
                    dst = (S["QnT"] if j < 4 else S["KnT"])[(j % 4) * 128:(j % 4 + 1) * 128, t0:t0 + TB]
                    P.dma(lambda e, qo=qo, dst=dst: e.dma_start(out=dst, in_=qo[:]), reads=[qot], writes=["dram_qkn"])
            for b in range(NB):
                main(b)
            P.emit()

    def phase_B(self, l):
        nc, I, S = self.nc, self.I, self.S
        P = Prog(self.ctx)
        with ExitStack() as es:
            def sb(name, shape, dt):
                return es.enter_context(nc.sbuf_tensor(f"L{l}" + name, list(shape), dt))

            def psum(name, shape, dt=F32):
                return es.enter_context(nc.psum_tensor(f"L{l}" + name, list(shape), dt))

            NP = 36
            idf = sb("B_idf", [128, 128], F32)
            idb = sb("B_idb", [128, 128], BF16)
            trif = sb("B_trif", [128, 128], F32)
            trib = sb("B_trib", [128, 128], F32)
            tokser = sb("B_tokser", [128, 4, NCH, 8], F32)
            dcb = sb("B_dcb", [128, 8, NCH], F32)
            mnw = sb("B_mnw", [128, 512], F32)
            epsb = sb("B_epsb", [128, 1], F32)
            es1 = ExitStack()

            def sb1(name, shape, dt):
                return es1.enter_context(nc.sbuf_tensor(f"L{l}" + name, list(shape), dt))

            ser = {n: sb1("B_" + n, [NP, T], F32) for n in ("Ipre", "F", "Bp", "U", "M", "tmp", "W", "EM", "ES")}
            onesr = sb1("B_onesr", [NP, T], F32)
            gb = sb1("B_gb", [NP, DEPTH, 2], F32)
            ngb = sb1("B_ngb", [NP, 1], F32)
            mend = sb1("B_mend", [NP, NCH], F32)
            mprev = sb1("B_mprev", [NP, NCH], F32)
            dcs = sb1("B_dcs", [NP, NCH], F32)
            ps_ser = es1.enter_context(nc.psum_tensor(f"L{l}B_ps_ser", [128, 512], F32))

            for n in ("Ipre", "F"):
                P.dve(lambda e, n=n: e.memset(ser[n][:], 0.0), writes=[n])
            P.pool(lambda e: e.memset(onesr[:], 1.0), writes=["onesr"])
            P.dve(lambda e: e.memset(epsb[:], float(128 * EPS)), writes=["epsb"], tiny=True)
            P.dma(lambda e: e.dma_start(out=gb[:], in_=I["gb"]), writes=["gb"])
            P.dma(lambda e: e.dma_start(out=idf[:], in_=I["ident"]), writes=["idf"])
            P.dma(lambda e: e.dma_start(out=trif[:], in_=I["trif"]), writes=["trif"])
            P.dma(lambda e: e.dma_start(out=trib[:], in_=I["trib"]), writes=["trib"])
            P.dma(lambda e: e.dma_start(out=mnw[:], in_=I["mnw"][:, l, :]), writes=["mnw"])
            P.dve(lambda e: e.tensor_copy(out=idb[:], in_=idf[:]), reads=["idf"], writes=["idb"])
            P.dve(lambda e: e.tensor_scalar(out=mnw[:], in0=mnw[:], scalar1=float(128.0 ** 0.5), scalar2=None, op0=ALU.mult), reads=["mnw"], writes=["mnw"])
            G = S["G"]
            P.dma(lambda e: e.dma_start(out=ser["Ipre"][0:4, :], in_=G[0:4, :]), writes=["Ipre"])
            P.dma(lambda e: e.dma_start(out=ser["F"][0:4, :], in_=G[4:8, :]), writes=["F"])
            P.dma(lambda e: e.dma_start(out=ser["Ipre"][32:36, :], in_=G[8:12, :]), writes=["Ipre"])
            P.dma(lambda e: e.dma_start(out=ser["F"][32:36, :], in_=G[12:16, :]), writes=["F"])
            P.dve(lambda e: e.tensor_scalar(out=ngb[:], in0=gb[:, l, 1:2], scalar1=-1.0, scalar2=None, op0=ALU.mult), reads=["gb"], writes=["ngb"], tiny=True)
            P.dve(lambda e: e.tensor_scalar(out=ser["Ipre"][:], in0=ser["Ipre"][:], scalar1=gb[:, l, 0:1], scalar2=None, op0=ALU.add),
                  reads=["Ipre", "gb"], writes=["Ipre"])
            P.act(lambda e: e.activation(out=ser["tmp"][:], in_=ser["F"][:], func=AF.Exp, bias=ngb[:, 0:1], scale=-1.0), reads=["F", "ngb"], writes=["tmp"])
            P.act(lambda e: e.activation(out=ser["F"][:], in_=ser["tmp"][:], func=AF.Ln, bias=1.0), reads=["tmp"], writes=["F"])

            def rev(ap):
                a = ap.ap
                return bass.AP(ap.tensor, ap.offset + (a[-1][1] - 1) * a[-1][0], [list(p) for p in a[:-1]] + [[-a[-1][0], a[-1][1]]])

            def scan(out, data, op1, wtok, rtok):
                P.dve(lambda e: e.tensor_tensor_scan(out=out[0:4, :], data0=onesr[0:4, :], data1=data[0:4, :], initial=0.0, op0=ALU.mult, op1=op1),
                      reads=[rtok, "onesr"], writes=[wtok])
                P.dve(lambda e: e.tensor_tensor_scan(out=rev(out[32:36, :]), data0=onesr[32:36, :], data1=rev(data[32:36, :]), initial=0.0, op0=ALU.mult, op1=op1),
                      reads=[rtok, "onesr"], writes=[wtok])

            P.dve(lambda e: e.memset(ser["Bp"][:], 0.0), writes=["Bp"])
            P.pool(lambda e: e.memset(ser["M"][:], 0.0), writes=["M"])
            scan(ser["Bp"], ser["F"], ALU.add, "Bp", "F")
            P.dve(lambda e: e.tensor_tensor(out=ser["U"][:], in0=ser["Ipre"][:], in1=ser["Bp"][:], op=ALU.add), reads=["Ipre", "Bp"], writes=["U"])
            scan(ser["M"], ser["U"], ALU.max, "M", "U")
            P.dve(lambda e: e.tensor_tensor(out=ser["tmp"][:], in0=ser["Bp"][:], in1=ser["M"][:], op=ALU.subtract), reads=["Bp", "M"], writes=["tmp"])
            P.act(lambda e: e.activation(out=ser["EM"][:], in_=ser["tmp"][:], func=AF.Exp), reads=["tmp"], writes=["EM"])
            Mv = ser["M"]
            P.dve(lambda e: e.memset(mprev[:], 0.0), writes=["mprev"], tiny=True)
            P.dve(lambda e: e.memset(mend[:], 0.0), writes=["mend"], tiny=True)
            P.dve(lambda e: e.tensor_copy(out=mend[0:4, :], in_=mk(Mv[0:4, 127:128], [[128, NCH]])), reads=["M"], writes=["mend"], tiny=True)
            P.dve(lambda e: e.tensor_copy(out=mend[32:36, :], in_=mk(Mv[32:36, 0:1], [[128, NCH]])), reads=["M"], writes=["mend"], tiny=True)
            P.dve(lambda e: e.tensor_copy(out=mprev[0:4, 1:NCH], in_=mk(Mv[0:4, 127:128], [[128, NCH - 1]])), reads=["M"], writes=["mprev"], tiny=True)
            P.dve(lambda e: e.tensor_copy(out=mprev[32:36, 0:NCH - 1], in_=mk(Mv[32:36, 128:129], [[128, NCH - 1]])), reads=["M"], writes=["mprev"], tiny=True)
            P.dve(lambda e: e.tensor_tensor(out=ser["tmp"][:].rearrange("p (c t) -> p c t", t=128), in0=mk(mprev[:], [[1, NCH], [0, 128]]),
                                            in1=ser["M"][:].rearrange("p (c t) -> p c t", t=128), op=ALU.subtract),
                  reads=["mprev", "M", "EM"], writes=["tmp"])
            P.act(lambda e: e.activation(out=ser["W"][:], in_=ser["tmp"][:], func=AF.Exp), reads=["tmp"], writes=["W"])
            P.dve(lambda e: e.tensor_tensor(out=ser["tmp"][:].rearrange("p (c t) -> p c t", t=128), in0=ser["U"][:].rearrange("p (c t) -> p c t", t=128),
                                            in1=mk(mend[:], [[1, NCH], [0, 128]]), op=ALU.subtract),
                  reads=["mend", "U", "W"], writes=["tmp"])
            P.act(lambda e: e.activation(out=ser["ES"][:], in_=ser["tmp"][:], func=AF.Exp), reads=["tmp"], writes=["ES"])
            P.dve(lambda e: e.tensor_tensor(out=dcs[:], in0=mprev[:], in1=mend[:], op=ALU.subtract), reads=["mprev", "mend"], writes=["dcs"], tiny=True)
            P.act(lambda e: e.activation(out=dcs[:], in_=dcs[:], func=AF.Exp), reads=["dcs"], writes=["dcs"], tiny=True)
            P.dma(lambda e: e.dma_start(out=S["Mser"][0:4, :], in_=ser["M"][0:4, :]), reads=["M"], writes=["dram_M"])
            P.dma(lambda e: e.dma_start(out=S["Mser"][4:8, :], in_=ser["M"][32:36, :]), reads=["M"], writes=["dram_M"])
            P.dma(lambda e: e.dma_start(out=S["Wser"][0:4, :], in_=ser["W"][0:4, :]), reads=["W"], writes=["dram_W"])
            P.dma(lambda e: e.dma_start(out=S["Wser"][4:8, :], in_=ser["W"][32:36, :]), reads=["W"], writes=["dram_W"])
            P.dma(lambda e: e.dma_start(out=S["DCs"][0:4, :], in_=dcs[0:4, :]), reads=["dcs"], writes=["dram_DC"])
            P.dma(lambda e: e.dma_start(out=S["DCs"][4:8, :], in_=dcs[32:36, :]), reads=["dcs"], writes=["dram_DC"])
            P.dma(lambda e: e.dma_start(out=dcb[:], in_=bass.AP(S["DCs"].tensor, S["DCs"].offset, [[0, 128], [NCH, 8], [1, NCH]])),
                  reads=["dram_DC"], writes=["dcb"])
            for si, n in enumerate(("U", "W", "EM", "ES")):
                for c0 in range(0, NCH, 8):
                    for c in range(c0, c0 + 8):
                        P.pe(lambda e, n=n, c=c, c0=c0: e.matmul(ps_ser[:, (c - c0) * 64:(c - c0) * 64 + NP], lhsT=ser[n][:, c * 128:(c + 1) * 128], rhs=idf[0:NP, 0:NP], start=True, stop=True),
                             reads=[n, "idf"], writes=["ps_ser"])
                    P.dve(lambda e, si=si, c0=c0: e.tensor_copy(out=tokser[:, si, c0:c0 + 8, 0:4], in_=mk(ps_ser[:, 0:1], [[64, 8], [1, 4]])),
                          reads=["ps_ser"], writes=["tokser"])
                    P.dve(lambda e, si=si, c0=c0: e.tensor_copy(out=tokser[:, si, c0:c0 + 8, 4:8], in_=mk(ps_ser[:, 32:33], [[64, 8], [1, 4]])),
                          reads=["ps_ser"], writes=["tokser"])
            if self.debug:
                P.dma(lambda e: e.dma_start(out=S["dbgser"], in_=tokser[:]), reads=["tokser"], writes=["dram_dbgser"])

            P.emit()
            es1.close()

            cst = sb("B_cst", [128, 2, NCH, 4 * 129], BF16)
            c32 = sb("B_c32", [128, 2, 4 * 129], F32)
            v1_ = [sb(f"B_v1{i}", [128, 4, 129], BF16) for i in range(4)]
            P = Prog(self.ctx)
            es2 = ExitStack()
            kt_ = [es2.enter_context(nc.sbuf_tensor(f"L{l}B_ktok{i}", [128, 512], BF16)) for i in range(4)]
            vs_ = [es2.enter_context(nc.sbuf_tensor(f"L{l}B_vs{i}", [128, 4, 129], BF16)) for i in range(4)]
            ps_c = [[es2.enter_context(nc.psum_tensor(f"L{l}B_ps_c{d}{hp}", [128, 512], F32)) for hp in range(2)] for d in range(2)]
            P.dve(lambda e: e.memset(c32[:], 0.0), writes=[f"c32_{d}_{h}" for d in range(2) for h in range(4)])
            for i in range(4):
                P.pool(lambda e, i=i: e.memset(v1_[i][:, :, 128:129], 1.0), writes=[f"v1{i}"])
            li = 0
            for step in range(NCH):
                for d in range(2):
                    c = step if d == 0 else NCH - 1 - step
                    P.act(lambda e, d=d, c=c: e.activation(out=cst[:, d, c, :], in_=c32[:, d, :], func=AF.Copy),
                          reads=[f"c32_{d}_{h}" for h in range(4)], writes=[f"cst{d}_{c}"])
                    if step == NCH - 1:
                        continue
                    kt = kt_[li % 4]
                    v1 = v1_[li % 4]
                    vs = vs_[li % 4]
                    ktt = f"ktokb{li % 4}"
                    v1t = f"v1{li % 4}"
                    vst = f"vs{li % 4}"
                    li += 1
                    P.dma(lambda e, kt=kt, c=c: e.dma_start(out=kt[:], in_=S["Ktok"][c * 128:(c + 1) * 128, :]), writes=[ktt])
                    P.dma(lambda e, v1=v1, c=c: e.dma_start(out=v1[:, :, 0:128], in_=S["Vtok"][c * 128:(c + 1) * 128, :].rearrange("p (h x) -> p h x", h=4)), writes=[v1t])
                    P.dve(lambda e, vs=vs, v1=v1, c=c, d=d: e.tensor_tensor(out=vs[:], in0=v1[:], in1=mk(tokser[:, 3, c, d * 4:d * 4 + 1], [[1, 4], [0, 129]]), op=ALU.mult),
                           reads=[v1t, "tokser"], writes=[vst])
                    for hp in range(2):
                        pc = ps_c[d][hp]
                        for hh in range(2):
                            h = hp * 2 + hh
                            P.pe(lambda e, pc=pc, hh=hh, h=h, kt=kt, vs=vs: e.matmul(pc[:, hh * 129:(hh + 1) * 129], lhsT=kt[:, h * 128:(h + 1) * 128], rhs=vs[:, h, :], start=True, stop=True),
                                 reads=[ktt, vst], writes=[f"ps_c{d}{hp}{hh}"])
                        for hh in range(2):
                            h = hp * 2 + hh
                            P.dve(lambda e, pc=pc, hh=hh, h=h, d=d, c=c: e.scalar_tensor_tensor(out=c32[:, d, h * 129:(h + 1) * 129], in0=c32[:, d, h * 129:(h + 1) * 129],
                                                                                     scalar=dcb[:, d * 4 + h, c:c + 1], in1=pc[:, hh * 129:(hh + 1) * 129], op0=ALU.mult, op1=ALU.add),
                                  reads=[f"ps_c{d}{hp}{hh}", "dcb", f"c32_{d}_{h}"], writes=[f"c32_{d}_{h}"])
            P.emit()
            es2.close()

            import os
            if os.environ.get("BSTOP") == "2":
                return
            P = Prog(self.ctx)
            two = range(2)
            qT_ = [sb(f"B_qT{i}", [128, 4, 128], BF16) for i in two]
            kT_ = [sb(f"B_kT{i}", [128, 4, 128], BF16) for i in two]
            mb_ = [sb(f"B_mb{i}", [128, 8, 128], F32) for i in two]
            wb_ = [sb(f"B_wb{i}", [128, 8, 128], F32) for i in two]
            og_ = [sb(f"B_og{i}", [128, 512], F32) for i in two]
            gg_ = [sb(f"B_gg{i}", [128, 512], F32) for i in range(3)]
            sm_ = [sb(f"B_sm{i}", [128, 8, 128], F32) for i in two]
            ee_ = [sb(f"B_ee{i}", [128, 8, 128], F32) for i in two]
            pp_ = [sb(f"B_pp{i}", [128, 8, 128], BF16) for i in two]
            qw_ = [sb(f"B_qw{i}", [128, 8, 128], BF16) for i in two]
            hs_ = [sb(f"B_hs{i}", [128, 8, 128], F32) for i in two]
            hall_ = [sb(f"B_hall{i}", [128, 512], F32) for i in two]
            hsq_ = [sb(f"B_hsq{i}", [128, 512], F32) for i in two]
            dd_ = [sb(f"B_dd{i}", [128, 8], F32) for i in two]
            ss_ = [sb(f"B_ss{i}", [128, 4], F32) for i in two]
            ybf_ = [sb(f"B_ybf{i}", [128, 512], BF16) for i in two]
            yT_ = [sb(f"B_yT{i}", [128, 4, 128], BF16) for i in two]
            ps_s = [psum(f"B_ps_s{i}", [128, 512]) for i in two]
            ps_o = [psum(f"B_ps_o{i}", [128, 1024]) for i in two]
            ps_d = psum("B_ps_d", [128, 512])
            ps_y = psum("B_ps_y", [128, 1024], BF16)
            for i in range(4):
                P.pool(lambda e, i=i: e.memset(v1_[i][:, :, 128:129], 1.0), writes=[f"v1{i}"])
            tri4 = sb("B_tri4", [128, 8, 128], F32)
            for i in range(4):
                P.pool(lambda e, i=i: e.tensor_copy(out=tri4[:, i, :], in_=trif[:]), writes=["tri4"])
                P.pool(lambda e, i=i: e.tensor_copy(out=tri4[:, 4 + i, :], in_=trib[:]), writes=["tri4"])

            def b3_load(c):
                s = c % 2
                tk = c * 128
                P.dma(lambda e: e.dma_start(out=qT_[s][:], in_=S["QT"][:, tk:tk + 128].rearrange("(h p) t -> p h t", p=128)), writes=[f"qT{s}"])
                P.dma(lambda e: e.dma_start(out=kT_[s][:], in_=S["KT"][:, tk:tk + 128].rearrange("(h p) t -> p h t", p=128)), writes=[f"kT{s}"])
                P.dma(lambda e: e.dma_start(out=mb_[s][:], in_=bass.AP(S["Mser"].tensor, S["Mser"].offset + tk, [[0, 128], [T, 8], [1, 128]])), writes=[f"mb{s}"])
                P.dma(lambda e: e.dma_start(out=wb_[s][:], in_=bass.AP(S["Wser"].tensor, S["Wser"].offset + tk, [[0, 128], [T, 8], [1, 128]])), writes=[f"wb{s}"])
                P.dma(lambda e: e.dma_start(out=og_[s][:], in_=S["Osig"][tk:tk + 128, :]), writes=[f"og{s}"])
                P.dma(lambda e: e.dma_start(out=v1_[c % 4][:, :, 0:128], in_=S["Vtok"][tk:tk + 128, :].rearrange("p (h x) -> p h x", h=4)), writes=[f"v1{c % 4}"])

            def b3_stage(c, stage):
                s = c % 2
                tk = c * 128
                qT, kT, mb, wb, og, gg = qT_[s], kT_[s], mb_[s], wb_[s], og_[s], gg_[c % 3]
                sm, ee, pp, qw, hs, hall, hsq, dd, ss, ybf, yT = sm_[s], ee_[s], pp_[s], qw_[s], hs_[s], hall_[s], hsq_[s], dd_[s], ss_[s], ybf_[s], yT_[s]
                ps, po = ps_s[s], ps_o[s]
                v1 = v1_[c % 4]
                v1t = f"v1{c % 4}"
                if stage == 1:
                    for h in range(4):
                        P.pe(lambda e, ps=ps, kT=kT, qT=qT, h=h: e.matmul(ps[:, h * 128:(h + 1) * 128], lhsT=kT[:, h, :], rhs=qT[:, h, :], start=True, stop=True),
                             reads=[f"kT{s}", f"qT{s}"], writes=[f"ps_s{s}"])
                    psv = ps[:].rearrange("p (h t) -> p h t", h=4)
                    P.dve(lambda e, sm=sm, psv=psv: e.tensor_tensor(out=sm[:, 0:4, :], in0=psv, in1=tri4[:, 0:4, :], op=ALU.mult),
                          reads=[f"ps_s{s}", "tri4"], writes=[f"smf{s}"])
                    P.dve(lambda e, sm=sm, psv=psv: e.tensor_tensor(out=sm[:, 4:8, :], in0=psv, in1=tri4[:, 4:8, :], op=ALU.mult),
                          reads=[f"ps_s{s}", "tri4"], writes=[f"smb{s}"])
                    for r in range(8):
                        P.act(lambda e, ee=ee, mb=mb, r=r, c=c: e.activation(out=ee[:, r, :], in_=mb[:, r, :], func=AF.Exp, bias=tokser[:, 0, c, r:r + 1], scale=-1.0),
                              reads=[f"mb{s}", "tokser"], writes=[f"ee{s}"])
                    P.dve(lambda e, pp=pp, ee=ee, sm=sm: e.scalar_tensor_tensor(out=pp[:], in0=ee[:], scalar=1.0, in1=sm[:], op0=ALU.min, op1=ALU.mult),
                          reads=[f"ee{s}", f"smf{s}", f"smb{s}"], writes=[f"pp{s}"])
                    for d in range(2):
                        P.pool(lambda e, qw=qw, qT=qT, wb=wb, d=d: e.tensor_tensor(out=qw[:, d * 4:(d + 1) * 4, :], in0=qT[:], in1=wb[:, d * 4:(d + 1) * 4, :], op=ALU.mult),
                               reads=[f"qT{s}", f"wb{s}"], writes=[f"qw{s}"])
                    P.pool(lambda e, gg=gg, og=og: e.tensor_tensor(out=gg[:], in0=og[:], in1=mnw[:], op=ALU.mult), reads=[f"og{s}", "mnw"], writes=[f"gg{c % 3}"])
                    for r in range(8):
                        d, h = r // 4, r % 4
                        P.pe(lambda e, po=po, qw=qw, r=r, d=d, h=h, c=c: e.matmul(po[:, r * 128:(r + 1) * 128], lhsT=qw[:, r, :], rhs=cst[:, d, c, h * 129:h * 129 + 128], start=True, stop=False),
                             reads=[f"qw{s}", f"cst{d}_{c}"], writes=[f"ps_o{s}"])
                        P.pe(lambda e, po=po, pp=pp, v1=v1, r=r, h=h: e.matmul(po[:, r * 128:(r + 1) * 128], lhsT=pp[:, r, :], rhs=v1[:, h, 0:128], start=False, stop=True),
                             reads=[f"pp{s}", v1t], writes=[f"ps_o{s}"])
                        P.pe(lambda e, qw=qw, r=r, d=d, h=h, c=c, s=s: e.matmul(ps_d[:, s * 8 + r:s * 8 + r + 1], lhsT=qw[:, r, :], rhs=cst[:, d, c, h * 129 + 128:h * 129 + 129], start=True, stop=False),
                             reads=[f"qw{s}", f"cst{d}_{c}"], writes=[f"ps_d{s}"])
                        P.pe(lambda e, pp=pp, v1=v1, r=r, h=h, s=s: e.matmul(ps_d[:, s * 8 + r:s * 8 + r + 1], lhsT=pp[:, r, :], rhs=v1[:, h, 128:129], start=False, stop=True),
                             reads=[f"pp{s}", v1t], writes=[f"ps_d{s}"])
                if stage == 2:
                    P.act(lambda e, dd=dd, s=s: e.activation(out=dd[:], in_=ps_d[:, s * 8:s * 8 + 8], func=AF.Copy), reads=[f"ps_d{s}"], writes=[f"dd{s}"], tiny=True)
                    P.dve(lambda e, dd=dd: e.scalar_tensor_tensor(out=dd[:], in0=dd[:], scalar=-1.0, in1=dd[:], op0=ALU.mult, op1=ALU.max), reads=[f"dd{s}"], writes=[f"dd{s}"], tiny=True)
                    P.dve(lambda e, dd=dd, c=c: e.tensor_tensor(out=dd[:], in0=dd[:], in1=tokser[:, 2, c, :], op=ALU.max), reads=[f"dd{s}", "tokser"], writes=[f"dd{s}"], tiny=True)
                    P.dve(lambda e, dd=dd: e.reciprocal(out=dd[:], in_=dd[:]), reads=[f"dd{s}"], writes=[f"dd{s}"], tiny=True)
                    P.act(lambda e, hs=hs, po=po: e.activation(out=hs[:], in_=po[:].rearrange("p (r t) -> p r t", r=8), func=AF.Copy), reads=[f"ps_o{s}"], writes=[f"hs{s}"])
                    P.dve(lambda e, hs=hs, dd=dd: e.tensor_tensor(out=hs[:], in0=hs[:], in1=mk(dd[:, 0:1], [[1, 8], [0, 128]]), op=ALU.mult),
                          reads=[f"hs{s}", f"dd{s}"], writes=[f"hs{s}"])
                    P.pool(lambda e, hall=hall, hs=hs: e.tensor_tensor(out=hall[:].rearrange("p (h x) -> p h x", h=4), in0=hs[:, 0:4, :], in1=hs[:, 4:8, :], op=ALU.add),
                           reads=[f"hs{s}"], writes=[f"hall{s}"])
                if stage == 3:
                    P.pool(lambda e, hall=hall, hsq=hsq: e.tensor_tensor(out=hsq[:], in0=hall[:], in1=hall[:], op=ALU.mult), reads=[f"hall{s}"], writes=[f"hsq{s}"])
                    P.dve(lambda e, ss=ss, hsq=hsq: e.tensor_reduce(out=ss[:], in_=hsq[:].rearrange("p (h x) -> p h x", h=4), axis=AX.X, op=ALU.add), reads=[f"hsq{s}"], writes=[f"ss{s}"], tiny=True)
                    P.act(lambda e, ss=ss: e.activation(out=ss[:], in_=ss[:], func=AF.Ln, bias=epsb[:, 0:1]), reads=[f"ss{s}", "epsb"], writes=[f"ss{s}"], tiny=True)
                    P.act(lambda e, ss=ss: e.activation(out=ss[:], in_=ss[:], func=AF.Exp, scale=-0.5), reads=[f"ss{s}"], writes=[f"ss{s}"], tiny=True)
                    P.dve(lambda e, hsq=hsq, hall=hall, ss=ss: e.tensor_tensor(out=hsq[:].rearrange("p (h x) -> p h x", h=4), in0=hall[:].rearrange("p (h x) -> p h x", h=4), in1=mk(ss[:, 0:1], [[1, 4], [0, 128]]), op=ALU.mult),
                          reads=[f"hall{s}", f"ss{s}", f"hsq{s}"], writes=[f"hsq{s}"])
                    P.pool(lambda e, ybf=ybf, hsq=hsq, gg=gg: e.tensor_tensor(out=ybf[:], in0=hsq[:], in1=gg[:], op=ALU.mult), reads=[f"hsq{s}", f"gg{c % 3}"], writes=[f"ybf{s}"])
                    for h in range(4):
                        P.pe(lambda e, h=h, s=s, ybf=ybf: e.transpose(ps_y[:, s * 512 + h * 128: s * 512 + (h + 1) * 128], ybf[:, h * 128:(h + 1) * 128], idb[:]),
                             reads=[f"ybf{s}", "idb"], writes=[f"ps_y{s}"])
                    P.act(lambda e, yT=yT, s=s: e.activation(out=yT[:], in_=ps_y[:, s * 512:(s + 1) * 512].rearrange("p (h t) -> p h t", h=4), func=AF.Copy),
                          reads=[f"ps_y{s}"], writes=[f"yT{s}"])
                    P.dma(lambda e, yT=yT, tk=tk: e.dma_start(out=S["YT"][0:512, tk:tk + 128].rearrange("(h p) t -> p h t", p=128), in_=yT[:]),
                          reads=[f"yT{s}"], writes=["dram_yt"])

            import os
            B3D = int(os.environ.get("B3DEPTH", 2))
            if B3D == 2:
                b3_load(0)
                b3_load(1)
                b3_stage(0, 1)
                for c in range(NCH):
                    if c + 1 < NCH:
                        b3_stage(c + 1, 1)
                    if c + 2 < NCH:
                        b3_load(c + 2)
                    b3_stage(c, 2)
                    b3_stage(c, 3)
            else:
                b3_load(0)
                b3_load(1)
                b3_stage(0, 1)
                b3_stage(1, 1)
                b3_load(2)
                b3_stage(0, 2)
                for c in range(NCH):
                    if c + 2 < NCH:
                        b3_stage(c + 2, 1)
                    if c + 3 < NCH:
                        b3_load(c + 3)
                    if c + 1 < NCH:
                        b3_stage(c + 1, 2)
                    b3_stage(c, 3)
            P.emit()

    def phase_C(self, l):
        nc, I, S = self.nc, self.I, self.S
        P = Prog(self.ctx)
        with ExitStack() as es:
            def sb(name, shape, dt):
                return es.enter_context(nc.sbuf_tensor(f"L{l}" + name, list(shape), dt))

            def psum(name, shape, dt=F32):
                return es.enter_context(nc.psum_tensor(f"L{l}" + name, list(shape), dt))

            qn = sb("C_qn", [128, 4, T], BF16)
            kn = sb("C_kn", [128, 4, T], BF16)
            vn = sb("C_vn", [128, 32, 8, 65], BF16)
            nbi = sb("C_nbi", [128, 8, 5, 128], F32)
            nbe_ = [sb(f"C_nbe{i}", [128, 8, 4, 128], F32) for i in range(2)]
            idf = sb("C_idf", [128, 128], F32)
            idb = sb("C_idb", [128, 128], BF16)
            sbb_ = [sb(f"C_sb{i}", [128, 5, 128], F32) for i in range(3)]
            pt_ = [sb(f"C_pt{i}", [128, 5, 128], BF16) for i in range(3)]
            rd = sb("C_rd", [128, 8], F32)
            yn_ = [sb(f"C_yn{i}", [128, 8, 64], BF16) for i in range(2)]
            yT_ = [sb(f"C_yT{i}", [128, 4, 128], BF16) for i in range(2)]
            ps_s = [psum(f"C_ps_s{i}", [128, 1024]) for i in range(3)]
            ps_o = psum("C_ps_o", [128, 512])
            ps_y = psum("C_ps_y", [128, 1024], BF16)
            P.dma(lambda e: e.dma_start(out=idf[:], in_=I["ident"]), writes=["idf"])
            P.dve(lambda e: e.tensor_copy(out=idb[:], in_=idf[:]), reads=["idf"], writes=["idb"])
            P.dma(lambda e: e.dma_start(out=nbi[:], in_=I["nbi"][l]), writes=["nbi"])
            for g in range(4):
                P.dma(lambda e, g=g: e.dma_start(out=qn[:, g, :], in_=S["QnT"][g * 128:(g + 1) * 128, :]), writes=["qn"])
                P.dma(lambda e, g=g: e.dma_start(out=kn[:, g, :], in_=S["KnT"][g * 128:(g + 1) * 128, :]), writes=["kn"])
            P.pool(lambda e: e.memset(vn[:, :, :, 64:65], 1.0), writes=["vn"])
            for m in range(32):
                P.dma(lambda e, m=m: e.dma_start(out=vn[:, m, :, 0:64], in_=S["Vn"][m * 128:(m + 1) * 128, :].rearrange("p (h x) -> p h x", h=8)), writes=["vn"])
            def blk(i):
                if i < 2:
                    return [0, 1, 2, 3], i
                if i >= 30:
                    return [28, 29, 30, 31], i - 28
                return [i - 2, i - 1, i, i + 1, i + 2], None

            import os
            NU = int(os.environ.get("CNU", 32 * 8))

            def st_S(u):
                i, h = u // 8, u % 8
                tiles, eidx = blk(i)
                g, hp = h // 2, (h % 2) * 64
                ps = ps_s[u % 3]
                for j, m in enumerate(tiles):
                    P.pe(lambda e, ps=ps, j=j, m=m, g=g, hp=hp, i=i: e.matmul(ps[:, j * 128:(j + 1) * 128], lhsT=kn[hp:hp + 64, g, m * 128:(m + 1) * 128], rhs=qn[hp:hp + 64, g, i * 128:(i + 1) * 128], start=True, stop=True),
                         reads=["kn", "qn"], writes=[f"ps_s{u % 3}"])

            def st_E(u):
                i, h = u // 8, u % 8
                tiles, eidx = blk(i)
                nt = len(tiles)
                ps = ps_s[u % 3]
                sbb, pt = sbb_[u % 3], pt_[u % 3]
                if eidx is not None and h == 0:
                    P.dma(lambda e, eidx=eidx, i=i: e.dma_start(out=nbe_[i % 2][:], in_=I["nbe"][l, eidx]), writes=[f"nbe{i % 2}"])
                bias = nbi[:, h, :, :] if eidx is None else nbe_[i % 2][:, h, :, :]
                btok = "nbi" if eidx is None else f"nbe{i % 2}"
                P.dve(lambda e, ps=ps, sbb=sbb, bias=bias, nt=nt: e.tensor_tensor(out=sbb[:, 0:nt, :], in0=ps[:, 0:nt * 128].rearrange("p (j x) -> p j x", x=128), in1=bias, op=ALU.add),
                      reads=[f"ps_s{u % 3}", btok], writes=[f"sb{u % 3}"])
                P.act(lambda e, sbb=sbb, pt=pt, nt=nt: e.activation(out=pt[:, 0:nt, :], in_=sbb[:, 0:nt, :], func=AF.Exp), reads=[f"sb{u % 3}"], writes=[f"pt{u % 3}"])

            def st_PV(u):
                i, h = u // 8, u % 8
                tiles, eidx = blk(i)
                nt = len(tiles)
                pt = pt_[u % 3]
                so = (u % 4) * 128
                sot = f"ps_o{u % 4}"
                for j, m in enumerate(tiles):
                    P.pe(lambda e, pt=pt, j=j, m=m, h=h, nt=nt, so=so: e.matmul(ps_o[:, so:so + 65], lhsT=pt[:, j, :], rhs=vn[:, m, h, :], start=(j == 0), stop=(j == nt - 1)),
                         reads=[f"pt{u % 3}", "vn"], writes=[sot])
                P.dve(lambda e, h=h, so=so: e.reciprocal(out=rd[:, h:h + 1], in_=ps_o[:, so + 64:so + 65]), reads=[sot], writes=[f"rd{h}"], tiny=True)
                P.act(lambda e, h=h, so=so, i=i: e.activation(out=yn_[i % 2][:, h, :], in_=ps_o[:, so:so + 64], func=AF.Copy, scale=rd[:, h:h + 1]), reads=[sot, f"rd{h}"], writes=[f"yn{i % 2}"])
                if h == 7:
                    sl = i % 2
                    yT = yT_[sl]
                    for g in range(4):
                        P.pe(lambda e, g=g, sl=sl: e.transpose(ps_y[:, sl * 512 + g * 128:sl * 512 + (g + 1) * 128], yn_[sl][:, 2 * g:2 * g + 2, :].rearrange("p a b -> p (a b)"), idb[:]),
                             reads=[f"yn{sl}", "idb"], writes=[f"ps_y{sl}"])
                    P.dve(lambda e, yT=yT, sl=sl: e.tensor_copy(out=yT[:], in_=ps_y[:, sl * 512:(sl + 1) * 512].rearrange("p (h t) -> p h t", h=4)),
                          reads=[f"ps_y{sl}"], writes=[f"yT{sl}"])
                    P.dma(lambda e, yT=yT, i=i: e.dma_start(out=S["YT"][512:1024, i * 128:(i + 1) * 128].rearrange("(h p) t -> p h t", p=128), in_=yT[:]),
                          reads=[f"yT{sl}"], writes=["dram_yt"])

            CD = int(os.environ.get("CDEPTH", 1))
            if CD == 0:
                for u in range(NU):
                    st_S(u)
                    st_E(u)
                    st_PV(u)
            elif CD == 1:
                st_S(0)
                for u in range(NU):
                    if u + 1 < NU:
                        st_S(u + 1)
                    st_E(u)
                    st_PV(u)
            else:
                st_S(0)
                st_S(1)
                st_E(0)
                for u in range(NU):
                    if u + 2 < NU:
                        st_S(u + 2)
                    if u + 1 < NU:
                        st_E(u + 1)
                    st_PV(u)
            P.emit()

    def phase_D(self, l):
        nc, I, S = self.nc, self.I, self.S
        P = Prog(self.ctx)
        xsrc = I["xT"] if l == 0 else S["xres"]
        self.esE = ExitStack()
        esE = self.esE
        w1 = esE.enter_context(nc.sbuf_tensor(f"L{l}E_w1", [128, 8, DFF], BF16))
        w2 = esE.enter_context(nc.sbuf_tensor(f"L{l}E_w2", [128, 32, D], BF16))
        n2 = esE.enter_context(nc.sbuf_tensor(f"L{l}E_n2", [128, DEPTH, 8], F32))
        self.Ew = (w1, w2, n2)
        with ExitStack() as es:
            def sb(name, shape, dt):
                return es.enter_context(nc.sbuf_tensor(f"L{l}" + name, list(shape), dt))

            def psum(name, shape, dt=F32):
                return es.enter_context(nc.psum_tensor(f"L{l}" + name, list(shape), dt))

            wbf = sb("D_wbf", [128, 8, D], BF16)
            stE = [sb(f"D_stE{i}", [128, 1024], F32) for i in range(3)]
            xin = [sb(f"D_xin{i}", [128, 8, TB], F32) for i in range(2)]
            yin = [sb(f"D_yin{i}", [128, 8, TB], BF16) for i in range(2)]
            pso = [psum(f"D_ps{i}", [128, 512]) for i in range(4)]
            P.dma(lambda e: e.dma_start(out=n2[:], in_=I["n2"]), writes=["n2"])
            for kc in range(8):
                st = stE[kc % 3]
                stt = f"stE{kc % 3}"
                P.dma(lambda e, st=st, kc=kc: e.dma_start(out=st[:, 0:512], in_=I["w_out"][l, kc * 128:(kc + 1) * 128, 0:512]), writes=[stt + "a"])
                P.dma(lambda e, st=st, kc=kc: e.dma_start(out=st[:, 512:1024], in_=I["w_out"][l, kc * 128:(kc + 1) * 128, 512:1024]), writes=[stt + "b"])
                P.pool(lambda e, st=st, kc=kc: e.tensor_copy(out=wbf[:, kc, 0:512], in_=st[:, 0:512]), reads=[stt + "a"], writes=[f"wbf{kc}_0"])
                P.act(lambda e, st=st, kc=kc: e.activation(out=wbf[:, kc, 512:1024], in_=st[:, 512:1024], func=AF.Copy), reads=[stt + "b"], writes=[f"wbf{kc}_1"])
            WB = [f"wbf{kc}_{hf}" for kc in range(8) for hf in range(2)]

            def prefetch(pc):
                pj = pc + 8
                st = stE[pj % 3]
                ta, tb_ = f"stE{pj % 3}a", f"stE{pj % 3}b"
                if pc < 32:
                    kc, hf = pc // 4, pc % 4
                    c0 = hf * 1024
                    P.dma(lambda e: e.dma_start(out=st[:, 0:512], in_=I["w_ff1"][l, kc * 128:(kc + 1) * 128, c0:c0 + 512]), writes=[ta])
                    P.dma(lambda e: e.dma_start(out=st[:, 512:1024], in_=I["w_ff1"][l, kc * 128:(kc + 1) * 128, c0 + 512:c0 + 1024]), writes=[tb_])
                    P.pool(lambda e: e.tensor_scalar(out=w1[:, kc, c0:c0 + 512], in0=st[:, 0:512], scalar1=n2[:, l, kc:kc + 1], scalar2=None, op0=ALU.mult),
                           reads=[ta, "n2"], writes=[f"w1_{pc}a"])
                    P.act(lambda e: e.activation(out=w1[:, kc, c0 + 512:c0 + 1024], in_=st[:, 512:1024], func=AF.Copy, scale=n2[:, l, kc:kc + 1]),
                          reads=[tb_, "n2"], writes=[f"w1_{pc}b"])
                else:
                    k2 = pc - 32
                    P.dma(lambda e: e.dma_start(out=st[:, 0:512], in_=I["w_ff2"][l, k2 * 128:(k2 + 1) * 128, 0:512]), writes=[ta])
                    P.dma(lambda e: e.dma_start(out=st[:, 512:1024], in_=I["w_ff2"][l, k2 * 128:(k2 + 1) * 128, 512:1024]), writes=[tb_])
                    P.pool(lambda e: e.tensor_copy(out=w2[:, k2, 0:512], in_=st[:, 0:512]), reads=[ta], writes=[f"w2_{k2}a"])
                    P.act(lambda e: e.activation(out=w2[:, k2, 512:1024], in_=st[:, 512:1024], func=AF.Copy), reads=[tb_], writes=[f"w2_{k2}b"])

            def load(b):
                t0 = b * TB
                P.dma(lambda e, b=b, t0=t0: e.dma_start(out=xin[b % 2][:], in_=xsrc[:, t0:t0 + TB].rearrange("(kc p) t -> p kc t", p=128)), writes=[f"xin{b % 2}"])
                P.dma(lambda e, b=b, t0=t0: e.dma_start(out=yin[b % 2][:], in_=S["YT"][:, t0:t0 + TB].rearrange("(kc p) t -> p kc t", p=128)), writes=[f"yin{b % 2}"])

            load(0)
            pi = 0
            for b in range(NB):
                if b + 1 < NB:
                    load(b + 1)
                for pc in range(8 * b, 8 * b + 8):
                    prefetch(pc)
                t0 = b * TB
                xi, yi = xin[b % 2], yin[b % 2]
                for dc in range(8):
                    ps = pso[pi % 4]
                    pst = f"ps{pi % 4}"
                    pi += 1
                    for kc in range(8):
                        P.pe(lambda e, ps=ps, kc=kc, dc=dc, yi=yi: e.matmul(ps[:], lhsT=wbf[:, kc, dc * 128:(dc + 1) * 128], rhs=yi[:, kc, :], start=(kc == 0), stop=(kc == 7)),
                             reads=WB + [f"yin{b % 2}"], writes=[pst])
                    P.dve(lambda e, ps=ps, dc=dc, xi=xi: e.tensor_tensor(out=xi[:, dc, :], in0=xi[:, dc, :], in1=ps[:], op=ALU.add),
                          reads=[pst, f"xin{b % 2}"], writes=[f"xin{b % 2}"])
                P.dma(lambda e, xi=xi, t0=t0: e.dma_start(out=S["xres"][:, t0:t0 + TB].rearrange("(kc p) t -> p kc t", p=128), in_=xi[:]),
                      reads=[f"xin{b % 2}"], writes=["dram_x"])
            P.emit()

    def phase_E(self, l):
        nc, I, S = self.nc, self.I, self.S
        P = Prog(self.ctx)
        last = (l == DEPTH - 1)
        w1, w2, n2 = self.Ew
        with ExitStack() as es:
            def sb(name, shape, dt):
                return es.enter_context(nc.sbuf_tensor(f"L{l}" + name, list(shape), dt))

            def psum(name, shape, dt=F32):
                return es.enter_context(nc.psum_tensor(f"L{l}" + name, list(shape), dt))

            nf = sb("E_nf", [128, 8], F32)
            ones = sb("E_ones", [128, 128], BF16)
            epsb = sb("E_epsb", [128, 1], F32)
            xin = [sb(f"E_xin{i}", [128, 8, TB], F32) for i in range(2)]
            xsq = sb("E_xsq", [128, 8, TB], BF16)
            rstd = sb("E_rstd", [128, TB], F32)
            hn = sb("E_hn", [128, 8, TB], BF16)
            uu = [sb(f"E_u{i}", [128, 8, TB], BF16) for i in range(2)]
            rtmp = [sb(f"E_rtmp{i}", [128, TB], F32) for i in range(2)]
            ps_ssq = psum("E_ps_ssq", [128, 512])
            ps1 = [psum(f"E_ps1_{i}", [128, 512]) for i in range(3)]
            ps2 = [psum(f"E_ps2_{i}", [128, 512]) for i in range(3)]
            P.dma(lambda e: e.dma_start(out=nf[:], in_=I["nf"]), writes=["nf"])
            P.dve(lambda e: e.tensor_scalar(out=nf[:], in0=nf[:], scalar1=32.0, scalar2=None, op0=ALU.mult), reads=["nf"], writes=["nf"], tiny=True)
            P.dve(lambda e: e.memset(ones[:], 1.0), writes=["ones"])
            P.dve(lambda e: e.memset(epsb[:], float(D * EPS)), writes=["epsb"], tiny=True)
            cnt = {"p1": 0, "p2": 0}
            HN = [f"hn{kc}" for kc in range(8)]

            def load(b):
                t0 = b * TB
                P.dma(lambda e, b=b, t0=t0: e.dma_start(out=xin[b % 2][:], in_=S["xres"][:, t0:t0 + TB].rearrange("(kc p) t -> p kc t", p=128)), writes=[f"xin{b % 2}"])

            def rms(xi, xt):
                P.act(lambda e: e.activation(out=xsq[:], in_=xi[:], func=AF.Square), reads=[xt], writes=["xsq"])
                for kc in range(8):
                    P.pe(lambda e, kc=kc: e.matmul(ps_ssq[:], lhsT=ones[:], rhs=xsq[:, kc, :], start=(kc == 0), stop=(kc == 7)), reads=["ones", "xsq"], writes=["ps_ssq"])
                P.act(lambda e: e.activation(out=rstd[:], in_=ps_ssq[:], func=AF.Ln, bias=epsb[:, 0:1]), reads=["ps_ssq", "epsb"], writes=["rstd"])
                P.act(lambda e: e.activation(out=rstd[:], in_=rstd[:], func=AF.Exp, scale=-0.5), reads=["rstd"], writes=["rstd"])

            def prep(b):
                xi = xin[b % 2]
                xt = f"xin{b % 2}"
                rms(xi, xt)
                for kc in range(8):
                    P.dve(lambda e, kc=kc, xi=xi: e.scalar_tensor_tensor(out=hn[:, kc, :], in0=xi[:, kc, :], scalar=32.0, in1=rstd[:], op0=ALU.mult, op1=ALU.mult),
                          reads=[xt, "rstd"], writes=[f"hn{kc}"])

            def S1(b, grp):
                gi = (b * 4 + grp) % 2
                u = uu[gi]
                ut = f"u{gi}"
                for fj in range(8):
                    fc = grp * 8 + fj
                    ps = ps1[cnt["p1"] % 3]
                    pst = f"ps1_{cnt['p1'] % 3}"
                    rt = rtmp[cnt["p1"] % 2]
                    rtt = f"rtmp{cnt['p1'] % 2}"
                    cnt["p1"] += 1
                    for kc in range(8):
                        P.pe(lambda e, ps=ps, kc=kc, fc=fc: e.matmul(ps[:], lhsT=w1[:, kc, fc * 128:(fc + 1) * 128], rhs=hn[:, kc, :], start=(kc == 0), stop=(kc == 7)),
                             reads=HN, writes=[pst])
                    P.act(lambda e, ps=ps, rt=rt: e.activation(out=rt[:], in_=ps[:], func=AF.Relu), reads=[pst], writes=[rtt])
                    eng = P.dve if fj % 2 == 0 else P.pool
                    eng(lambda e, u=u, fj=fj, rt=rt: e.tensor_tensor(out=u[:, fj, :], in0=rt[:], in1=rt[:], op=ALU.mult), reads=[rtt], writes=[f"{ut}_{fj}"])

            def S2(b, grp):
                gi = (b * 4 + grp) % 2
                u = uu[gi]
                ut = f"u{gi}"
                xi = xin[b % 2]
                xt = f"xin{b % 2}"
                UT = [f"{ut}_{fj}" for fj in range(8)]
                for dc in range(8):
                    ps = ps2[cnt["p2"] % 3]
                    pst = f"ps2_{cnt['p2'] % 3}"
                    cnt["p2"] += 1
                    for fj in range(8):
                        P.pe(lambda e, ps=ps, fj=fj, grp=grp, dc=dc, u=u: e.matmul(ps[:], lhsT=w2[:, grp * 8 + fj, dc * 128:(dc + 1) * 128], rhs=u[:, fj, :], start=(fj == 0), stop=(fj == 7)),
                             reads=UT, writes=[pst])
                    P.dve(lambda e, ps=ps, dc=dc, xi=xi: e.tensor_tensor(out=xi[:, dc, :], in0=xi[:, dc, :], in1=ps[:], op=ALU.add),
                          reads=[pst, xt], writes=[xt])

            def fin(b):
                t0 = b * TB
                xi = xin[b % 2]
                xt = f"xin{b % 2}"
                if not last:
                    P.dma(lambda e, xi=xi, t0=t0: e.dma_start(out=S["xres"][:, t0:t0 + TB].rearrange("(kc p) t -> p kc t", p=128), in_=xi[:]),
                          reads=[xt], writes=["dram_x"])
                else:
                    rms(xi, xt)
                    for kc in range(8):
                        P.dve(lambda e, kc=kc, xi=xi: e.scalar_tensor_tensor(out=xi[:, kc, :], in0=xi[:, kc, :], scalar=nf[:, kc:kc + 1], in1=rstd[:], op0=ALU.mult, op1=ALU.mult),
                              reads=[xt, "rstd", "nf"], writes=[xt])
                    P.dma(lambda e, xi=xi, t0=t0: e.dma_start(out=self.out[:, t0:t0 + TB].rearrange("(kc p) t -> p kc t", p=128), in_=xi[:]),
                          reads=[xt], writes=["dram_out"])

            load(0)
            prep(0)
            S1(0, 0)
            for b in range(NB):
                if b + 1 < NB:
                    load(b + 1)
                for grp in range(4):
                    if grp < 3:
                        S1(b, grp + 1)
                    elif b + 1 < NB:
                        prep(b + 1)
                        S1(b + 1, 0)
                    S2(b, grp)
                fin(b)
            P.emit()
        self.esE.close()


def _na_bias_tables(rpb_l):
    H = rpb_l.shape[0]
    kl = np.arange(128)
    kr_l, kc = kl // 64, kl % 64
    ql = np.arange(128)
    qr_l, qc = ql // 64, ql % 64

    def tile(i, m):
        kr = 2 * m + kr_l[:, None]
        qr = 2 * i + qr_l[None, :]
        rs = np.clip(qr - 4, 0, 56)
        vr = (kr >= rs) & (kr < rs + 8)
        cs = np.clip(qc[None, :] - 8, 0, 48)
        vc = (kc[:, None] >= cs) & (kc[:, None] < cs + 16)
        ri = np.clip(kr - qr + 7, 0, 14)
        ci = np.clip(kc[:, None] - qc[None, :] + 15, 0, 30)
        vals = rpb_l[:, ri, ci]
        return np.where((vr & vc)[None], vals, np.float32(-30000.0)).astype(np.float32)

    nbi = np.stack([tile(2, m) for m in range(5)], axis=1)
    nbi = np.ascontiguousarray(nbi.transpose(2, 0, 1, 3))
    nbe = []
    for i, ms in ((0, range(4)), (1, range(4)), (30, range(28, 32)), (31, range(28, 32))):
        t = np.stack([tile(i, m) for m in ms], axis=1)
        nbe.append(t.transpose(2, 0, 1, 3))
    nbe = np.ascontiguousarray(np.stack(nbe, axis=0))
    return nbi, nbe


def _prep_shared(norm1_w, conv_w, conv_b, gate_b, mlstm_norm_w, rpb, norm2_w, final_norm_w):
    f = np.float32
    sh = {}
    sh["n1"] = np.ascontiguousarray(norm1_w.reshape(DEPTH, 8, 128).transpose(2, 0, 1)).astype(f)
    sh["n2"] = np.ascontiguousarray(norm2_w.reshape(DEPTH, 8, 128).transpose(2, 0, 1)).astype(f)
    sh["nf"] = np.ascontiguousarray(final_norm_w.reshape(8, 128).T).astype(f)
    sh["cw"] = np.ascontiguousarray(conv_w.reshape(DEPTH, 3, 8, 128).transpose(3, 0, 1, 2)).astype(f)
    sh["cb"] = np.ascontiguousarray(conv_b.reshape(DEPTH, 8, 128).transpose(2, 0, 1)).astype(f)
    gb = np.zeros((36, DEPTH, 2), f)
    g4 = gate_b.reshape(DEPTH, 4, 4)
    gb[0:4, :, 0] = g4[:, 0, :].T
    gb[0:4, :, 1] = g4[:, 1, :].T
    gb[32:36, :, 0] = g4[:, 2, :].T
    gb[32:36, :, 1] = g4[:, 3, :].T
    sh["gb"] = gb
    sh["mnw"] = np.ascontiguousarray(np.broadcast_to(mlstm_norm_w[None], (128, DEPTH, 512))).astype(f)
    nbi, nbe = zip(*[_na_bias_tables(np.asarray(rpb[l], f)) for l in range(DEPTH)])
    sh["nbi"] = np.stack(nbi, 0)
    sh["nbe"] = np.stack(nbe, 0)
    sh["ident"] = np.eye(128, dtype=f)
    sh["trif"] = np.triu(np.ones((128, 128), f))
    sh["trib"] = np.tril(np.ones((128, 128), f))
    return sh


_CACHE = {}


def _get_nc(debug=False, stop_after=None):
    key = (debug, stop_after)
    if key not in _CACHE:
        b = Builder(debug=debug, stop_after=stop_after)
        nc = b.build()
        _CACHE[key] = (nc, b)
    return _CACHE[key]


def run(inputs, debug=False, stop_after=None, cores=8):
    x = np.asarray(inputs["x"], np.float32)
    sh = _prep_shared(*[np.asarray(inputs[k], np.float32) for k in
                        ("norm1_w", "conv_w", "conv_b", "gate_b", "mlstm_norm_w", "rpb", "norm2_w", "final_norm_w")])
    for k in ("w_in", "w_out", "w_ff1", "w_ff2"):
        sh[k] = np.ascontiguousarray(np.asarray(inputs[k], np.float32))
    nc, b = _get_nc(debug, stop_after)
    in_maps = []
    for c in range(cores):
        m = dict(sh)
        m["xT"] = np.ascontiguousarray(x[c].T)
        in_maps.append(m)
    res = run_bass_kernel_spmd(nc, in_maps, core_ids=list(range(cores)))
    return res, b


def kernel(x, norm1_w, w_in, conv_w, conv_b, gate_b, mlstm_norm_w, rpb, w_out,
           norm2_w, w_ff1, w_ff2, final_norm_w):
    inputs = dict(x=x, norm1_w=norm1_w, w_in=w_in, conv_w=conv_w, conv_b=conv_b, gate_b=gate_b,
                  mlstm_norm_w=mlstm_norm_w, rpb=rpb, w_out=w_out, norm2_w=norm2_w, w_ff1=w_ff1,
                  w_ff2=w_ff2, final_norm_w=final_norm_w)
    res, _ = run(inputs)
    out = np.stack([np.ascontiguousarray(r["outT"].T) for r in res.results], axis=0)
    return out.astype(np.float32)
```

```python
import numpy as np
from contextlib import ExitStack
import concourse.bass as bass
import concourse.mybir as mybir
from concourse.bass_utils import run_bass_kernel_spmd

F32 = mybir.dt.float32
BF16 = mybir.dt.bfloat16
AF = mybir.ActivationFunctionType
ALU = mybir.AluOpType
AX = mybir.AxisListType

T = 4096
D = 1024
DEPTH = 2
DIN = 3600
DFF = 4096
NB = 8
TB = 512
NCH = 32
EPS = 1e-6
ALPHA = 128.0 ** -0.5

ENGS = ("pe", "act", "dve", "pool", "sp")
EPOCH = 4000
NDMASEM = 8
SAME_SYNC = True


class Op:
    __slots__ = ("eng", "fn", "deps", "dma", "idx", "sig", "semk", "val", "tiny")

    def __init__(self, eng, fn, dma, tiny=False):
        self.eng = eng
        self.fn = fn
        self.dma = dma
        self.tiny = tiny
        self.deps = set()
        self.sig = False
        self.semk = None
        self.val = 0


class Ctx:
    def __init__(self, nc):
        self.nc = nc
        self.sems = {}
        self.ccount = {e: 0 for e in ENGS}
        self.dval = {}
        self.dk = {e: 0 for e in ENGS}
        self.waited = {e: {} for e in ENGS}
        self.nops = 0
        self.nwaits = 0

    def sem(self, k):
        if k not in self.sems:
            self.sems[k] = self.nc.alloc_semaphore("s_" + "_".join(str(x) for x in k))
        return self.sems[k]


class Prog:
    def __init__(self, ctx):
        self.ctx = ctx
        self.nc = ctx.nc
        self.ops = []
        self.last_w = {}
        self.readers = {}

    def op(self, eng, fn, reads=(), writes=(), dma=False, tiny=False):
        o = Op(eng, fn, dma, tiny)
        o.idx = len(self.ops)
        for r in reads:
            w = self.last_w.get(r)
            if w is not None:
                o.deps.add(w)
        for w_ in writes:
            w = self.last_w.get(w_)
            if w is not None:
                o.deps.add(w)
            for rd in self.readers.get(w_, ()):
                o.deps.add(rd)
        o.deps.discard(o.idx)
        for r in reads:
            self.readers.setdefault(r, []).append(o.idx)
        for w_ in writes:
            self.last_w[w_] = o.idx
            self.readers[w_] = []
        self.ops.append(o)
        return o

    def pe(self, fn, reads=(), writes=()):
        return self.op("pe", fn, reads, writes)

    def act(self, fn, reads=(), writes=(), tiny=False):
        return self.op("act", fn, reads, writes, tiny=tiny)

    def dve(self, fn, reads=(), writes=(), tiny=False):
        return self.op("dve", fn, reads, writes, tiny=tiny)

    def pool(self, fn, reads=(), writes=(), tiny=False):
        return self.op("pool", fn, reads, writes, tiny=tiny)

    def dma(self, fn, reads=(), writes=(), q="sp"):
        return self.op(q, fn, reads, writes, dma=True)

    def _unsynced(self, od, o):
        return (od.eng == o.eng and not od.dma and not o.dma
                and (od.eng == "pe" or not SAME_SYNC or not od.tiny))

    def emit(self):
        ctx = self.ctx
        nc = self.nc
        ops = self.ops
        per = {e: [o for o in ops if o.eng == e] for e in ENGS}
        for o in ops:
            for d in o.deps:
                od = ops[d]
                if od.dma or self._unsynced(od, o):
                    continue
                od.sig = True
        for e in ENGS:
            for o in reversed(per[e]):
                if not o.dma:
                    o.sig = True
                    break
        final = {}
        for e in ENGS:
            for o in per[e]:
                if o.dma:
                    i = ctx.dk[e] % NDMASEM
                    ctx.dk[e] += 1
                    k = ("d", e, i)
                    ctx.dval[k] = ctx.dval.get(k, 0) + 16
                    o.semk = k
                    o.val = ctx.dval[k]
                    final[k] = o.val
                elif o.sig:
                    n = ctx.ccount[e]
                    ctx.ccount[e] += 1
                    k = ("c", e, n // EPOCH)
                    o.semk = k
                    o.val = n % EPOCH + 1
                    final[k] = o.val
        for k in final:
            ctx.sem(k)
        ctx.nops += len(ops)

        def run_engine(e, eng):
            waited = ctx.waited[e]

            def wait(k, v):
                if waited.get(k, 0) >= v:
                    return
                waited[k] = v
                eng.wait_ge(ctx.sems[k], v)
                ctx.nwaits += 1

            for o in per[e]:
                need = {}
                for d in o.deps:
                    od = ops[d]
                    if od.semk is None or self._unsynced(od, o):
                        continue
                    if need.get(od.semk, 0) < od.val:
                        need[od.semk] = od.val
                if o.dma and o.val > 16:
                    if need.get(o.semk, 0) < o.val - 16:
                        need[o.semk] = o.val - 16
                for k, v in need.items():
                    wait(k, v)
                ins = o.fn(eng)
                if o.dma:
                    ins.then_inc(ctx.sems[o.semk], 16)
                elif o.sig:
                    ins.then_inc(ctx.sems[o.semk], 1)
            for k, v in final.items():
                if k[0] == "c" and k[1] == e:
                    if e == "pe" or not SAME_SYNC:
                        pass
                wait(k, v)

        with nc.Block() as block:
            block.tensor(lambda eng: run_engine("pe", eng))
            block.scalar(lambda eng: run_engine("act", eng))
            block.vector(lambda eng: run_engine("dve", eng))
            block.gpsimd(lambda eng: run_engine("pool", eng))
            block.sync(lambda eng: run_engine("sp", eng))


def mk(ap, dims, off=0):
    a = ap.ap
    return bass.AP(ap.tensor, ap.offset + off, [list(a[0])] + [list(d) for d in dims])


class Builder:
    def __init__(self, debug=False, stop_after=None):
        self.debug = debug
        self.stop_after = stop_after
        self.nc = bass.Bass("TRN2", target_bir_lowering=False)
        self.ctx = Ctx(self.nc)
        self.dbg_names = []

    def din(self, name, shape, dt=F32):
        return self.nc.dram_tensor(name, list(shape), dt, kind="ExternalInput").ap()

    def dscr(self, name, shape, dt):
        kind = "ExternalOutput" if self.debug else "Internal"
        if self.debug:
            self.dbg_names.append(name)
        return self.nc.dram_tensor(name, list(shape), dt, kind=kind).ap()

    def build(self):
        nc = self.nc
        I = {}
        I["xT"] = self.din("xT", [D, T])
        I["w_in"] = self.din("w_in", [DEPTH, D, DIN])
        I["w_out"] = self.din("w_out", [DEPTH, D, D])
        I["w_ff1"] = self.din("w_ff1", [DEPTH, D, DFF])
        I["w_ff2"] = self.din("w_ff2", [DEPTH, DFF, D])
        I["n1"] = self.din("n1", [128, DEPTH, 8])
        I["n2"] = self.din("n2", [128, DEPTH, 8])
        I["nf"] = self.din("nf", [128, 8])
        I["cw"] = self.din("cw", [128, DEPTH, 3, 8])
        I["cb"] = self.din("cb", [128, DEPTH, 8])
        I["gb"] = self.din("gb", [36, DEPTH, 2])
        I["mnw"] = self.din("mnw", [128, DEPTH, 512])
        I["nbi"] = self.din("nbi", [DEPTH, 128, 8, 5, 128])
        I["nbe"] = self.din("nbe", [DEPTH, 4, 128, 8, 4, 128])
        I["ident"] = self.din("ident", [128, 128])
        I["trif"] = self.din("trif", [128, 128])
        I["trib"] = self.din("trib", [128, 128])
        self.I = I
        self.out = nc.dram_tensor("outT", [D, T], F32, kind="ExternalOutput").ap()
        S = {}
        S["QT"] = self.dscr("QT", [512, T], BF16)
        S["KT"] = self.dscr("KT", [512, T], BF16)
        S["Ktok"] = self.dscr("Ktok", [T, 512], BF16)
        S["Vtok"] = self.dscr("Vtok", [T, 512], BF16)
        S["Osig"] = self.dscr("Osig", [T, 512], F32)
        S["G"] = self.dscr("G", [16, T], F32)
        S["QnT"] = self.dscr("QnT", [512, T], BF16)
        S["KnT"] = self.dscr("KnT", [512, T], BF16)
        S["Vn"] = self.dscr("Vn", [T, 512], BF16)
        S["YT"] = self.dscr("YT", [D, T], BF16)
        S["Mser"] = self.dscr("Mser", [8, T], F32)
        S["Wser"] = self.dscr("Wser", [8, T], F32)
        S["DCs"] = self.dscr("DCs", [8, NCH], F32)
        S["xres"] = self.dscr("xres", [D, T], F32)
        if self.debug:
            S["dbgser"] = self.dscr("dbgser", [128, 4, NCH, 8], F32)
        self.S = S
        phases = []
        for l in range(DEPTH):
            phases += [("A", l), ("B", l), ("C", l), ("D", l), ("E", l)]
        for i, (ph, l) in enumerate(phases):
            getattr(self, "phase_" + ph)(l)
            if self.stop_after is not None and i + 1 >= self.stop_after:
                break
        return nc

    def phase_A(self, l):
        nc, I, S = self.nc, self.I, self.S
        P = Prog(self.ctx)
        xsrc = I["xT"] if l == 0 else S["xres"]
        with ExitStack() as es:
            def sb(name, shape, dt):
                return es.enter_context(nc.sbuf_tensor(f"L{l}" + name, list(shape), dt))

            def psum(name, shape, dt=F32):
                return es.enter_context(nc.psum_tensor(f"L{l}" + name, list(shape), dt))

            wbf = sb("A_wbf", [128, 8, DIN], BF16)
            wst = [sb(f"A_wst{i}", [128, DIN], F32) for i in range(2)]
            n1 = sb("A_n1", [128, DEPTH, 8], F32)
            cw = sb("A_cw", [128, DEPTH, 3, 8], F32)
            cb = sb("A_cb", [128, DEPTH, 8], F32)
            cws = sb("A_cws", [128, 3, 8], F32)
            cbs = sb("A_cbs", [128, 8], F32)
            idf = sb("A_idf", [128, 128], F32)
            idb = sb("A_idb", [128, 128], BF16)
            ones = sb("A_ones", [128, 128], BF16)
            epsb = sb("A_epsb", [128, 1], F32)
            xin = [sb(f"A_xin{i}", [128, 8, 514], F32) for i in range(2)]
            xsq = sb("A_xsq", [128, 8, 514], BF16)
            rstd = sb("A_rstd", [128, 514], F32)
            hn_ = [sb(f"A_hn{i}", [128, 8, 514], BF16) for i in range(2)]
            pre = [sb(f"A_pre{i}", [128, 514], F32) for i in range(2)]
            acc = [sb(f"A_acc{i}", [128, 512], F32) for i in range(2)]
            sg = [sb(f"A_sg{i}", [128, 512], F32) for i in range(2)]
            qko = [sb(f"A_qko{i}", [128, 512], BF16) for i in range(3)]
            ktok = sb("A_ktok", [128, 4, 512], BF16)
            vtok = sb("A_vtok", [128, 4, 512], BF16)
            vntok = sb("A_vntok", [128, 4, 512], BF16)
            osig = sb("A_osig", [128, 4, 512], F32)
            gsb = sb("A_gsb", [16, 512], F32)
            ps_ssq = psum("A_ps_ssq", [128, 512])
            ps_h = psum("A_ps_h", [128, 512])
            ps_pre = [psum(f"A_ps_pre{i}", [128, 512]) for i in range(2)]
            ps_tok = [psum(f"A_ps_tok{i}", [128, 512]) for i in range(2)]
            ps_g = psum("A_ps_g", [128, 512])
            ps_t = psum("A_ps_t", [128, 1024], BF16)

            P.dma(lambda e: e.dma_start(out=n1[:], in_=I["n1"]), writes=["n1"])
            P.dma(lambda e: e.dma_start(out=cw[:], in_=I["cw"]), writes=["cw"])
            P.dma(lambda e: e.dma_start(out=cb[:], in_=I["cb"]), writes=["cb"])
            P.dma(lambda e: e.dma_start(out=idf[:], in_=I["ident"]), writes=["idf"])
            P.dve(lambda e: e.tensor_copy(out=idb[:], in_=idf[:]), reads=["idf"], writes=["idb"])
            P.dve(lambda e: e.memset(ones[:], 1.0), writes=["ones"])
            P.dve(lambda e: e.memset(epsb[:], float(D * EPS)), writes=["epsb"], tiny=True)
            P.dve(lambda e: e.tensor_scalar(out=cws[:, :, 0:4], in0=cw[:, l, :, 0:4], scalar1=ALPHA, scalar2=None, op0=ALU.mult),
                  reads=["cw"], writes=["cws"], tiny=True)
            P.dve(lambda e: e.tensor_copy(out=cws[:, :, 4:8], in_=cw[:, l, :, 4:8]), reads=["cw"], writes=["cws"], tiny=True)
            P.dve(lambda e: e.tensor_scalar(out=cbs[:, 0:4], in0=cb[:, l, 0:4], scalar1=ALPHA, scalar2=None, op0=ALU.mult),
                  reads=["cb"], writes=["cbs"], tiny=True)
            P.dve(lambda e: e.tensor_copy(out=cbs[:, 4:8], in_=cb[:, l, 4:8]), reads=["cb"], writes=["cbs"], tiny=True)
            for kc in range(8):
                st = wst[kc % 2]
                half = DIN // 2
                P.dma(lambda e, st=st, kc=kc: e.dma_start(out=st[:, 0:half], in_=I["w_in"][l, kc * 128:(kc + 1) * 128, 0:half]),
                      writes=[f"wst{kc % 2}a"])
                P.dma(lambda e, st=st, kc=kc: e.dma_start(out=st[:, half:DIN], in_=I["w_in"][l, kc * 128:(kc + 1) * 128, half:DIN]),
                      writes=[f"wst{kc % 2}b"])
                P.pool(lambda e, st=st, kc=kc: e.tensor_scalar(out=wbf[:, kc, 0:half], in0=st[:, 0:half], scalar1=n1[:, l, kc:kc + 1], scalar2=None, op0=ALU.mult),
                       reads=[f"wst{kc % 2}a", "n1"], writes=[f"wbf{kc}a"])
                P.dve(lambda e, st=st, kc=kc: e.tensor_scalar(out=wbf[:, kc, half:DIN], in0=st[:, half:DIN], scalar1=n1[:, l, kc:kc + 1], scalar2=None, op0=ALU.mult),
                      reads=[f"wst{kc % 2}b", "n1"], writes=[f"wbf{kc}b"])
            WB = [f"wbf{kc}{h}" for kc in range(8) for h in "ab"]

            def load_x(b):
                xi = xin[b % 2]
                t0 = b * TB
                lo = t0 - 1
                hi = t0 + TB + 1
                c0 = 0
                tok = f"xin{b % 2}"
                if b == 0:
                    lo = 0
                    c0 = 1
                    P.dve(lambda e, xi=xi: e.memset(xi[:, :, 0:1], 0.0), writes=[tok], tiny=True)
                if b == NB - 1:
                    hi = T
                    P.dve(lambda e, xi=xi: e.memset(xi[:, :, 513:514], 0.0), writes=[tok], tiny=True)
                n = hi - lo
                src = xsrc[:, lo:hi].rearrange("(kc p) t -> p kc t", p=128)
                P.dma(lambda e, xi=xi, src=src, c0=c0, n=n: e.dma_start(out=xi[:, :, c0:c0 + n], in_=src), writes=[tok])

            def prep(b):
                xi = xin[b % 2]
                xt = f"xin{b % 2}"
                hn = hn_[b % 2]
                P.act(lambda e, xi=xi: e.activation(out=xsq[:], in_=xi[:], func=AF.Square), reads=[xt], writes=["xsq"])
                for kc in range(8):
                    P.pe(lambda e, kc=kc: e.matmul(ps_ssq[:], lhsT=ones[:], rhs=xsq[:, kc, 1:513], start=(kc == 0), stop=(kc == 7)),
                         reads=["ones", "xsq"], writes=["ps_ssq"])
                for kc in range(8):
                    P.pe(lambda e, kc=kc: e.matmul(ps_h[:, 0:2], lhsT=ones[:], rhs=mk(xsq[:, kc, 0:1], [[513, 2]]), start=(kc == 0), stop=(kc == 7)),
                         reads=["ones", "xsq"], writes=["ps_hs"])
                P.act(lambda e: e.activation(out=rstd[:, 1:513], in_=ps_ssq[:], func=AF.Ln, bias=epsb[:, 0:1]), reads=["ps_ssq", "epsb"], writes=["rstd"])
                P.act(lambda e: e.activation(out=mk(rstd[:, 0:1], [[513, 2]]), in_=ps_h[:, 0:2], func=AF.Ln, bias=epsb[:, 0:1]), reads=["ps_hs", "epsb"], writes=["rstd"], tiny=True)
                P.act(lambda e: e.activation(out=rstd[:], in_=rstd[:], func=AF.Exp, scale=-0.5), reads=["rstd"], writes=["rstd"])
                for kc in range(8):
                    P.dve(lambda e, kc=kc, xi=xi, hn=hn: e.scalar_tensor_tensor(out=hn[:, kc, :], in0=xi[:, kc, :], scalar=32.0, in1=rstd[:], op0=ALU.mult, op1=ALU.mult),
                          reads=[xt, "rstd"], writes=[f"hn{b % 2}_{kc}"])

            load_x(0)
            load_x(1)
            prep(0)
            qi = 0
            pending = []
            def main(b):
                nonlocal qi
                hn = hn_[b % 2]
                t0 = b * TB
                HN = [f"hn{b % 2}_{kc}" for kc in range(8)]

                for fc in range(8):
                    pp = ps_pre[qi % 2]
                    ppt = f"ps_pre{qi % 2}"
                    pr = pre[qi % 2]
                    prt = f"pre{qi % 2}"
                    ac = acc[qi % 2]
                    act_ = f"acc{qi % 2}"
                    sgg = sg[qi % 2]
                    sgt = f"sg{qi % 2}"
                    qo = qko[qi % 3]
                    qot = f"qko{qi % 3}"
                    qi += 1
                    c0 = fc * 128
                    for kc in range(8):
                        P.pe(lambda e, kc=kc, pp=pp, c0=c0: e.matmul(pp[:], lhsT=wbf[:, kc, c0:c0 + 128], rhs=hn[:, kc, 1:513], start=(kc == 0), stop=(kc == 7)),
                             reads=WB + HN, writes=[ppt])
                    hc = 8 + 2 * (fc % 2)
                    for kc in range(8):
                        P.pe(lambda e, kc=kc, c0=c0, hc=hc: e.matmul(ps_h[:, hc:hc + 2], lhsT=wbf[:, kc, c0:c0 + 128], rhs=mk(hn[:, kc, 0:1], [[513, 2]]), start=(kc == 0), stop=(kc == 7)),
                             reads=WB + HN, writes=[f"ps_hp{fc % 2}"])
                    P.act(lambda e, pr=pr, pp=pp: e.activation(out=pr[:, 1:513], in_=pp[:], func=AF.Copy), reads=[ppt], writes=[prt])
                    P.act(lambda e, pr=pr, hc=hc: e.activation(out=mk(pr[:, 0:1], [[513, 2]]), in_=ps_h[:, hc:hc + 2], func=AF.Copy),
                          reads=[f"ps_hp{fc % 2}"], writes=[prt], tiny=True)
                    def post(fc=fc, pr=pr, prt=prt, ac=ac, act_=act_, sgg=sgg, sgt=sgt, qo=qo, qot=qot, t0=t0):
                        P.pool(lambda e, ac=ac, pr=pr, fc=fc: e.tensor_scalar(out=ac[:], in0=pr[:, 0:512], scalar1=cws[:, 0, fc:fc + 1], scalar2=None, op0=ALU.mult),
                               reads=[prt, "cws"], writes=[act_])
                        P.dve(lambda e, ac=ac, pr=pr, fc=fc: e.scalar_tensor_tensor(out=ac[:], in0=pr[:, 1:513], scalar=cws[:, 1, fc:fc + 1], in1=ac[:], op0=ALU.mult, op1=ALU.add),
                               reads=[prt, "cws", act_], writes=[act_])
                        P.dve(lambda e, ac=ac, pr=pr, fc=fc: e.scalar_tensor_tensor(out=ac[:], in0=pr[:, 2:514], scalar=cws[:, 2, fc:fc + 1], in1=ac[:], op0=ALU.mult, op1=ALU.add),
                               reads=[prt, "cws", act_], writes=[act_])
                        sc = (1.0 / ALPHA) if fc < 4 else 1.0
                        P.act(lambda e, ac=ac, sgg=sgg, fc=fc, sc=sc: e.activation(out=sgg[:], in_=ac[:], func=AF.Sigmoid, bias=cb[:, l, fc:fc + 1], scale=sc),
                              reads=[act_, "cb"], writes=[sgt])
                        P.dve(lambda e, ac=ac, sgg=sgg, fc=fc, qo=qo: e.scalar_tensor_tensor(out=qo[:], in0=ac[:], scalar=cbs[:, fc:fc + 1], in1=sgg[:], op0=ALU.add, op1=ALU.mult),
                              reads=[act_, sgt, "cbs"], writes=[qot])
                        dst = (S["QT"] if fc < 4 else S["KT"])[(fc % 4) * 128:(fc % 4 + 1) * 128, t0:t0 + TB]
                        P.dma(lambda e, qo=qo, dst=dst: e.dma_start(out=dst, in_=qo[:]), reads=[qot], writes=["dram_qk"])
                        if fc >= 4:
                            h = fc - 4
                            half = (h % 2) * 512
                            for tt in range(4):
                                P.pe(lambda e, qo=qo, tt=tt, half=half: e.transpose(ps_t[:, half + tt * 128: half + (tt + 1) * 128], qo[:, tt * 128:(tt + 1) * 128], idb[:]),
                                     reads=[qot, "idb"], writes=[f"ps_t{h % 2}"])
                            P.act(lambda e, h=h, half=half: e.activation(out=ktok[:, :, h * 128:(h + 1) * 128], in_=mk(ps_t[:, half:half + 1], [[128, 4], [1, 128]]), func=AF.Copy),
                                  reads=[f"ps_t{h % 2}"], writes=["ktok"])
                    if pending:
                        pending.pop()()
                    pending.append(post)

                ti = 0
                for (c0, kind) in ((1024, "v"), (1536, "o"), (3088, "vn")):
                    for tt in range(4):
                        pt = ps_tok[ti % 2]
                        ptt = f"ps_tok{ti % 2}"
                        ti += 1
                        for kc in range(8):
                            P.pe(lambda e, kc=kc, pt=pt, tt=tt, c0=c0: e.matmul(pt[:], lhsT=hn[:, kc, 1 + tt * 128:1 + (tt + 1) * 128], rhs=wbf[:, kc, c0:c0 + 512], start=(kc == 0), stop=(kc == 7)),
                                 reads=WB + HN, writes=[ptt])
                        if pending:
                            pending.pop()()
                            P.dma(lambda e, t0=t0: e.dma_start(out=S["Ktok"][t0:t0 + TB, :].rearrange("(tt p) c -> p tt c", p=128), in_=ktok[:]),
                                  reads=["ktok"], writes=["dram_ktok"])
                            if b + 1 < NB:
                                prep(b + 1)
                            if b + 2 < NB:
                                load_x(b + 2)
                        if kind == "v":
                            P.act(lambda e, pt=pt, tt=tt: e.activation(out=vtok[:, tt, :], in_=pt[:], func=AF.Copy), reads=[ptt], writes=["vtok"])
                        elif kind == "vn":
                            P.dve(lambda e, pt=pt, tt=tt: e.tensor_copy(out=vntok[:, tt, :], in_=pt[:]), reads=[ptt], writes=["vntok"])
                        else:
                            P.act(lambda e, pt=pt, tt=tt: e.activation(out=osig[:, tt, :], in_=pt[:], func=AF.Sigmoid), reads=[ptt], writes=["osig"])
                P.dma(lambda e, t0=t0: e.dma_start(out=S["Vtok"][t0:t0 + TB, :].rearrange("(tt p) c -> p tt c", p=128), in_=vtok[:]),
                      reads=["vtok"], writes=["dram_vtok"])
                P.dma(lambda e, t0=t0: e.dma_start(out=S["Vn"][t0:t0 + TB, :].rearrange("(tt p) c -> p tt c", p=128), in_=vntok[:]),
                      reads=["vntok"], writes=["dram_vn"])
                P.dma(lambda e, t0=t0: e.dma_start(out=S["Osig"][t0:t0 + TB, :].rearrange("(tt p) c -> p tt c", p=128), in_=osig[:]),
                      reads=["osig"], writes=["dram_osig"])

                for kc in range(8):
                    P.pe(lambda e, kc=kc: e.matmul(ps_g[0:16, :], lhsT=wbf[:, kc, 2048:2064], rhs=hn[:, kc, 1:513], start=(kc == 0), stop=(kc == 7)),
                         reads=WB + HN, writes=["ps_g"])
                P.dve(lambda e: e.tensor_copy(out=gsb[:], in_=ps_g[0:16, :]), reads=["ps_g"], writes=["gsb"])
                P.dma(lambda e, t0=t0: e.dma_start(out=S["G"][:, t0:t0 + TB], in_=gsb[:]), reads=["gsb"], writes=["dram_g"])

                for j in range(8):
                    pp = ps_pre[qi % 2]
                    ppt = f"ps_pre{qi % 2}"
                    qo = qko[qi % 3]
                    qot = f"qko{qi % 3}"
                    qi += 1
                    c0 = 2064 + j * 128
                    for kc in range(8):
                        P.pe(lambda e, kc=kc, pp=pp, c0=c0: e.matmul(pp[:], lhsT=wbf[:, kc, c0:c0 + 128], rhs=hn[:, kc, 1:513], start=(kc == 0), stop=(kc == 7)),
                             reads=WB + HN, writes=[ppt])
                    sc = 0.125 if j < 4 else 1.0
                    P.act(lambda e, pp=pp, qo=qo, sc=sc: e.activation(out=qo[:], in_=pp[:], func=AF.Copy, scale=sc), reads=[ppt], writes=[qot])
                    dst = (S["QnT"] if j < 4 else S["KnT"])[(j % 4) * 128:(j % 4 + 1) * 128, t0:t0 + TB]
                    P.dma(lambda e, qo=qo, dst=dst: e.dma_start(out=dst, in_=qo[:]), reads=[qot], writes=["dram_qkn"])
            for b in range(NB):
                main(b)
            P.emit()

    def phase_B(self, l):
        nc, I, S = self.nc, self.I, self.S
        P = Prog(self.ctx)
        with ExitStack() as es:
            def sb(name, shape, dt):
                return es.enter_context(nc.sbuf_tensor(f"L{l}" + name, list(shape), dt))

            def psum(name, shape, dt=F32):
                return es.enter_context(nc.psum_tensor(f"L{l}" + name, list(shape), dt))

            NP = 36
            idf = sb("B_idf", [128, 128], F32)
            idb = sb("B_idb", [128, 128], BF16)
            trif = sb("B_trif", [128, 128], F32)
            trib = sb("B_trib", [128, 128], F32)
            tokser = sb("B_tokser", [128, 4, NCH, 8], F32)
            dcb = sb("B_dcb", [128, 8, NCH], F32)
            mnw = sb("B_mnw", [128, 512], F32)
            epsb = sb("B_epsb", [128, 1], F32)
            es1 = ExitStack()

            def sb1(name, shape, dt):
                return es1.enter_context(nc.sbuf_tensor(f"L{l}" + name, list(shape), dt))

            ser = {n: sb1("B_" + n, [NP, T], F32) for n in ("Ipre", "F", "Bp", "U", "M", "tmp", "W", "EM", "ES")}
            onesr = sb1("B_onesr", [NP, T], F32)
            gb = sb1("B_gb", [NP, DEPTH, 2], F32)
            ngb = sb1("B_ngb", [NP, 1], F32)
            mend = sb1("B_mend", [NP, NCH], F32)
            mprev = sb1("B_mprev", [NP, NCH], F32)
            dcs = sb1("B_dcs", [NP, NCH], F32)
            ps_ser = es1.enter_context(nc.psum_tensor(f"L{l}B_ps_ser", [128, 512], F32))

            for n in ("Ipre", "F"):
                P.dve(lambda e, n=n: e.memset(ser[n][:], 0.0), writes=[n])
            P.pool(lambda e: e.memset(onesr[:], 1.0), writes=["onesr"])
            P.dve(lambda e: e.memset(epsb[:], float(128 * EPS)), writes=["epsb"], tiny=True)
            P.dma(lambda e: e.dma_start(out=gb[:], in_=I["gb"]), writes=["gb"])
            P.dma(lambda e: e.dma_start(out=idf[:], in_=I["ident"]), writes=["idf"])
            P.dma(lambda e: e.dma_start(out=trif[:], in_=I["trif"]), writes=["trif"])
            P.dma(lambda e: e.dma_start(out=trib[:], in_=I["trib"]), writes=["trib"])
            P.dma(lambda e: e.dma_start(out=mnw[:], in_=I["mnw"][:, l, :]), writes=["mnw"])
            P.dve(lambda e: e.tensor_copy(out=idb[:], in_=idf[:]), reads=["idf"], writes=["idb"])
            P.dve(lambda e: e.tensor_scalar(out=mnw[:], in0=mnw[:], scalar1=float(128.0 ** 0.5), scalar2=None, op0=ALU.mult), reads=["mnw"], writes=["mnw"])
            G = S["G"]
            P.dma(lambda e: e.dma_start(out=ser["Ipre"][0:4, :], in_=G[0:4, :]), writes=["Ipre"])
            P.dma(lambda e: e.dma_start(out=ser["F"][0:4, :], in_=G[4:8, :]), writes=["F"])
            P.dma(lambda e: e.dma_start(out=ser["Ipre"][32:36, :], in_=G[8:12, :]), writes=["Ipre"])
            P.dma(lambda e: e.dma_start(out=ser["F"][32:36, :], in_=G[12:16, :]), writes=["F"])
            P.dve(lambda e: e.tensor_scalar(out=ngb[:], in0=gb[:, l, 1:2], scalar1=-1.0, scalar2=None, op0=ALU.mult), reads=["gb"], writes=["ngb"], tiny=True)
            P.dve(lambda e: e.tensor_scalar(out=ser["Ipre"][:], in0=ser["Ipre"][:], scalar1=gb[:, l, 0:1], scalar2=None, op0=ALU.add),
                  reads=["Ipre", "gb"], writes=["Ipre"])
            P.act(lambda e: e.activation(out=ser["tmp"][:], in_=ser["F"][:], func=AF.Exp, bias=ngb[:, 0:1], scale=-1.0), reads=["F", "ngb"], writes=["tmp"])
            P.act(lambda e: e.activation(out=ser["F"][:], in_=ser["tmp"][:], func=AF.Ln, bias=1.0), reads=["tmp"], writes=["F"])

            def rev(ap):
                a = ap.ap
                return bass.AP(ap.tensor, ap.offset + (a[-1][1] - 1) * a[-1][0], [list(p) for p in a[:-1]] + [[-a[-1][0], a[-1][1]]])

            def scan(out, data, op1, wtok, rtok):
                P.dve(lambda e: e.tensor_tensor_scan(out=out[0:4, :], data0=onesr[0:4, :], data1=data[0:4, :], initial=0.0, op0=ALU.mult, op1=op1),
                      reads=[rtok, "onesr"], writes=[wtok])
                P.dve(lambda e: e.tensor_tensor_scan(out=rev(out[32:36, :]), data0=onesr[32:36, :], data1=rev(data[32:36, :]), initial=0.0, op0=ALU.mult, op1=op1),
                      reads=[rtok, "onesr"], writes=[wtok])

            P.dve(lambda e: e.memset(ser["Bp"][:], 0.0), writes=["Bp"])
            P.pool(lambda e: e.memset(ser["M"][:], 0.0), writes=["M"])
            scan(ser["Bp"], ser["F"], ALU.add, "Bp", "F")
            P.dve(lambda e: e.tensor_tensor(out=ser["U"][:], in0=ser["Ipre"][:], in1=ser["Bp"][:], op=ALU.add), reads=["Ipre", "Bp"], writes=["U"])
            scan(ser["M"], ser["U"], ALU.max, "M", "U")
            P.dve(lambda e: e.tensor_tensor(out=ser["tmp"][:], in0=ser["Bp"][:], in1=ser["M"][:], op=ALU.subtract), reads=["Bp", "M"], writes=["tmp"])
            P.act(lambda e: e.activation(out=ser["EM"][:], in_=ser["tmp"][:], func=AF.Exp), reads=["tmp"], writes=["EM"])
            Mv = ser["M"]
            P.dve(lambda e: e.memset(mprev[:], 0.0), writes=["mprev"], tiny=True)
            P.dve(lambda e: e.memset(mend[:], 0.0), writes=["mend"], tiny=True)
            P.dve(lambda e: e.tensor_copy(out=mend[0:4, :], in_=mk(Mv[0:4, 127:128], [[128, NCH]])), reads=["M"], writes=["mend"], tiny=True)
            P.dve(lambda e: e.tensor_copy(out=mend[32:36, :], in_=mk(Mv[32:36, 0:1], [[128, NCH]])), reads=["M"], writes=["mend"], tiny=True)
            P.dve(lambda e: e.tensor_copy(out=mprev[0:4, 1:NCH], in_=mk(Mv[0:4, 127:128], [[128, NCH - 1]])), reads=["M"], writes=["mprev"], tiny=True)
            P.dve(lambda e: e.tensor_copy(out=mprev[32:36, 0:NCH - 1], in_=mk(Mv[32:36, 128:129], [[128, NCH - 1]])), reads=["M"], writes=["mprev"], tiny=True)
            P.dve(lambda e: e.tensor_tensor(out=ser["tmp"][:].rearrange("p (c t) -> p c t", t=128), in0=mk(mprev[:], [[1, NCH], [0, 128]]),
                                            in1=ser["M"][:].rearrange("p (c t) -> p c t", t=128), op=ALU.subtract),
                  reads=["mprev", "M", "EM"], writes=["tmp"])
            P.act(lambda e: e.activation(out=ser["W"][:], in_=ser["tmp"][:], func=AF.Exp), reads=["tmp"], writes=["W"])
            P.dve(lambda e: e.tensor_tensor(out=ser["tmp"][:].rearrange("p (c t) -> p c t", t=128), in0=ser["U"][:].rearrange("p (c t) -> p c t", t=128),
                                            in1=mk(mend[:], [[1, NCH], [0, 128]]), op=ALU.subtract),
                  reads=["mend", "U", "W"], writes=["tmp"])
            P.act(lambda e: e.activation(out=ser["ES"][:], in_=ser["tmp"][:], func=AF.Exp), reads=["tmp"], writes=["ES"])
            P.dve(lambda e: e.tensor_tensor(out=dcs[:], in0=mprev[:], in1=mend[:], op=ALU.subtract), reads=["mprev", "mend"], writes=["dcs"], tiny=True)
            P.act(lambda e: e.activation(out=dcs[:], in_=dcs[:], func=AF.Exp), reads=["dcs"], writes=["dcs"], tiny=True)
            P.dma(lambda e: e.dma_start(out=S["Mser"][0:4, :], in_=ser["M"][0:4, :]), reads=["M"], writes=["dram_M"])
            P.dma(lambda e: e.dma_start(out=S["Mser"][4:8, :], in_=ser["M"][32:36, :]), reads=["M"], writes=["dram_M"])
            P.dma(lambda e: e.dma_start(out=S["Wser"][0:4, :], in_=ser["W"][0:4, :]), reads=["W"], writes=["dram_W"])
            P.dma(lambda e: e.dma_start(out=S["Wser"][4:8, :], in_=ser["W"][32:36, :]), reads=["W"], writes=["dram_W"])
            P.dma(lambda e: e.dma_start(out=S["DCs"][0:4, :], in_=dcs[0:4, :]), reads=["dcs"], writes=["dram_DC"])
            P.dma(lambda e: e.dma_start(out=S["DCs"][4:8, :], in_=dcs[32:36, :]), reads=["dcs"], writes=["dram_DC"])
            P.dma(lambda e: e.dma_start(out=dcb[:], in_=bass.AP(S["DCs"].tensor, S["DCs"].offset, [[0, 128], [NCH, 8], [1, NCH]])),
                  reads=["dram_DC"], writes=["dcb"])
            for si, n in enumerate(("U", "W", "EM", "ES")):
                for c0 in range(0, NCH, 8):
                    for c in range(c0, c0 + 8):
                        P.pe(lambda e, n=n, c=c, c0=c0: e.matmul(ps_ser[:, (c - c0) * 64:(c - c0) * 64 + NP], lhsT=ser[n][:, c * 128:(c + 1) * 128], rhs=idf[0:NP, 0:NP], start=True, stop=True),
                             reads=[n, "idf"], writes=["ps_ser"])
                    P.dve(lambda e, si=si, c0=c0: e.tensor_copy(out=tokser[:, si, c0:c0 + 8, 0:4], in_=mk(ps_ser[:, 0:1], [[64, 8], [1, 4]])),
                          reads=["ps_ser"], writes=["tokser"])
                    P.dve(lambda e, si=si, c0=c0: e.tensor_copy(out=tokser[:, si, c0:c0 + 8, 4:8], in_=mk(ps_ser[:, 32:33], [[64, 8], [1, 4]])),
                          reads=["ps_ser"], writes=["tokser"])
            if self.debug:
                P.dma(lambda e: e.dma_start(out=S["dbgser"], in_=tokser[:]), reads=["tokser"], writes=["dram_dbgser"])

            P.emit()
            es1.close()

            cst = sb("B_cst", [128, 2, NCH, 4 * 129], BF16)
            c32 = sb("B_c32", [128, 2, 4 * 129], F32)
            v1_ = [sb(f"B_v1{i}", [128, 4, 129], BF16) for i in range(4)]
            P = Prog(self.ctx)
            es2 = ExitStack()
            kt_ = [es2.enter_context(nc.sbuf_tensor(f"L{l}B_ktok{i}", [128, 512], BF16)) for i in range(4)]
            vs_ = [es2.enter_context(nc.sbuf_tensor(f"L{l}B_vs{i}", [128, 4, 129], BF16)) for i in range(4)]
            ps_c = [[es2.enter_context(nc.psum_tensor(f"L{l}B_ps_c{d}{hp}", [128, 512], F32)) for hp in range(2)] for d in range(2)]
            P.dve(lambda e: e.memset(c32[:], 0.0), writes=[f"c32_{d}_{h}" for d in range(2) for h in range(4)])
            for i in range(4):
                P.pool(lambda e, i=i: e.memset(v1_[i][:, :, 128:129], 1.0), writes=[f"v1{i}"])
            li = 0
            for step in range(NCH):
                for d in range(2):
                    c = step if d == 0 else NCH - 1 - step
                    P.act(lambda e, d=d, c=c: e.activation(out=cst[:, d, c, :], in_=c32[:, d, :], func=AF.Copy),
                          reads=[f"c32_{d}_{h}" for h in range(4)], writes=[f"cst{d}_{c}"])
                    if step == NCH - 1:
                        continue
                    kt = kt_[li % 4]
                    v1 = v1_[li % 4]
                    vs = vs_[li % 4]
                    ktt = f"ktokb{li % 4}"
                    v1t = f"v1{li % 4}"
                    vst = f"vs{li % 4}"
                    li += 1
                    P.dma(lambda e, kt=kt, c=c: e.dma_start(out=kt[:], in_=S["Ktok"][c * 128:(c + 1) * 128, :]), writes=[ktt])
                    P.dma(lambda e, v1=v1, c=c: e.dma_start(out=v1[:, :, 0:128], in_=S["Vtok"][c * 128:(c + 1) * 128, :].rearrange("p (h x) -> p h x", h=4)), writes=[v1t])
                    P.dve(lambda e, vs=vs, v1=v1, c=c, d=d: e.tensor_tensor(out=vs[:], in0=v1[:], in1=mk(tokser[:, 3, c, d * 4:d * 4 + 1], [[1, 4], [0, 129]]), op=ALU.mult),
                           reads=[v1t, "tokser"], writes=[vst])
                    for hp in range(2):
                        pc = ps_c[d][hp]
                        for hh in range(2):
                            h = hp * 2 + hh
                            P.pe(lambda e, pc=pc, hh=hh, h=h, kt=kt, vs=vs: e.matmul(pc[:, hh * 129:(hh + 1) * 129], lhsT=kt[:, h * 128:(h + 1) * 128], rhs=vs[:, h, :], start=True, stop=True),
                                 reads=[ktt, vst], writes=[f"ps_c{d}{hp}{hh}"])
                        for hh in range(2):
                            h = hp * 2 + hh
                            P.dve(lambda e, pc=pc, hh=hh, h=h, d=d, c=c: e.scalar_tensor_tensor(out=c32[:, d, h * 129:(h + 1) * 129], in0=c32[:, d, h * 129:(h + 1) * 129],
                                                                                     scalar=dcb[:, d * 4 + h, c:c + 1], in1=pc[:, hh * 129:(hh + 1) * 129], op0=ALU.mult, op1=ALU.add),
                                  reads=[f"ps_c{d}{hp}{hh}", "dcb", f"c32_{d}_{h}"], writes=[f"c32_{d}_{h}"])
            P.emit()
            es2.close()

            import os
            if os.environ.get("BSTOP") == "2":
                return
            P = Prog(self.ctx)
            two = range(2)
            qT_ = [sb(f"B_qT{i}", [128, 4, 128], BF16) for i in two]
            kT_ = [sb(f"B_kT{i}", [128, 4, 128], BF16) for i in two]
            mb_ = [sb(f"B_mb{i}", [128, 8, 128], F32) for i in two]
            wb_ = [sb(f"B_wb{i}", [128, 8, 128], F32) for i in two]
            og_ = [sb(f"B_og{i}", [128, 512], F32) for i in two]
            gg_ = [sb(f"B_gg{i}", [128, 512], F32) for i in range(3)]
            sm_ = [sb(f"B_sm{i}", [128, 8, 128], F32) for i in two]
            ee_ = [sb(f"B_ee{i}", [128, 8, 128], F32) for i in two]
            pp_ = [sb(f"B_pp{i}", [128, 8, 128], BF16) for i in two]
            qw_ = [sb(f"B_qw{i}", [128, 8, 128], BF16) for i in two]
            hs_ = [sb(f"B_hs{i}", [128, 8, 128], F32) for i in two]
            hall_ = [sb(f"B_hall{i}", [128, 512], F32) for i in two]
            hsq_ = [sb(f"B_hsq{i}", [128, 512], F32) for i in two]
            dd_ = [sb(f"B_dd{i}", [128, 8], F32) for i in two]
            ss_ = [sb(f"B_ss{i}", [128, 4], F32) for i in two]
            ybf_ = [sb(f"B_ybf{i}", [128, 512], BF16) for i in two]
            yT_ = [sb(f"B_yT{i}", [128, 4, 128], BF16) for i in two]
            ps_s = [psum(f"B_ps_s{i}", [128, 512]) for i in two]
            ps_o = [psum(f"B_ps_o{i}", [128, 1024]) for i in two]
            ps_d = psum("B_ps_d", [128, 512])
            ps_y = psum("B_ps_y", [128, 1024], BF16)
            for i in range(4):
                P.pool(lambda e, i=i: e.memset(v1_[i][:, :, 128:129], 1.0), writes=[f"v1{i}"])
            tri4 = sb("B_tri4", [128, 8, 128], F32)
            for i in range(4):
                P.pool(lambda e, i=i: e.tensor_copy(out=tri4[:, i, :], in_=trif[:]), writes=["tri4"])
                P.pool(lambda e, i=i: e.tensor_copy(out=tri4[:, 4 + i, :], in_=trib[:]), writes=["tri4"])

            def b3_load(c):
                s = c % 2
                tk = c * 128
                P.dma(lambda e: e.dma_start(out=qT_[s][:], in_=S["QT"][:, tk:tk + 128].rearrange("(h p) t -> p h t", p=128)), writes=[f"qT{s}"])
                P.dma(lambda e: e.dma_start(out=kT_[s][:], in_=S["KT"][:, tk:tk + 128].rearrange("(h p) t -> p h t", p=128)), writes=[f"kT{s}"])
                P.dma(lambda e: e.dma_start(out=mb_[s][:], in_=bass.AP(S["Mser"].tensor, S["Mser"].offset + tk, [[0, 128], [T, 8], [1, 128]])), writes=[f"mb{s}"])
                P.dma(lambda e: e.dma_start(out=wb_[s][:], in_=bass.AP(S["Wser"].tensor, S["Wser"].offset + tk, [[0, 128], [T, 8], [1, 128]])), writes=[f"wb{s}"])
                P.dma(lambda e: e.dma_start(out=og_[s][:], in_=S["Osig"][tk:tk + 128, :]), writes=[f"og{s}"])
                P.dma(lambda e: e.dma_start(out=v1_[c % 4][:, :, 0:128], in_=S["Vtok"][tk:tk + 128, :].rearrange("p (h x) -> p h x", h=4)), writes=[f"v1{c % 4}"])

            def b3_stage(c, stage):
                s = c % 2
                tk = c * 128
                qT, kT, mb, wb, og, gg = qT_[s], kT_[s], mb_[s], wb_[s], og_[s], gg_[c % 3]
                sm, ee, pp, qw, hs, hall, hsq, dd, ss, ybf, yT = sm_[s], ee_[s], pp_[s], qw_[s], hs_[s], hall_[s], hsq_[s], dd_[s], ss_[s], ybf_[s], yT_[s]
                ps, po = ps_s[s], ps_o[s]
                v1 = v1_[c % 4]
                v1t = f"v1{c % 4}"
                if stage == 1:
                    for h in range(4):
                        P.pe(lambda e, ps=ps, kT=kT, qT=qT, h=h: e.matmul(ps[:, h * 128:(h + 1) * 128], lhsT=kT[:, h, :], rhs=qT[:, h, :], start=True, stop=True),
                             reads=[f"kT{s}", f"qT{s}"], writes=[f"ps_s{s}"])
                    psv = ps[:].rearrange("p (h t) -> p h t", h=4)
                    P.dve(lambda e, sm=sm, psv=psv: e.tensor_tensor(out=sm[:, 0:4, :], in0=psv, in1=tri4[:, 0:4, :], op=ALU.mult),
                          reads=[f"ps_s{s}", "tri4"], writes=[f"smf{s}"])
                    P.dve(lambda e, sm=sm, psv=psv: e.tensor_tensor(out=sm[:, 4:8, :], in0=psv, in1=tri4[:, 4:8, :], op=ALU.mult),
                          reads=[f"ps_s{s}", "tri4"], writes=[f"smb{s}"])
                    for r in range(8):
                        P.act(lambda e, ee=ee, mb=mb, r=r, c=c: e.activation(out=ee[:, r, :], in_=mb[:, r, :], func=AF.Exp, bias=tokser[:, 0, c, r:r + 1], scale=-1.0),
                              reads=[f"mb{s}", "tokser"], writes=[f"ee{s}"])
                    P.dve(lambda e, pp=pp, ee=ee, sm=sm: e.scalar_tensor_tensor(out=pp[:], in0=ee[:], scalar=1.0, in1=sm[:], op0=ALU.min, op1=ALU.mult),
                          reads=[f"ee{s}", f"smf{s}", f"smb{s}"], writes=[f"pp{s}"])
                    for d in range(2):
                        P.pool(lambda e, qw=qw, qT=qT, wb=wb, d=d: e.tensor_tensor(out=qw[:, d * 4:(d + 1) * 4, :], in0=qT[:], in1=wb[:, d * 4:(d + 1) * 4, :], op=ALU.mult),
                               reads=[f"qT{s}", f"wb{s}"], writes=[f"qw{s}"])
                    P.pool(lambda e, gg=gg, og=og: e.tensor_tensor(out=gg[:], in0=og[:], in1=mnw[:], op=ALU.mult), reads=[f"og{s}", "mnw"], writes=[f"gg{c % 3}"])
                    for r in range(8):
                        d, h = r // 4, r % 4
                        P.pe(lambda e, po=po, qw=qw, r=r, d=d, h=h, c=c: e.matmul(po[:, r * 128:(r + 1) * 128], lhsT=qw[:, r, :], rhs=cst[:, d, c, h * 129:h * 129 + 128], start=True, stop=False),
                             reads=[f"qw{s}", f"cst{d}_{c}"], writes=[f"ps_o{s}"])
                        P.pe(lambda e, po=po, pp=pp, v1=v1, r=r, h=h: e.matmul(po[:, r * 128:(r + 1) * 128], lhsT=pp[:, r, :], rhs=v1[:, h, 0:128], start=False, stop=True),
                             reads=[f"pp{s}", v1t], writes=[f"ps_o{s}"])
                        P.pe(lambda e, qw=qw, r=r, d=d, h=h, c=c, s=s: e.matmul(ps_d[:, s * 8 + r:s * 8 + r + 1], lhsT=qw[:, r, :], rhs=cst[:, d, c, h * 129 + 128:h * 129 + 129], start=True, stop=False),
                             reads=[f"qw{s}", f"cst{d}_{c}"], writes=[f"ps_d{s}"])
                        P.pe(lambda e, pp=pp, v1=v1, r=r, h=h, s=s: e.matmul(ps_d[:, s * 8 + r:s * 8 + r + 1], lhsT=pp[:, r, :], rhs=v1[:, h, 128:129], start=False, stop=True),
                             reads=[f"pp{s}", v1t], writes=[f"ps_d{s}"])
                if stage == 2:
                    P.act(lambda e, dd=dd, s=s: e.activation(out=dd[:], in_=ps_d[:, s * 8:s * 8 + 8], func=AF.Copy), reads=[f"ps_d{s}"], writes=[f"dd{s}"], tiny=True)
                    P.dve(lambda e, dd=dd: e.scalar_tensor_tensor(out=dd[:], in0=dd[:], scalar=-1.0, in1=dd[:], op0=ALU.mult, op1=ALU.max), reads=[f"dd{s}"], writes=[f"dd{s}"], tiny=True)
                    P.dve(lambda e, dd=dd, c=c: e.tensor_tensor(out=dd[:], in0=dd[:], in1=tokser[:, 2, c, :], op=ALU.max), reads=[f"dd{s}", "tokser"], writes=[f"dd{s}"], tiny=True)
                    P.dve(lambda e, dd=dd: e.reciprocal(out=dd[:], in_=dd[:]), reads=[f"dd{s}"], writes=[f"dd{s}"], tiny=True)
                    P.act(lambda e, hs=hs, po=po: e.activation(out=hs[:], in_=po[:].rearrange("p (r t) -> p r t", r=8), func=AF.Copy), reads=[f"ps_o{s}"], writes=[f"hs{s}"])
                    P.dve(lambda e, hs=hs, dd=dd: e.tensor_tensor(out=hs[:], in0=hs[:], in1=mk(dd[:, 0:1], [[1, 8], [0, 128]]), op=ALU.mult),
                          reads=[f"hs{s}", f"dd{s}"], writes=[f"hs{s}"])
                    P.pool(lambda e, hall=hall, hs=hs: e.tensor_tensor(out=hall[:].rearrange("p (h x) -> p h x", h=4), in0=hs[:, 0:4, :], in1=hs[:, 4:8, :], op=ALU.add),
                           reads=[f"hs{s}"], writes=[f"hall{s}"])
                if stage == 3:
                    P.pool(lambda e, hall=hall, hsq=hsq: e.tensor_tensor(out=hsq[:], in0=hall[:], in1=hall[:], op=ALU.mult), reads=[f"hall{s}"], writes=[f"hsq{s}"])
                    P.dve(lambda e, ss=ss, hsq=hsq: e.tensor_reduce(out=ss[:], in_=hsq[:].rearrange("p (h x) -> p h x", h=4), axis=AX.X, op=ALU.add), reads=[f"hsq{s}"], writes=[f"ss{s}"], tiny=True)
                    P.act(lambda e, ss=ss: e.activation(out=ss[:], in_=ss[:], func=AF.Ln, bias=epsb[:, 0:1]), reads=[f"ss{s}", "epsb"], writes=[f"ss{s}"], tiny=True)
                    P.act(lambda e, ss=ss: e.activation(out=ss[:], in_=ss[:], func=AF.Exp, scale=-0.5), reads=[f"ss{s}"], writes=[f"ss{s}"], tiny=True)
                    P.dve(lambda e, hsq=hsq, hall=hall, ss=ss: e.tensor_tensor(out=hsq[:].rearrange("p (h x) -> p h x", h=4), in0=hall[:].rearrange("p (h x) -> p h x", h=4), in1=mk(ss[:, 0:1], [[1, 4], [0, 128]]), op=ALU.mult),
                          reads=[f"hall{s}", f"ss{s}", f"hsq{s}"], writes=[f"hsq{s}"])
                    P.pool(lambda e, ybf=ybf, hsq=hsq, gg=gg: e.tensor_tensor(out=ybf[:], in0=hsq[:], in1=gg[:], op=ALU.mult), reads=[f"hsq{s}", f"gg{c % 3}"], writes=[f"ybf{s}"])
                    for h in range(4):
                        P.pe(lambda e, h=h, s=s, ybf=ybf: e.transpose(ps_y[:, s * 512 + h * 128: s * 512 + (h + 1) * 128], ybf[:, h * 128:(h + 1) * 128], idb[:]),
                             reads=[f"ybf{s}", "idb"], writes=[f"ps_y{s}"])
                    P.act(lambda e, yT=yT, s=s: e.activation(out=yT[:], in_=ps_y[:, s * 512:(s + 1) * 512].rearrange("p (h t) -> p h t", h=4), func=AF.Copy),
                          reads=[f"ps_y{s}"], writes=[f"yT{s}"])
                    P.dma(lambda e, yT=yT, tk=tk: e.dma_start(out=S["YT"][0:512, tk:tk + 128].rearrange("(h p) t -> p h t", p=128), in_=yT[:]),
                          reads=[f"yT{s}"], writes=["dram_yt"])

            import os
            B3D = int(os.environ.get("B3DEPTH", 3))
            if B3D == 2:
                b3_load(0)
                b3_load(1)
                b3_stage(0, 1)
                for c in range(NCH):
                    if c + 1 < NCH:
                        b3_stage(c + 1, 1)
                    if c + 2 < NCH:
                        b3_load(c + 2)
                    b3_stage(c, 2)
                    b3_stage(c, 3)
            else:
                b3_load(0)
                b3_load(1)
                b3_stage(0, 1)
                b3_stage(1, 1)
                b3_load(2)
                b3_stage(0, 2)
                for c in range(NCH):
                    if c + 2 < NCH:
                        b3_stage(c + 2, 1)
                    if c + 3 < NCH:
                        b3_load(c + 3)
                    if c + 1 < NCH:
                        b3_stage(c + 1, 2)
                    b3_stage(c, 3)
            P.emit()

    def phase_C(self, l):
        nc, I, S = self.nc, self.I, self.S
        P = Prog(self.ctx)
        with ExitStack() as es:
            def sb(name, shape, dt):
                return es.enter_context(nc.sbuf_tensor(f"L{l}" + name, list(shape), dt))

            def psum(name, shape, dt=F32):
                return es.enter_context(nc.psum_tensor(f"L{l}" + name, list(shape), dt))

            qn = sb("C_qn", [128, 4, T], BF16)
            kn = sb("C_kn", [128, 4, T], BF16)
            vn = sb("C_vn", [128, 32, 8, 65], BF16)
            nbi = sb("C_nbi", [128, 8, 5, 128], F32)
            nbe_ = [sb(f"C_nbe{i}", [128, 8, 4, 128], F32) for i in range(2)]
            idf = sb("C_idf", [128, 128], F32)
            idb = sb("C_idb", [128, 128], BF16)
            sbb_ = [sb(f"C_sb{i}", [128, 5, 128], F32) for i in range(3)]
            pt_ = [sb(f"C_pt{i}", [128, 5, 128], BF16) for i in range(3)]
            rd = sb("C_rd", [128, 8], F32)
            yn_ = [sb(f"C_yn{i}", [128, 8, 64], BF16) for i in range(2)]
            yT_ = [sb(f"C_yT{i}", [128, 4, 128], BF16) for i in range(2)]
            ps_s = [psum(f"C_ps_s{i}", [128, 1024]) for i in range(3)]
            ps_o = psum("C_ps_o", [128, 512])
            ps_y = psum("C_ps_y", [128, 1024], BF16)
            P.dma(lambda e: e.dma_start(out=idf[:], in_=I["ident"]), writes=["idf"])
            P.dve(lambda e: e.tensor_copy(out=idb[:], in_=idf[:]), reads=["idf"], writes=["idb"])
            P.dma(lambda e: e.dma_start(out=nbi[:], in_=I["nbi"][l]), writes=["nbi"])
            for g in range(4):
                P.dma(lambda e, g=g: e.dma_start(out=qn[:, g, :], in_=S["QnT"][g * 128:(g + 1) * 128, :]), writes=["qn"])
                P.dma(lambda e, g=g: e.dma_start(out=kn[:, g, :], in_=S["KnT"][g * 128:(g + 1) * 128, :]), writes=["kn"])
            P.pool(lambda e: e.memset(vn[:, :, :, 64:65], 1.0), writes=["vn"])
            for m in range(32):
                P.dma(lambda e, m=m: e.dma_start(out=vn[:, m, :, 0:64], in_=S["Vn"][m * 128:(m + 1) * 128, :].rearrange("p (h x) -> p h x", h=8)), writes=["vn"])
            def blk(i):
                if i < 2:
                    return [0, 1, 2, 3], i
                if i >= 30:
                    return [28, 29, 30, 31], i - 28
                return [i - 2, i - 1, i, i + 1, i + 2], None

            import os
            NU = int(os.environ.get("CNU", 32 * 8))

            def st_S(u):
                i, h = u // 8, u % 8
                tiles, eidx = blk(i)
                g, hp = h // 2, (h % 2) * 64
                ps = ps_s[u % 3]
                for j, m in enumerate(tiles):
                    P.pe(lambda e, ps=ps, j=j, m=m, g=g, hp=hp, i=i: e.matmul(ps[:, j * 128:(j + 1) * 128], lhsT=kn[hp:hp + 64, g, m * 128:(m + 1) * 128], rhs=qn[hp:hp + 64, g, i * 128:(i + 1) * 128], start=True, stop=True),
                         reads=["kn", "qn"], writes=[f"ps_s{u % 3}"])

            def st_E(u):
                i, h = u // 8, u % 8
                tiles, eidx = blk(i)
                nt = len(tiles)
                ps = ps_s[u % 3]
                sbb, pt = sbb_[u % 3], pt_[u % 3]
                if eidx is not None and h == 0:
                    P.dma(lambda e, eidx=eidx, i=i: e.dma_start(out=nbe_[i % 2][:], in_=I["nbe"][l, eidx]), writes=[f"nbe{i % 2}"])
                bias = nbi[:, h, :, :] if eidx is None else nbe_[i % 2][:, h, :, :]
                btok = "nbi" if eidx is None else f"nbe{i % 2}"
                P.dve(lambda e, ps=ps, sbb=sbb, bias=bias, nt=nt: e.tensor_tensor(out=sbb[:, 0:nt, :], in0=ps[:, 0:nt * 128].rearrange("p (j x) -> p j x", x=128), in1=bias, op=ALU.add),
                      reads=[f"ps_s{u % 3}", btok], writes=[f"sb{u % 3}"])
                P.act(lambda e, sbb=sbb, pt=pt, nt=nt: e.activation(out=pt[:, 0:nt, :], in_=sbb[:, 0:nt, :], func=AF.Exp), reads=[f"sb{u % 3}"], writes=[f"pt{u % 3}"])

            def st_PV(u):
                i, h = u // 8, u % 8
                tiles, eidx = blk(i)
                nt = len(tiles)
                pt = pt_[u % 3]
                so = (u % 4) * 128
                sot = f"ps_o{u % 4}"
                for j, m in enumerate(tiles):
                    P.pe(lambda e, pt=pt, j=j, m=m, h=h, nt=nt, so=so: e.matmul(ps_o[:, so:so + 65], lhsT=pt[:, j, :], rhs=vn[:, m, h, :], start=(j == 0), stop=(j == nt - 1)),
                         reads=[f"pt{u % 3}", "vn"], writes=[sot])
                P.dve(lambda e, h=h, so=so: e.reciprocal(out=rd[:, h:h + 1], in_=ps_o[:, so + 64:so + 65]), reads=[sot], writes=[f"rd{h}"], tiny=True)
                P.act(lambda e, h=h, so=so, i=i: e.activation(out=yn_[i % 2][:, h, :], in_=ps_o[:, so:so + 64], func=AF.Copy, scale=rd[:, h:h + 1]), reads=[sot, f"rd{h}"], writes=[f"yn{i % 2}"])
                if h == 7:
                    sl = i % 2
                    yT = yT_[sl]
                    for g in range(4):
                        P.pe(lambda e, g=g, sl=sl: e.transpose(ps_y[:, sl * 512 + g * 128:sl * 512 + (g + 1) * 128], yn_[sl][:, 2 * g:2 * g + 2, :].rearrange("p a b -> p (a b)"), idb[:]),
                             reads=[f"yn{sl}", "idb"], writes=[f"ps_y{sl}"])
                    P.dve(lambda e, yT=yT, sl=sl: e.tensor_copy(out=yT[:], in_=ps_y[:, sl * 512:(sl + 1) * 512].rearrange("p (h t) -> p h t", h=4)),
                          reads=[f"ps_y{sl}"], writes=[f"yT{sl}"])
                    P.dma(lambda e, yT=yT, i=i: e.dma_start(out=S["YT"][512:1024, i * 128:(i + 1) * 128].rearrange("(h p) t -> p h t", p=128), in_=yT[:]),
                          reads=[f"yT{sl}"], writes=["dram_yt"])

            CD = int(os.environ.get("CDEPTH", 1))
            if CD == 0:
                for u in range(NU):
                    st_S(u)
                    st_E(u)
                    st_PV(u)
            elif CD == 1:
                st_S(0)
                for u in range(NU):
                    if u + 1 < NU:
                        st_S(u + 1)
                    st_E(u)
                    st_PV(u)
            else:
                st_S(0)
                st_S(1)
                st_E(0)
                for u in range(NU):
                    if u + 2 < NU:
                        st_S(u + 2)
                    if u + 1 < NU:
                        st_E(u + 1)
                    st_PV(u)
            P.emit()

    def phase_D(self, l):
        nc, I, S = self.nc, self.I, self.S
        P = Prog(self.ctx)
        xsrc = I["xT"] if l == 0 else S["xres"]
        self.esE = ExitStack()
        esE = self.esE
        w1 = esE.enter_context(nc.sbuf_tensor(f"L{l}E_w1", [128, 8, DFF], BF16))
        w2 = esE.enter_context(nc.sbuf_tensor(f"L{l}E_w2", [128, 32, D], BF16))
        n2 = esE.enter_context(nc.sbuf_tensor(f"L{l}E_n2", [128, DEPTH, 8], F32))
        self.Ew = (w1, w2, n2)
        with ExitStack() as es:
            def sb(name, shape, dt):
                return es.enter_context(nc.sbuf_tensor(f"L{l}" + name, list(shape), dt))

            def psum(name, shape, dt=F32):
                return es.enter_context(nc.psum_tensor(f"L{l}" + name, list(shape), dt))

            wbf = sb("D_wbf", [128, 8, D], BF16)
            stE = [sb(f"D_stE{i}", [128, 1024], F32) for i in range(3)]
            xin = [sb(f"D_xin{i}", [128, 8, TB], F32) for i in range(2)]
            yin = [sb(f"D_yin{i}", [128, 8, TB], BF16) for i in range(2)]
            pso = [psum(f"D_ps{i}", [128, 512]) for i in range(4)]
            P.dma(lambda e: e.dma_start(out=n2[:], in_=I["n2"]), writes=["n2"])
            for kc in range(8):
                st = stE[kc % 3]
                stt = f"stE{kc % 3}"
                P.dma(lambda e, st=st, kc=kc: e.dma_start(out=st[:, 0:512], in_=I["w_out"][l, kc * 128:(kc + 1) * 128, 0:512]), writes=[stt + "a"])
                P.dma(lambda e, st=st, kc=kc: e.dma_start(out=st[:, 512:1024], in_=I["w_out"][l, kc * 128:(kc + 1) * 128, 512:1024]), writes=[stt + "b"])
                P.pool(lambda e, st=st, kc=kc: e.tensor_copy(out=wbf[:, kc, 0:512], in_=st[:, 0:512]), reads=[stt + "a"], writes=[f"wbf{kc}_0"])
                P.act(lambda e, st=st, kc=kc: e.activation(out=wbf[:, kc, 512:1024], in_=st[:, 512:1024], func=AF.Copy), reads=[stt + "b"], writes=[f"wbf{kc}_1"])
            WB = [f"wbf{kc}_{hf}" for kc in range(8) for hf in range(2)]

            def prefetch(pc):
                pj = pc + 8
                st = stE[pj % 3]
                ta, tb_ = f"stE{pj % 3}a", f"stE{pj % 3}b"
                if pc < 32:
                    kc, hf = pc // 4, pc % 4
                    c0 = hf * 1024
                    P.dma(lambda e: e.dma_start(out=st[:, 0:512], in_=I["w_ff1"][l, kc * 128:(kc + 1) * 128, c0:c0 + 512]), writes=[ta])
                    P.dma(lambda e: e.dma_start(out=st[:, 512:1024], in_=I["w_ff1"][l, kc * 128:(kc + 1) * 128, c0 + 512:c0 + 1024]), writes=[tb_])
                    P.pool(lambda e: e.tensor_scalar(out=w1[:, kc, c0:c0 + 512], in0=st[:, 0:512], scalar1=n2[:, l, kc:kc + 1], scalar2=None, op0=ALU.mult),
                           reads=[ta, "n2"], writes=[f"w1_{pc}a"])
                    P.act(lambda e: e.activation(out=w1[:, kc, c0 + 512:c0 + 1024], in_=st[:, 512:1024], func=AF.Copy, scale=n2[:, l, kc:kc + 1]),
                          reads=[tb_, "n2"], writes=[f"w1_{pc}b"])
                else:
                    k2 = pc - 32
                    P.dma(lambda e: e.dma_start(out=st[:, 0:512], in_=I["w_ff2"][l, k2 * 128:(k2 + 1) * 128, 0:512]), writes=[ta])
                    P.dma(lambda e: e.dma_start(out=st[:, 512:1024], in_=I["w_ff2"][l, k2 * 128:(k2 + 1) * 128, 512:1024]), writes=[tb_])
                    P.pool(lambda e: e.tensor_copy(out=w2[:, k2, 0:512], in_=st[:, 0:512]), reads=[ta], writes=[f"w2_{k2}a"])
                    P.act(lambda e: e.activation(out=w2[:, k2, 512:1024], in_=st[:, 512:1024], func=AF.Copy), reads=[tb_], writes=[f"w2_{k2}b"])

            def load(b):
                t0 = b * TB
                P.dma(lambda e, b=b, t0=t0: e.dma_start(out=xin[b % 2][:], in_=xsrc[:, t0:t0 + TB].rearrange("(kc p) t -> p kc t", p=128)), writes=[f"xin{b % 2}"])
                P.dma(lambda e, b=b, t0=t0: e.dma_start(out=yin[b % 2][:], in_=S["YT"][:, t0:t0 + TB].rearrange("(kc p) t -> p kc t", p=128)), writes=[f"yin{b % 2}"])

            load(0)
            pi = 0
            for b in range(NB):
                if b + 1 < NB:
                    load(b + 1)
                for pc in range(8 * b, 8 * b + 8):
                    prefetch(pc)
                t0 = b * TB
                xi, yi = xin[b % 2], yin[b % 2]
                for dc in range(8):
                    ps = pso[pi % 4]
                    pst = f"ps{pi % 4}"
                    pi += 1
                    for kc in range(8):
                        P.pe(lambda e, ps=ps, kc=kc, dc=dc, yi=yi: e.matmul(ps[:], lhsT=wbf[:, kc, dc * 128:(dc + 1) * 128], rhs=yi[:, kc, :], start=(kc == 0), stop=(kc == 7)),
                             reads=WB + [f"yin{b % 2}"], writes=[pst])
                    P.dve(lambda e, ps=ps, dc=dc, xi=xi: e.tensor_tensor(out=xi[:, dc, :], in0=xi[:, dc, :], in1=ps[:], op=ALU.add),
                          reads=[pst, f"xin{b % 2}"], writes=[f"xin{b % 2}"])
                P.dma(lambda e, xi=xi, t0=t0: e.dma_start(out=S["xres"][:, t0:t0 + TB].rearrange("(kc p) t -> p kc t", p=128), in_=xi[:]),
                      reads=[f"xin{b % 2}"], writes=["dram_x"])
            P.emit()

    def phase_E(self, l):
        nc, I, S = self.nc, self.I, self.S
        P = Prog(self.ctx)
        last = (l == DEPTH - 1)
        w1, w2, n2 = self.Ew
        with ExitStack() as es:
            def sb(name, shape, dt):
                return es.enter_context(nc.sbuf_tensor(f"L{l}" + name, list(shape), dt))

            def psum(name, shape, dt=F32):
                return es.enter_context(nc.psum_tensor(f"L{l}" + name, list(shape), dt))

            nf = sb("E_nf", [128, 8], F32)
            ones = sb("E_ones", [128, 128], BF16)
            epsb = sb("E_epsb", [128, 1], F32)
            xin = [sb(f"E_xin{i}", [128, 8, TB], F32) for i in range(2)]
            xsq = sb("E_xsq", [128, 8, TB], BF16)
            rstd = sb("E_rstd", [128, TB], F32)
            hn = sb("E_hn", [128, 8, TB], BF16)
            uu = [sb(f"E_u{i}", [128, 8, TB], BF16) for i in range(2)]
            rtmp = [sb(f"E_rtmp{i}", [128, TB], F32) for i in range(2)]
            ps_ssq = psum("E_ps_ssq", [128, 512])
            ps1 = [psum(f"E_ps1_{i}", [128, 512]) for i in range(3)]
            ps2 = [psum(f"E_ps2_{i}", [128, 512]) for i in range(3)]
            P.dma(lambda e: e.dma_start(out=nf[:], in_=I["nf"]), writes=["nf"])
            P.dve(lambda e: e.tensor_scalar(out=nf[:], in0=nf[:], scalar1=32.0, scalar2=None, op0=ALU.mult), reads=["nf"], writes=["nf"], tiny=True)
            P.dve(lambda e: e.memset(ones[:], 1.0), writes=["ones"])
            P.dve(lambda e: e.memset(epsb[:], float(D * EPS)), writes=["epsb"], tiny=True)
            cnt = {"p1": 0, "p2": 0}
            HN = [f"hn{kc}" for kc in range(8)]

            def load(b):
                t0 = b * TB
                P.dma(lambda e, b=b, t0=t0: e.dma_start(out=xin[b % 2][:], in_=S["xres"][:, t0:t0 + TB].rearrange("(kc p) t -> p kc t", p=128)), writes=[f"xin{b % 2}"])

            def rms(xi, xt):
                P.act(lambda e: e.activation(out=xsq[:], in_=xi[:], func=AF.Square), reads=[xt], writes=["xsq"])
                for kc in range(8):
                    P.pe(lambda e, kc=kc: e.matmul(ps_ssq[:], lhsT=ones[:], rhs=xsq[:, kc, :], start=(kc == 0), stop=(kc == 7)), reads=["ones", "xsq"], writes=["ps_ssq"])
                P.act(lambda e: e.activation(out=rstd[:], in_=ps_ssq[:], func=AF.Ln, bias=epsb[:, 0:1]), reads=["ps_ssq", "epsb"], writes=["rstd"])
                P.act(lambda e: e.activation(out=rstd[:], in_=rstd[:], func=AF.Exp, scale=-0.5), reads=["rstd"], writes=["rstd"])

            def prep(b):
                xi = xin[b % 2]
                xt = f"xin{b % 2}"
                rms(xi, xt)
                for kc in range(8):
                    P.dve(lambda e, kc=kc, xi=xi: e.scalar_tensor_tensor(out=hn[:, kc, :], in0=xi[:, kc, :], scalar=32.0, in1=rstd[:], op0=ALU.mult, op1=ALU.mult),
                          reads=[xt, "rstd"], writes=[f"hn{kc}"])

            def S1(b, grp):
                gi = (b * 4 + grp) % 2
                u = uu[gi]
                ut = f"u{gi}"
                for fj in range(8):
                    fc = grp * 8 + fj
                    ps = ps1[cnt["p1"] % 3]
                    pst = f"ps1_{cnt['p1'] % 3}"
                    rt = rtmp[cnt["p1"] % 2]
                    rtt = f"rtmp{cnt['p1'] % 2}"
                    cnt["p1"] += 1
                    for kc in range(8):
                        P.pe(lambda e, ps=ps, kc=kc, fc=fc: e.matmul(ps[:], lhsT=w1[:, kc, fc * 128:(fc + 1) * 128], rhs=hn[:, kc, :], start=(kc == 0), stop=(kc == 7)),
                             reads=HN, writes=[pst])
                    P.act(lambda e, ps=ps, rt=rt: e.activation(out=rt[:], in_=ps[:], func=AF.Relu), reads=[pst], writes=[rtt])
                    eng = P.dve if fj % 2 == 0 else P.pool
                    eng(lambda e, u=u, fj=fj, rt=rt: e.tensor_tensor(out=u[:, fj, :], in0=rt[:], in1=rt[:], op=ALU.mult), reads=[rtt], writes=[f"{ut}_{fj}"])

            def S2(b, grp):
                gi = (b * 4 + grp) % 2
                u = uu[gi]
                ut = f"u{gi}"
                xi = xin[b % 2]
                xt = f"xin{b % 2}"
                UT = [f"{ut}_{fj}" for fj in range(8)]
                for dc in range(8):
                    ps = ps2[cnt["p2"] % 3]
                    pst = f"ps2_{cnt['p2'] % 3}"
                    cnt["p2"] += 1
                    for fj in range(8):
                        P.pe(lambda e, ps=ps, fj=fj, grp=grp, dc=dc, u=u: e.matmul(ps[:], lhsT=w2[:, grp * 8 + fj, dc * 128:(dc + 1) * 128], rhs=u[:, fj, :], start=(fj == 0), stop=(fj == 7)),
                             reads=UT, writes=[pst])
                    P.dve(lambda e, ps=ps, dc=dc, xi=xi: e.tensor_tensor(out=xi[:, dc, :], in0=xi[:, dc, :], in1=ps[:], op=ALU.add),
                          reads=[pst, xt], writes=[xt])

            def fin(b):
                t0 = b * TB
                xi = xin[b % 2]
                xt = f"xin{b % 2}"
                if not last:
                    P.dma(lambda e, xi=xi, t0=t0: e.dma_start(out=S["xres"][:, t0:t0 + TB].rearrange("(kc p) t -> p kc t", p=128), in_=xi[:]),
                          reads=[xt], writes=["dram_x"])
                else:
                    rms(xi, xt)
                    for kc in range(8):
                        P.dve(lambda e, kc=kc, xi=xi: e.scalar_tensor_tensor(out=xi[:, kc, :], in0=xi[:, kc, :], scalar=nf[:, kc:kc + 1], in1=rstd[:], op0=ALU.mult, op1=ALU.mult),
                              reads=[xt, "rstd", "nf"], writes=[xt])
                    P.dma(lambda e, xi=xi, t0=t0: e.dma_start(out=self.out[:, t0:t0 + TB].rearrange("(kc p) t -> p kc t", p=128), in_=xi[:]),
                          reads=[xt], writes=["dram_out"])

            load(0)
            prep(0)
            S1(0, 0)
            for b in range(NB):
                if b + 1 < NB:
                    load(b + 1)
                for grp in range(4):
                    if grp < 3:
                        S1(b, grp + 1)
                    elif b + 1 < NB:
                        prep(b + 1)
                        S1(b + 1, 0)
                    S2(b, grp)
                fin(b)
            P.emit()
        self.esE.close()


def _na_bias_tables(rpb_l):
    H = rpb_l.shape[0]
    kl = np.arange(128)
    kr_l, kc = kl // 64, kl % 64
    ql = np.arange(128)
    qr_l, qc = ql // 64, ql % 64

    def tile(i, m):
        kr = 2 * m + kr_l[:, None]
        qr = 2 * i + qr_l[None, :]
        rs = np.clip(qr - 4, 0, 56)
        vr = (kr >= rs) & (kr < rs + 8)
        cs = np.clip(qc[None, :] - 8, 0, 48)
        vc = (kc[:, None] >= cs) & (kc[:, None] < cs + 16)
        ri = np.clip(kr - qr + 7, 0, 14)
        ci = np.clip(kc[:, None] - qc[None, :] + 15, 0, 30)
        vals = rpb_l[:, ri, ci]
        return np.where((vr & vc)[None], vals, np.float32(-30000.0)).astype(np.float32)

    nbi = np.stack([tile(2, m) for m in range(5)], axis=1)
    nbi = np.ascontiguousarray(nbi.transpose(2, 0, 1, 3))
    nbe = []
    for i, ms in ((0, range(4)), (1, range(4)), (30, range(28, 32)), (31, range(28, 32))):
        t = np.stack([tile(i, m) for m in ms], axis=1)
        nbe.append(t.transpose(2, 0, 1, 3))
    nbe = np.ascontiguousarray(np.stack(nbe, axis=0))
    return nbi, nbe


def _prep_shared(norm1_w, conv_w, conv_b, gate_b, mlstm_norm_w, rpb, norm2_w, final_norm_w):
    f = np.float32
    sh = {}
    sh["n1"] = np.ascontiguousarray(norm1_w.reshape(DEPTH, 8, 128).transpose(2, 0, 1)).astype(f)
    sh["n2"] = np.ascontiguousarray(norm2_w.reshape(DEPTH, 8, 128).transpose(2, 0, 1)).astype(f)
    sh["nf"] = np.ascontiguousarray(final_norm_w.reshape(8, 128).T).astype(f)
    sh["cw"] = np.ascontiguousarray(conv_w.reshape(DEPTH, 3, 8, 128).transpose(3, 0, 1, 2)).astype(f)
    sh["cb"] = np.ascontiguousarray(conv_b.reshape(DEPTH, 8, 128).transpose(2, 0, 1)).astype(f)
    gb = np.zeros((36, DEPTH, 2), f)
    g4 = gate_b.reshape(DEPTH, 4, 4)
    gb[0:4, :, 0] = g4[:, 0, :].T
    gb[0:4, :, 1] = g4[:, 1, :].T
    gb[32:36, :, 0] = g4[:, 2, :].T
    gb[32:36, :, 1] = g4[:, 3, :].T
    sh["gb"] = gb
    sh["mnw"] = np.ascontiguousarray(np.broadcast_to(mlstm_norm_w[None], (128, DEPTH, 512))).astype(f)
    nbi, nbe = zip(*[_na_bias_tables(np.asarray(rpb[l], f)) for l in range(DEPTH)])
    sh["nbi"] = np.stack(nbi, 0)
    sh["nbe"] = np.stack(nbe, 0)
    sh["ident"] = np.eye(128, dtype=f)
    sh["trif"] = np.triu(np.ones((128, 128), f))
    sh["trib"] = np.tril(np.ones((128, 128), f))
    return sh


_CACHE = {}


def _get_nc(debug=False, stop_after=None):
    key = (debug, stop_after)
    if key not in _CACHE:
        b = Builder(debug=debug, stop_after=stop_after)
        nc = b.build()
        _CACHE[key] = (nc, b)
    return _CACHE[key]


def run(inputs, debug=False, stop_after=None, cores=8):
    x = np.asarray(inputs["x"], np.float32)
    sh = _prep_shared(*[np.asarray(inputs[k], np.float32) for k in
                        ("norm1_w", "conv_w", "conv_b", "gate_b", "mlstm_norm_w", "rpb", "norm2_w", "final_norm_w")])
    for k in ("w_in", "w_out", "w_ff1", "w_ff2"):
        sh[k] = np.ascontiguousarray(np.asarray(inputs[k], np.float32))
    nc, b = _get_nc(debug, stop_after)
    in_maps = []
    for c in range(cores):
        m = dict(sh)
        m["xT"] = np.ascontiguousarray(x[c].T)
        in_maps.append(m)
    res = run_bass_kernel_spmd(nc, in_maps, core_ids=list(range(cores)))
    return res, b


def kernel(x, norm1_w, w_in, conv_w, conv_b, gate_b, mlstm_norm_w, rpb, w_out,
           norm2_w, w_ff1, w_ff2, final_norm_w):
    inputs = dict(x=x, norm1_w=norm1_w, w_in=w_in, conv_w=conv_w, conv_b=conv_b, gate_b=gate_b,
                  mlstm_norm_w=mlstm_norm_w, rpb=rpb, w_out=w_out, norm2_w=norm2_w, w_ff1=w_ff1,
                  w_ff2=w_ff2, final_norm_w=final_norm_w)
    res, _ = run(inputs)
    out = np.stack([np.ascontiguousarray(r["outT"].T) for r in res.results], axis=0)
    return out.astype(np.float32)
```

```python
import numpy as np
from contextlib import ExitStack
import concourse.bass as bass
import concourse.mybir as mybir
from concourse.bass_utils import run_bass_kernel_spmd

F32 = mybir.dt.float32
BF16 = mybir.dt.bfloat16
AF = mybir.ActivationFunctionType
ALU = mybir.AluOpType
AX = mybir.AxisListType

T = 4096
D = 1024
DEPTH = 2
DIN = 3600
DFF = 4096
NB = 8
TB = 512
NCH = 32
EPS = 1e-6
ALPHA = 128.0 ** -0.5

ENGS = ("pe", "act", "dve", "pool", "sp")
EPOCH = 4000
NDMASEM = 8
SAME_SYNC = True


class Op:
    __slots__ = ("eng", "fn", "deps", "dma", "idx", "sig", "semk", "val", "tiny")

    def __init__(self, eng, fn, dma, tiny=False):
        self.eng = eng
        self.fn = fn
        self.dma = dma
        self.tiny = tiny
        self.deps = set()
        self.sig = False
        self.semk = None
        self.val = 0


class Ctx:
    def __init__(self, nc):
        self.nc = nc
        self.sems = {}
        self.ccount = {e: 0 for e in ENGS}
        self.dval = {}
        self.dk = {e: 0 for e in ENGS}
        self.waited = {e: {} for e in ENGS}
        self.nops = 0
        self.nwaits = 0

    def sem(self, k):
        if k not in self.sems:
            self.sems[k] = self.nc.alloc_semaphore("s_" + "_".join(str(x) for x in k))
        return self.sems[k]


class Prog:
    def __init__(self, ctx):
        self.ctx = ctx
        self.nc = ctx.nc
        self.ops = []
        self.last_w = {}
        self.readers = {}

    def op(self, eng, fn, reads=(), writes=(), dma=False, tiny=False):
        o = Op(eng, fn, dma, tiny)
        o.idx = len(self.ops)
        for r in reads:
            w = self.last_w.get(r)
            if w is not None:
                o.deps.add(w)
        for w_ in writes:
            w = self.last_w.get(w_)
            if w is not None:
                o.deps.add(w)
            for rd in self.readers.get(w_, ()):
                o.deps.add(rd)
        o.deps.discard(o.idx)
        for r in reads:
            self.readers.setdefault(r, []).append(o.idx)
        for w_ in writes:
            self.last_w[w_] = o.idx
            self.readers[w_] = []
        self.ops.append(o)
        return o

    def pe(self, fn, reads=(), writes=()):
        return self.op("pe", fn, reads, writes)

    def act(self, fn, reads=(), writes=(), tiny=False):
        return self.op("act", fn, reads, writes, tiny=tiny)

    def dve(self, fn, reads=(), writes=(), tiny=False):
        return self.op("dve", fn, reads, writes, tiny=tiny)

    def pool(self, fn, reads=(), writes=(), tiny=False):
        return self.op("pool", fn, reads, writes, tiny=tiny)

    def dma(self, fn, reads=(), writes=(), q="sp"):
        return self.op(q, fn, reads, writes, dma=True)

    def _unsynced(self, od, o):
        return (od.eng == o.eng and not od.dma and not o.dma
                and (od.eng == "pe" or not SAME_SYNC or not od.tiny))

    def emit(self):
        ctx = self.ctx
        nc = self.nc
        ops = self.ops
        per = {e: [o for o in ops if o.eng == e] for e in ENGS}
        for o in ops:
            for d in o.deps:
                od = ops[d]
                if od.dma or self._unsynced(od, o):
                    continue
                od.sig = True
        for e in ENGS:
            for o in reversed(per[e]):
                if not o.dma:
                    o.sig = True
                    break
        final = {}
        for e in ENGS:
            for o in per[e]:
                if o.dma:
                    i = ctx.dk[e] % NDMASEM
                    ctx.dk[e] += 1
                    k = ("d", e, i)
                    ctx.dval[k] = ctx.dval.get(k, 0) + 16
                    o.semk = k
                    o.val = ctx.dval[k]
                    final[k] = o.val
                elif o.sig:
                    n = ctx.ccount[e]
                    ctx.ccount[e] += 1
                    k = ("c", e, n // EPOCH)
                    o.semk = k
                    o.val = n % EPOCH + 1
                    final[k] = o.val
        for k in final:
            ctx.sem(k)
        ctx.nops += len(ops)

        def run_engine(e, eng):
            waited = ctx.waited[e]

            def wait(k, v):
                if waited.get(k, 0) >= v:
                    return
                waited[k] = v
                eng.wait_ge(ctx.sems[k], v)
                ctx.nwaits += 1

            for o in per[e]:
                need = {}
                for d in o.deps:
                    od = ops[d]
                    if od.semk is None or self._unsynced(od, o):
                        continue
                    if need.get(od.semk, 0) < od.val:
                        need[od.semk] = od.val
                if o.dma and o.val > 16:
                    if need.get(o.semk, 0) < o.val - 16:
                        need[o.semk] = o.val - 16
                for k, v in need.items():
                    wait(k, v)
                ins = o.fn(eng)
                if o.dma:
                    ins.then_inc(ctx.sems[o.semk], 16)
                elif o.sig:
                    ins.then_inc(ctx.sems[o.semk], 1)
            for k, v in final.items():
                if k[0] == "c" and k[1] == e:
                    if e == "pe" or not SAME_SYNC:
                        pass
                wait(k, v)

        with nc.Block() as block:
            block.tensor(lambda eng: run_engine("pe", eng))
            block.scalar(lambda eng: run_engine("act", eng))
            block.vector(lambda eng: run_engine("dve", eng))
            block.gpsimd(lambda eng: run_engine("pool", eng))
            block.sync(lambda eng: run_engine("sp", eng))


def mk(ap, dims, off=0):
    a = ap.ap
    return bass.AP(ap.tensor, ap.offset + off, [list(a[0])] + [list(d) for d in dims])


class Builder:
    def __init__(self, debug=False, stop_after=None):
        self.debug = debug
        self.stop_after = stop_after
        self.nc = bass.Bass("TRN2", target_bir_lowering=False)
        self.ctx = Ctx(self.nc)
        self.dbg_names = []

    def din(self, name, shape, dt=F32):
        return self.nc.dram_tensor(name, list(shape), dt, kind="ExternalInput").ap()

    def dscr(self, name, shape, dt):
        kind = "ExternalOutput" if self.debug else "Internal"
        if self.debug:
            self.dbg_names.append(name)
        return self.nc.dram_tensor(name, list(shape), dt, kind=kind).ap()

    def build(self):
        nc = self.nc
        I = {}
        I["xT"] = self.din("xT", [D, T])
        I["w_in"] = self.din("w_in", [DEPTH, D, DIN])
        I["w_out"] = self.din("w_out", [DEPTH, D, D])
        I["w_ff1"] = self.din("w_ff1", [DEPTH, D, DFF])
        I["w_ff2"] = self.din("w_ff2", [DEPTH, DFF, D])
        I["n1"] = self.din("n1", [128, DEPTH, 8])
        I["n2"] = self.din("n2", [128, DEPTH, 8])
        I["nf"] = self.din("nf", [128, 8])
        I["cw"] = self.din("cw", [128, DEPTH, 3, 8])
        I["cb"] = self.din("cb", [128, DEPTH, 8])
        I["gb"] = self.din("gb", [36, DEPTH, 2])
        I["mnw"] = self.din("mnw", [128, DEPTH, 512])
        I["nbi"] = self.din("nbi", [DEPTH, 128, 8, 5, 128])
        I["nbe"] = self.din("nbe", [DEPTH, 4, 128, 8, 4, 128])
        I["ident"] = self.din("ident", [128, 128])
        I["trif"] = self.din("trif", [128, 128])
        I["trib"] = self.din("trib", [128, 128])
        self.I = I
        self.out = nc.dram_tensor("outT", [D, T], F32, kind="ExternalOutput").ap()
        S = {}
        S["QT"] = self.dscr("QT", [512, T], BF16)
        S["KT"] = self.dscr("KT", [512, T], BF16)
        S["Ktok"] = self.dscr("Ktok", [T, 512], BF16)
        S["Vtok"] = self.dscr("Vtok", [T, 512], BF16)
        S["Osig"] = self.dscr("Osig", [T, 512], F32)
        S["G"] = self.dscr("G", [16, T], F32)
        S["QnT"] = self.dscr("QnT", [512, T], BF16)
        S["KnT"] = self.dscr("KnT", [512, T], BF16)
        S["Vn"] = self.dscr("Vn", [T, 512], BF16)
        S["YT"] = self.dscr("YT", [D, T], BF16)
        S["Mser"] = self.dscr("Mser", [8, T], F32)
        S["Wser"] = self.dscr("Wser", [8, T], F32)
        S["DCs"] = self.dscr("DCs", [8, NCH], F32)
        S["xres"] = self.dscr("xres", [D, T], F32)
        if self.debug:
            S["dbgser"] = self.dscr("dbgser", [128, 4, NCH, 8], F32)
        self.S = S
        phases = []
        for l in range(DEPTH):
            phases += [("A", l), ("B", l), ("C", l), ("D", l), ("E", l)]
        for i, (ph, l) in enumerate(phases):
            getattr(self, "phase_" + ph)(l)
            if self.stop_after is not None and i + 1 >= self.stop_after:
                break
        return nc

    def phase_A(self, l):
        nc, I, S = self.nc, self.I, self.S
        P = Prog(self.ctx)
        xsrc = I["xT"] if l == 0 else S["xres"]
        with ExitStack() as es:
            def sb(name, shape, dt):
                return es.enter_context(nc.sbuf_tensor(f"L{l}" + name, list(shape), dt))

            def psum(name, shape, dt=F32):
                return es.enter_context(nc.psum_tensor(f"L{l}" + name, list(shape), dt))

            wbf = sb("A_wbf", [128, 8, DIN], BF16)
            wst = [sb(f"A_wst{i}", [128, DIN], F32) for i in range(2)]
            n1 = sb("A_n1", [128, DEPTH, 8], F32)
            cw = sb("A_cw", [128, DEPTH, 3, 8], F32)
            cb = sb("A_cb", [128, DEPTH, 8], F32)
            cws = sb("A_cws", [128, 3, 8], F32)
            cbs = sb("A_cbs", [128, 8], F32)
            idf = sb("A_idf", [128, 128], F32)
            idb = sb("A_idb", [128, 128], BF16)
            ones = sb("A_ones", [128, 128], BF16)
            epsb = sb("A_epsb", [128, 1], F32)
            xin = [sb(f"A_xin{i}", [128, 8, 514], F32) for i in range(2)]
            xsq = sb("A_xsq", [128, 8, 514], BF16)
            rstd = sb("A_rstd", [128, 514], F32)
            hn_ = [sb(f"A_hn{i}", [128, 8, 514], BF16) for i in range(2)]
            pre = [sb(f"A_pre{i}", [128, 514], F32) for i in range(2)]
            acc = [sb(f"A_acc{i}", [128, 512], F32) for i in range(2)]
            sg = [sb(f"A_sg{i}", [128, 512], F32) for i in range(2)]
            qko = [sb(f"A_qko{i}", [128, 512], BF16) for i in range(3)]
            ktok = sb("A_ktok", [128, 4, 512], BF16)
            vtok = sb("A_vtok", [128, 4, 512], BF16)
            vntok = sb("A_vntok", [128, 4, 512], BF16)
            osig = sb("A_osig", [128, 4, 512], F32)
            gsb = sb("A_gsb", [16, 512], F32)
            ps_ssq = psum("A_ps_ssq", [128, 512])
            ps_h = psum("A_ps_h", [128, 512])
            ps_pre = [psum(f"A_ps_pre{i}", [128, 512]) for i in range(2)]
            ps_tok = [psum(f"A_ps_tok{i}", [128, 512]) for i in range(2)]
            ps_g = psum("A_ps_g", [128, 512])
            ps_t = psum("A_ps_t", [128, 1024], BF16)

            P.dma(lambda e: e.dma_start(out=n1[:], in_=I["n1"]), writes=["n1"])
            P.dma(lambda e: e.dma_start(out=cw[:], in_=I["cw"]), writes=["cw"])
            P.dma(lambda e: e.dma_start(out=cb[:], in_=I["cb"]), writes=["cb"])
            P.dma(lambda e: e.dma_start(out=idf[:], in_=I["ident"]), writes=["idf"])
            P.dve(lambda e: e.tensor_copy(out=idb[:], in_=idf[:]), reads=["idf"], writes=["idb"])
            P.dve(lambda e: e.memset(ones[:], 1.0), writes=["ones"])
            P.dve(lambda e: e.memset(epsb[:], float(D * EPS)), writes=["epsb"], tiny=True)
            P.dve(lambda e: e.tensor_scalar(out=cws[:, :, 0:4], in0=cw[:, l, :, 0:4], scalar1=ALPHA, scalar2=None, op0=ALU.mult),
                  reads=["cw"], writes=["cws"], tiny=True)
            P.dve(lambda e: e.tensor_copy(out=cws[:, :, 4:8], in_=cw[:, l, :, 4:8]), reads=["cw"], writes=["cws"], tiny=True)
            P.dve(lambda e: e.tensor_scalar(out=cbs[:, 0:4], in0=cb[:, l, 0:4], scalar1=ALPHA, scalar2=None, op0=ALU.mult),
                  reads=["cb"], writes=["cbs"], tiny=True)
            P.dve(lambda e: e.tensor_copy(out=cbs[:, 4:8], in_=cb[:, l, 4:8]), reads=["cb"], writes=["cbs"], tiny=True)
            for kc in range(8):
                st = wst[kc % 2]
                half = DIN // 2
                P.dma(lambda e, st=st, kc=kc: e.dma_start(out=st[:, 0:half], in_=I["w_in"][l, kc * 128:(kc + 1) * 128, 0:half]),
                      writes=[f"wst{kc % 2}a"])
                P.dma(lambda e, st=st, kc=kc: e.dma_start(out=st[:, half:DIN], in_=I["w_in"][l, kc * 128:(kc + 1) * 128, half:DIN]),
                      writes=[f"wst{kc % 2}b"])
                P.pool(lambda e, st=st, kc=kc: e.tensor_scalar(out=wbf[:, kc, 0:half], in0=st[:, 0:half], scalar1=n1[:, l, kc:kc + 1], scalar2=None, op0=ALU.mult),
                       reads=[f"wst{kc % 2}a", "n1"], writes=[f"wbf{kc}a"])
                P.dve(lambda e, st=st, kc=kc: e.tensor_scalar(out=wbf[:, kc, half:DIN], in0=st[:, half:DIN], scalar1=n1[:, l, kc:kc + 1], scalar2=None, op0=ALU.mult),
                      reads=[f"wst{kc % 2}b", "n1"], writes=[f"wbf{kc}b"])
            WB = [f"wbf{kc}{h}" for kc in range(8) for h in "ab"]

            def load_x(b):
                xi = xin[b % 2]
                t0 = b * TB
                lo = t0 - 1
                hi = t0 + TB + 1
                c0 = 0
                tok = f"xin{b % 2}"
                if b == 0:
                    lo = 0
                    c0 = 1
                    P.dve(lambda e, xi=xi: e.memset(xi[:, :, 0:1], 0.0), writes=[tok], tiny=True)
                if b == NB - 1:
                    hi = T
                    P.dve(lambda e, xi=xi: e.memset(xi[:, :, 513:514], 0.0), writes=[tok], tiny=True)
                n = hi - lo
                src = xsrc[:, lo:hi].rearrange("(kc p) t -> p kc t", p=128)
                P.dma(lambda e, xi=xi, src=src, c0=c0, n=n: e.dma_start(out=xi[:, :, c0:c0 + n], in_=src), writes=[tok])

            def prep(b):
                xi = xin[b % 2]
                xt = f"xin{b % 2}"
                hn = hn_[b % 2]
                P.act(lambda e, xi=xi: e.activation(out=xsq[:], in_=xi[:], func=AF.Square), reads=[xt], writes=["xsq"])
                for kc in range(8):
                    P.pe(lambda e, kc=kc: e.matmul(ps_ssq[:], lhsT=ones[:], rhs=xsq[:, kc, 1:513], start=(kc == 0), stop=(kc == 7)),
                         reads=["ones", "xsq"], writes=["ps_ssq"])
                for kc in range(8):
                    P.pe(lambda e, kc=kc: e.matmul(ps_h[:, 0:2], lhsT=ones[:], rhs=mk(xsq[:, kc, 0:1], [[513, 2]]), start=(kc == 0), stop=(kc == 7)),
                         reads=["ones", "xsq"], writes=["ps_hs"])
                P.act(lambda e: e.activation(out=rstd[:, 1:513], in_=ps_ssq[:], func=AF.Ln, bias=epsb[:, 0:1]), reads=["ps_ssq", "epsb"], writes=["rstd"])
                P.act(lambda e: e.activation(out=mk(rstd[:, 0:1], [[513, 2]]), in_=ps_h[:, 0:2], func=AF.Ln, bias=epsb[:, 0:1]), reads=["ps_hs", "epsb"], writes=["rstd"], tiny=True)
                P.act(lambda e: e.activation(out=rstd[:], in_=rstd[:], func=AF.Exp, scale=-0.5), reads=["rstd"], writes=["rstd"])
                for kc in range(8):
                    P.dve(lambda e, kc=kc, xi=xi, hn=hn: e.scalar_tensor_tensor(out=hn[:, kc, :], in0=xi[:, kc, :], scalar=32.0, in1=rstd[:], op0=ALU.mult, op1=ALU.mult),
                          reads=[xt, "rstd"], writes=[f"hn{b % 2}_{kc}"])

            load_x(0)
            load_x(1)
            prep(0)
            qi = 0
            pending = []
            def main(b):
                nonlocal qi
                hn = hn_[b % 2]
                t0 = b * TB
                HN = [f"hn{b % 2}_{kc}" for kc in range(8)]

                for fc in range(8):
                    pp = ps_pre[qi % 2]
                    ppt = f"ps_pre{qi % 2}"
                    pr = pre[qi % 2]
                    prt = f"pre{qi % 2}"
                    ac = acc[qi % 2]
                    act_ = f"acc{qi % 2}"
                    sgg = sg[qi % 2]
                    sgt = f"sg{qi % 2}"
                    qo = qko[qi % 3]
                    qot = f"qko{qi % 3}"
                    qi += 1
                    c0 = fc * 128
                    for kc in range(8):
                        P.pe(lambda e, kc=kc, pp=pp, c0=c0: e.matmul(pp[:], lhsT=wbf[:, kc, c0:c0 + 128], rhs=hn[:, kc, 1:513], start=(kc == 0), stop=(kc == 7)),
                             reads=WB + HN, writes=[ppt])
                    hc = 8 + 2 * (fc % 2)
                    for kc in range(8):
                        P.pe(lambda e, kc=kc, c0=c0, hc=hc: e.matmul(ps_h[:, hc:hc + 2], lhsT=wbf[:, kc, c0:c0 + 128], rhs=mk(hn[:, kc, 0:1], [[513, 2]]), start=(kc == 0), stop=(kc == 7)),
                             reads=WB + HN, writes=[f"ps_hp{fc % 2}"])
                    P.act(lambda e, pr=pr, pp=pp: e.activation(out=pr[:, 1:513], in_=pp[:], func=AF.Copy), reads=[ppt], writes=[prt])
                    P.act(lambda e, pr=pr, hc=hc: e.activation(out=mk(pr[:, 0:1], [[513, 2]]), in_=ps_h[:, hc:hc + 2], func=AF.Copy),
                          reads=[f"ps_hp{fc % 2}"], writes=[prt], tiny=True)
                    def post(fc=fc, pr=pr, prt=prt, ac=ac, act_=act_, sgg=sgg, sgt=sgt, qo=qo, qot=qot, t0=t0):
                        P.pool(lambda e, ac=ac, pr=pr, fc=fc: e.tensor_scalar(out=ac[:], in0=pr[:, 0:512], scalar1=cws[:, 0, fc:fc + 1], scalar2=None, op0=ALU.mult),
                               reads=[prt, "cws"], writes=[act_])
                        P.dve(lambda e, ac=ac, pr=pr, fc=fc: e.scalar_tensor_tensor(out=ac[:], in0=pr[:, 1:513], scalar=cws[:, 1, fc:fc + 1], in1=ac[:], op0=ALU.mult, op1=ALU.add),
                               reads=[prt, "cws", act_], writes=[act_])
                        P.dve(lambda e, ac=ac, pr=pr, fc=fc: e.scalar_tensor_tensor(out=ac[:], in0=pr[:, 2:514], scalar=cws[:, 2, fc:fc + 1], in1=ac[:], op0=ALU.mult, op1=ALU.add),
                               reads=[prt, "cws", act_], writes=[act_])
                        sc = (1.0 / ALPHA) if fc < 4 else 1.0
                        P.act(lambda e, ac=ac, sgg=sgg, fc=fc, sc=sc: e.activation(out=sgg[:], in_=ac[:], func=AF.Sigmoid, bias=cb[:, l, fc:fc + 1], scale=sc),
                              reads=[act_, "cb"], writes=[sgt])
                        P.dve(lambda e, ac=ac, sgg=sgg, fc=fc, qo=qo: e.scalar_tensor_tensor(out=qo[:], in0=ac[:], scalar=cbs[:, fc:fc + 1], in1=sgg[:], op0=ALU.add, op1=ALU.mult),
                              reads=[act_, sgt, "cbs"], writes=[qot])
                        dst = (S["QT"] if fc < 4 else S["KT"])[(fc % 4) * 128:(fc % 4 + 1) * 128, t0:t0 + TB]
                        P.dma(lambda e, qo=qo, dst=dst: e.dma_start(out=dst, in_=qo[:]), reads=[qot], writes=["dram_qk"])
                        if fc >= 4:
                            h = fc - 4
                            half = (h % 2) * 512
                            for tt in range(4):
                                P.pe(lambda e, qo=qo, tt=tt, half=half: e.transpose(ps_t[:, half + tt * 128: half + (tt + 1) * 128], qo[:, tt * 128:(tt + 1) * 128], idb[:]),
                                     reads=[qot, "idb"], writes=[f"ps_t{h % 2}"])
                            P.act(lambda e, h=h, half=half: e.activation(out=ktok[:, :, h * 128:(h + 1) * 128], in_=mk(ps_t[:, half:half + 1], [[128, 4], [1, 128]]), func=AF.Copy),
                                  reads=[f"ps_t{h % 2}"], writes=["ktok"])
                    if pending:
                        pending.pop()()
                    pending.append(post)

                ti = 0
                for (c0, kind) in ((1024, "v"), (1536, "o"), (3088, "vn")):
                    for tt in range(4):
                        pt = ps_tok[ti % 2]
                        ptt = f"ps_tok{ti % 2}"
                        ti += 1
                        for kc in range(8):
                            P.pe(lambda e, kc=kc, pt=pt, tt=tt, c0=c0: e.matmul(pt[:], lhsT=hn[:, kc, 1 + tt * 128:1 + (tt + 1) * 128], rhs=wbf[:, kc, c0:c0 + 512], start=(kc == 0), stop=(kc == 7)),
                                 reads=WB + HN, writes=[ptt])
                        if pending:
                            pending.pop()()
                            P.dma(lambda e, t0=t0: e.dma_start(out=S["Ktok"][t0:t0 + TB, :].rearrange("(tt p) c -> p tt c", p=128), in_=ktok[:]),
                                  reads=["ktok"], writes=["dram_ktok"])
                            if b + 1 < NB:
                                prep(b + 1)
                            if b + 2 < NB:
                                load_x(b + 2)
                        if kind == "v":
                            P.act(lambda e, pt=pt, tt=tt: e.activation(out=vtok[:, tt, :], in_=pt[:], func=AF.Copy), reads=[ptt], writes=["vtok"])
                        elif kind == "vn":
                            P.dve(lambda e, pt=pt, tt=tt: e.tensor_copy(out=vntok[:, tt, :], in_=pt[:]), reads=[ptt], writes=["vntok"])
                        else:
                            P.act(lambda e, pt=pt, tt=tt: e.activation(out=osig[:, tt, :], in_=pt[:], func=AF.Sigmoid), reads=[ptt], writes=["osig"])
                P.dma(lambda e, t0=t0: e.dma_start(out=S["Vtok"][t0:t0 + TB, :].rearrange("(tt p) c -> p tt c", p=128), in_=vtok[:]),
                      reads=["vtok"], writes=["dram_vtok"])
                P.dma(lambda e, t0=t0: e.dma_start(out=S["Vn"][t0:t0 + TB, :].rearrange("(tt p) c -> p tt c", p=128), in_=vntok[:]),
                      reads=["vntok"], writes=["dram_vn"])
                P.dma(lambda e, t0=t0: e.dma_start(out=S["Osig"][t0:t0 + TB, :].rearrange("(tt p) c -> p tt c", p=128), in_=osig[:]),
                      reads=["osig"], writes=["dram_osig"])

                for kc in range(8):
                    P.pe(lambda e, kc=kc: e.matmul(ps_g[0:16, :], lhsT=wbf[:, kc, 2048:2064], rhs=hn[:, kc, 1:513], start=(kc == 0), stop=(kc == 7)),
                         reads=WB + HN, writes=["ps_g"])
                P.dve(lambda e: e.tensor_copy(out=gsb[:], in_=ps_g[0:16, :]), reads=["ps_g"], writes=["gsb"])
                P.dma(lambda e, t0=t0: e.dma_start(out=S["G"][:, t0:t0 + TB], in_=gsb[:]), reads=["gsb"], writes=["dram_g"])

                for j in range(8):
                    pp = ps_pre[qi % 2]
                    ppt = f"ps_pre{qi % 2}"
                    qo = qko[qi % 3]
                    qot = f"qko{qi % 3}"
                    qi += 1
                    c0 = 2064 + j * 128
                    for kc in range(8):
                        P.pe(lambda e, kc=kc, pp=pp, c0=c0: e.matmul(pp[:], lhsT=wbf[:, kc, c0:c0 + 128], rhs=hn[:, kc, 1:513], start=(kc == 0), stop=(kc == 7)),
                             reads=WB + HN, writes=[ppt])
                    sc = 0.125 if j < 4 else 1.0
                    P.act(lambda e, pp=pp, qo=qo, sc=sc: e.activation(out=qo[:], in_=pp[:], func=AF.Copy, scale=sc), reads=[ppt], writes=[qot])
                    dst = (S["QnT"] if j < 4 else S["KnT"])[(j % 4) * 128:(j % 4 + 1) * 128, t0:t0 + TB]
                    P.dma(lambda e, qo=qo, dst=dst: e.dma_start(out=dst, in_=qo[:]), reads=[qot], writes=["dram_qkn"])
            for b in range(NB):
                main(b)
            P.emit()

    def phase_B(self, l):
        nc, I, S = self.nc, self.I, self.S
        P = Prog(self.ctx)
        with ExitStack() as es:
            def sb(name, shape, dt):
                return es.enter_context(nc.sbuf_tensor(f"L{l}" + name, list(shape), dt))

            def psum(name, shape, dt=F32):
                return es.enter_context(nc.psum_tensor(f"L{l}" + name, list(shape), dt))

            NP = 36
            idf = sb("B_idf", [128, 128], F32)
            idb = sb("B_idb", [128, 128], BF16)
            trif = sb("B_trif", [128, 128], F32)
            trib = sb("B_trib", [128, 128], F32)
            tokser = sb("B_tokser", [128, 4, NCH, 8], F32)
            dcb = sb("B_dcb", [128, 8, NCH], F32)
            mnw = sb("B_mnw", [128, 512], F32)
            epsb = sb("B_epsb", [128, 1], F32)
            es1 = ExitStack()

            def sb1(name, shape, dt):
                return es1.enter_context(nc.sbuf_tensor(f"L{l}" + name, list(shape), dt))

            ser = {n: sb1("B_" + n, [NP, T], F32) for n in ("Ipre", "F", "Bp", "U", "M", "tmp", "W", "EM", "ES")}
            onesr = sb1("B_onesr", [NP, T], F32)
            gb = sb1("B_gb", [NP, DEPTH, 2], F32)
            ngb = sb1("B_ngb", [NP, 1], F32)
            mend = sb1("B_mend", [NP, NCH], F32)
            mprev = sb1("B_mprev", [NP, NCH], F32)
            dcs = sb1("B_dcs", [NP, NCH], F32)
            ps_ser = es1.enter_context(nc.psum_tensor(f"L{l}B_ps_ser", [128, 512], F32))

            for n in ("Ipre", "F"):
                P.dve(lambda e, n=n: e.memset(ser[n][:], 0.0), writes=[n])
            P.pool(lambda e: e.memset(onesr[:], 1.0), writes=["onesr"])
            P.dve(lambda e: e.memset(epsb[:], float(128 * EPS)), writes=["epsb"], tiny=True)
            P.dma(lambda e: e.dma_start(out=gb[:], in_=I["gb"]), writes=["gb"])
            P.dma(lambda e: e.dma_start(out=idf[:], in_=I["ident"]), writes=["idf"])
            P.dma(lambda e: e.dma_start(out=trif[:], in_=I["trif"]), writes=["trif"])
            P.dma(lambda e: e.dma_start(out=trib[:], in_=I["trib"]), writes=["trib"])
            P.dma(lambda e: e.dma_start(out=mnw[:], in_=I["mnw"][:, l, :]), writes=["mnw"])
            P.dve(lambda e: e.tensor_copy(out=idb[:], in_=idf[:]), reads=["idf"], writes=["idb"])
            P.dve(lambda e: e.tensor_scalar(out=mnw[:], in0=mnw[:], scalar1=float(128.0 ** 0.5), scalar2=None, op0=ALU.mult), reads=["mnw"], writes=["mnw"])
            G = S["G"]
            P.dma(lambda e: e.dma_start(out=ser["Ipre"][0:4, :], in_=G[0:4, :]), writes=["Ipre"])
            P.dma(lambda e: e.dma_start(out=ser["F"][0:4, :], in_=G[4:8, :]), writes=["F"])
            P.dma(lambda e: e.dma_start(out=ser["Ipre"][32:36, :], in_=G[8:12, :]), writes=["Ipre"])
            P.dma(lambda e: e.dma_start(out=ser["F"][32:36, :], in_=G[12:16, :]), writes=["F"])
            P.dve(lambda e: e.tensor_scalar(out=ngb[:], in0=gb[:, l, 1:2], scalar1=-1.0, scalar2=None, op0=ALU.mult), reads=["gb"], writes=["ngb"], tiny=True)
            P.dve(lambda e: e.tensor_scalar(out=ser["Ipre"][:], in0=ser["Ipre"][:], scalar1=gb[:, l, 0:1], scalar2=None, op0=ALU.add),
                  reads=["Ipre", "gb"], writes=["Ipre"])
            P.act(lambda e: e.activation(out=ser["tmp"][:], in_=ser["F"][:], func=AF.Exp, bias=ngb[:, 0:1], scale=-1.0), reads=["F", "ngb"], writes=["tmp"])
            P.act(lambda e: e.activation(out=ser["F"][:], in_=ser["tmp"][:], func=AF.Ln, bias=1.0), reads=["tmp"], writes=["F"])

            def rev(ap):
                a = ap.ap
                return bass.AP(ap.tensor, ap.offset + (a[-1][1] - 1) * a[-1][0], [list(p) for p in a[:-1]] + [[-a[-1][0], a[-1][1]]])

            def scan(out, data, op1, wtok, rtok):
                P.dve(lambda e: e.tensor_tensor_scan(out=out[0:4, :], data0=onesr[0:4, :], data1=data[0:4, :], initial=0.0, op0=ALU.mult, op1=op1),
                      reads=[rtok, "onesr"], writes=[wtok])
                P.dve(lambda e: e.tensor_tensor_scan(out=rev(out[32:36, :]), data0=onesr[32:36, :], data1=rev(data[32:36, :]), initial=0.0, op0=ALU.mult, op1=op1),
                      reads=[rtok, "onesr"], writes=[wtok])

            P.dve(lambda e: e.memset(ser["Bp"][:], 0.0), writes=["Bp"])
            P.pool(lambda e: e.memset(ser["M"][:], 0.0), writes=["M"])
            scan(ser["Bp"], ser["F"], ALU.add, "Bp", "F")
            P.dve(lambda e: e.tensor_tensor(out=ser["U"][:], in0=ser["Ipre"][:], in1=ser["Bp"][:], op=ALU.add), reads=["Ipre", "Bp"], writes=["U"])
            scan(ser["M"], ser["U"], ALU.max, "M", "U")
            P.dve(lambda e: e.tensor_tensor(out=ser["tmp"][:], in0=ser["Bp"][:], in1=ser["M"][:], op=ALU.subtract), reads=["Bp", "M"], writes=["tmp"])
            P.act(lambda e: e.activation(out=ser["EM"][:], in_=ser["tmp"][:], func=AF.Exp), reads=["tmp"], writes=["EM"])
            Mv = ser["M"]
            P.dve(lambda e: e.memset(mprev[:], 0.0), writes=["mprev"], tiny=True)
            P.dve(lambda e: e.memset(mend[:], 0.0), writes=["mend"], tiny=True)
            P.dve(lambda e: e.tensor_copy(out=mend[0:4, :], in_=mk(Mv[0:4, 127:128], [[128, NCH]])), reads=["M"], writes=["mend"], tiny=True)
            P.dve(lambda e: e.tensor_copy(out=mend[32:36, :], in_=mk(Mv[32:36, 0:1], [[128, NCH]])), reads=["M"], writes=["mend"], tiny=True)
            P.dve(lambda e: e.tensor_copy(out=mprev[0:4, 1:NCH], in_=mk(Mv[0:4, 127:128], [[128, NCH - 1]])), reads=["M"], writes=["mprev"], tiny=True)
            P.dve(lambda e: e.tensor_copy(out=mprev[32:36, 0:NCH - 1], in_=mk(Mv[32:36, 128:129], [[128, NCH - 1]])), reads=["M"], writes=["mprev"], tiny=True)
            P.dve(lambda e: e.tensor_tensor(out=ser["tmp"][:].rearrange("p (c t) -> p c t", t=128), in0=mk(mprev[:], [[1, NCH], [0, 128]]),
                                            in1=ser["M"][:].rearrange("p (c t) -> p c t", t=128), op=ALU.subtract),
                  reads=["mprev", "M", "EM"], writes=["tmp"])
            P.act(lambda e: e.activation(out=ser["W"][:], in_=ser["tmp"][:], func=AF.Exp), reads=["tmp"], writes=["W"])
            P.dve(lambda e: e.tensor_tensor(out=ser["tmp"][:].rearrange("p (c t) -> p c t", t=128), in0=ser["U"][:].rearrange("p (c t) -> p c t", t=128),
                                            in1=mk(mend[:], [[1, NCH], [0, 128]]), op=ALU.subtract),
                  reads=["mend", "U", "W"], writes=["tmp"])
            P.act(lambda e: e.activation(out=ser["ES"][:], in_=ser["tmp"][:], func=AF.Exp), reads=["tmp"], writes=["ES"])
            P.dve(lambda e: e.tensor_tensor(out=dcs[:], in0=mprev[:], in1=mend[:], op=ALU.subtract), reads=["mprev", "mend"], writes=["dcs"], tiny=True)
            P.act(lambda e: e.activation(out=dcs[:], in_=dcs[:], func=AF.Exp), reads=["dcs"], writes=["dcs"], tiny=True)
            P.dma(lambda e: e.dma_start(out=S["Mser"][0:4, :], in_=ser["M"][0:4, :]), reads=["M"], writes=["dram_M"])
            P.dma(lambda e: e.dma_start(out=S["Mser"][4:8, :], in_=ser["M"][32:36, :]), reads=["M"], writes=["dram_M"])
            P.dma(lambda e: e.dma_start(out=S["Wser"][0:4, :], in_=ser["W"][0:4, :]), reads=["W"], writes=["dram_W"])
            P.dma(lambda e: e.dma_start(out=S["Wser"][4:8, :], in_=ser["W"][32:36, :]), reads=["W"], writes=["dram_W"])
            P.dma(lambda e: e.dma_start(out=S["DCs"][0:4, :], in_=dcs[0:4, :]), reads=["dcs"], writes=["dram_DC"])
            P.dma(lambda e: e.dma_start(out=S["DCs"][4:8, :], in_=dcs[32:36, :]), reads=["dcs"], writes=["dram_DC"])
            P.dma(lambda e: e.dma_start(out=dcb[:], in_=bass.AP(S["DCs"].tensor, S["DCs"].offset, [[0, 128], [NCH, 8], [1, NCH]])),
                  reads=["dram_DC"], writes=["dcb"])
            for si, n in enumerate(("U", "W", "EM", "ES")):
                for c0 in range(0, NCH, 8):
                    for c in range(c0, c0 + 8):
                        P.pe(lambda e, n=n, c=c, c0=c0: e.matmul(ps_ser[:, (c - c0) * 64:(c - c0) * 64 + NP], lhsT=ser[n][:, c * 128:(c + 1) * 128], rhs=idf[0:NP, 0:NP], start=True, stop=True),
                             reads=[n, "idf"], writes=["ps_ser"])
                    P.dve(lambda e, si=si, c0=c0: e.tensor_copy(out=tokser[:, si, c0:c0 + 8, 0:4], in_=mk(ps_ser[:, 0:1], [[64, 8], [1, 4]])),
                          reads=["ps_ser"], writes=["tokser"])
                    P.dve(lambda e, si=si, c0=c0: e.tensor_copy(out=tokser[:, si, c0:c0 + 8, 4:8], in_=mk(ps_ser[:, 32:33], [[64, 8], [1, 4]])),
                          reads=["ps_ser"], writes=["tokser"])
            if self.debug:
                P.dma(lambda e: e.dma_start(out=S["dbgser"], in_=tokser[:]), reads=["tokser"], writes=["dram_dbgser"])

            P.emit()
            es1.close()

            cst = sb("B_cst", [128, 2, NCH, 4 * 129], BF16)
            c32 = sb("B_c32", [128, 2, 4 * 129], F32)
            v1_ = [sb(f"B_v1{i}", [128, 4, 129], BF16) for i in range(4)]
            P = Prog(self.ctx)
            es2 = ExitStack()
            kt_ = [es2.enter_context(nc.sbuf_tensor(f"L{l}B_ktok{i}", [128, 512], BF16)) for i in range(4)]
            vs_ = [es2.enter_context(nc.sbuf_tensor(f"L{l}B_vs{i}", [128, 4, 129], BF16)) for i in range(4)]
            ps_c = [[es2.enter_context(nc.psum_tensor(f"L{l}B_ps_c{d}{hp}", [128, 512], F32)) for hp in range(2)] for d in range(2)]
            P.dve(lambda e: e.memset(c32[:], 0.0), writes=[f"c32_{d}_{h}" for d in range(2) for h in range(4)])
            for i in range(4):
                P.pool(lambda e, i=i: e.memset(v1_[i][:, :, 128:129], 1.0), writes=[f"v1{i}"])
            li = 0
            for step in range(NCH):
                for d in range(2):
                    c = step if d == 0 else NCH - 1 - step
                    P.act(lambda e, d=d, c=c: e.activation(out=cst[:, d, c, :], in_=c32[:, d, :], func=AF.Copy),
                          reads=[f"c32_{d}_{h}" for h in range(4)], writes=[f"cst{d}_{c}"])
                    if step == NCH - 1:
                        continue
                    kt = kt_[li % 4]
                    v1 = v1_[li % 4]
                    vs = vs_[li % 4]
                    ktt = f"ktokb{li % 4}"
                    v1t = f"v1{li % 4}"
                    vst = f"vs{li % 4}"
                    li += 1
                    P.dma(lambda e, kt=kt, c=c: e.dma_start(out=kt[:], in_=S["Ktok"][c * 128:(c + 1) * 128, :]), writes=[ktt])
                    P.dma(lambda e, v1=v1, c=c: e.dma_start(out=v1[:, :, 0:128], in_=S["Vtok"][c * 128:(c + 1) * 128, :].rearrange("p (h x) -> p h x", h=4)), writes=[v1t])
                    P.dve(lambda e, vs=vs, v1=v1, c=c, d=d: e.tensor_tensor(out=vs[:], in0=v1[:], in1=mk(tokser[:, 3, c, d * 4:d * 4 + 1], [[1, 4], [0, 129]]), op=ALU.mult),
                           reads=[v1t, "tokser"], writes=[vst])
                    for hp in range(2):
                        pc = ps_c[d][hp]
                        for hh in range(2):
                            h = hp * 2 + hh
                            P.pe(lambda e, pc=pc, hh=hh, h=h, kt=kt, vs=vs: e.matmul(pc[:, hh * 129:(hh + 1) * 129], lhsT=kt[:, h * 128:(h + 1) * 128], rhs=vs[:, h, :], start=True, stop=True),
                                 reads=[ktt, vst], writes=[f"ps_c{d}{hp}{hh}"])
                        for hh in range(2):
                            h = hp * 2 + hh
                            P.dve(lambda e, pc=pc, hh=hh, h=h, d=d, c=c: e.scalar_tensor_tensor(out=c32[:, d, h * 129:(h + 1) * 129], in0=c32[:, d, h * 129:(h + 1) * 129],
                                                                                     scalar=dcb[:, d * 4 + h, c:c + 1], in1=pc[:, hh * 129:(hh + 1) * 129], op0=ALU.mult, op1=ALU.add),
                                  reads=[f"ps_c{d}{hp}{hh}", "dcb", f"c32_{d}_{h}"], writes=[f"c32_{d}_{h}"])
            P.emit()
            es2.close()

            import os
            if os.environ.get("BSTOP") == "2":
                return
            P = Prog(self.ctx)
            two = range(2)
            qT_ = [sb(f"B_qT{i}", [128, 4, 512], BF16) for i in two]
            kT_ = [sb(f"B_kT{i}", [128, 4, 512], BF16) for i in two]
            mb_ = [sb(f"B_mb{i}", [128, 8, 256], F32) for i in two]
            wb_ = [sb(f"B_wb{i}", [128, 8, 256], F32) for i in two]
            og_ = [sb(f"B_og{i}", [128, 512], F32) for i in two]
            gg_ = [sb(f"B_gg{i}", [128, 512], F32) for i in range(3)]
            sm_ = [sb(f"B_sm{i}", [128, 8, 128], F32) for i in two]
            ee_ = [sb(f"B_ee{i}", [128, 8, 128], F32) for i in two]
            pp_ = [sb(f"B_pp{i}", [128, 8, 128], BF16) for i in two]
            qw_ = [sb(f"B_qw{i}", [128, 8, 128], BF16) for i in two]
            hs_ = [sb(f"B_hs{i}", [128, 8, 128], F32) for i in two]
            hall_ = [sb(f"B_hall{i}", [128, 512], F32) for i in two]
            hsq_ = [sb(f"B_hsq{i}", [128, 512], F32) for i in two]
            dd_ = [sb(f"B_dd{i}", [128, 8], F32) for i in two]
            ss_ = [sb(f"B_ss{i}", [128, 4], F32) for i in two]
            ybf_ = [sb(f"B_ybf{i}", [128, 512], BF16) for i in two]
            yT_ = [sb(f"B_yT{i}", [128, 4, 128], BF16) for i in two]
            ps_s = [psum(f"B_ps_s{i}", [128, 512]) for i in two]
            ps_o = [psum(f"B_ps_o{i}", [128, 1024]) for i in two]
            ps_d = psum("B_ps_d", [128, 512])
            ps_y = psum("B_ps_y", [128, 1024], BF16)
            for i in range(4):
                P.pool(lambda e, i=i: e.memset(v1_[i][:, :, 128:129], 1.0), writes=[f"v1{i}"])
            tri4 = sb("B_tri4", [128, 8, 128], F32)
            for i in range(4):
                P.pool(lambda e, i=i: e.tensor_copy(out=tri4[:, i, :], in_=trif[:]), writes=["tri4"])
                P.pool(lambda e, i=i: e.tensor_copy(out=tri4[:, 4 + i, :], in_=trib[:]), writes=["tri4"])

            def b3_load(c):
                s = c % 2
                tk = c * 128
                if c % 4 == 0:
                    s4 = (c // 4) % 2
                    P.dma(lambda e: e.dma_start(out=qT_[s4][:], in_=S["QT"][:, tk:tk + 512].rearrange("(h p) t -> p h t", p=128)), writes=[f"qT{s4}"])
                    P.dma(lambda e: e.dma_start(out=kT_[s4][:], in_=S["KT"][:, tk:tk + 512].rearrange("(h p) t -> p h t", p=128)), writes=[f"kT{s4}"])
                if c % 2 == 0:
                    s2 = (c // 2) % 2
                    P.dma(lambda e: e.dma_start(out=mb_[s2][:], in_=bass.AP(S["Mser"].tensor, S["Mser"].offset + tk, [[0, 128], [T, 8], [1, 256]])), writes=[f"mb{s2}"])
                    P.dma(lambda e: e.dma_start(out=wb_[s2][:], in_=bass.AP(S["Wser"].tensor, S["Wser"].offset + tk, [[0, 128], [T, 8], [1, 256]])), writes=[f"wb{s2}"])
                P.dma(lambda e: e.dma_start(out=og_[s][:], in_=S["Osig"][tk:tk + 128, :]), writes=[f"og{s}"])
                P.dma(lambda e: e.dma_start(out=v1_[c % 4][:, :, 0:128], in_=S["Vtok"][tk:tk + 128, :].rearrange("p (h x) -> p h x", h=4)), writes=[f"v1{c % 4}"])

            def b3_stage(c, stage):
                s = c % 2
                tk = c * 128
                s4, s2 = (c // 4) % 2, (c // 2) % 2
                o4, o2 = (c % 4) * 128, (c % 2) * 128
                qT, kT = qT_[s4][:, :, o4:o4 + 128], kT_[s4][:, :, o4:o4 + 128]
                mb, wb = mb_[s2][:, :, o2:o2 + 128], wb_[s2][:, :, o2:o2 + 128]
                og, gg = og_[s], gg_[c % 3]
                sm, ee, pp, qw, hs, hall, hsq, dd, ss, ybf, yT = sm_[s], ee_[s], pp_[s], qw_[s], hs_[s], hall_[s], hsq_[s], dd_[s], ss_[s], ybf_[s], yT_[s]
                ps, po = ps_s[s], ps_o[s]
                v1 = v1_[c % 4]
                v1t = f"v1{c % 4}"
                if stage == 1:
                    for h in range(4):
                        P.pe(lambda e, ps=ps, kT=kT, qT=qT, h=h: e.matmul(ps[:, h * 128:(h + 1) * 128], lhsT=kT[:, h, :], rhs=qT[:, h, :], start=True, stop=True),
                             reads=[f"kT{s4}", f"qT{s4}"], writes=[f"ps_s{s}"])
                    psv = ps[:].rearrange("p (h t) -> p h t", h=4)
                    P.dve(lambda e, sm=sm, psv=psv: e.tensor_tensor(out=sm[:, 0:4, :], in0=psv, in1=tri4[:, 0:4, :], op=ALU.mult),
                          reads=[f"ps_s{s}", "tri4"], writes=[f"smf{s}"])
                    P.dve(lambda e, sm=sm, psv=psv: e.tensor_tensor(out=sm[:, 4:8, :], in0=psv, in1=tri4[:, 4:8, :], op=ALU.mult),
                          reads=[f"ps_s{s}", "tri4"], writes=[f"smb{s}"])
                    for r in range(8):
                        P.act(lambda e, ee=ee, mb=mb, r=r, c=c: e.activation(out=ee[:, r, :], in_=mb[:, r, :], func=AF.Exp, bias=tokser[:, 0, c, r:r + 1], scale=-1.0),
                              reads=[f"mb{s2}", "tokser"], writes=[f"ee{s}"])
                    P.dve(lambda e, pp=pp, ee=ee, sm=sm: e.scalar_tensor_tensor(out=pp[:], in0=ee[:], scalar=1.0, in1=sm[:], op0=ALU.min, op1=ALU.mult),
                          reads=[f"ee{s}", f"smf{s}", f"smb{s}"], writes=[f"pp{s}"])
                    for d in range(2):
                        P.pool(lambda e, qw=qw, qT=qT, wb=wb, d=d: e.tensor_tensor(out=qw[:, d * 4:(d + 1) * 4, :], in0=qT, in1=wb[:, d * 4:(d + 1) * 4, :], op=ALU.mult),
                               reads=[f"qT{s4}", f"wb{s2}"], writes=[f"qw{s}"])
                    P.pool(lambda e, gg=gg, og=og: e.tensor_tensor(out=gg[:], in0=og[:], in1=mnw[:], op=ALU.mult), reads=[f"og{s}", "mnw"], writes=[f"gg{c % 3}"])
                    for r in range(8):
                        d, h = r // 4, r % 4
                        P.pe(lambda e, po=po, qw=qw, r=r, d=d, h=h, c=c: e.matmul(po[:, r * 128:(r + 1) * 128], lhsT=qw[:, r, :], rhs=cst[:, d, c, h * 129:h * 129 + 128], start=True, stop=False),
                             reads=[f"qw{s}", f"cst{d}_{c}"], writes=[f"ps_o{s}"])
                        P.pe(lambda e, po=po, pp=pp, v1=v1, r=r, h=h: e.matmul(po[:, r * 128:(r + 1) * 128], lhsT=pp[:, r, :], rhs=v1[:, h, 0:128], start=False, stop=True),
                             reads=[f"pp{s}", v1t], writes=[f"ps_o{s}"])
                        P.pe(lambda e, qw=qw, r=r, d=d, h=h, c=c, s=s: e.matmul(ps_d[:, s * 8 + r:s * 8 + r + 1], lhsT=qw[:, r, :], rhs=cst[:, d, c, h * 129 + 128:h * 129 + 129], start=True, stop=False),
                             reads=[f"qw{s}", f"cst{d}_{c}"], writes=[f"ps_d{s}"])
                        P.pe(lambda e, pp=pp, v1=v1, r=r, h=h, s=s: e.matmul(ps_d[:, s * 8 + r:s * 8 + r + 1], lhsT=pp[:, r, :], rhs=v1[:, h, 128:129], start=False, stop=True),
                             reads=[f"pp{s}", v1t], writes=[f"ps_d{s}"])
                if stage == 2:
                    P.act(lambda e, dd=dd, s=s: e.activation(out=dd[:], in_=ps_d[:, s * 8:s * 8 + 8], func=AF.Copy), reads=[f"ps_d{s}"], writes=[f"dd{s}"], tiny=True)
                    P.dve(lambda e, dd=dd: e.scalar_tensor_tensor(out=dd[:], in0=dd[:], scalar=-1.0, in1=dd[:], op0=ALU.mult, op1=ALU.max), reads=[f"dd{s}"], writes=[f"dd{s}"], tiny=True)
                    P.dve(lambda e, dd=dd, c=c: e.tensor_tensor(out=dd[:], in0=dd[:], in1=tokser[:, 2, c, :], op=ALU.max), reads=[f"dd{s}", "tokser"], writes=[f"dd{s}"], tiny=True)
                    P.dve(lambda e, dd=dd: e.reciprocal(out=dd[:], in_=dd[:]), reads=[f"dd{s}"], writes=[f"dd{s}"], tiny=True)
                    P.act(lambda e, hs=hs, po=po: e.activation(out=hs[:], in_=po[:].rearrange("p (r t) -> p r t", r=8), func=AF.Copy), reads=[f"ps_o{s}"], writes=[f"hs{s}"])
                    P.dve(lambda e, hs=hs, dd=dd: e.tensor_tensor(out=hs[:], in0=hs[:], in1=mk(dd[:, 0:1], [[1, 8], [0, 128]]), op=ALU.mult),
                          reads=[f"hs{s}", f"dd{s}"], writes=[f"hs{s}"])
                    P.pool(lambda e, hall=hall, hs=hs: e.tensor_tensor(out=hall[:].rearrange("p (h x) -> p h x", h=4), in0=hs[:, 0:4, :], in1=hs[:, 4:8, :], op=ALU.add),
                           reads=[f"hs{s}"], writes=[f"hall{s}"])
                if stage == 3:
                    P.pool(lambda e, hall=hall, hsq=hsq: e.tensor_tensor(out=hsq[:], in0=hall[:], in1=hall[:], op=ALU.mult), reads=[f"hall{s}"], writes=[f"hsq{s}"])
                    P.dve(lambda e, ss=ss, hsq=hsq: e.tensor_reduce(out=ss[:], in_=hsq[:].rearrange("p (h x) -> p h x", h=4), axis=AX.X, op=ALU.add), reads=[f"hsq{s}"], writes=[f"ss{s}"], tiny=True)
                    P.act(lambda e, ss=ss: e.activation(out=ss[:], in_=ss[:], func=AF.Ln, bias=epsb[:, 0:1]), reads=[f"ss{s}", "epsb"], writes=[f"ss{s}"], tiny=True)
                    P.act(lambda e, ss=ss: e.activation(out=ss[:], in_=ss[:], func=AF.Exp, scale=-0.5), reads=[f"ss{s}"], writes=[f"ss{s}"], tiny=True)
                    P.dve(lambda e, hsq=hsq, hall=hall, ss=ss: e.tensor_tensor(out=hsq[:].rearrange("p (h x) -> p h x", h=4), in0=hall[:].rearrange("p (h x) -> p h x", h=4), in1=mk(ss[:, 0:1], [[1, 4], [0, 128]]), op=ALU.mult),
                          reads=[f"hall{s}", f"ss{s}", f"hsq{s}"], writes=[f"hsq{s}"])
                    P.pool(lambda e, ybf=ybf, hsq=hsq, gg=gg: e.tensor_tensor(out=ybf[:], in0=hsq[:], in1=gg[:], op=ALU.mult), reads=[f"hsq{s}", f"gg{c % 3}"], writes=[f"ybf{s}"])
                    for h in range(4):
                        P.pe(lambda e, h=h, s=s, ybf=ybf: e.transpose(ps_y[:, s * 512 + h * 128: s * 512 + (h + 1) * 128], ybf[:, h * 128:(h + 1) * 128], idb[:]),
                             reads=[f"ybf{s}", "idb"], writes=[f"ps_y{s}"])
                    P.act(lambda e, yT=yT, s=s: e.activation(out=yT[:], in_=ps_y[:, s * 512:(s + 1) * 512].rearrange("p (h t) -> p h t", h=4), func=AF.Copy),
                          reads=[f"ps_y{s}"], writes=[f"yT{s}"])
                    P.dma(lambda e, yT=yT, tk=tk: e.dma_start(out=S["YT"][0:512, tk:tk + 128].rearrange("(h p) t -> p h t", p=128), in_=yT[:]),
                          reads=[f"yT{s}"], writes=["dram_yt"])

            import os
            B3D = int(os.environ.get("B3DEPTH", 3))
            if B3D == 2:
                b3_load(0)
                b3_load(1)
                b3_stage(0, 1)
                for c in range(NCH):
                    if c + 1 < NCH:
                        b3_stage(c + 1, 1)
                    if c + 2 < NCH:
                        b3_load(c + 2)
                    b3_stage(c, 2)
                    b3_stage(c, 3)
            else:
                b3_load(0)
                b3_load(1)
                b3_stage(0, 1)
                b3_stage(1, 1)
                b3_load(2)
                b3_stage(0, 2)
                for c in range(NCH):
                    if c + 2 < NCH:
                        b3_stage(c + 2, 1)
                    if c + 3 < NCH:
                        b3_load(c + 3)
                    if c + 1 < NCH:
                        b3_stage(c + 1, 2)
                    b3_stage(c, 3)
            P.emit()

    def phase_C(self, l):
        nc, I, S = self.nc, self.I, self.S
        P = Prog(self.ctx)
        with ExitStack() as es:
            def sb(name, shape, dt):
                return es.enter_context(nc.sbuf_tensor(f"L{l}" + name, list(shape), dt))

            def psum(name, shape, dt=F32):
                return es.enter_context(nc.psum_tensor(f"L{l}" + name, list(shape), dt))

            qn = sb("C_qn", [128, 4, T], BF16)
            kn = sb("C_kn", [128, 4, T], BF16)
            vn = sb("C_vn", [128, 32, 8, 65], BF16)
            nbi = sb("C_nbi", [128, 8, 5, 128], F32)
            nbe_ = [sb(f"C_nbe{i}", [128, 8, 4, 128], F32) for i in range(2)]
            idf = sb("C_idf", [128, 128], F32)
            idb = sb("C_idb", [128, 128], BF16)
            sbb_ = [sb(f"C_sb{i}", [128, 5, 128], F32) for i in range(3)]
            pt_ = [sb(f"C_pt{i}", [128, 5, 128], BF16) for i in range(3)]
            rd = sb("C_rd", [128, 8], F32)
            yn_ = [sb(f"C_yn{i}", [128, 8, 64], BF16) for i in range(2)]
            yT_ = [sb(f"C_yT{i}", [128, 4, 128], BF16) for i in range(2)]
            ps_s = [psum(f"C_ps_s{i}", [128, 1024]) for i in range(3)]
            ps_o = psum("C_ps_o", [128, 512])
            ps_y = psum("C_ps_y", [128, 1024], BF16)
            P.dma(lambda e: e.dma_start(out=idf[:], in_=I["ident"]), writes=["idf"])
            P.dve(lambda e: e.tensor_copy(out=idb[:], in_=idf[:]), reads=["idf"], writes=["idb"])
            P.dma(lambda e: e.dma_start(out=nbi[:], in_=I["nbi"][l]), writes=["nbi"])
            for g in range(4):
                P.dma(lambda e, g=g: e.dma_start(out=qn[:, g, :], in_=S["QnT"][g * 128:(g + 1) * 128, :]), writes=["qn"])
                P.dma(lambda e, g=g: e.dma_start(out=kn[:, g, :], in_=S["KnT"][g * 128:(g + 1) * 128, :]), writes=["kn"])
            P.pool(lambda e: e.memset(vn[:, :, :, 64:65], 1.0), writes=["vn"])
            for m in range(32):
                P.dma(lambda e, m=m: e.dma_start(out=vn[:, m, :, 0:64], in_=S["Vn"][m * 128:(m + 1) * 128, :].rearrange("p (h x) -> p h x", h=8)), writes=["vn"])
            def blk(i):
                if i < 2:
                    return [0, 1, 2, 3], i
                if i >= 30:
                    return [28, 29, 30, 31], i - 28
                return [i - 2, i - 1, i, i + 1, i + 2], None

            import os
            NU = int(os.environ.get("CNU", 32 * 8))

            def st_S(u):
                i, h = u // 8, u % 8
                tiles, eidx = blk(i)
                g, hp = h // 2, (h % 2) * 64
                ps = ps_s[u % 3]
                for j, m in enumerate(tiles):
                    P.pe(lambda e, ps=ps, j=j, m=m, g=g, hp=hp, i=i: e.matmul(ps[:, j * 128:(j + 1) * 128], lhsT=kn[hp:hp + 64, g, m * 128:(m + 1) * 128], rhs=qn[hp:hp + 64, g, i * 128:(i + 1) * 128], start=True, stop=True),
                         reads=["kn", "qn"], writes=[f"ps_s{u % 3}"])

            def st_E(u):
                i, h = u // 8, u % 8
                tiles, eidx = blk(i)
                nt = len(tiles)
                ps = ps_s[u % 3]
                sbb, pt = sbb_[u % 3], pt_[u % 3]
                if eidx is not None and h == 0:
                    P.dma(lambda e, eidx=eidx, i=i: e.dma_start(out=nbe_[i % 2][:], in_=I["nbe"][l, eidx]), writes=[f"nbe{i % 2}"])
                bias = nbi[:, h, :, :] if eidx is None else nbe_[i % 2][:, h, :, :]
                btok = "nbi" if eidx is None else f"nbe{i % 2}"
                P.dve(lambda e, ps=ps, sbb=sbb, bias=bias, nt=nt: e.tensor_tensor(out=sbb[:, 0:nt, :], in0=ps[:, 0:nt * 128].rearrange("p (j x) -> p j x", x=128), in1=bias, op=ALU.add),
                      reads=[f"ps_s{u % 3}", btok], writes=[f"sb{u % 3}"])
                P.act(lambda e, sbb=sbb, pt=pt, nt=nt: e.activation(out=pt[:, 0:nt, :], in_=sbb[:, 0:nt, :], func=AF.Exp), reads=[f"sb{u % 3}"], writes=[f"pt{u % 3}"])

            def st_PV(u):
                i, h = u // 8, u % 8
                tiles, eidx = blk(i)
                nt = len(tiles)
                pt = pt_[u % 3]
                so = (u % 4) * 128
                sot = f"ps_o{u % 4}"
                for j, m in enumerate(tiles):
                    P.pe(lambda e, pt=pt, j=j, m=m, h=h, nt=nt, so=so: e.matmul(ps_o[:, so:so + 65], lhsT=pt[:, j, :], rhs=vn[:, m, h, :], start=(j == 0), stop=(j == nt - 1)),
                         reads=[f"pt{u % 3}", "vn"], writes=[sot])
                P.dve(lambda e, h=h, so=so: e.reciprocal(out=rd[:, h:h + 1], in_=ps_o[:, so + 64:so + 65]), reads=[sot], writes=[f"rd{h}"], tiny=True)
                P.act(lambda e, h=h, so=so, i=i: e.activation(out=yn_[i % 2][:, h, :], in_=ps_o[:, so:so + 64], func=AF.Copy, scale=rd[:, h:h + 1]), reads=[sot, f"rd{h}"], writes=[f"yn{i % 2}"])
                if h == 7:
                    sl = i % 2
                    yT = yT_[sl]
                    for g in range(4):
                        P.pe(lambda e, g=g, sl=sl: e.transpose(ps_y[:, sl * 512 + g * 128:sl * 512 + (g + 1) * 128], yn_[sl][:, 2 * g:2 * g + 2, :].rearrange("p a b -> p (a b)"), idb[:]),
                             reads=[f"yn{sl}", "idb"], writes=[f"ps_y{sl}"])
                    P.dve(lambda e, yT=yT, sl=sl: e.tensor_copy(out=yT[:], in_=ps_y[:, sl * 512:(sl + 1) * 512].rearrange("p (h t) -> p h t", h=4)),
                          reads=[f"ps_y{sl}"], writes=[f"yT{sl}"])
                    P.dma(lambda e, yT=yT, i=i: e.dma_start(out=S["YT"][512:1024, i * 128:(i + 1) * 128].rearrange("(h p) t -> p h t", p=128), in_=yT[:]),
                          reads=[f"yT{sl}"], writes=["dram_yt"])

            CD = int(os.environ.get("CDEPTH", 1))
            if CD == 0:
                for u in range(NU):
                    st_S(u)
                    st_E(u)
                    st_PV(u)
            elif CD == 1:
                st_S(0)
                for u in range(NU):
                    if u + 1 < NU:
                        st_S(u + 1)
                    st_E(u)
                    st_PV(u)
            else:
                st_S(0)
                st_S(1)
                st_E(0)
                for u in range(NU):
                    if u + 2 < NU:
                        st_S(u + 2)
                    if u + 1 < NU:
                        st_E(u + 1)
                    st_PV(u)
            P.emit()

    def phase_D(self, l):
        nc, I, S = self.nc, self.I, self.S
        P = Prog(self.ctx)
        xsrc = I["xT"] if l == 0 else S["xres"]
        self.esE = ExitStack()
        esE = self.esE
        w1 = esE.enter_context(nc.sbuf_tensor(f"L{l}E_w1", [128, 8, DFF], BF16))
        w2 = esE.enter_context(nc.sbuf_tensor(f"L{l}E_w2", [128, 32, D], BF16))
        n2 = esE.enter_context(nc.sbuf_tensor(f"L{l}E_n2", [128, DEPTH, 8], F32))
        self.Ew = (w1, w2, n2)
        with ExitStack() as es:
            def sb(name, shape, dt):
                return es.enter_context(nc.sbuf_tensor(f"L{l}" + name, list(shape), dt))

            def psum(name, shape, dt=F32):
                return es.enter_context(nc.psum_tensor(f"L{l}" + name, list(shape), dt))

            wbf = sb("D_wbf", [128, 8, D], BF16)
            stE = [sb(f"D_stE{i}", [128, 1024], F32) for i in range(3)]
            xin = [sb(f"D_xin{i}", [128, 8, TB], F32) for i in range(2)]
            yin = [sb(f"D_yin{i}", [128, 8, TB], BF16) for i in range(2)]
            pso = [psum(f"D_ps{i}", [128, 512]) for i in range(4)]
            P.dma(lambda e: e.dma_start(out=n2[:], in_=I["n2"]), writes=["n2"])
            for kc in range(8):
                st = stE[kc % 3]
                stt = f"stE{kc % 3}"
                P.dma(lambda e, st=st, kc=kc: e.dma_start(out=st[:, 0:512], in_=I["w_out"][l, kc * 128:(kc + 1) * 128, 0:512]), writes=[stt + "a"])
                P.dma(lambda e, st=st, kc=kc: e.dma_start(out=st[:, 512:1024], in_=I["w_out"][l, kc * 128:(kc + 1) * 128, 512:1024]), writes=[stt + "b"])
                P.pool(lambda e, st=st, kc=kc: e.tensor_copy(out=wbf[:, kc, 0:512], in_=st[:, 0:512]), reads=[stt + "a"], writes=[f"wbf{kc}_0"])
                P.act(lambda e, st=st, kc=kc: e.activation(out=wbf[:, kc, 512:1024], in_=st[:, 512:1024], func=AF.Copy), reads=[stt + "b"], writes=[f"wbf{kc}_1"])
            WB = [f"wbf{kc}_{hf}" for kc in range(8) for hf in range(2)]

            def prefetch(pc):
                pj = pc + 8
                st = stE[pj % 3]
                ta, tb_ = f"stE{pj % 3}a", f"stE{pj % 3}b"
                if pc < 32:
                    kc, hf = pc // 4, pc % 4
                    c0 = hf * 1024
                    P.dma(lambda e: e.dma_start(out=st[:, 0:512], in_=I["w_ff1"][l, kc * 128:(kc + 1) * 128, c0:c0 + 512]), writes=[ta])
                    P.dma(lambda e: e.dma_start(out=st[:, 512:1024], in_=I["w_ff1"][l, kc * 128:(kc + 1) * 128, c0 + 512:c0 + 1024]), writes=[tb_])
                    P.pool(lambda e: e.tensor_scalar(out=w1[:, kc, c0:c0 + 512], in0=st[:, 0:512], scalar1=n2[:, l, kc:kc + 1], scalar2=None, op0=ALU.mult),
                           reads=[ta, "n2"], writes=[f"w1_{pc}a"])
                    P.act(lambda e: e.activation(out=w1[:, kc, c0 + 512:c0 + 1024], in_=st[:, 512:1024], func=AF.Copy, scale=n2[:, l, kc:kc + 1]),
                          reads=[tb_, "n2"], writes=[f"w1_{pc}b"])
                else:
                    k2 = pc - 32
                    P.dma(lambda e: e.dma_start(out=st[:, 0:512], in_=I["w_ff2"][l, k2 * 128:(k2 + 1) * 128, 0:512]), writes=[ta])
                    P.dma(lambda e: e.dma_start(out=st[:, 512:1024], in_=I["w_ff2"][l, k2 * 128:(k2 + 1) * 128, 512:1024]), writes=[tb_])
                    P.pool(lambda e: e.tensor_copy(out=w2[:, k2, 0:512], in_=st[:, 0:512]), reads=[ta], writes=[f"w2_{k2}a"])
                    P.act(lambda e: e.activation(out=w2[:, k2, 512:1024], in_=st[:, 512:1024], func=AF.Copy), reads=[tb_], writes=[f"w2_{k2}b"])

            def load(b):
                t0 = b * TB
                P.dma(lambda e, b=b, t0=t0: e.dma_start(out=xin[b % 2][:], in_=xsrc[:, t0:t0 + TB].rearrange("(kc p) t -> p kc t", p=128)), writes=[f"xin{b % 2}"])
                P.dma(lambda e, b=b, t0=t0: e.dma_start(out=yin[b % 2][:], in_=S["YT"][:, t0:t0 + TB].rearrange("(kc p) t -> p kc t", p=128)), writes=[f"yin{b % 2}"])

            load(0)
            pi = 0
            for b in range(NB):
                if b + 1 < NB:
                    load(b + 1)
                for pc in range(8 * b, 8 * b + 8):
                    prefetch(pc)
                t0 = b * TB
                xi, yi = xin[b % 2], yin[b % 2]
                for dc in range(8):
                    ps = pso[pi % 4]
                    pst = f"ps{pi % 4}"
                    pi += 1
                    for kc in range(8):
                        P.pe(lambda e, ps=ps, kc=kc, dc=dc, yi=yi: e.matmul(ps[:], lhsT=wbf[:, kc, dc * 128:(dc + 1) * 128], rhs=yi[:, kc, :], start=(kc == 0), stop=(kc == 7)),
                             reads=WB + [f"yin{b % 2}"], writes=[pst])
                    P.dve(lambda e, ps=ps, dc=dc, xi=xi: e.tensor_tensor(out=xi[:, dc, :], in0=xi[:, dc, :], in1=ps[:], op=ALU.add),
                          reads=[pst, f"xin{b % 2}"], writes=[f"xin{b % 2}"])
                P.dma(lambda e, xi=xi, t0=t0: e.dma_start(out=S["xres"][:, t0:t0 + TB].rearrange("(kc p) t -> p kc t", p=128), in_=xi[:]),
                      reads=[f"xin{b % 2}"], writes=["dram_x"])
            P.emit()

    def phase_E(self, l):
        nc, I, S = self.nc, self.I, self.S
        P = Prog(self.ctx)
        last = (l == DEPTH - 1)
        w1, w2, n2 = self.Ew
        with ExitStack() as es:
            def sb(name, shape, dt):
                return es.enter_context(nc.sbuf_tensor(f"L{l}" + name, list(shape), dt))

            def psum(name, shape, dt=F32):
                return es.enter_context(nc.psum_tensor(f"L{l}" + name, list(shape), dt))

            nf = sb("E_nf", [128, 8], F32)
            ones = sb("E_ones", [128, 128], BF16)
            epsb = sb("E_epsb", [128, 1], F32)
            xin = [sb(f"E_xin{i}", [128, 8, TB], F32) for i in range(2)]
            xsq = sb("E_xsq", [128, 8, TB], BF16)
            rstd = sb("E_rstd", [128, TB], F32)
            hn = sb("E_hn", [128, 8, TB], BF16)
            uu = [sb(f"E_u{i}", [128, 8, TB], BF16) for i in range(2)]
            rtmp = [sb(f"E_rtmp{i}", [128, TB], F32) for i in range(2)]
            ps_ssq = psum("E_ps_ssq", [128, 512])
            ps1 = [psum(f"E_ps1_{i}", [128, 512]) for i in range(3)]
            ps2 = [psum(f"E_ps2_{i}", [128, 512]) for i in range(3)]
            P.dma(lambda e: e.dma_start(out=nf[:], in_=I["nf"]), writes=["nf"])
            P.dve(lambda e: e.tensor_scalar(out=nf[:], in0=nf[:], scalar1=32.0, scalar2=None, op0=ALU.mult), reads=["nf"], writes=["nf"], tiny=True)
            P.dve(lambda e: e.memset(ones[:], 1.0), writes=["ones"])
            P.dve(lambda e: e.memset(epsb[:], float(D * EPS)), writes=["epsb"], tiny=True)
            cnt = {"p1": 0, "p2": 0}
            HN = [f"hn{kc}" for kc in range(8)]

            def load(b):
                t0 = b * TB
                P.dma(lambda e, b=b, t0=t0: e.dma_start(out=xin[b % 2][:], in_=S["xres"][:, t0:t0 + TB].rearrange("(kc p) t -> p kc t", p=128)), writes=[f"xin{b % 2}"])

            def rms(xi, xt):
                P.act(lambda e: e.activation(out=xsq[:], in_=xi[:], func=AF.Square), reads=[xt], writes=["xsq"])
                for kc in range(8):
                    P.pe(lambda e, kc=kc: e.matmul(ps_ssq[:], lhsT=ones[:], rhs=xsq[:, kc, :], start=(kc == 0), stop=(kc == 7)), reads=["ones", "xsq"], writes=["ps_ssq"])
                P.act(lambda e: e.activation(out=rstd[:], in_=ps_ssq[:], func=AF.Ln, bias=epsb[:, 0:1]), reads=["ps_ssq", "epsb"], writes=["rstd"])
                P.act(lambda e: e.activation(out=rstd[:], in_=rstd[:], func=AF.Exp, scale=-0.5), reads=["rstd"], writes=["rstd"])

            def prep(b):
                xi = xin[b % 2]
                xt = f"xin{b % 2}"
                rms(xi, xt)
                for kc in range(8):
                    P.dve(lambda e, kc=kc, xi=xi: e.scalar_tensor_tensor(out=hn[:, kc, :], in0=xi[:, kc, :], scalar=32.0, in1=rstd[:], op0=ALU.mult, op1=ALU.mult),
                          reads=[xt, "rstd"], writes=[f"hn{kc}"])

            def S1(b, grp):
                gi = (b * 4 + grp) % 2
                u = uu[gi]
                ut = f"u{gi}"
                for fj in range(8):
                    fc = grp * 8 + fj
                    ps = ps1[cnt["p1"] % 3]
                    pst = f"ps1_{cnt['p1'] % 3}"
                    rt = rtmp[cnt["p1"] % 2]
                    rtt = f"rtmp{cnt['p1'] % 2}"
                    cnt["p1"] += 1
                    for kc in range(8):
                        P.pe(lambda e, ps=ps, kc=kc, fc=fc: e.matmul(ps[:], lhsT=w1[:, kc, fc * 128:(fc + 1) * 128], rhs=hn[:, kc, :], start=(kc == 0), stop=(kc == 7)),
                             reads=HN, writes=[pst])
                    P.act(lambda e, ps=ps, rt=rt: e.activation(out=rt[:], in_=ps[:], func=AF.Relu), reads=[pst], writes=[rtt])
                    eng = P.dve if fj % 2 == 0 else P.pool
                    eng(lambda e, u=u, fj=fj, rt=rt: e.tensor_tensor(out=u[:, fj, :], in0=rt[:], in1=rt[:], op=ALU.mult), reads=[rtt], writes=[f"{ut}_{fj}"])

            def S2(b, grp):
                gi = (b * 4 + grp) % 2
                u = uu[gi]
                ut = f"u{gi}"
                xi = xin[b % 2]
                xt = f"xin{b % 2}"
                UT = [f"{ut}_{fj}" for fj in range(8)]
                for dc in range(8):
                    ps = ps2[cnt["p2"] % 3]
                    pst = f"ps2_{cnt['p2'] % 3}"
                    cnt["p2"] += 1
                    for fj in range(8):
                        P.pe(lambda e, ps=ps, fj=fj, grp=grp, dc=dc, u=u: e.matmul(ps[:], lhsT=w2[:, grp * 8 + fj, dc * 128:(dc + 1) * 128], rhs=u[:, fj, :], start=(fj == 0), stop=(fj == 7)),
                             reads=UT, writes=[pst])
                    P.dve(lambda e, ps=ps, dc=dc, xi=xi: e.tensor_tensor(out=xi[:, dc, :], in0=xi[:, dc, :], in1=ps[:], op=ALU.add),
                          reads=[pst, xt], writes=[xt])

            def fin(b):
                t0 = b * TB
                xi = xin[b % 2]
                xt = f"xin{b % 2}"
                if not last:
                    P.dma(lambda e, xi=xi, t0=t0: e.dma_start(out=S["xres"][:, t0:t0 + TB].rearrange("(kc p) t -> p kc t", p=128), in_=xi[:]),
                          reads=[xt], writes=["dram_x"])
                else:
                    rms(xi, xt)
                    for kc in range(8):
                        P.dve(lambda e, kc=kc, xi=xi: e.scalar_tensor_tensor(out=xi[:, kc, :], in0=xi[:, kc, :], scalar=nf[:, kc:kc + 1], in1=rstd[:], op0=ALU.mult, op1=ALU.mult),
                              reads=[xt, "rstd", "nf"], writes=[xt])
                    P.dma(lambda e, xi=xi, t0=t0: e.dma_start(out=self.out[:, t0:t0 + TB].rearrange("(kc p) t -> p kc t", p=128), in_=xi[:]),
                          reads=[xt], writes=["dram_out"])

            load(0)
            prep(0)
            S1(0, 0)
            for b in range(NB):
                if b + 1 < NB:
                    load(b + 1)
                for grp in range(4):
                    if grp < 3:
                        S1(b, grp + 1)
                    elif b + 1 < NB:
                        prep(b + 1)
                        S1(b + 1, 0)
                    S2(b, grp)
                fin(b)
            P.emit()
        self.esE.close()


def _na_bias_tables(rpb_l):
    H = rpb_l.shape[0]
    kl = np.arange(128)
    kr_l, kc = kl // 64, kl % 64
    ql = np.arange(128)
    qr_l, qc = ql // 64, ql % 64

    def tile(i, m):
        kr = 2 * m + kr_l[:, None]
        qr = 2 * i + qr_l[None, :]
        rs = np.clip(qr - 4, 0, 56)
        vr = (kr >= rs) & (kr < rs + 8)
        cs = np.clip(qc[None, :] - 8, 0, 48)
        vc = (kc[:, None] >= cs) & (kc[:, None] < cs + 16)
        ri = np.clip(kr - qr + 7, 0, 14)
        ci = np.clip(kc[:, None] - qc[None, :] + 15, 0, 30)
        vals = rpb_l[:, ri, ci]
        return np.where((vr & vc)[None], vals, np.float32(-30000.0)).astype(np.float32)

    nbi = np.stack([tile(2, m) for m in range(5)], axis=1)
    nbi = np.ascontiguousarray(nbi.transpose(2, 0, 1, 3))
    nbe = []
    for i, ms in ((0, range(4)), (1, range(4)), (30, range(28, 32)), (31, range(28, 32))):
        t = np.stack([tile(i, m) for m in ms], axis=1)
        nbe.append(t.transpose(2, 0, 1, 3))
    nbe = np.ascontiguousarray(np.stack(nbe, axis=0))
    return nbi, nbe


def _prep_shared(norm1_w, conv_w, conv_b, gate_b, mlstm_norm_w, rpb, norm2_w, final_norm_w):
    f = np.float32
    sh = {}
    sh["n1"] = np.ascontiguousarray(norm1_w.reshape(DEPTH, 8, 128).transpose(2, 0, 1)).astype(f)
    sh["n2"] = np.ascontiguousarray(norm2_w.reshape(DEPTH, 8, 128).transpose(2, 0, 1)).astype(f)
    sh["nf"] = np.ascontiguousarray(final_norm_w.reshape(8, 128).T).astype(f)
    sh["cw"] = np.ascontiguousarray(conv_w.reshape(DEPTH, 3, 8, 128).transpose(3, 0, 1, 2)).astype(f)
    sh["cb"] = np.ascontiguousarray(conv_b.reshape(DEPTH, 8, 128).transpose(2, 0, 1)).astype(f)
    gb = np.zeros((36, DEPTH, 2), f)
    g4 = gate_b.reshape(DEPTH, 4, 4)
    gb[0:4, :, 0] = g4[:, 0, :].T
    gb[0:4, :, 1] = g4[:, 1, :].T
    gb[32:36, :, 0] = g4[:, 2, :].T
    gb[32:36, :, 1] = g4[:, 3, :].T
    sh["gb"] = gb
    sh["mnw"] = np.ascontiguousarray(np.broadcast_to(mlstm_norm_w[None], (128, DEPTH, 512))).astype(f)
    nbi, nbe = zip(*[_na_bias_tables(np.asarray(rpb[l], f)) for l in range(DEPTH)])
    sh["nbi"] = np.stack(nbi, 0)
    sh["nbe"] = np.stack(nbe, 0)
    sh["ident"] = np.eye(128, dtype=f)
    sh["trif"] = np.triu(np.ones((128, 128), f))
    sh["trib"] = np.tril(np.ones((128, 128), f))
    return sh


_CACHE = {}


def _get_nc(debug=False, stop_after=None):
    key = (debug, stop_after)
    if key not in _CACHE:
        b = Builder(debug=debug, stop_after=stop_after)
        nc = b.build()
        _CACHE[key] = (nc, b)
    return _CACHE[key]


def run(inputs, debug=False, stop_after=None, cores=8):
    x = np.asarray(inputs["x"], np.float32)
    sh = _prep_shared(*[np.asarray(inputs[k], np.float32) for k in
                        ("norm1_w", "conv_w", "conv_b", "gate_b", "mlstm_norm_w", "rpb", "norm2_w", "final_norm_w")])
    for k in ("w_in", "w_out", "w_ff1", "w_ff2"):
        sh[k] = np.ascontiguousarray(np.asarray(inputs[k], np.float32))
    nc, b = _get_nc(debug, stop_after)
    in_maps = []
    for c in range(cores):
        m = dict(sh)
        m["xT"] = np.ascontiguousarray(x[c].T)
        in_maps.append(m)
    res = run_bass_kernel_spmd(nc, in_maps, core_ids=list(range(cores)))
    return res, b


def kernel(x, norm1_w, w_in, conv_w, conv_b, gate_b, mlstm_norm_w, rpb, w_out,
           norm2_w, w_ff1, w_ff2, final_norm_w):
    inputs = dict(x=x, norm1_w=norm1_w, w_in=w_in, conv_w=conv_w, conv_b=conv_b, gate_b=gate_b,
                  mlstm_norm_w=mlstm_norm_w, rpb=rpb, w_out=w_out, norm2_w=norm2_w, w_ff1=w_ff1,
                  w_ff2=w_ff2, final_norm_w=final_norm_w)
    res, _ = run(inputs)
    out = np.stack([np.ascontiguousarray(r["outT"].T) for r in res.results], axis=0)
    return out.astype(np.float32)
```
